# Optimizing a Trainium2 kernel written in Bass

```python
import math
import jax, jax.numpy as jnp
from jax import lax
import numpy as np

D_MODEL = 2048
BATCH = 32
SEQ = 256
DEPTH = 2
DEC_BATCH = 4
DEC_SEQ = 4096
PAST_LEN = 256

GRID_W = 64
HEAD_DIM = 128
N_HEADS_A = D_MODEL // (2 * HEAD_DIM)
N_HEADS_B = D_MODEL // (2 * HEAD_DIM)
WIDTH_A = N_HEADS_A * HEAD_DIM
WIDTH_B = N_HEADS_B * HEAD_DIM
NA_ROWS = 8
NA_COLS = 16
CONV_W = 3
CHUNK = 64
HEAD_DIM_C = 64
N_HEADS_C = D_MODEL // HEAD_DIM_C
N_KV_C = N_HEADS_C // 8
WINDOW = 128
Q_BLOCK = 128
D_FF = 4 * D_MODEL
ROPE_BASE = 10000.0
EPS = 1e-6
N_AB_LAYERS = (DEPTH + 1) // 2
N_C_LAYERS = DEPTH // 2
AB_SPLITS = [WIDTH_A, WIDTH_A, WIDTH_A, 3 * WIDTH_B, WIDTH_B, N_HEADS_B, N_HEADS_B, N_HEADS_B, N_HEADS_B]
AB_IN = sum(AB_SPLITS)
C_SPLITS = [N_HEADS_C * HEAD_DIM_C, N_KV_C * HEAD_DIM_C, N_KV_C * HEAD_DIM_C]
C_IN = sum(C_SPLITS)

kernel_name = 'hybrid_flow_prefix_trunk_step'


def split_cols(x, sizes):
    idx = np.cumsum(sizes)[:-1].tolist()
    return jnp.split(x, idx, axis=-1)


def rms_norm(x, g):
    xf = x.astype(jnp.float32)
    y = xf * lax.rsqrt(jnp.mean(xf * xf, axis=-1, keepdims=True) + EPS)
    return (y * g.astype(jnp.float32)).astype(x.dtype)


def l2norm(x):
    xf = x.astype(jnp.float32)
    return xf * lax.rsqrt(jnp.sum(xf * xf, axis=-1, keepdims=True) + EPS)


def ada_params(cond, w, b):
    m = jnp.dot(jax.nn.silu(cond), w) + b
    return jnp.split(m[:, None, :], 6, axis=-1)


def modulate(x, g, shift, scale):
    return rms_norm(x, g) * (1.0 + scale) + shift


def sq_relu_mlp(h, w1, w2):
    return jnp.dot(jnp.square(jax.nn.relu(jnp.dot(h, w1))), w2)


def axial_rope(x):
    b, t, h, d = x.shape
    nf = d // 4
    inv = ROPE_BASE ** (-jnp.arange(nf, dtype=jnp.float32) / nf)
    tok = jnp.arange(t)
    pos = jnp.stack([tok // GRID_W, tok % GRID_W], axis=-1).astype(jnp.float32)
    ang = pos[:, :, None] * inv
    cos = jnp.cos(ang)[None, :, None]
    sin = jnp.sin(ang)[None, :, None]
    xr = x.astype(jnp.float32).reshape(b, t, h, 2, 2, nf)
    x1, x2 = xr[..., 0, :], xr[..., 1, :]
    out = jnp.stack([x1 * cos - x2 * sin, x2 * cos + x1 * sin], axis=-2)
    return out.reshape(b, t, h, d).astype(x.dtype)


def context_attention(q, k, v, sink):
    b, L, hq, d = q.shape
    hkv = k.shape[2]
    g = hq // hkv
    nb = L // Q_BLOCK
    scale = d ** -0.5
    qb = jnp.moveaxis(q.reshape(b, nb, Q_BLOCK, hkv, g, d), 1, 0)

    def block(qi):
        s = jnp.einsum('bqkgd,bskd->bkgqs', qi, k).astype(jnp.float32) * scale
        if sink is not None:
            snk = jnp.broadcast_to(sink.astype(jnp.float32).reshape(1, hkv, g, 1, 1), s.shape[:-1] + (1,))
            s = jnp.concatenate([s, snk], axis=-1)
        p = jax.nn.softmax(s, axis=-1)[..., :L].astype(v.dtype)
        return jnp.einsum('bkgqs,bskd->bqkgd', p, v)

    o = lax.map(block, qb)
    return jnp.moveaxis(o, 0, 1).reshape(b, L, hq, d)


def neighbourhood_attention(q, k, v, ctx_k, ctx_v, rel_bias):
    b, t, h, d = q.shape
    rows = t // GRID_W
    kr = min(NA_ROWS, rows)
    kc = NA_COLS
    n_loc = kr * kc
    scale = d ** -0.5
    qg = q.reshape(b, rows, GRID_W, h, d)
    kg = k.reshape(b, rows, GRID_W, h, d)
    vg = v.reshape(b, rows, GRID_W, h, d)
    cols = jnp.arange(GRID_W)
    col_start = jnp.clip(cols - kc // 2, 0, GRID_W - kc)
    col_idx = col_start[:, None] + jnp.arange(kc)
    dc = col_idx - cols[:, None] + (NA_COLS - 1)
    row_ids = jnp.arange(rows)
    row_start = jnp.clip(row_ids - kr // 2, 0, rows - kr)

    def one_row(args):
        r, rs, q_r = args
        k_rows = lax.dynamic_slice_in_dim(kg, rs, kr, axis=1)
        v_rows = lax.dynamic_slice_in_dim(vg, rs, kr, axis=1)
        k_nb = k_rows[:, :, col_idx]
        v_nb = v_rows[:, :, col_idx]
        dr = rs + jnp.arange(kr) - r + (NA_ROWS - 1)
        bias = rel_bias.astype(jnp.float32)[:, dr[:, None, None], dc[None, :, :]]
        s_loc = jnp.einsum('bwhd,bxwchd->bhwxc', q_r, k_nb).astype(jnp.float32) * scale
        s_loc = (s_loc + bias.transpose(0, 2, 1, 3)[None]).reshape(b, h, GRID_W, n_loc)
        s_ctx = jnp.einsum('bwhd,bshd->bhws', q_r, ctx_k).astype(jnp.float32) * scale
        p = jax.nn.softmax(jnp.concatenate([s_loc, s_ctx], axis=-1), axis=-1).astype(v.dtype)
        p_loc = p[..., :n_loc].reshape(b, h, GRID_W, kr, kc)
        o = jnp.einsum('bhwxc,bxwchd->bwhd', p_loc, v_nb)
        return o + jnp.einsum('bhws,bshd->bwhd', p[..., n_loc:], ctx_v)

    o = lax.map(one_row, (row_ids, row_start, jnp.moveaxis(qg, 1, 0)))
    return jnp.moveaxis(o, 0, 1).reshape(b, t, h, d)


def windowed_attention(q, k, v, ctx_k, ctx_v, sink):
    b, t, hq, d = q.shape
    hkv = k.shape[2]
    g = hq // hkv
    nb = t // Q_BLOCK
    n_loc = 3 * Q_BLOCK
    n_ctx = ctx_k.shape[1]
    scale = d ** -0.5
    pad = ((0, 0), (Q_BLOCK, Q_BLOCK), (0, 0), (0, 0))
    kp = jnp.pad(k, pad)
    vp = jnp.pad(v, pad)
    qb = jnp.moveaxis(q.reshape(b, nb, Q_BLOCK, hkv, g, d), 1, 0)
    q_off = jnp.arange(Q_BLOCK)
    k_off = jnp.arange(n_loc) - Q_BLOCK
    band = jnp.abs(k_off[None, :] - q_off[:, None]) <= WINDOW
    sink_l = sink.astype(jnp.float32).reshape(1, hkv, g, 1, 1)

    def block(args):
        i, qi = args
        ki = lax.dynamic_slice_in_dim(kp, i * Q_BLOCK, n_loc, axis=1)
        vi = lax.dynamic_slice_in_dim(vp, i * Q_BLOCK, n_loc, axis=1)
        kpos = i * Q_BLOCK + k_off
        valid = band & ((kpos >= 0) & (kpos < t))[None, :]
        s_loc = jnp.einsum('bqkgd,bskd->bkgqs', qi, ki).astype(jnp.float32) * scale
        s_loc = jnp.where(valid, s_loc, -jnp.inf)
        s_ctx = jnp.einsum('bqkgd,bskd->bkgqs', qi, ctx_k).astype(jnp.float32) * scale
        s_snk = jnp.broadcast_to(sink_l, s_loc.shape[:-1] + (1,))
        p = jax.nn.softmax(jnp.concatenate([s_loc, s_ctx, s_snk], axis=-1), axis=-1).astype(v.dtype)
        o = jnp.einsum('bkgqs,bskd->bqkgd', p[..., :n_loc], vi)
        return o + jnp.einsum('bkgqs,bskd->bqkgd', p[..., n_loc:n_loc + n_ctx], ctx_v)

    o = lax.map(block, (jnp.arange(nb), qb))
    return jnp.moveaxis(o, 0, 1).reshape(b, t, hq, d)


def centred_depthwise_conv(x, w):
    pad = CONV_W // 2
    return lax.conv_general_dilated(x, w[:, None, :].astype(x.dtype), window_strides=(1,),
                                    padding=[(pad, pad)], dimension_numbers=('NWC', 'WIO', 'NWC'),
                                    feature_group_count=x.shape[-1])


def gated_delta_rule(q, k, v, beta, g, s0):
    b, L, h, dk = q.shape
    n = L // CHUNK

    def chunks(x):
        return jnp.moveaxis(x.reshape((b, n, CHUNK, h) + x.shape[3:]), 3, 1)

    q, k, v, beta, g = (chunks(a) for a in (q, k, v, beta, g))
    gc = jnp.cumsum(g, axis=-1)
    tri = jnp.tril(jnp.ones((CHUNK, CHUNK), dtype=bool))
    strict = jnp.tril(jnp.ones((CHUNK, CHUNK), dtype=bool), -1)
    decay = jnp.exp(jnp.where(tri, gc[..., :, None] - gc[..., None, :], -jnp.inf))
    kb = k * beta[..., None]
    m = jnp.where(strict, jnp.einsum('bhncd,bhnsd->bhncs', kb, k) * decay, 0.0)
    eye = jnp.eye(CHUNK, dtype=jnp.float32)
    t_inv = lax.linalg.triangular_solve(eye + m, jnp.broadcast_to(eye, m.shape), left_side=True,
                                        lower=True, unit_diagonal=True)
    u = jnp.einsum('bhncs,bhnsv->bhncv', t_inv, v * beta[..., None])
    w = jnp.einsum('bhncs,bhnsk->bhnck', t_inv, kb * jnp.exp(gc)[..., None])
    qk = jnp.einsum('bhncd,bhnsd->bhncs', q, k) * decay
    q_dec = q * jnp.exp(gc)[..., None]
    g_last = gc[..., -1:]
    k_dec = k * jnp.exp(g_last - gc)[..., None]
    c_dec = jnp.exp(g_last[..., 0])
    xs = tuple(jnp.moveaxis(a, 2, 0) for a in (u, w, qk, q_dec, k_dec, c_dec))

    def step(s, inp):
        u_i, w_i, qk_i, qd_i, kd_i, cd_i = inp
        v_new = u_i - jnp.einsum('bhck,bhkv->bhcv', w_i, s)
        o = jnp.einsum('bhck,bhkv->bhcv', qd_i, s) + jnp.einsum('bhcs,bhsv->bhcv', qk_i, v_new)
        s = s * cd_i[..., None, None] + jnp.einsum('bhck,bhcv->bhkv', kd_i, v_new)
        return s, o

    s_final, o = lax.scan(step, s0, xs)
    o = jnp.moveaxis(jnp.moveaxis(o, 0, 2), 1, 3)
    return o.reshape(b, L, h, v.shape[-1]), s_final


def deltanet_mixer(qkv, z, beta_f, beta_b, a_f, a_b, conv_w, a_log, dt_bias, onorm_g, s0_f, s0_b):
    b, L, _ = qkv.shape
    qkv = jax.nn.silu(centred_depthwise_conv(qkv, conv_w))
    q, k, v = jnp.split(qkv, 3, axis=-1)
    heads = lambda x: x.reshape(b, L, N_HEADS_B, HEAD_DIM)
    q = l2norm(heads(q)) * (HEAD_DIM ** -0.5)
    k = l2norm(heads(k))
    v = heads(v).astype(jnp.float32)

    def gates(beta_raw, a_raw, dr):
        beta = jax.nn.sigmoid(beta_raw.astype(jnp.float32))
        g = -jnp.exp(a_log[dr].astype(jnp.float32)) * jax.nn.softplus(
            a_raw.astype(jnp.float32) + dt_bias[dr].astype(jnp.float32))
        return beta, g

    bf, gf = gates(beta_f, a_f, 0)
    bb, gb = gates(beta_b, a_b, 1)
    flip = lambda x: jnp.flip(x, axis=1)
    o_f, s_f = gated_delta_rule(q, k, v, bf, gf, s0_f.astype(jnp.float32))
    o_b, s_b = gated_delta_rule(flip(q), flip(k), flip(v), flip(bb), flip(gb), s0_b.astype(jnp.float32))
    o = o_f + flip(o_b)
    o = rms_norm(o, onorm_g) * jax.nn.silu(heads(z).astype(jnp.float32))
    return o.reshape(b, L, WIDTH_B).astype(qkv.dtype), s_f, s_b


def ab_mixer(h, attend_a, s0_f, s0_b, w_in, w_out, conv_w, a_log, dt_bias, onorm_g):
    b, L, _ = h.shape
    qa, ka, va, qkv_b, z, bf, bb, af, ab = split_cols(jnp.dot(h, w_in), AB_SPLITS)
    hd = lambda x: x.reshape(b, L, N_HEADS_A, HEAD_DIM)
    qa, ka, va = hd(qa), hd(ka), hd(va)
    o_a = attend_a(qa, ka, va).reshape(b, L, WIDTH_A)
    o_b, s_f, s_b = deltanet_mixer(qkv_b, z, bf, bb, af, ab, conv_w, a_log, dt_bias, onorm_g, s0_f, s0_b)
    out = jnp.dot(jnp.concatenate([o_a, o_b.astype(o_a.dtype)], axis=-1), w_out)
    return out, ka, va, s_f, s_b


def c_mixer(h, attend, w_qkv, w_out):
    b, L, _ = h.shape
    q, k, v = split_cols(jnp.dot(h, w_qkv), C_SPLITS)
    q = q.reshape(b, L, N_HEADS_C, HEAD_DIM_C)
    k = k.reshape(b, L, N_KV_C, HEAD_DIM_C)
    v = v.reshape(b, L, N_KV_C, HEAD_DIM_C)
    o = attend(q, k, v).reshape(b, L, N_HEADS_C * HEAD_DIM_C)
    return jnp.dot(o, w_out), k, v


def setup_inputs(seed: int = 0) -> dict:
    key = jax.random.key(seed)
    ks = jax.random.split(key, 32)
    D = D_MODEL
    nrm = lambda kk, shape, s: jax.random.normal(kk, shape, jnp.float32) * s
    dt = jnp.exp(jax.random.uniform(ks[21], (N_AB_LAYERS, 2, N_HEADS_B), jnp.float32,
                                    minval=math.log(1e-3), maxval=math.log(1e-1)))
    return {
        'x_prompt': nrm(ks[0], (BATCH, SEQ, D), 1.0),
        'x_sample': nrm(ks[1], (DEC_BATCH, DEC_SEQ, D), 1.0),
        'cache_a_k': nrm(ks[2], (DEC_BATCH, N_AB_LAYERS, PAST_LEN, N_HEADS_A, HEAD_DIM), 1.0),
        'cache_a_v': nrm(ks[3], (DEC_BATCH, N_AB_LAYERS, PAST_LEN, N_HEADS_A, HEAD_DIM), 1.0),
        'state_b_fwd': nrm(ks[4], (DEC_BATCH, N_AB_LAYERS, N_HEADS_B, HEAD_DIM, HEAD_DIM), 0.3),
        'state_b_bwd': nrm(ks[5], (DEC_BATCH, N_AB_LAYERS, N_HEADS_B, HEAD_DIM, HEAD_DIM), 0.3),
        'cache_c_k': nrm(ks[6], (DEC_BATCH, N_C_LAYERS, PAST_LEN, N_KV_C, HEAD_DIM_C), 1.0),
        'cache_c_v': nrm(ks[7], (DEC_BATCH, N_C_LAYERS, PAST_LEN, N_KV_C, HEAD_DIM_C), 1.0),
        'c': nrm(ks[8], (DEC_BATCH, D), 1.0),
        'c_ctx': nrm(ks[9], (D,), 1.0),
        'w_ada': nrm(ks[10], (DEPTH, D, 6 * D), 0.5 * D ** -0.5),
        'b_ada': nrm(ks[11], (DEPTH, 6 * D), 0.02),
        'norm_mix': 1.0 + nrm(ks[12], (DEPTH, D), 0.02),
        'norm_mlp': 1.0 + nrm(ks[13], (DEPTH, D), 0.02),
        'w_mlp_in': nrm(ks[14], (DEPTH, D, D_FF), D ** -0.5),
        'w_mlp_out': nrm(ks[15], (DEPTH, D_FF, D), D_FF ** -0.5),
        'ab_w_in': nrm(ks[16], (N_AB_LAYERS, D, AB_IN), D ** -0.5),
        'ab_w_out': nrm(ks[17], (N_AB_LAYERS, WIDTH_A + WIDTH_B, D), (WIDTH_A + WIDTH_B) ** -0.5),
        'a_rel_bias': nrm(ks[18], (N_AB_LAYERS, N_HEADS_A, 2 * NA_ROWS - 1, 2 * NA_COLS - 1), 0.1),
        'b_conv': nrm(ks[19], (N_AB_LAYERS, CONV_W, 3 * WIDTH_B), CONV_W ** -0.5),
        'b_a_log': jnp.log(jax.random.uniform(ks[20], (N_AB_LAYERS, 2, N_HEADS_B), jnp.float32, minval=1.0, maxval=16.0)),
        'b_dt_bias': dt + jnp.log(-jnp.expm1(-dt)),
        'b_out_norm': 1.0 + nrm(ks[22], (N_AB_LAYERS, HEAD_DIM), 0.02),
        'c_w_qkv': nrm(ks[23], (N_C_LAYERS, D, C_IN), D ** -0.5),
        'c_w_out': nrm(ks[24], (N_C_LAYERS, N_HEADS_C * HEAD_DIM_C, D), (N_HEADS_C * HEAD_DIM_C) ** -0.5),
        'c_sink': nrm(ks[25], (N_C_LAYERS, N_HEADS_C), 0.5),
        'final_norm': 1.0 + nrm(ks[26], (D,), 0.02),
    }


def reference(x_prompt, x_sample, cache_a_k, cache_a_v, state_b_fwd, state_b_bwd, cache_c_k, cache_c_v,
              c, c_ctx, w_ada, b_ada, norm_mix, norm_mlp, w_mlp_in, w_mlp_out, ab_w_in, ab_w_out,
              a_rel_bias, b_conv, b_a_log, b_dt_bias, b_out_norm, c_w_qkv, c_w_out, c_sink, final_norm):
    xp, xs = x_prompt, x_sample
    cond_ctx = c_ctx[None, :]
    new_a_k, new_a_v, new_b_fwd, new_b_bwd, new_c_k, new_c_v = [], [], [], [], [], []
    for layer in range(DEPTH):
        j = layer // 2
        mod_p = ada_params(cond_ctx, w_ada[layer], b_ada[layer])
        mod_s = ada_params(c, w_ada[layer], b_ada[layer])
        hp = modulate(xp, norm_mix[layer], mod_p[0], mod_p[1])
        hs = modulate(xs, norm_mix[layer], mod_s[0], mod_s[1])
        if layer % 2 == 0:
            ab_w = (ab_w_in[j], ab_w_out[j], b_conv[j], b_a_log[j], b_dt_bias[j], b_out_norm[j])
            zeros = jnp.zeros((xp.shape[0], N_HEADS_B, HEAD_DIM, HEAD_DIM), jnp.float32)
            op, ka, va, sf, sb = ab_mixer(hp, lambda q, k, v: context_attention(q, k, v, None),
                                          zeros, zeros, *ab_w)
            rel, ck, cv = a_rel_bias[j], cache_a_k[:, j], cache_a_v[:, j]
            os_, _, _, _, _ = ab_mixer(hs, lambda q, k, v: neighbourhood_attention(q, k, v, ck, cv, rel),
                                       state_b_fwd[:, j], state_b_bwd[:, j], *ab_w)
            new_a_k.append(ka)
            new_a_v.append(va)
            new_b_fwd.append(sf.astype(xp.dtype))
            new_b_bwd.append(sb.astype(xp.dtype))
        else:
            sink, ck, cv = c_sink[j], cache_c_k[:, j], cache_c_v[:, j]
            op, kc, vc = c_mixer(hp, lambda q, k, v: context_attention(q, k, v, sink), c_w_qkv[j], c_w_out[j])
            os_, _, _ = c_mixer(hs, lambda q, k, v: windowed_attention(axial_rope(q), axial_rope(k), v, ck, cv, sink),
                                c_w_qkv[j], c_w_out[j])
            new_c_k.append(kc)
            new_c_v.append(vc)
        xp = xp + mod_p[2] * op
        xs = xs + mod_s[2] * os_
        xp = xp + mod_p[5] * sq_relu_mlp(modulate(xp, norm_mlp[layer], mod_p[3], mod_p[4]), w_mlp_in[layer], w_mlp_out[layer])
        xs = xs + mod_s[5] * sq_relu_mlp(modulate(xs, norm_mlp[layer], mod_s[3], mod_s[4]), w_mlp_in[layer], w_mlp_out[layer])
    y_prompt = rms_norm(xp, final_norm)
    y_sample = rms_norm(xs, final_norm)
    return (y_prompt, y_sample, jnp.stack(new_a_k, axis=1), jnp.stack(new_a_v, axis=1),
            jnp.stack(new_b_fwd, axis=1), jnp.stack(new_b_bwd, axis=1),
            jnp.stack(new_c_k, axis=1), jnp.stack(new_c_v, axis=1))
```

```python
import contextlib
import numpy as np
import concourse.bass as bass
import concourse.mybir as mybir

F32 = mybir.dt.float32
BF16 = mybir.dt.bfloat16
I32 = mybir.dt.int32
AF = mybir.ActivationFunctionType
ALU = mybir.AluOpType
AX = mybir.AxisListType

ENGS = ("pe", "act", "dve", "pool", "sp")
HANDLES = {"pe": "tensor", "act": "scalar", "dve": "vector", "pool": "gpsimd", "sp": "sync"}
SEM_LIMIT = 16000


class Tile:
    __slots__ = ("name", "writer", "readers", "excl")

    def __init__(self, name=""):
        self.name = name
        self.writer = None
        self.readers = {}
        self.excl = False


class DSem:
    def __init__(self, prog, name):
        self.prog = prog
        self.name = name
        self.gen = 0
        self.h = prog.nc.alloc_semaphore(name=name)
        self.count = 0
        self.last = None

    def bump(self):
        if self.count + 16 > SEM_LIMIT:
            self.gen += 1
            self.h = self.prog.nc.alloc_semaphore(name=f"{self.name}_g{self.gen}")
            self.count = 0
        self.count += 16
        return self.h, self.count


class Ins:
    __slots__ = ("eng", "fn", "deps", "sem", "count", "needed", "is_dma", "epoch")

    def __init__(self, eng, fn, is_dma=False):
        self.eng = eng
        self.fn = fn
        self.deps = []
        self.sem = None
        self.count = None
        self.needed = False
        self.is_dma = is_dma
        self.epoch = 0


class Prog:
    def __init__(self, nc):
        self.nc = nc
        self.lists = {e: [] for e in ENGS}
        self.esem = {e: nc.alloc_semaphore(name=f"es_{e}_0") for e in ENGS}
        self.esem_gen = {e: 0 for e in ENGS}
        self.ecount = {e: 0 for e in ENGS}
        self.known = {e: {} for e in ENGS}
        self.epoch = 0
        self.n_ins = 0
        self.dsems = []
        self.last_ins = {e: None for e in ENGS}

    def tile(self, name=""):
        return Tile(name)

    def tiles(self, n, name=""):
        return [Tile(f"{name}{i}") for i in range(n)]

    def dsem(self, name):
        d = DSem(self, name)
        self.dsems.append(d)
        return d

    def _add(self, eng, fn, reads, writes, dsem=None):
        ins = Ins(eng, fn, is_dma=dsem is not None)
        ins.epoch = self.epoch
        deps = []
        for t in reads:
            if t.writer is not None:
                deps.append(t.writer)
            if t.excl:
                deps.extend(r for r in t.readers.values() if r.eng != eng)
        for t in writes:
            if t.writer is not None:
                deps.append(t.writer)
            deps.extend(t.readers.values())
        if dsem is not None:
            if dsem.last is not None:
                deps.append(dsem.last)
            ins.sem, ins.count = dsem.bump()
            ins.needed = True
            dsem.last = ins
        out = []
        seen = set()
        for d in deps:
            if d is ins or id(d) in seen:
                continue
            seen.add(id(d))
            if d.epoch < self.epoch:
                continue
            if eng == "pe" and d.eng == "pe" and not d.is_dma and dsem is None:
                continue
            out.append(d)
        ins.deps = out
        for d in out:
            d.needed = True
        for t in reads:
            key = (eng, dsem.name) if dsem is not None else eng
            t.readers[key] = ins
        for t in writes:
            t.writer = ins
            t.readers = {}
        self.lists[eng].append(ins)
        self.last_ins[eng] = ins
        self.n_ins += 1
        return ins

    def op(self, eng, fn, reads=(), writes=()):
        return self._add(eng, fn, list(reads), list(writes))

    def dma(self, eng, out, in_, dsem, reads=(), writes=()):
        return self._add(eng, lambda e: e.dma_start(out=out, in_=in_), list(reads), list(writes), dsem=dsem)

    def barrier(self):
        deps = [i for i in self.last_ins.values() if i is not None]
        deps += [d.last for d in self.dsems if d.last is not None]
        deps = [d for d in deps if d.epoch == self.epoch]
        for d in deps:
            d.needed = True
        for e in ENGS:
            ins = Ins(e, None)
            ins.epoch = self.epoch
            ins.deps = list(deps)
            self.lists[e].append(ins)

    def flush(self, final=False):
        self.barrier()
        for e in ENGS:
            for ins in self.lists[e]:
                if ins.is_dma or ins.fn is None:
                    continue
                if ins.needed:
                    if self.ecount[e] + 1 > SEM_LIMIT:
                        self.esem_gen[e] += 1
                        self.esem[e] = self.nc.alloc_semaphore(name=f"es_{e}_{self.esem_gen[e]}")
                        self.ecount[e] = 0
                    self.ecount[e] += 1
                    ins.sem = self.esem[e]
                    ins.count = self.ecount[e]
        prog = self

        def run(e, h):
            known = prog.known[e]
            for ins in prog.lists[e]:
                need = {}
                for d in ins.deps:
                    k = id(d.sem)
                    if known.get(k, 0) >= d.count:
                        continue
                    if k not in need or need[k][1] < d.count:
                        need[k] = (d.sem, d.count)
                for k, (s, v) in need.items():
                    h.wait_ge(s, v)
                    known[k] = v
                if ins.fn is None:
                    continue
                bi = ins.fn(h)
                if ins.is_dma:
                    bi.then_inc(ins.sem, 16)
                elif ins.needed:
                    bi.then_inc(ins.sem, 1)

        with self.nc.Block() as block:
            for e in ENGS:
                if not self.lists[e]:
                    continue
                dec = getattr(block, HANDLES[e])

                def mk(e):
                    def _f(h):
                        run(e, h)
                    return _f
                dec(mk(e))
        self.lists = {e: [] for e in ENGS}
        self.epoch += 1

from concourse.bass_utils import run_bass_kernel_spmd

D = 2048
KC = 16
EPS = 1e-6
NPT = 8
NS1 = 19
NG1 = NPT + NS1
NS2 = 13
TOK1 = NG1 * 128
TOKB = (NG1 + NS2) * 128
SEXT = 17


class Ring:
    def __init__(self, K, name, shape, dt, n, sw=False):
        self.bufs = [K.sb(f"{name}{i}", shape, dt) for i in range(n)]
        self.tiles = [Tile(f"{name}{i}") for i in range(n)]
        self.sems = [K.getsem(sw) for i in range(n)]
        self.i = 0

    def next(self):
        k = self.i % len(self.bufs)
        self.i += 1
        return self.bufs[k], self.tiles[k], self.sems[k]


class Ctx:
    def __init__(self, nc):
        self.nc = nc
        self.P = Prog(nc)
        self.es = None
        self.pes = contextlib.ExitStack()
        self.uid = 0
        self.sem_pool = {False: [], True: []}
        self.sem_used = {False: [], True: []}
        self.PS = [nc.alloc_psum_tensor(f"psb{i}", [128, 512], F32) for i in range(8)]
        self.TPS = [Tile(f"ps{i}") for i in range(8)]
        for t_ in self.TPS:
            t_.excl = True
        self.psi = [0, 0]
        self.din = {}
        self.dout = {}
        self.dscr = {}

    def inp(self, name, shape, dt=F32):
        self.din[name] = self.nc.dram_tensor(name, list(shape), dt, kind="ExternalInput").ap()
        return self.din[name]

    def outp(self, name, shape, dt=F32):
        self.dout[name] = self.nc.dram_tensor(name, list(shape), dt, kind="ExternalOutput").ap()
        return self.dout[name]

    def scr(self, name, shape, dt):
        self.dscr[name] = self.nc.dram_tensor(name, list(shape), dt).ap()
        return self.dscr[name]

    def begin(self):
        self.es = contextlib.ExitStack()

    def end(self):
        self.P.flush()
        self.es.close()
        self.es = None
        for k in (False, True):
            self.sem_pool[k].extend(self.sem_used[k])
            self.sem_used[k] = []

    def sb(self, name, shape, dt):
        self.uid += 1
        return self.es.enter_context(self.nc.sbuf_tensor(f"{name}_{self.uid}", list(shape), dt))

    def psb(self, name, shape, dt):
        self.uid += 1
        return self.pes.enter_context(self.nc.sbuf_tensor(f"{name}_{self.uid}", list(shape), dt))

    def getsem(self, sw=False):
        if self.sem_pool[sw]:
            s = self.sem_pool[sw].pop()
        else:
            self.uid += 1
            s = self.P.dsem(f"ds{'w' if sw else 'h'}{self.uid}")
        self.sem_used[sw].append(s)
        return s

    def dump(self, name, ap, tiles):
        import os
        if not os.environ.get("DN_DEBUG") or name in self.dscr:
            return
        d = self.scr(name, list(ap.shape), ap.dtype)
        self.P.dma("sp", d, ap, self.getsem(), reads=tiles)

    def ps(self, g=0):
        k = g * 4 + self.psi[g] % 4
        self.psi[g] += 1
        return self.PS[k], self.TPS[k]


def rows_T(K, dst, T_dst, src2d, n, stage, T_stage, sem):
    P = K.P
    P.dma("sp", stage[0:n, :], src2d, sem, writes=[T_stage])
    ps, Tp = K.ps()
    P.op("pe", lambda e: e.transpose(ps[:, 0:n], stage[0:n, :], K.identf[0:n, 0:n]), reads=[T_stage, K.T_const], writes=[Tp])
    P.op("dve", lambda e: e.tensor_copy(dst, ps[:, 0:n]), reads=[Tp], writes=[T_dst])


def phase_consts(K):
    P = K.P
    K.begin()
    K.T_const = Tile("const")
    K.identf = K.psb("identf", [128, 128], F32)
    K.identb = K.psb("identb", [128, 128], BF16)
    K.onesf = K.psb("onesf", [128, 128], F32)
    K.onesb = K.psb("onesb", [128, 128], BF16)
    s = K.getsem()
    s2 = K.getsem(True)
    P.dma("sp", K.identf[:], K.din["ident"], s, writes=[K.T_const])
    P.dma("pool", K.identb[:], K.din["ident"], s2, writes=[K.T_const])
    P.op("dve", lambda e: e.memset(K.onesf[:], 1.0), writes=[K.T_const])
    P.op("dve", lambda e: e.memset(K.onesb[:], 1.0), writes=[K.T_const])
    K.end()


def phase_ada(K):
    nc, P = K.nc, K.P
    modrow_d = K.scr("modrow_d", [2, 2, 6 * D], F32)
    K.begin()
    cond = K.sb("cond", [2, D], F32)
    cs = K.sb("cs", [2, D], F32)
    condT = K.sb("condT", [128, KC, 2], BF16)
    brow = K.sb("brow", [2, 6 * D], F32)
    modrow = K.sb("modrow", [2, 6 * D], F32)
    T_cond, T_cs, T_condT, T_brow, T_modrow, T_mrd = P.tiles(6, "ada")
    s0, s1 = K.getsem(), K.getsem()
    wr = Ring(K, "adaw", [128, KC, 512], BF16, 3, sw=True)
    P.dma("sp", cond[:], K.din["cond"], s0, writes=[T_cond])
    P.op("act", lambda e: e.activation(out=cs[:], in_=cond[:], func=AF.Silu), reads=[T_cond], writes=[T_cs])
    ps, Tp = K.ps()
    for kc in range(KC):
        P.op("pe", lambda e, kc=kc, ps=ps: e.transpose(ps[:, kc * 2:kc * 2 + 2], cs[0:2, kc * 128:(kc + 1) * 128], K.identf[0:2, 0:2]),
             reads=[T_cs, K.T_const], writes=[Tp])
    P.op("dve", lambda e, ps=ps: e.tensor_copy(condT[:].rearrange("p k c -> p (k c)"), ps[:, 0:2 * KC]), reads=[Tp], writes=[T_condT])
    for l in range(2):
        P.dma("sp", brow[:], K.din["b_ada"][l:l + 1, :].partition_broadcast(2), s0, writes=[T_brow])
        for cg in range(24):
            wt, Tw, sw = wr.next()
            P.dma("pool", wt[:], K.din["w_ada"][l, :, cg * 512:(cg + 1) * 512].rearrange("(kc p) n -> p kc n", p=128), sw, writes=[Tw])
            ps, Tp = K.ps()
            for kc in range(KC):
                P.op("pe", lambda e, kc=kc, ps=ps, wt=wt: e.matmul(ps[0:2, :], condT[:, kc, :], wt[:, kc, :], start=(kc == 0), stop=(kc == KC - 1)),
                     reads=[T_condT, Tw], writes=[Tp])
            P.op("dve", lambda e, ps=ps, cg=cg: e.tensor_tensor(modrow[0:2, cg * 512:(cg + 1) * 512], ps[0:2, :], brow[0:2, cg * 512:(cg + 1) * 512], ALU.add),
                 reads=[Tp, T_brow], writes=[T_modrow])
        P.dma("sp", modrow_d[l], modrow[:], s1, reads=[T_modrow], writes=[T_mrd])
    K.end()
    K.begin()
    K.T_mod = Tile("mod")
    K.modF = [[K.psb(f"modF{l}{c}", [128, 96], F32) for c in range(2)] for l in range(2)]
    K.gsF = [[[K.psb(f"gsF{l}{w}{c}", [128, KC], F32) for c in range(2)] for w in range(2)] for l in range(2)]
    K.fnorm = K.psb("fnormF", [128, KC], F32)
    stage = K.sb("stg", [128, 128], F32)
    gF = K.sb("gF", [128, KC], F32)
    T_stage, T_g = P.tiles(2, "adaf")
    s0 = K.getsem()
    for l in range(2):
        for c in range(2):
            rows_T(K, K.modF[l][c][:], K.T_mod, modrow_d[l, c].rearrange("(r p) -> r p", p=128), 96, stage, T_stage, s0)
        for w, nm in enumerate(("norm_mix", "norm_mlp")):
            rows_T(K, gF[:], T_g, K.din[nm][l].rearrange("(r p) -> r p", p=128), KC, stage, T_stage, s0)
            for c in range(2):
                sc = K.modF[l][c][:, (1 + 3 * w) * KC:(2 + 3 * w) * KC]
                P.op("dve", lambda e, l=l, w=w, c=c, sc=sc: e.scalar_tensor_tensor(K.gsF[l][w][c][:], sc, 1.0, gF[:], ALU.add, ALU.mult),
                     reads=[K.T_mod, T_g], writes=[K.T_mod])
    K.end()


class NormT:
    def __init__(self, K, n=2):
        self.K = K
        self.ss = Ring(K, "nss", [128, 2], F32, n)
        self.xn = Ring(K, "nxn", [128, D], BF16, n)

    def run(self, xt, T_x, gs, shift, dst_fn, T_dst_fn):
        K = self.K
        P = K.P
        ss, T_ss, _ = self.ss.next()
        xn, T_xn, _ = self.xn.next()
        P.op("act", lambda e: e.activation(out=xn[:], in_=xt, func=AF.Square, accum_out=ss[:, 0:1]), reads=[T_x], writes=[T_xn, T_ss])
        import os
        NTL = int(os.environ.get("NT_LEVEL", "9"))
        if NTL < 2:
            return
        P.op("act", lambda e: e.activation(out=ss[:, 1:2], in_=ss[:, 0:1], func=AF.Sqrt, bias=EPS, scale=1.0 / D), reads=[T_ss], writes=[T_ss])
        P.op("dve", lambda e: e.reciprocal(ss[:, 1:2], ss[:, 1:2]), reads=[T_ss], writes=[T_ss])
        if NTL < 3:
            return
        P.op("act", lambda e: e.activation(out=xn[:], in_=xt, func=AF.Identity, scale=ss[:, 1:2]), reads=[T_x, T_ss], writes=[T_xn])
        if NTL < 4:
            return
        for half in range(2):
            ps, Tp = K.ps()
            psb = ps[:].bitcast(BF16)
            for j in range(8):
                kc = half * 8 + j
                P.op("pe", lambda e, j=j, kc=kc, psb=psb: e.transpose(psb[:, j * 128:(j + 1) * 128], xn[:, kc * 128:(kc + 1) * 128], K.identb[:]),
                     reads=[T_xn, K.T_const], writes=[Tp])
            for j in range(8):
                kc = half * 8 + j
                if True:
                    P.op("dve", lambda e, j=j, kc=kc, psb=psb: e.tensor_scalar(dst_fn(kc), psb[:, j * 128:(j + 1) * 128], gs[:, kc:kc + 1], shift[:, kc:kc + 1], ALU.mult, ALU.add),
                         reads=[Tp, K.T_mod], writes=[T_dst_fn(kc)])
                else:
                    P.op("act", lambda e, j=j, kc=kc, psb=psb: e.activation(out=dst_fn(kc), in_=psb[:, j * 128:(j + 1) * 128], func=AF.Identity, scale=gs[:, kc:kc + 1], bias=shift[:, kc:kc + 1]),
                         reads=[Tp, K.T_mod], writes=[T_dst_fn(kc)])


def token_groups(n_tb, breaks=()):
    out = []
    pts = [0] + list(breaks) + [n_tb]
    for a, b in zip(pts[:-1], pts[1:]):
        t = a
        while t < b:
            m = min(4, b - t)
            out.append((t, m))
            t += m
    return out


def phase_l0_inproj(K, which_pass):
    nc, P = K.nc, K.P
    if which_pass == 1:
        K.scr("qaT_d", [1024, TOK1], BF16)
        K.scr("kaT_d", [1024, TOK1], BF16)
        K.scr("va_d", [TOK1, 1024], BF16)
        K.scr("qkvT_d", [3072, TOKB], F32)
        K.scr("z_d", [TOK1, 1024], F32)
        K.scr("gates_d", [TOKB, 32], F32)
        ntb = NG1
        srcs = [(K.din["xp"][g * 128:(g + 1) * 128, :], 0) for g in range(NPT)] + \
               [(K.din["xs"][g * 128:(g + 1) * 128, :], 1) for g in range(NS1)]
        tok0 = 0
        groups = token_groups(NG1, breaks=(NPT,))
    else:
        ntb = NS2
        srcs = [(K.din["xs"][(NS1 + g) * 128:(NS1 + g + 1) * 128, :], 1) for g in range(NS2)]
        tok0 = TOK1
        groups = token_groups(NS2)
    K.begin()
    hT = K.sb("hT", [128, KC, ntb * 128], BF16)
    T_h = [[Tile(f"h{g}_{kc}") for kc in range(KC)] for g in range(ntb)]
    xr = Ring(K, "xin", [128, D], F32, 3)
    nt = NormT(K)
    for g, (src, c) in enumerate(srcs):
        xt, T_x, sx = xr.next()
        P.dma("sp", xt[:], src, sx, writes=[T_x])
        nt.run(xt[:], T_x, K.gsF[0][0][c], K.modF[0][c][:, 0:KC],
               lambda kc, g=g: hT[:, kc, g * 128:(g + 1) * 128], lambda kc, g=g: T_h[g][kc])
    wr = Ring(K, "w0", [128, KC, 512], BF16, 3, sw=True)
    st32 = Ring(K, "st32", [128, 512], F32, 3)
    st16 = Ring(K, "st16", [128, 512], BF16, 3)
    W = K.din["ab_w_in"]
    evi = [0]

    def evac(dst, src, T_src, T_dst):
        evi[0] += 1
        if evi[0] % 2:
            P.op("dve", lambda e: e.tensor_copy(dst, src), reads=[T_src], writes=[T_dst])
        else:
            P.op("act", lambda e: e.copy(dst, src), reads=[T_src], writes=[T_dst])

    def fm(wt, Tw, ncol_blocks, dst_d, row0, f32):
        for sub in range(ncol_blocks):
            for (t0, m) in groups:
                n = m * 128
                ps, Tp = K.ps()
                for kc in range(KC):
                    P.op("pe", lambda e, kc=kc, ps=ps, sub=sub, t0=t0, n=n: e.matmul(ps[:, 0:n], wt[:, kc, sub * 128:(sub + 1) * 128], hT[:, kc, t0 * 128:t0 * 128 + n], start=(kc == 0), stop=(kc == KC - 1)),
                         reads=[Tw] + [T_h[t0 + i][kc] for i in range(m)], writes=[Tp])
                sg, Ts, ss_ = (st32 if f32 else st16).next()
                evac(sg[:, 0:n], ps[:, 0:n], Tp, Ts)
                P.dma("sp", dst_d[row0 + sub * 128:row0 + (sub + 1) * 128, tok0 + t0 * 128:tok0 + t0 * 128 + n], sg[:, 0:n], ss_, reads=[Ts])

    def tm(wt, Tw, ncols, tbs, dests):
        for g in tbs:
            ps, Tp = K.ps()
            for kc in range(KC):
                P.op("pe", lambda e, kc=kc, ps=ps, g=g: e.matmul(ps[:, 0:ncols], hT[:, kc, g * 128:(g + 1) * 128], wt[:, kc, 0:ncols], start=(kc == 0), stop=(kc == KC - 1)),
                     reads=[Tw, T_h[g][kc]], writes=[Tp])
            for (dfn, f32) in dests:
                d = dfn(g)
                if d is None:
                    continue
                sg, Ts, ss_ = (st32 if f32 else st16).next()
                evac(sg[:, 0:ncols], ps[:, 0:ncols], Tp, Ts)
                P.dma("sp", d, sg[:, 0:ncols], ss_, reads=[Ts])

    wg32 = K.sb("wg32", [128, KC, 32], F32)
    T_wg32 = Tile("wg32")
    s_wg = K.getsem()

    def loadw(src, ncols=512):
        wt, Tw, sw = wr.next()
        if ncols == 512:
            P.dma("pool", wt[:, :, 0:ncols], src.rearrange("(kc p) n -> p kc n", p=128), sw, writes=[Tw])
        else:
            P.dma("sp", wg32[:], src.rearrange("(kc p) n -> p kc n", p=128), s_wg, writes=[T_wg32])
            P.op("dve", lambda e, wt=wt: e.tensor_copy(wt[:, :, 0:ncols], wg32[:]), reads=[T_wg32], writes=[Tw])
        return wt, Tw

    S = K.dscr
    if which_pass == 1:
        import os
        for t in range(int(os.environ.get('KDBG_NT', '14'))):
            wt, Tw = loadw(W[:, t * 512:(t + 1) * 512])
            if t < 2:
                fm(wt, Tw, 4, S["qaT_d"], t * 512, False)
            elif t < 4:
                fm(wt, Tw, 4, S["kaT_d"], (t - 2) * 512, False)
                tm(wt, Tw, 512, range(NPT), [(lambda g, t=t: K.dout["nak"][g * 128:(g + 1) * 128, (t - 2) * 512:(t - 1) * 512], True)])
            elif t < 6:
                tm(wt, Tw, 512, range(NG1), [(lambda g, t=t: S["va_d"][g * 128:(g + 1) * 128, (t - 4) * 512:(t - 3) * 512], False),
                                            (lambda g, t=t: K.dout["nav"][g * 128:(g + 1) * 128, (t - 4) * 512:(t - 3) * 512] if g < NPT else None, True)])
            elif t < 12:
                fm(wt, Tw, 4, S["qkvT_d"], (t - 6) * 512, True)
            else:
                tm(wt, Tw, 512, range(NG1), [(lambda g, t=t: S["z_d"][g * 128:(g + 1) * 128, (t - 12) * 512:(t - 11) * 512], True)])
        if int(os.environ.get('KDBG_G', '1')):
          wt, Tw = loadw(K.din["w_gates"], 32)
          tm(wt, Tw, 32, range(NG1), [(lambda g: S["gates_d"][g * 128:(g + 1) * 128, :], True)])
    else:
        for t in range(8, 12):
            wt, Tw = loadw(W[:, t * 512:(t + 1) * 512])
            fm(wt, Tw, 4, S["qkvT_d"], (t - 6) * 512, True)
        wt, Tw = loadw(K.din["w_gates"], 32)
        tm(wt, Tw, 32, range(NS2), [(lambda g: S["gates_d"][tok0 + g * 128:tok0 + (g + 1) * 128, :], True)])
    K.end()


NCAT = NPT + SEXT
TOKC = NCAT * 128


def attn_core(K, S_list, nq, rhs_q, out_ap, T_out, scale, extra_den=None, pools=None, g4=False):
    P = K.P
    q_ap, q_tiles = rhs_q
    M = out_ap.shape[0]
    psn, Tn = K.ps(1)
    psd, Td = K.ps(1)
    pt_ring, rec_ring = pools
    n = len(S_list)
    v3 = (lambda ap: ap.rearrange("p (g t) -> p g t", g=4)) if g4 else (lambda ap: ap)
    for i, (lk, tk, lv, tv, mask, tm_, _) in enumerate(S_list):
        pss, Ts = K.ps(0)
        P.op("pe", lambda e, pss=pss, lk=lk: e.matmul(v3(pss[:, 0:nq]), lk, q_ap, start=True, stop=True), reads=tk + q_tiles, writes=[Ts])
        pt, Tpt, _ = pt_ring.next()
        P.op("act", lambda e, pss=pss, pt=pt: e.activation(out=pt[:, 0:nq], in_=pss[:, 0:nq], func=AF.Exp, scale=scale), reads=[Ts], writes=[Tpt])
        if mask is not None:
            P.op("pool", lambda e, pt=pt, mask=mask: e.tensor_tensor(v3(pt[:, 0:nq]), v3(pt[:, 0:nq]), mask, ALU.mult), reads=[Tpt] + tm_, writes=[Tpt])
        P.op("pe", lambda e, pt=pt, lv=lv, i=i: e.matmul(psn[0:M, 0:nq], lv, pt[:, 0:nq], start=(i == 0), stop=(i == n - 1)), reads=tv + [Tpt], writes=[Tn])
        P.op("pe", lambda e, pt=pt, i=i: e.matmul(psd[0:M, 0:nq], K.onesb[:, 0:M], pt[:, 0:nq], start=(i == 0), stop=(i == n - 1)), reads=[K.T_const, Tpt], writes=[Td])
    rec, Trec, _ = rec_ring.next()
    if extra_den is not None:
        ed, ted = extra_den
        if g4:
            P.op("dve", lambda e: e.tensor_tensor(v3(rec[0:M, 0:nq]), v3(psd[0:M, 0:nq]), ed, ALU.add), reads=[Td] + ted, writes=[Trec])
        else:
            P.op("dve", lambda e: e.tensor_scalar_add(rec[0:M, 0:nq], psd[0:M, 0:nq], ed), reads=[Td] + ted, writes=[Trec])
        P.op("dve", lambda e: e.reciprocal(rec[0:M, 0:nq], rec[0:M, 0:nq]), reads=[Trec], writes=[Trec])
    else:
        P.op("dve", lambda e: e.reciprocal(rec[0:M, 0:nq], psd[0:M, 0:nq]), reads=[Td], writes=[Trec])
    P.op("dve", lambda e: e.tensor_tensor(out_ap, v3(psn[0:M, 0:nq]), v3(rec[0:M, 0:nq]), ALU.mult), reads=[Tn, Trec], writes=[T_out])


def phase_attn_a(K):
    nc, P = K.nc, K.P
    S = K.dscr
    catT = K.scr("catT_d", [D, TOKC], BF16)
    scale = 128 ** -0.5
    K.begin()
    pt_ring = Ring(K, "pt", [128, 256], BF16, 4)
    rec_ring = Ring(K, "rec", [128, 256], F32, 2)
    pools = (pt_ring, rec_ring)
    qr = Ring(K, "cq", [128, 8, 256], BF16, 2)
    kr = Ring(K, "ck", [128, 8, 256], BF16, 2)
    vr = Ring(K, "cv", [128, 2, 1024], BF16, 2)
    orr = Ring(K, "co", [128, 8, 256], BF16, 2)
    for s in range(4):
        qt, Tq, sq = qr.next()
        kt, Tk, sk = kr.next()
        vt, Tv, sv = vr.next()
        ot, To, so = orr.next()
        P.dma("sp", qt[:], S["qaT_d"][:, s * 256:(s + 1) * 256].rearrange("(h p) t -> p h t", p=128), sq, writes=[Tq])
        P.dma("sp", kt[:], S["kaT_d"][:, s * 256:(s + 1) * 256].rearrange("(h p) t -> p h t", p=128), sk, writes=[Tk])
        P.dma("sp", vt[:], S["va_d"][s * 256:(s + 1) * 256, :].rearrange("(c p) f -> p c f", p=128), sv, writes=[Tv])
        for h in range(8):
            sl = [(kt[:, h, c * 128:(c + 1) * 128], [Tk], vt[:, c, h * 128:(h + 1) * 128], [Tv], None, [], 0) for c in range(2)]
            attn_core(K, sl, 256, (qt[:, h, :], [Tq]), ot[:, h, :], To, scale, pools=pools)
        P.dma("sp", catT[0:1024, s * 256:(s + 1) * 256].rearrange("(h p) t -> p h t", p=128), ot[:], so, reads=[To])
    ck_tm = K.sb("ck_tm", [128, 2, 1024], BF16)
    cvt = K.sb("cvt", [128, 2, 1024], BF16)
    ckT = K.sb("ckT", [128, 8, 256], BF16)
    T_ck, T_cv, T_ckT, T_eb = P.tiles(4, "na")
    sw0 = K.getsem(True)
    P.dma("pool", ck_tm[:], K.din["cache_a_k"].rearrange("(c p) f -> p c f", p=128), sw0, writes=[T_ck])
    P.dma("pool", cvt[:], K.din["cache_a_v"].rearrange("(c p) f -> p c f", p=128), sw0, writes=[T_cv])
    for c in range(2):
        ps, Tp = K.ps(0)
        psb = ps[:].bitcast(BF16)
        for h in range(8):
            P.op("pe", lambda e, c=c, h=h, psb=psb: e.transpose(psb[:, h * 128:(h + 1) * 128], ck_tm[:, c, h * 128:(h + 1) * 128], K.identb[:]), reads=[T_ck, K.T_const], writes=[Tp])
        P.op("dve", lambda e, c=c, psb=psb: e.tensor_copy(ckT[:, :, c * 128:(c + 1) * 128], psb.rearrange("p (h k) -> p h k", h=8)), reads=[Tp], writes=[T_ckT])
    EB = K.sb("EB", [128, 2, 8, 6, 256], BF16)
    mr = Ring(K, "mstage", [128, 6, 256], F32, 2)
    for cl in range(2):
        for h in range(8):
            mt, Tm, sm = mr.next()
            P.dma("sp", mt[:], K.din["na_mask"][cl, h].rearrange("c k q -> k c q"), sm, writes=[Tm])
            P.op("act", lambda e, cl=cl, h=h, mt=mt: e.activation(out=EB[:, cl, h, :, :], in_=mt[:], func=AF.Exp), reads=[Tm], writes=[T_eb])
    NQ = SEXT * 128
    NK = NS1 * 128
    qh = Ring(K, "nq", [128, NQ], BF16, 2)
    kh = Ring(K, "nk", [128, NK], BF16, 2)
    vh = Ring(K, "nv", [128, NS1, 128], BF16, 2)
    oh = Ring(K, "no", [128, NQ], BF16, 2)
    for h in range(8):
        qt, Tq, sq = qh.next()
        kt, Tk, sk = kh.next()
        vt, Tv, sv = vh.next()
        ot, To, so = oh.next()
        P.dma("sp", qt[:], S["qaT_d"][h * 128:(h + 1) * 128, 1024:1024 + NQ], sq, writes=[Tq])
        P.dma("sp", kt[:], S["kaT_d"][h * 128:(h + 1) * 128, 1024:1024 + NK], sk, writes=[Tk])
        P.dma("sp", vt[:], S["va_d"][1024:1024 + NK, h * 128:(h + 1) * 128].rearrange("(c p) f -> p c f", p=128), sv, writes=[Tv])
        for i in range(9):
            nq = 256 if i < 8 else 128
            cl = 0 if i == 0 else 1
            base = 0 if i == 0 else (i - 1) * 256
            nch = 6 if i < 8 else 5
            sl = []
            for ch in range(nch):
                t0 = base + ch * 128
                sl.append((kt[:, t0:t0 + 128], [Tk], vt[:, t0 // 128, :], [Tv], EB[:, cl, h, ch, 0:nq], [T_eb], 0))
            for c in range(2):
                sl.append((ckT[:, h, c * 128:(c + 1) * 128], [T_ckT], cvt[:, c, h * 128:(h + 1) * 128], [T_cv], None, [], 0))
            attn_core(K, sl, nq, (qt[:, i * 256:i * 256 + nq], [Tq]), ot[:, i * 256:i * 256 + nq], To, scale, pools=pools)
        P.dma("sp", catT[h * 128:(h + 1) * 128, 1024:1024 + NQ], ot[:], so, reads=[To])
    K.end()


def phase_dn_prep(K):
    nc, P = K.nc, K.P
    S = K.dscr
    K.scr("qnT_d", [1024, TOK1], BF16)
    K.scr("knT_d", [1024, TOKB], BF16)
    K.scr("ktm_d", [TOKB, 1024], BF16)
    K.scr("vtm_d", [TOKB, 1024], BF16)
    K.begin()
    cwF = K.sb("cwF", [128, 3, 24], F32)
    stage = K.sb("cstg", [128, 128], F32)
    T_cw, T_stage = P.tiles(2, "cw")
    s0 = K.getsem()
    for j in range(3):
        rows_T(K, cwF[:, j, :], T_cw, K.din["conv_w"][j].rearrange("(r p) -> r p", p=128), 24, stage, T_stage, s0)
    pieces = [(s * 256, 256, True, True, 256) for s in range(4)]
    for i in range(8):
        nqv = 512 if i < 4 else (256 if i == 4 else 0)
        pieces.append((1024 + i * 512, 512, i == 0, i == 7, nqv))
    xr = Ring(K, "dx", [128, 514], F32, 3)
    yr = Ring(K, "dy", [128, 512], F32, 2)
    sqr = Ring(K, "dsq", [128, 512], F32, 2)
    rsr = Ring(K, "drs", [128, 512], F32, 2)
    ynr = Ring(K, "dyn", [128, 512], BF16, 3)
    tmr = Ring(K, "dtm", [128, 4, 128], BF16, 3)
    for fb in range(24):
        kind, h = fb // 8, fb % 8
        for (tok0, n0, le, re_, nq) in pieces:
            n = nq if kind == 0 else n0
            if n == 0:
                continue
            re2 = re_ and n == n0
            x, Tx, sx = xr.next()
            a = 1 if le else 0
            b = n + 1 if re2 else n + 2
            if le:
                P.op("pool", lambda e, x=x: e.memset(x[:, 0:1], 0.0), writes=[Tx])
            if re2:
                P.op("pool", lambda e, x=x, n=n: e.memset(x[:, n + 1:n + 2], 0.0), writes=[Tx])
            P.dma("sp", x[:, a:b], S["qkvT_d"][fb * 128:(fb + 1) * 128, tok0 - 1 + a:tok0 - 1 + b], sx, writes=[Tx])
            y, Ty, _ = yr.next()
            P.op("pool", lambda e, x=x, y=y, n=n, fb=fb: e.tensor_scalar_mul(y[:, 0:n], x[:, 0:n], cwF[:, 0, fb:fb + 1]), reads=[Tx, T_cw], writes=[Ty])
            P.op("dve", lambda e, x=x, y=y, n=n, fb=fb: e.scalar_tensor_tensor(y[:, 0:n], x[:, 1:n + 1], cwF[:, 1, fb:fb + 1], y[:, 0:n], ALU.mult, ALU.add), reads=[Tx, T_cw, Ty], writes=[Ty])
            P.op("dve", lambda e, x=x, y=y, n=n, fb=fb: e.scalar_tensor_tensor(y[:, 0:n], x[:, 2:n + 2], cwF[:, 2, fb:fb + 1], y[:, 0:n], ALU.mult, ALU.add), reads=[Tx, T_cw, Ty], writes=[Ty])
            P.op("act", lambda e, y=y, n=n: e.activation(out=y[:, 0:n], in_=y[:, 0:n], func=AF.Silu), reads=[Ty], writes=[Ty])
            yn, Tyn, syn = ynr.next()
            if kind < 2:
                sq, Tsq, _ = sqr.next()
                rs, Trs, _ = rsr.next()
                P.op("pool", lambda e, y=y, sq=sq, n=n: e.tensor_tensor(sq[:, 0:n], y[:, 0:n], y[:, 0:n], ALU.mult), reads=[Ty], writes=[Tsq])
                ps, Tp = K.ps(0)
                P.op("pe", lambda e, ps=ps, sq=sq, n=n: e.matmul(ps[:, 0:n], K.onesf[:], sq[:, 0:n], start=True, stop=True), reads=[Tsq, K.T_const], writes=[Tp])
                P.op("act", lambda e, ps=ps, rs=rs, n=n: e.activation(out=rs[:, 0:n], in_=ps[:, 0:n], func=AF.Sqrt, bias=EPS, scale=1.0), reads=[Tp], writes=[Trs])
                P.op("dve", lambda e, rs=rs, n=n: e.reciprocal(rs[:, 0:n], rs[:, 0:n]), reads=[Trs], writes=[Trs])
                cc = 128 ** -0.5 if kind == 0 else 1.0
                P.op("dve", lambda e, y=y, rs=rs, yn=yn, n=n, cc=cc: e.scalar_tensor_tensor(yn[:, 0:n], y[:, 0:n], cc, rs[:, 0:n], ALU.mult, ALU.mult), reads=[Ty, Trs], writes=[Tyn])
                dst = S["qnT_d"] if kind == 0 else S["knT_d"]
                P.dma("sp", dst[h * 128:(h + 1) * 128, tok0:tok0 + n], yn[:, 0:n], syn, reads=[Tyn])
            else:
                P.op("pool", lambda e, y=y, yn=yn, n=n: e.tensor_copy(yn[:, 0:n], y[:, 0:n]), reads=[Ty], writes=[Tyn])
            if kind >= 1:
                nb = n // 128
                ps, Tp = K.ps(0)
                psb = ps[:].bitcast(BF16)
                for j in range(nb):
                    P.op("pe", lambda e, j=j, psb=psb, yn=yn: e.transpose(psb[:, j * 128:(j + 1) * 128], yn[:, j * 128:(j + 1) * 128], K.identb[:]), reads=[Tyn, K.T_const], writes=[Tp])
                tm_, Ttm, stm = tmr.next()
                P.op("act", lambda e, psb=psb, tm_=tm_, nb=nb: e.copy(tm_[:, 0:nb, :], psb[:, 0:nb * 128].rearrange("p (j f) -> p j f", f=128)), reads=[Tp], writes=[Ttm])
                dst = S["ktm_d"] if kind == 1 else S["vtm_d"]
                P.dma("sp", dst[tok0:tok0 + n, h * 128:(h + 1) * 128].rearrange("(j p) f -> p j f", p=128), tm_[:, 0:nb, :], stm, reads=[Ttm])
    K.end()


def phase_dn_scan(K):
    nc, P = K.nc, K.P
    S = K.dscr
    K.scr("of_d", [TOKC, 1024], F32)
    catT = S["catT_d"]
    K.begin()
    H = 8
    tri = K.sb("tri", [128, 4, 128], F32)
    dtb = K.sb("dtb", [128, 16], F32)
    nea = K.sb("nea", [128, 16], F32)
    gon = K.sb("gon", [128, 128], F32)
    T_c = Tile("dnc")
    sc = K.getsem()
    P.dma("sp", tri[:], K.din["tri"].rearrange("m k c -> k m c"), sc, writes=[T_c])
    P.dma("sp", nea[:], K.din["alog_dt"][0:1, :].partition_broadcast(128), sc, writes=[T_c])
    P.dma("sp", dtb[:], K.din["alog_dt"][1:2, :].partition_broadcast(128), sc, writes=[T_c])
    P.dma("sp", gon[:], K.din["onorm"].partition_broadcast(128), sc, writes=[T_c])
    P.op("act", lambda e: e.activation(out=nea[:], in_=nea[:], func=AF.Exp), reads=[T_c], writes=[T_c])
    P.op("dve", lambda e: e.tensor_scalar_mul(nea[:], nea[:], -1.0), reads=[T_c], writes=[T_c])
    LM, UM, SLM, SUM = 0, 1, 2, 3
    St = K.sb("St", [128, H, 128], F32)
    Sb = K.sb("Sb", [128, H, 128], BF16)
    T_S, T_Sb = P.tiles(2, "S")
    T_of = {}
    big = lambda name, dt, n=2: Ring(K, name, [128, H, 128], dt, n)
    r_kT, r_qT, r_k, r_v = big("lkT", BF16), big("lqT", BF16), big("lk", BF16), big("lv", BF16)
    r_Rg, r_Rb = big("Rg", F32, 1), big("Rb", BF16, 1)
    r_diff, r_x1, r_e1, r_e2i, r_e2s = big("diff", F32, 1), big("x1", F32, 1), big("e1", F32, 1), big("e2i", F32, 1), big("e2s", F32, 1)
    r_kbT, r_egb, r_qd = big("kbT", BF16, 1), big("egb", BF16, 1), big("qd", BF16, 1)
    r_N, r_M, r_X, r_Y = big("N", F32, 2), big("M", F32, 2), big("X", F32, 2), big("Y", F32, 2)
    r_Xb = big("Xb", BF16, 1)
    r_qk, r_vb, r_kbg, r_kd = big("qk", BF16, 1), big("vb", BF16, 1), big("kbg", BF16, 1), big("kdc", BF16, 1)
    r_u, r_wT, r_vn, r_o = big("u", F32, 1), big("wT", BF16, 1), big("vn", BF16, 1), big("o", F32, 2)
    r_z, r_sq, r_on, r_obT = Ring(K, "z", [128, 1024], F32, 1), big("osq", F32, 1), big("on", BF16, 1), big("obT", BF16, 2)
    r_st = Ring(K, "ost", [128, 16], F32, 2)

    def v8(ap):
        return ap.rearrange("p (h f) -> p h f", h=H)

    def bc_h(ap2):
        return ap2.unsqueeze(2).to_broadcast([128, H, 128])

    def bc_m(m):
        return tri[:, m, :].unsqueeze(1).to_broadcast([128, H, 128])

    def mm8(lhs_fn, rhs_fn, reads, g, extra=None):
        b0, T0 = K.ps(g)
        b1, T1 = K.ps(g)
        banks = ((b0, T0), (b1, T1))
        for h in range(H):
            b, Tb = banks[h // 4]
            o_ = b[:, (h % 4) * 128:(h % 4 + 1) * 128]
            if extra is None:
                P.op("pe", lambda e, h=h, o_=o_: e.matmul(o_, lhs_fn(h), rhs_fn(h), start=True, stop=True), reads=reads, writes=[Tb])
            else:
                l2, r2, reads2 = extra
                P.op("pe", lambda e, h=h, o_=o_: e.matmul(o_, lhs_fn(h), rhs_fn(h), start=True, stop=False), reads=reads, writes=[Tb])
                P.op("pe", lambda e, h=h, o_=o_: e.matmul(o_, l2(h), r2(h), start=False, stop=True), reads=reads2, writes=[Tb])
        return banks

    def ev(eng, banks, fn, reads, writes):
        for i, (b, Tb) in enumerate(banks):
            bv = b[:].rearrange("p (h f) -> p h f", h=4)
            P.op(eng, lambda e, bv=bv, i=i: fn(e, bv, slice(4 * i, 4 * i + 4)), reads=[Tb] + reads, writes=writes)

    def gates(tokd0, nch):
        G = {}
        graw = K.sb("graw", [128, nch, 32], F32)
        beta = K.sb("beta", [128, nch, 16], F32)
        g = K.sb("gg", [128, nch, 16], F32)
        Tg = Tile("gates")
        sg = K.getsem()
        P.dma("sp", graw[:], S["gates_d"][tokd0:tokd0 + nch * 128, :].rearrange("(c p) g -> p c g", p=128), sg, writes=[Tg])
        P.op("act", lambda e: e.activation(out=beta[:], in_=graw[:, :, 0:16], func=AF.Sigmoid), reads=[Tg], writes=[Tg])
        P.op("dve", lambda e: e.tensor_tensor(g[:], graw[:, :, 16:32], dtb[:].unsqueeze(1).to_broadcast([128, nch, 16]), ALU.add), reads=[Tg, T_c], writes=[Tg])
        P.op("act", lambda e: e.activation(out=g[:], in_=g[:], func=AF.Exp), reads=[Tg], writes=[Tg])
        P.op("act", lambda e: e.activation(out=g[:], in_=g[:], func=AF.Ln, bias=1.0, scale=1.0), reads=[Tg], writes=[Tg])
        P.op("dve", lambda e: e.tensor_tensor(g[:], g[:], nea[:].unsqueeze(1).to_broadcast([128, nch, 16]), ALU.mult), reads=[Tg, T_c], writes=[Tg])
        G["beta"], G["T"] = beta, Tg
        for dr in range(2):
            gc = K.sb(f"gc{dr}", [128, nch, 8], F32)
            gl = K.sb(f"gl{dr}", [128, nch, 8], F32)
            eg = K.sb(f"eg{dr}", [128, nch, 8], F32)
            bg = K.sb(f"bg{dr}", [128, nch, 8], F32)
            kd = K.sb(f"kd{dr}", [128, nch, 8], F32)
            cd = K.sb(f"cd{dr}", [128, nch, 8], F32)
            tr = UM if dr == 0 else LM
            ps, Tp = K.ps(0)
            gsl = g[:, :, dr * 8:(dr + 1) * 8]
            P.op("pe", lambda e, ps=ps, tr=tr, gsl=gsl: e.matmul(ps[:, 0:nch * 8].rearrange("p (c h) -> p c h", h=8), tri[:, tr, :], gsl, start=True, stop=True), reads=[Tg, T_c], writes=[Tp])
            P.op("dve", lambda e, ps=ps, gc=gc: e.tensor_copy(gc[:].rearrange("p c h -> p (c h)"), ps[:, 0:nch * 8]), reads=[Tp], writes=[Tg])
            ps2, Tp2 = K.ps(0)
            P.op("pe", lambda e, ps2=ps2, gsl=gsl: e.matmul(ps2[:, 0:nch * 8].rearrange("p (c h) -> p c h", h=8), K.onesf[:], gsl, start=True, stop=True), reads=[Tg, K.T_const], writes=[Tp2])
            P.op("dve", lambda e, ps2=ps2, gl=gl: e.tensor_copy(gl[:].rearrange("p c h -> p (c h)"), ps2[:, 0:nch * 8]), reads=[Tp2], writes=[Tg])
            P.op("act", lambda e, eg=eg, gc=gc: e.activation(out=eg[:], in_=gc[:], func=AF.Exp), reads=[Tg], writes=[Tg])
            P.op("dve", lambda e, bg=bg, eg=eg, dr=dr: e.tensor_tensor(bg[:], eg[:], beta[:, :, dr * 8:(dr + 1) * 8], ALU.mult), reads=[Tg], writes=[Tg])
            P.op("dve", lambda e, kd=kd, gl=gl, gc=gc: e.tensor_tensor(kd[:], gl[:], gc[:], ALU.subtract), reads=[Tg], writes=[Tg])
            P.op("act", lambda e, kd=kd: e.activation(out=kd[:], in_=kd[:], func=AF.Exp), reads=[Tg], writes=[Tg])
            P.op("act", lambda e, cd=cd, gl=gl: e.activation(out=cd[:], in_=gl[:], func=AF.Exp), reads=[Tg], writes=[Tg])
            G[dr] = dict(gc=gc, eg=eg, bg=bg, kd=kd, cd=cd)
        return G

    def chunk(G, tokd0, ch, dr, full, final, tokc0):
        t0 = tokd0 + ch * 128
        Tg = G["T"]
        gd = G[dr]
        beta_c = G["beta"][:, ch, dr * 8:(dr + 1) * 8]
        gc_c = gd["gc"][:, ch, :]
        mL, mU, mSL, mSU = (LM, UM, SLM, SUM) if dr == 0 else (UM, LM, SUM, SLM)
        kT, TkT, s1 = r_kT.next()
        ktm, Tk, s2 = r_k.next()
        vtm, Tv, s3 = r_v.next()
        P.dma("sp", kT[:], S["knT_d"][:, t0:t0 + 128].rearrange("(h p) t -> p h t", p=128), s1, writes=[TkT])
        P.dma("sp", ktm[:].rearrange("p h f -> p (h f)"), S["ktm_d"][t0:t0 + 128, :], s2, writes=[Tk])
        P.dma("sp", vtm[:].rearrange("p h f -> p (h f)"), S["vtm_d"][t0:t0 + 128, :], s3, writes=[Tv])
        if full:
            qT, TqT, s4 = r_qT.next()
            P.dma("sp", qT[:], S["qnT_d"][:, t0:t0 + 128].rearrange("(h p) t -> p h t", p=128), s4, writes=[TqT])
        Rg, TRg, _ = r_Rg.next()
        Rb, TRb, _ = r_Rb.next()
        idb = K.identf[:].unsqueeze(1).to_broadcast([128, H, 128])
        P.op("dve", lambda e: e.tensor_tensor(Rg[:], bc_h(gc_c), idb, ALU.mult), reads=[Tg, K.T_const], writes=[TRg])
        P.op("pool", lambda e: e.tensor_tensor(Rb[:], bc_h(beta_c), idb, ALU.mult), reads=[Tg, K.T_const], writes=[TRb])
        gcb = mm8(lambda h: K.onesf[:], lambda h: Rg[:, h, :], [TRg, K.T_const], 0)
        btb = mm8(lambda h: K.onesb[:], lambda h: Rb[:, h, :], [TRb, K.T_const], 0)
        diff, Tdiff, _ = r_diff.next()
        ev("dve", gcb, lambda e, bv, hs: e.tensor_tensor(diff[:, hs, :], bc_h(gc_c)[:, hs, :], bv, ALU.subtract), [Tg], [Tdiff])
        kbT, TkbT, _ = r_kbT.next()
        ev("dve", btb, lambda e, bv, hs: e.tensor_tensor(kbT[:, hs, :], kT[:, hs, :], bv, ALU.mult), [TkT], [TkbT])
        if full:
            egb, Tegb, _ = r_egb.next()
            qd, Tqd, _ = r_qd.next()
            ev("act", gcb, lambda e, bv, hs: e.activation(out=egb[:, hs, :], in_=bv, func=AF.Exp), [], [Tegb])
            P.op("pool", lambda e: e.tensor_tensor(qd[:], qT[:], egb[:], ALU.mult), reads=[TqT, Tegb], writes=[Tqd])
        x1, Tx1, _ = r_x1.next()
        e1, Te1, _ = r_e1.next()
        e2i, Te2i, _ = r_e2i.next()
        e2s, Te2s, _ = r_e2s.next()
        P.op("dve", lambda e: e.tensor_tensor(x1[:], diff[:], bc_m(mL), ALU.mult), reads=[Tdiff, T_c], writes=[Tx1])
        P.op("act", lambda e: e.activation(out=e1[:], in_=x1[:], func=AF.Exp), reads=[Tx1], writes=[Te1])
        P.op("pool", lambda e: e.tensor_tensor(e1[:], e1[:], bc_m(mSL), ALU.mult), reads=[Te1, T_c], writes=[Te1])
        P.op("dve", lambda e: e.tensor_tensor(x1[:], diff[:], bc_m(mU), ALU.mult), reads=[Tdiff, T_c, Te1], writes=[Tx1])
        P.op("act", lambda e: e.activation(out=e2i[:], in_=x1[:], func=AF.Exp, scale=-1.0), reads=[Tx1], writes=[Te2i])
        P.op("pool", lambda e: e.tensor_tensor(e2s[:], e2i[:], bc_m(mSU), ALU.mult), reads=[Te2i, T_c], writes=[Te2s])
        P.op("pool", lambda e: e.tensor_tensor(e2i[:], e2i[:], bc_m(mU), ALU.mult), reads=[Te2i, T_c, Te2s], writes=[Te2i])
        a1 = mm8(lambda h: kbT[:, h, :], lambda h: kT[:, h, :], [TkbT, TkT], 0)
        a2 = mm8(lambda h: kT[:, h, :], lambda h: kbT[:, h, :], [TkbT, TkT], 1)
        Mj, TM, _ = r_M.next()
        Nj, TN, _ = r_N.next()
        ev("dve", a1, lambda e, bv, hs, Mj=Mj: e.tensor_tensor(Mj[:, hs, :], bv, e1[:, hs, :], ALU.mult), [Te1], [TM])
        ev("dve", a2, lambda e, bv, hs, Nj=Nj: e.tensor_tensor(Nj[:, hs, :], bv, e2s[:, hs, :], ALU.mult), [Te2s], [TN])
        if full:
            a3 = mm8(lambda h: kT[:, h, :], lambda h: qT[:, h, :], [TkT, TqT], 0)
            qk, Tqk, _ = r_qk.next()
            ev("dve", a3, lambda e, bv, hs: e.tensor_tensor(qk[:, hs, :], bv, e2i[:, hs, :], ALU.mult), [Te2i], [Tqk])
        X, TX, _ = r_X.next()
        Y, TY, _ = r_Y.next()
        idbb = K.identf[:].unsqueeze(1).to_broadcast([128, H, 128])
        P.op("pool", lambda e, X=X, Nj=Nj: e.tensor_tensor(X[:], idbb, Nj[:], ALU.subtract), reads=[TN, K.T_const], writes=[TX])
        P.op("pool", lambda e, Y=Y, Mj=Mj: e.tensor_tensor(Y[:], idbb, Mj[:], ALU.subtract), reads=[TM, K.T_const], writes=[TY])
        for j in range(1, 7):
            last = j == 6
            Nn, TNn, _ = r_N.next()
            pn = mm8(lambda h, Mj=Mj: Mj[:, h, :], lambda h, Nj=Nj: Nj[:, h, :], [TM, TN], 0)
            ev("act", pn, lambda e, bv, hs, Nn=Nn: e.copy(Nn[:, hs, :], bv), [], [TNn])
            if not last:
                Mn, TMn, _ = r_M.next()
                pm = mm8(lambda h, Nj=Nj: Nj[:, h, :], lambda h, Mj=Mj: Mj[:, h, :], [TM, TN], 0)
                ev("act", pm, lambda e, bv, hs, Mn=Mn: e.copy(Mn[:, hs, :], bv), [], [TMn])
            Xn, TXn, _ = r_X.next()
            px = mm8(lambda h, Y=Y: Y[:, h, :], lambda h, Nn=Nn: Nn[:, h, :], [TY, TNn], 1)
            ev("dve", px, lambda e, bv, hs, Xn=Xn, X=X: e.tensor_tensor(Xn[:, hs, :], bv, X[:, hs, :], ALU.add), [TX], [TXn])
            if not last:
                Yn, TYn, _ = r_Y.next()
                py = mm8(lambda h, X=X: X[:, h, :], lambda h, Mn=Mn: Mn[:, h, :], [TX, TMn], 1)
                ev("dve", py, lambda e, bv, hs, Yn=Yn, Y=Y: e.tensor_tensor(Yn[:, hs, :], bv, Y[:, hs, :], ALU.add), [TY], [TYn])
                Mj, TM, Y, TY = Mn, TMn, Yn, TYn
            Nj, TN, X, TX = Nn, TNn, Xn, TXn
        K.dump("dbg_diff", diff[:], [Tdiff]); K.dump("dbg_e1", e1[:], [Te1]); K.dump("dbg_e2i", e2i[:], [Te2i])
        K.dump("dbg_kbT", kbT[:], [TkbT]); K.dump("dbg_X", X[:], [TX]); K.dump("dbg_gc", gd["gc"][:], [Tg]); K.dump("dbg_beta", G["beta"][:], [Tg])
        K.dump("dbg_kd", gd["kd"][:], [Tg]); K.dump("dbg_cd", gd["cd"][:], [Tg])
        vb, Tvb, _ = r_vb.next()
        kbg, Tkbg, _ = r_kbg.next()
        kdc, Tkdc, _ = r_kd.next()
        P.op("pool", lambda e: e.tensor_tensor(vb[:], vtm[:], bc_h(beta_c), ALU.mult), reads=[Tv, Tg], writes=[Tvb])
        P.op("pool", lambda e: e.tensor_tensor(kbg[:], ktm[:], bc_h(gd["bg"][:, ch, :]), ALU.mult), reads=[Tk, Tg], writes=[Tkbg])
        P.op("pool", lambda e: e.tensor_tensor(kdc[:], ktm[:], bc_h(gd["kd"][:, ch, :]), ALU.mult), reads=[Tk, Tg], writes=[Tkdc])
        Xb, TXb, _ = r_Xb.next()
        P.op("act", lambda e, X=X: e.copy(Xb[:], X[:]), reads=[TX], writes=[TXb])
        pu = mm8(lambda h: Xb[:, h, :], lambda h: vb[:, h, :], [TXb, Tvb], 0)
        u, Tu, _ = r_u.next()
        ev("act", pu, lambda e, bv, hs: e.copy(u[:, hs, :], bv), [], [Tu])
        pw = mm8(lambda h: kbg[:, h, :], lambda h: Xb[:, h, :], [TXb, Tkbg], 0)
        wT, TwT, _ = r_wT.next()
        ev("act", pw, lambda e, bv, hs: e.copy(wT[:, hs, :], bv), [], [TwT])
        pws = mm8(lambda h: wT[:, h, :], lambda h: Sb[:, h, :], [TwT, T_Sb], 1)
        vn, Tvn, _ = r_vn.next()
        ev("dve", pws, lambda e, bv, hs: e.tensor_tensor(vn[:, hs, :], u[:, hs, :], bv, ALU.subtract), [Tu], [Tvn])
        if full:
            po = mm8(lambda h: qd[:, h, :], lambda h: Sb[:, h, :], [Tqd, T_Sb], 1,
                     extra=(lambda h: qk[:, h, :], lambda h: vn[:, h, :], [Tqk, Tvn]))
            o, To, so = r_o.next()
            if not final:
                ev("act", po, lambda e, bv, hs: e.copy(o[:, hs, :], bv), [], [To])
                P.dma("sp", S["of_d"][tokc0 + ch * 128:tokc0 + (ch + 1) * 128, :], o[:].rearrange("p h f -> p (h f)"), so, reads=[To], writes=[T_of.setdefault(tokc0 + ch * 128, Tile("of"))])
            else:
                P.dma("sp", o[:].rearrange("p h f -> p (h f)"), S["of_d"][tokc0 + ch * 128:tokc0 + (ch + 1) * 128, :], so, reads=[T_of[tokc0 + ch * 128]], writes=[To])
                ev("dve", po, lambda e, bv, hs: e.tensor_tensor(o[:, hs, :], o[:, hs, :], bv, ALU.add), [To], [To])
        K.dump("dbg_u", u[:], [Tu]); K.dump("dbg_wT", wT[:], [TwT]); K.dump("dbg_vn", vn[:], [Tvn])
        if full:
            K.dump("dbg_o", o[:], [To]); K.dump("dbg_qk", qk[:], [Tqk])
        pds = mm8(lambda h: kdc[:, h, :], lambda h: vn[:, h, :], [Tkdc, Tvn], 1)
        for h in range(H):
            b, Tb = pds[h // 4]
            P.op("dve", lambda e, h=h, b=b: e.scalar_tensor_tensor(St[:, h, :], St[:, h, :], gd["cd"][:, ch, h:h + 1], b[:, (h % 4) * 128:(h % 4 + 1) * 128], ALU.mult, ALU.add),
                 reads=[Tb, Tg, T_S], writes=[T_S])
        P.op("act", lambda e: e.copy(Sb[:], St[:]), reads=[T_S], writes=[T_Sb])
        if full and final:
            z, Tz, sz = r_z.next()
            P.dma("sp", z[:], S["z_d"][tokc0 + ch * 128:tokc0 + (ch + 1) * 128, :], sz, writes=[Tz])
            sq, Tsq, _ = r_sq.next()
            st, Tst, _ = r_st.next()
            on, Ton, _ = r_on.next()
            P.op("pool", lambda e: e.tensor_tensor(sq[:], o[:], o[:], ALU.mult), reads=[To], writes=[Tsq])
            P.op("dve", lambda e: e.tensor_reduce(out=st[:, 0:8], in_=sq[:], axis=AX.X, op=ALU.add), reads=[Tsq], writes=[Tst])
            P.op("act", lambda e: e.activation(out=st[:, 8:16], in_=st[:, 0:8], func=AF.Sqrt, bias=EPS, scale=1.0 / 128), reads=[Tst], writes=[Tst])
            P.op("dve", lambda e: e.reciprocal(st[:, 8:16], st[:, 8:16]), reads=[Tst], writes=[Tst])
            P.op("act", lambda e: e.activation(out=z[:], in_=z[:], func=AF.Silu), reads=[Tz], writes=[Tz])
            P.op("dve", lambda e: e.tensor_tensor(sq[:], o[:], bc_h(st[:, 8:16]), ALU.mult), reads=[To, Tst, Tsq], writes=[Tsq])
            P.op("pool", lambda e: e.tensor_tensor(sq[:], sq[:], gon[:].unsqueeze(1).to_broadcast([128, H, 128]), ALU.mult), reads=[Tsq, T_c], writes=[Tsq])
            P.op("dve", lambda e: e.tensor_tensor(on[:], sq[:], v8(z[:]), ALU.mult), reads=[Tsq, Tz], writes=[Ton])
            ps, Tp = K.ps(0)
            psb = ps[:].bitcast(BF16)
            for h in range(H):
                P.op("pe", lambda e, h=h, psb=psb: e.transpose(psb[:, h * 128:(h + 1) * 128], on[:, h, :], K.identb[:]), reads=[Ton, K.T_const], writes=[Tp])
            obT, TobT, sob = r_obT.next()
            P.op("act", lambda e, psb=psb: e.copy(obT[:].rearrange("p h f -> p (h f)"), psb), reads=[Tp], writes=[TobT])
            P.dma("sp", catT[1024:2048, tokc0 + ch * 128:tokc0 + (ch + 1) * 128].rearrange("(h p) t -> p h t", p=128), obT[:], sob, reads=[TobT])

    def set_state(src):
        ss_ = K.getsem()
        if src is None:
            P.op("pool", lambda e: e.memset(St[:], 0.0), reads=[T_Sb], writes=[T_S])
        else:
            P.dma("sp", St[:], src.rearrange("h k v -> k h v"), ss_, reads=[T_Sb], writes=[T_S])
        P.op("act", lambda e: e.copy(Sb[:], St[:]), reads=[T_S], writes=[T_Sb])

    def save_state(dst):
        ss_ = K.getsem()
        P.dma("sp", dst.rearrange("h k v -> k h v"), St[:], ss_, reads=[T_S])

    import os
    which = os.environ.get("DN_SEQS", "ps")
    if "p" in which:
      for s in range(4):
        G = gates(s * 256, 2)
        set_state(None)
        for ch in (0, 1):
            chunk(G, s * 256, ch, 0, True, False, s * 256)
        save_state(K.dout["nbf"][s])
        set_state(None)
        for ch in (1, 0):
            chunk(G, s * 256, ch, 1, True, True, s * 256)
        save_state(K.dout["nbb"][s])
    if "s" in which:
        G = gates(1024, 32)
        set_state(K.din["s0"][0])
        for ch in range(SEXT):
            chunk(G, 1024, ch, 0, True, False, 1024)
        set_state(K.din["s0"][1])
        for ch in range(31, -1, -1):
            chunk(G, 1024, ch, 1, ch < SEXT, True, 1024)
    K.end()


def phase_mlp(K, l):
    nc, P = K.nc, K.P
    S = K.dscr
    ntb = NCAT if l == 0 else NPT + 16
    groups = token_groups(ntb, breaks=(NPT,))
    if l == 0:
        x1_d = K.scr("x1_d", [TOKC, D], F32)
        oT_d, Wo, W1, W2 = S["catT_d"], K.din["ab_w_out"], K.din["w_mlp_in"][0], K.din["w_mlp_out"][0]
        xsrc = lambda g: K.din["xp"][g * 128:(g + 1) * 128, :] if g < NPT else K.din["xs"][(g - NPT) * 128:(g - NPT + 1) * 128, :]
    else:
        oT_d, Wo, W1, W2 = S["o1T_d"], K.din["c_w_out"], K.din["w_mlp_in"][1], K.din["w_mlp_out"][1]
        xsrc = lambda g: S["x1_d"][g * 128:(g + 1) * 128, :]
    K.begin()
    actT = K.sb("actT", [128, KC, 512], BF16)
    T_act = [[Tile() for kc in range(KC)] for i in range(4)]
    xres = K.sb("xres", [128, 4, D], F32)
    T_x = [Tile() for i in range(4)]
    uT = K.sb("uT", [128, 64, 512], BF16)
    T_u = [Tile() for fc in range(64)]
    gate = [K.sb(f"gate{i}", [128, D], F32) for i in range(2)]
    T_gate = Tile("gate")
    wr = Ring(K, "wm", [128, KC, 512], BF16, 2, sw=True)
    tr = Ring(K, "tg", [128, 512], F32, 2)
    rr = Ring(K, "rl", [128, 512], F32, 2)
    nt = NormT(K)
    sx = [K.getsem() for i in range(4)]
    sg, so = K.getsem(), K.getsem()
    if l == 1:
        fng = K.sb("fng", [128, D], F32)
        fss = Ring(K, "fss", [128, 2], F32, 2)
        fjk = K.sb("fjk", [128, D], BF16)
        T_fng, T_fjk = P.tiles(2, "fn")
        P.dma("sp", fng[:], K.din["final_norm"].partition_broadcast(128), sg, writes=[T_fng])
    cur_c = None
    for (t0, m) in groups:
        n = m * 128
        c = 0 if t0 < NPT else 1
        if c != cur_c:
            P.dma("sp", gate[0][:], S["modrow_d"][l, c:c + 1, 2 * D:3 * D].partition_broadcast(128), sg, writes=[T_gate])
            P.dma("sp", gate[1][:], S["modrow_d"][l, c:c + 1, 5 * D:6 * D].partition_broadcast(128), sg, writes=[T_gate])
            cur_c = c
        P.dma("sp", actT[:, :, 0:n], oT_d[:, t0 * 128:t0 * 128 + n].rearrange("(fc p) t -> p fc t", p=128), so,
              writes=[T_act[i][kc] for i in range(m) for kc in range(KC)])
        for i in range(m):
            P.dma("sp", xres[:, i, :], xsrc(t0 + i), sx[i], writes=[T_x[i]])

        def second(Wsrc, nfq, lhs_fn, lhs_tiles_fn, gi):
            for dg in range(4):
                banks = [K.ps(1) for i in range(m)]
                for fq in range(nfq):
                    wt, Tw, sw = wr.next()
                    P.dma("pool", wt[:], Wsrc[fq * 2048:(fq + 1) * 2048, dg * 512:(dg + 1) * 512].rearrange("(kc p) n -> p kc n", p=128), sw, writes=[Tw])
                    for i in range(m):
                        b, Tb = banks[i]
                        for kc in range(KC):
                            fc = fq * KC + kc
                            P.op("pe", lambda e, b=b, i=i, fc=fc, kc=kc, wt=wt: e.matmul(b[:, :], lhs_fn(fc, i), wt[:, kc, :], start=(fc == 0), stop=(fc == nfq * KC - 1)),
                                 reads=[Tw] + lhs_tiles_fn(fc, i), writes=[Tb])
                for i in range(m):
                    b, Tb = banks[i]
                    tt, Tt, _ = tr.next()
                    P.op("dve", lambda e, b=b, tt=tt, dg=dg: e.tensor_tensor(tt[:], b[:, :], gate[gi][:, dg * 512:(dg + 1) * 512], ALU.mult), reads=[Tb, T_gate], writes=[Tt])
                    P.op("pool", lambda e, i=i, tt=tt, dg=dg: e.tensor_tensor(xres[:, i, dg * 512:(dg + 1) * 512], xres[:, i, dg * 512:(dg + 1) * 512], tt[:], ALU.add), reads=[Tt, T_x[i]], writes=[T_x[i]])

        second(Wo, 1, lambda fc, i: actT[:, fc, i * 128:(i + 1) * 128], lambda fc, i: [T_act[i][fc]], 0)
        for i in range(m):
            nt.run(xres[:, i, :], T_x[i], K.gsF[l][1][c], K.modF[l][c][:, 3 * KC:4 * KC],
                   lambda kc, i=i: actT[:, kc, i * 128:(i + 1) * 128], lambda kc, i=i: T_act[i][kc])
        for fg in range(16):
            wt, Tw, sw = wr.next()
            P.dma("pool", wt[:], W1[:, fg * 512:(fg + 1) * 512].rearrange("(kc p) n -> p kc n", p=128), sw, writes=[Tw])
            for sub in range(4):
                fc = fg * 4 + sub
                ps, Tp = K.ps(0)
                for kc in range(KC):
                    P.op("pe", lambda e, ps=ps, kc=kc, sub=sub, wt=wt, n=n: e.matmul(ps[:, 0:n], wt[:, kc, sub * 128:(sub + 1) * 128], actT[:, kc, 0:n], start=(kc == 0), stop=(kc == KC - 1)),
                         reads=[Tw] + [T_act[i][kc] for i in range(m)], writes=[Tp])
                r, Tr, _ = rr.next()
                P.op("act", lambda e, ps=ps, r=r, n=n: e.activation(out=r[:, 0:n], in_=ps[:, 0:n], func=AF.Relu), reads=[Tp], writes=[Tr])
                P.op("pool", lambda e, r=r, fc=fc, n=n: e.tensor_tensor(uT[:, fc, 0:n], r[:, 0:n], r[:, 0:n], ALU.mult), reads=[Tr], writes=[T_u[fc]])
        second(W2, 4, lambda fc, i: uT[:, fc, i * 128:(i + 1) * 128], lambda fc, i: [T_u[fc]], 1)
        for i in range(m):
            g = t0 + i
            if l == 0:
                P.dma("sp", x1_d[g * 128:(g + 1) * 128, :], xres[:, i, :], sx[i], reads=[T_x[i]])
            else:
                ss, Tss, _ = fss.next()
                P.op("act", lambda e, i=i, ss=ss: e.activation(out=fjk[:], in_=xres[:, i, :], func=AF.Square, accum_out=ss[:, 0:1]), reads=[T_x[i]], writes=[T_fjk, Tss])
                P.op("act", lambda e, ss=ss: e.activation(out=ss[:, 1:2], in_=ss[:, 0:1], func=AF.Sqrt, bias=EPS, scale=1.0 / D), reads=[Tss], writes=[Tss])
                P.op("dve", lambda e, ss=ss: e.reciprocal(ss[:, 1:2], ss[:, 1:2]), reads=[Tss], writes=[Tss])
                P.op("dve", lambda e, i=i, ss=ss: e.scalar_tensor_tensor(xres[:, i, :], xres[:, i, :], ss[:, 1:2], fng[:], ALU.mult, ALU.mult), reads=[T_x[i], Tss, T_fng], writes=[T_x[i]])
                dst = K.dout["y_p"][g * 128:(g + 1) * 128, :] if g < NPT else K.dout["y_s"][(g - NPT) * 128:(g - NPT + 1) * 128, :]
                P.dma("sp", dst, xres[:, i, :], sx[i], reads=[T_x[i]])
    K.end()


def phase_l1_inproj(K):
    nc, P = K.nc, K.P
    S = K.dscr
    q1T = K.scr("q1T_d", [D, TOKC], BF16)
    k1T = K.scr("k1T_d", [256, TOKC], BF16)
    v1 = K.scr("v1_d", [TOKC, 256], BF16)
    K.begin()
    ntb = NCAT
    groups = token_groups(ntb, breaks=(NPT,))
    hT = K.sb("h1T", [128, KC, ntb * 128], BF16)
    T_h = [[Tile() for kc in range(KC)] for g in range(ntb)]
    xr = Ring(K, "x1in", [128, D], F32, 2)
    nt = NormT(K)
    for g in range(ntb):
        c = 0 if g < NPT else 1
        xt, T_x, sx = xr.next()
        P.dma("sp", xt[:], S["x1_d"][g * 128:(g + 1) * 128, :], sx, writes=[T_x])
        nt.run(xt[:], T_x, K.gsF[1][0][c], K.modF[1][c][:, 0:KC],
               lambda kc, g=g: hT[:, kc, g * 128:(g + 1) * 128], lambda kc, g=g: T_h[g][kc])
    cosT = K.sb("cosT", [128, SEXT * 128], F32)
    sinT = K.sb("sinT", [128, SEXT * 128], F32)
    perm = K.sb("perm", [128, 128], F32)
    T_rc = Tile("ropec")
    sr = K.getsem()
    P.dma("sp", cosT[:], K.din["rope_cos"], sr, writes=[T_rc])
    P.dma("sp", sinT[:], K.din["rope_sin"], sr, writes=[T_rc])
    P.dma("sp", perm[:], K.din["rope_perm"], sr, writes=[T_rc])
    wr = Ring(K, "w1q", [128, KC, 512], BF16, 2, sw=True)
    q32r = Ring(K, "q32", [128, 512], F32, 2)
    t1r = Ring(K, "rt1", [128, 512], F32, 2)
    st16 = Ring(K, "s16", [128, 512], BF16, 3)
    st32 = Ring(K, "s32", [128, 512], F32, 2)
    W = K.din["c_w_qkv"]
    for t in range(5):
        wt, Tw, sw = wr.next()
        P.dma("pool", wt[:], W[:, t * 512:(t + 1) * 512].rearrange("(kc p) n -> p kc n", p=128), sw, writes=[Tw])
        nsub = 4 if t < 4 else 2
        for sub in range(nsub):
            dst, row0 = (q1T, t * 512 + sub * 128) if t < 4 else (k1T, sub * 128)
            for (t0, m) in groups:
                n = m * 128
                ps, Tp = K.ps(0)
                for kc in range(KC):
                    P.op("pe", lambda e, ps=ps, kc=kc, sub=sub, wt=wt, t0=t0, n=n: e.matmul(ps[:, 0:n], wt[:, kc, sub * 128:(sub + 1) * 128], hT[:, kc, t0 * 128:t0 * 128 + n], start=(kc == 0), stop=(kc == KC - 1)),
                         reads=[Tw] + [T_h[t0 + i][kc] for i in range(m)], writes=[Tp])
                sg, Ts, ss_ = st16.next()
                if t0 < NPT:
                    P.op("act", lambda e, ps=ps, sg=sg, n=n: e.copy(sg[:, 0:n], ps[:, 0:n]), reads=[Tp], writes=[Ts])
                else:
                    s0 = (t0 - NPT) * 128
                    q32, Tq, _ = q32r.next()
                    t1, Tt1, _ = t1r.next()
                    P.op("act", lambda e, ps=ps, q32=q32, n=n: e.copy(q32[:, 0:n], ps[:, 0:n]), reads=[Tp], writes=[Tq])
                    ps2, Tp2 = K.ps(1)
                    P.op("pe", lambda e, ps2=ps2, q32=q32, n=n: e.matmul(ps2[:, 0:n], perm[:], q32[:, 0:n], start=True, stop=True), reads=[Tq, T_rc], writes=[Tp2])
                    P.op("pool", lambda e, q32=q32, t1=t1, n=n, s0=s0: e.tensor_tensor(t1[:, 0:n], q32[:, 0:n], cosT[:, s0:s0 + n], ALU.mult), reads=[Tq, T_rc], writes=[Tt1])
                    P.op("dve", lambda e, ps2=ps2, q32=q32, n=n, s0=s0: e.tensor_tensor(q32[:, 0:n], ps2[:, 0:n], sinT[:, s0:s0 + n], ALU.mult), reads=[Tp2, T_rc, Tt1], writes=[Tq])
                    P.op("dve", lambda e, q32=q32, t1=t1, sg=sg, n=n: e.tensor_tensor(sg[:, 0:n], q32[:, 0:n], t1[:, 0:n], ALU.add), reads=[Tq, Tt1], writes=[Ts])
                P.dma("sp", dst[row0:row0 + 128, t0 * 128:t0 * 128 + n], sg[:, 0:n], ss_, reads=[Ts])
        if t == 4:
            for g in range(ntb):
                ps, Tp = K.ps(0)
                for kc in range(KC):
                    P.op("pe", lambda e, ps=ps, kc=kc, g=g, wt=wt: e.matmul(ps[:, :], hT[:, kc, g * 128:(g + 1) * 128], wt[:, kc, :], start=(kc == 0), stop=(kc == KC - 1)),
                         reads=[Tw, T_h[g][kc]], writes=[Tp])
                sg, Ts, ss_ = st16.next()
                P.op("act", lambda e, ps=ps, sg=sg: e.copy(sg[:, 0:256], ps[:, 256:512]), reads=[Tp], writes=[Ts])
                P.dma("sp", v1[g * 128:(g + 1) * 128, :], sg[:, 0:256], ss_, reads=[Ts])
                if g < NPT:
                    s32, Ts32, ss32 = st32.next()
                    P.op("dve", lambda e, ps=ps, s32=s32: e.tensor_copy(s32[:], ps[:, :]), reads=[Tp], writes=[Ts32])
                    P.dma("sp", K.dout["nck"][g * 128:(g + 1) * 128, :], s32[:, 0:256], ss32, reads=[Ts32])
                    P.dma("sp", K.dout["ncv"][g * 128:(g + 1) * 128, :], s32[:, 256:512], ss32, reads=[Ts32])
    K.end()


def phase_attn_c(K):
    nc, P = K.nc, K.P
    S = K.dscr
    o1T = K.scr("o1T_d", [D, TOKC], BF16)
    scale = 64 ** -0.5
    K.begin()
    pt_ring = Ring(K, "pt", [128, 512], BF16, 4)
    rec_ring = Ring(K, "rec", [128, 512], F32, 2)
    pools = (pt_ring, rec_ring)
    snk = K.sb("snk", [128, 32], F32)
    trib = K.sb("trib", [128, 2, 128], BF16)
    ck_tm = K.sb("cck", [128, 2, 256], BF16)
    cvt = K.sb("ccv", [128, 2, 256], BF16)
    ckT = K.sb("cckT", [64, 4, 256], BF16)
    T_c, T_ck, T_cv, T_ckT = P.tiles(4, "ac")
    s0, sw0 = K.getsem(), K.getsem(True)
    P.dma("sp", snk[:], K.din["c_sink"].partition_broadcast(128), s0, writes=[T_c])
    P.op("act", lambda e: e.activation(out=snk[:], in_=snk[:], func=AF.Exp), reads=[T_c], writes=[T_c])
    P.dma("pool", trib[:], K.din["tri"][0:2].rearrange("m k c -> k m c"), sw0, writes=[T_c])
    P.dma("pool", ck_tm[:], K.din["cache_c_k"].rearrange("(c p) f -> p c f", p=128), sw0, writes=[T_ck])
    P.dma("pool", cvt[:], K.din["cache_c_v"].rearrange("(c p) f -> p c f", p=128), sw0, writes=[T_cv])
    for c in range(2):
        ps, Tp = K.ps(0)
        psb = ps[:].bitcast(BF16)
        for kh in range(4):
            P.op("pe", lambda e, c=c, kh=kh, psb=psb: e.transpose(psb[0:64, kh * 128:(kh + 1) * 128], ck_tm[:, c, kh * 64:(kh + 1) * 64], K.identb[:]), reads=[T_ck, K.T_const], writes=[Tp])
        P.op("dve", lambda e, c=c, psb=psb: e.tensor_copy(ckT[:, :, c * 128:(c + 1) * 128], psb[0:64, 0:512].rearrange("p (h k) -> p h k", h=4)), reads=[Tp], writes=[T_ckT])
    qr = Ring(K, "pq", [64, 8, 256], BF16, 2)
    kr = Ring(K, "pk", [64, 256], BF16, 2)
    vr = Ring(K, "pv", [128, 2, 64], BF16, 2)
    orr = Ring(K, "po", [64, 8, 256], BF16, 2)
    for kh in range(4):
        for s in range(4):
            qt, Tq, sq = qr.next()
            kt, Tk, sk = kr.next()
            vt, Tv, sv = vr.next()
            ot, To, so = orr.next()
            P.dma("sp", qt[:], S["q1T_d"][kh * 512:(kh + 1) * 512, s * 256:(s + 1) * 256].rearrange("(g d) t -> d g t", d=64), sq, writes=[Tq])
            P.dma("sp", kt[:], S["k1T_d"][kh * 64:(kh + 1) * 64, s * 256:(s + 1) * 256], sk, writes=[Tk])
            P.dma("sp", vt[:], S["v1_d"][s * 256:(s + 1) * 256, kh * 64:(kh + 1) * 64].rearrange("(c p) f -> p c f", p=128), sv, writes=[Tv])
            for g in range(8):
                sl = [(kt[:, c * 128:(c + 1) * 128], [Tk], vt[:, c, :], [Tv], None, [], 0) for c in range(2)]
                hq = kh * 8 + g
                attn_core(K, sl, 256, (qt[:, g, :], [Tq]), ot[:, g, :], To, scale, extra_den=(snk[0:64, hq:hq + 1], [T_c]), pools=pools)
            P.dma("sp", o1T[kh * 512:(kh + 1) * 512, s * 256:(s + 1) * 256].rearrange("(g d) t -> d g t", d=64), ot[:], so, reads=[To])
    NT = SEXT * 128
    kh_k = Ring(K, "sk", [64, NT], BF16, 2)
    kh_v = Ring(K, "sv", [128, SEXT, 64], BF16, 2)
    kh_q = Ring(K, "sq", [64, 8, 2048], BF16, 1)
    kh_o = Ring(K, "so", [64, 8, 2048], BF16, 1)
    for kh in range(4):
        kt, Tk, sk = kh_k.next()
        vt, Tv, sv = kh_v.next()
        qt, Tq, sq = kh_q.next()
        ot, To, so = kh_o.next()
        P.dma("sp", kt[:], S["k1T_d"][kh * 64:(kh + 1) * 64, 1024:1024 + NT], sk, writes=[Tk])
        P.dma("sp", vt[:], S["v1_d"][1024:1024 + NT, kh * 64:(kh + 1) * 64].rearrange("(c p) f -> p c f", p=128), sv, writes=[Tv])
        P.dma("sp", qt[:], S["q1T_d"][kh * 512:(kh + 1) * 512, 1024:1024 + 2048].rearrange("(g d) t -> d g t", d=64), sq, writes=[Tq])
        for i in range(16):
            for gh in range(2):
                sl = []
                for j in (i - 1, i, i + 1):
                    if j < 0 or j > 16:
                        continue
                    mask = None
                    if j == i - 1:
                        mask = trib[:, 0, :].unsqueeze(1).to_broadcast([128, 4, 128])
                    elif j == i + 1:
                        mask = trib[:, 1, :].unsqueeze(1).to_broadcast([128, 4, 128])
                    sl.append((kt[:, j * 128:(j + 1) * 128], [Tk], vt[:, j, :], [Tv], mask, [T_c], 0))
                for c in range(2):
                    sl.append((ckT[:, kh, c * 128:(c + 1) * 128], [T_ckT], cvt[:, c, kh * 64:(kh + 1) * 64], [T_cv], None, [], 0))
                hq0 = kh * 8 + gh * 4
                ed = snk[0:64, hq0:hq0 + 4].unsqueeze(2).to_broadcast([64, 4, 128])
                attn_core(K, sl, 512, (qt[:, gh * 4:gh * 4 + 4, i * 128:(i + 1) * 128], [Tq]), ot[:, gh * 4:gh * 4 + 4, i * 128:(i + 1) * 128], To, scale,
                          extra_den=(ed, [T_c]), pools=pools, g4=True)
        P.dma("sp", o1T[kh * 512:(kh + 1) * 512, 1024:1024 + 2048].rearrange("(g d) t -> d g t", d=64), ot[:], so, reads=[To])
    K.end()


def declare_io(K):
    K.inp("ident", [128, 128])
    K.inp("cond", [2, D])
    K.inp("xp", [NPT * 128, D])
    K.inp("xs", [4096, D])
    K.inp("w_ada", [2, D, 6 * D])
    K.inp("b_ada", [2, 6 * D])
    K.inp("norm_mix", [2, D])
    K.inp("norm_mlp", [2, D])
    K.inp("ab_w_in", [D, 7200])
    K.inp("w_gates", [D, 32])
    K.inp("cache_a_k", [256, 1024])
    K.inp("cache_a_v", [256, 1024])
    K.inp("na_mask", [2, 8, 6, 128, 256])
    K.inp("conv_w", [3, 3072])
    K.inp("alog_dt", [2, 16])
    K.inp("tri", [4, 128, 128])
    K.inp("s0", [2, 8, 128, 128])
    K.inp("onorm", [1, 128])
    K.inp("ab_w_out", [D, D])
    K.inp("w_mlp_in", [2, D, 4 * D])
    K.inp("w_mlp_out", [2, 4 * D, D])
    K.inp("c_w_qkv", [D, 2560])
    K.inp("c_w_out", [D, D])
    K.inp("cache_c_k", [256, 256])
    K.inp("cache_c_v", [256, 256])
    K.inp("c_sink", [1, 32])
    K.inp("final_norm", [1, D])
    K.inp("rope_cos", [128, SEXT * 128])
    K.inp("rope_sin", [128, SEXT * 128])
    K.inp("rope_perm", [128, 128])
    K.outp("nck", [NPT * 128, 256])
    K.outp("ncv", [NPT * 128, 256])
    K.outp("y_p", [NPT * 128, D])
    K.outp("y_s", [2048, D])
    K.outp("nbf", [4, 8, 128, 128])
    K.outp("nbb", [4, 8, 128, 128])
    K.outp("nak", [NPT * 128, 1024])
    K.outp("nav", [NPT * 128, 1024])


def build(stop=99, debug=()):
    nc = bass.Bass("TRN2", target_bir_lowering=False)
    K = Ctx(nc)
    declare_io(K)
    phases = [phase_consts, phase_ada, lambda K: phase_l0_inproj(K, 1), lambda K: phase_l0_inproj(K, 2), phase_attn_a, phase_dn_prep, phase_dn_scan, lambda K: phase_mlp(K, 0), phase_l1_inproj, phase_attn_c, lambda K: phase_mlp(K, 1)]
    for i, ph in enumerate(phases):
        if i >= stop:
            break
        ph(K)
    if debug:
        K.begin()
        s = K.getsem()
        for name in debug:
            src = K.dscr[name]
            o = K.outp("dbg_" + name, src.shape, src.dtype)
            nr = src.shape[0]
            step = max(1, min(nr, (1 << 20) // (src.shape[1] * 4)))
            for r0 in range(0, nr, step):
                K.P.dma("sp", o[r0:min(nr, r0 + step)], src[r0:min(nr, r0 + step)], s)
        K.end()
    K.pes.close()
    return nc, K


def na_mask_host(rel_bias, flip):
    out = np.full((2, 8, 6, 128, 256), -30000.0, np.float32)
    qq = np.arange(256)
    kk = np.arange(768)
    for cl in range(2):
        qr = (0 if cl == 0 else 12) + qq // 64
        qc = qq % 64
        kr = (0 if cl == 0 else 8) + kk // 64
        kc = kk % 64
        if flip:
            qr, qc, kr, kc = 63 - qr, 63 - qc, 63 - kr, 63 - kc
        rs = np.clip(qr - 4, 0, 56)
        cs = np.clip(qc - 8, 0, 48)
        vr = (kr[:, None] >= rs[None, :]) & (kr[:, None] < rs[None, :] + 8)
        vc = (kc[:, None] >= cs[None, :]) & (kc[:, None] < cs[None, :] + 16)
        valid = vr & vc
        dr = np.clip(kr[:, None] - qr[None, :] + 7, 0, 14)
        dc = np.clip(kc[:, None] - qc[None, :] + 15, 0, 30)
        for h in range(8):
            b = rel_bias[h][dr, dc]
            m = np.where(valid, b, np.float32(-30000.0)).astype(np.float32)
            out[cl, h] = m.reshape(6, 128, 256)
    return out


def tri_host():
    i = np.arange(128)[:, None]
    j = np.arange(128)[None, :]
    return np.stack([(j <= i), (j >= i), (j < i), (j > i)]).astype(np.float32)


def rope_host(flip):
    nf = 16
    inv = (10000.0 ** (-np.arange(nf, dtype=np.float32) / nf)).astype(np.float32)
    tloc = np.arange(SEXT * 128)
    tok = (4095 - tloc) if flip else tloc
    pos = np.stack([tok // 64, tok % 64], 0).astype(np.float32)
    d = np.arange(128) % 64
    a, b, fidx = d // 32, (d % 32) // 16, d % 16
    ang = pos[a, :] * inv[fidx][:, None]
    cos = np.cos(ang).astype(np.float32)
    sin = np.sin(ang).astype(np.float32)
    perm = np.zeros((128, 128), np.float32)
    for m in range(128):
        bm = (m % 32) // 16
        partner = m + 16 if bm == 0 else m - 16
        perm[partner, m] = -1.0 if bm == 0 else 1.0
    return cos, sin, perm


def host_inputs(inputs, c):
    seq, flip = c // 2, c % 2
    f = lambda a: np.ascontiguousarray(a, dtype=np.float32)
    xp = inputs["x_prompt"][4 * c:4 * c + 4]
    xs = inputs["x_sample"][seq]
    if flip:
        xp = xp[:, ::-1]
        xs = xs[::-1]
    wg = inputs["ab_w_in"][0][:, 7168:7200]
    if flip:
        wg = np.concatenate([wg[:, 8:16], wg[:, 0:8], wg[:, 24:32], wg[:, 16:24]], axis=1)
    m = {
        "ident": np.eye(128, dtype=np.float32),
        "cond": f(np.stack([inputs["c_ctx"], inputs["c"][seq]])),
        "xp": f(xp.reshape(NPT * 128, D)),
        "xs": f(xs),
        "w_ada": inputs["w_ada"], "b_ada": inputs["b_ada"],
        "norm_mix": inputs["norm_mix"], "norm_mlp": inputs["norm_mlp"],
        "ab_w_in": inputs["ab_w_in"][0], "w_gates": f(wg),
        "cache_a_k": f(inputs["cache_a_k"][seq, 0].reshape(256, 1024)),
        "cache_a_v": f(inputs["cache_a_v"][seq, 0].reshape(256, 1024)),
        "na_mask": na_mask_host(inputs["a_rel_bias"][0], flip),
        "ab_w_out": inputs["ab_w_out"][0], "w_mlp_in": inputs["w_mlp_in"], "w_mlp_out": inputs["w_mlp_out"],
        "c_w_qkv": inputs["c_w_qkv"][0], "c_w_out": inputs["c_w_out"][0],
        "cache_c_k": f(inputs["cache_c_k"][seq, 0].reshape(256, 256)), "cache_c_v": f(inputs["cache_c_v"][seq, 0].reshape(256, 256)),
        "c_sink": f(inputs["c_sink"][0].reshape(1, 32)), "final_norm": f(inputs["final_norm"].reshape(1, D)),
        "rope_cos": rope_host(flip)[0], "rope_sin": rope_host(flip)[1], "rope_perm": rope_host(flip)[2],
        "conv_w": f(inputs["b_conv"][0][::-1] if flip else inputs["b_conv"][0]),
        "alog_dt": f(np.stack([(inputs["b_a_log"][0][::-1] if flip else inputs["b_a_log"][0]).reshape(16),
                               (inputs["b_dt_bias"][0][::-1] if flip else inputs["b_dt_bias"][0]).reshape(16)])),
        "tri": tri_host(),
        "s0": f(np.stack([inputs["state_b_bwd"][seq, 0], inputs["state_b_fwd"][seq, 0]]) if flip else
                np.stack([inputs["state_b_fwd"][seq, 0], inputs["state_b_bwd"][seq, 0]])),
        "onorm": f(inputs["b_out_norm"][0].reshape(1, 128)),
    }
    return m


_CACHE = {}


def kernel(**inputs):
    inputs = {k: np.asarray(v) for k, v in inputs.items()}
    if "nc" not in _CACHE:
        _CACHE["nc"] = build()
    nc, K = _CACHE["nc"]
    maps = [host_inputs(inputs, c) for c in range(8)]
    res = run_bass_kernel_spmd(nc, maps, core_ids=list(range(8))).results
    f32 = np.float32
    y_p = np.zeros((32, 256, D), f32)
    y_s = np.zeros((4, 4096, D), f32)
    nak = np.zeros((32, 1, 256, 8, 128), f32)
    nav = np.zeros((32, 1, 256, 8, 128), f32)
    nbf = np.zeros((32, 1, 8, 128, 128), f32)
    nbb = np.zeros((32, 1, 8, 128, 128), f32)
    nck = np.zeros((32, 1, 256, 4, 64), f32)
    ncv = np.zeros((32, 1, 256, 4, 64), f32)
    for c in range(8):
        seq, flip = c // 2, c % 2
        r = {k: np.asarray(v, dtype=f32) for k, v in res[c].items()}
        fl = (lambda a: a[:, ::-1]) if flip else (lambda a: a)
        sl = slice(4 * c, 4 * c + 4)
        y_p[sl] = fl(r["y_p"].reshape(4, 256, D))
        if flip:
            y_s[seq, 2048:4096] = r["y_s"][::-1]
        else:
            y_s[seq, 0:2048] = r["y_s"]
        nak[sl, 0] = fl(r["nak"].reshape(4, 256, 8, 128))
        nav[sl, 0] = fl(r["nav"].reshape(4, 256, 8, 128))
        nck[sl, 0] = fl(r["nck"].reshape(4, 256, 4, 64))
        ncv[sl, 0] = fl(r["ncv"].reshape(4, 256, 4, 64))
        if flip:
            nbf[sl, 0], nbb[sl, 0] = r["nbb"], r["nbf"]
        else:
            nbf[sl, 0], nbb[sl, 0] = r["nbf"], r["nbb"]
    return (y_p, y_s, nak, nav, nbf, nbb, nck, ncv)
```

```python
import contextlib
import numpy as np
import concourse.bass as bass
import concourse.mybir as mybir

F32 = mybir.dt.float32
BF16 = mybir.dt.bfloat16
I32 = mybir.dt.int32
AF = mybir.ActivationFunctionType
ALU = mybir.AluOpType
AX = mybir.AxisListType

ENGS = ("pe", "act", "dve", "pool", "sp")
HANDLES = {"pe": "tensor", "act": "scalar", "dve": "vector", "pool": "gpsimd", "sp": "sync"}
SEM_LIMIT = 16000


class Tile:
    __slots__ = ("name", "writer", "readers", "excl")

    def __init__(self, name=""):
        self.name = name
        self.writer = None
        self.readers = {}
        self.excl = False


class DSem:
    def __init__(self, prog, name):
        self.prog = prog
        self.name = name
        self.gen = 0
        self.h = prog.nc.alloc_semaphore(name=name)
        self.count = 0
        self.last = None

    def bump(self):
        if self.count + 16 > SEM_LIMIT:
            self.gen += 1
            self.h = self.prog.nc.alloc_semaphore(name=f"{self.name}_g{self.gen}")
            self.count = 0
        self.count += 16
        return self.h, self.count


class Ins:
    __slots__ = ("eng", "fn", "deps", "sem", "count", "needed", "is_dma", "epoch")

    def __init__(self, eng, fn, is_dma=False):
        self.eng = eng
        self.fn = fn
        self.deps = []
        self.sem = None
        self.count = None
        self.needed = False
        self.is_dma = is_dma
        self.epoch = 0


class Prog:
    def __init__(self, nc):
        self.nc = nc
        self.lists = {e: [] for e in ENGS}
        self.esem = {e: nc.alloc_semaphore(name=f"es_{e}_0") for e in ENGS}
        self.esem_gen = {e: 0 for e in ENGS}
        self.ecount = {e: 0 for e in ENGS}
        self.known = {e: {} for e in ENGS}
        self.epoch = 0
        self.n_ins = 0
        self.dsems = []
        self.last_ins = {e: None for e in ENGS}

    def tile(self, name=""):
        return Tile(name)

    def tiles(self, n, name=""):
        return [Tile(f"{name}{i}") for i in range(n)]

    def dsem(self, name):
        d = DSem(self, name)
        self.dsems.append(d)
        return d

    def _add(self, eng, fn, reads, writes, dsem=None):
        ins = Ins(eng, fn, is_dma=dsem is not None)
        ins.epoch = self.epoch
        deps = []
        for t in reads:
            if t.writer is not None:
                deps.append(t.writer)
            if t.excl:
                deps.extend(r for r in t.readers.values() if r.eng != eng)
        for t in writes:
            if t.writer is not None:
                deps.append(t.writer)
            deps.extend(t.readers.values())
        if dsem is not None:
            if dsem.last is not None:
                deps.append(dsem.last)
            ins.sem, ins.count = dsem.bump()
            ins.needed = True
            dsem.last = ins
        out = []
        seen = set()
        for d in deps:
            if d is ins or id(d) in seen:
                continue
            seen.add(id(d))
            if d.epoch < self.epoch:
                continue
            if eng == "pe" and d.eng == "pe" and not d.is_dma and dsem is None:
                continue
            out.append(d)
        ins.deps = out
        for d in out:
            d.needed = True
        for t in reads:
            key = (eng, dsem.name) if dsem is not None else eng
            t.readers[key] = ins
        for t in writes:
            t.writer = ins
            t.readers = {}
        self.lists[eng].append(ins)
        self.last_ins[eng] = ins
        self.n_ins += 1
        return ins

    def op(self, eng, fn, reads=(), writes=()):
        return self._add(eng, fn, list(reads), list(writes))

    def dma(self, eng, out, in_, dsem, reads=(), writes=()):
        return self._add(eng, lambda e: e.dma_start(out=out, in_=in_), list(reads), list(writes), dsem=dsem)

    def barrier(self):
        deps = [i for i in self.last_ins.values() if i is not None]
        deps += [d.last for d in self.dsems if d.last is not None]
        deps = [d for d in deps if d.epoch == self.epoch]
        for d in deps:
            d.needed = True
        for e in ENGS:
            ins = Ins(e, None)
            ins.epoch = self.epoch
            ins.deps = list(deps)
            self.lists[e].append(ins)

    def flush(self, final=False):
        self.barrier()
        for e in ENGS:
            for ins in self.lists[e]:
                if ins.is_dma or ins.fn is None:
                    continue
                if ins.needed:
                    if self.ecount[e] + 1 > SEM_LIMIT:
                        self.esem_gen[e] += 1
                        self.esem[e] = self.nc.alloc_semaphore(name=f"es_{e}_{self.esem_gen[e]}")
                        self.ecount[e] = 0
                    self.ecount[e] += 1
                    ins.sem = self.esem[e]
                    ins.count = self.ecount[e]
        prog = self

        def run(e, h):
            known = prog.known[e]
            for ins in prog.lists[e]:
                need = {}
                for d in ins.deps:
                    k = id(d.sem)
                    if known.get(k, 0) >= d.count:
                        continue
                    if k not in need or need[k][1] < d.count:
                        need[k] = (d.sem, d.count)
                for k, (s, v) in need.items():
                    h.wait_ge(s, v)
                    known[k] = v
                if ins.fn is None:
                    continue
                bi = ins.fn(h)
                if ins.is_dma:
                    bi.then_inc(ins.sem, 16)
                elif ins.needed:
                    bi.then_inc(ins.sem, 1)

        with self.nc.Block() as block:
            for e in ENGS:
                if not self.lists[e]:
                    continue
                dec = getattr(block, HANDLES[e])

                def mk(e):
                    def _f(h):
                        run(e, h)
                    return _f
                dec(mk(e))
        self.lists = {e: [] for e in ENGS}
        self.epoch += 1

from concourse.bass_utils import run_bass_kernel_spmd

D = 2048
KC = 16
EPS = 1e-6
NPT = 8
NS1 = 19
NG1 = NPT + NS1
NS2 = 13
TOK1 = NG1 * 128
TOKB = (NG1 + NS2) * 128
SEXT = 17


class Ring:
    def __init__(self, K, name, shape, dt, n, sw=False):
        self.bufs = [K.sb(f"{name}{i}", shape, dt) for i in range(n)]
        self.tiles = [Tile(f"{name}{i}") for i in range(n)]
        self.sems = [K.getsem(sw) for i in range(n)]
        self.i = 0

    def next(self):
        k = self.i % len(self.bufs)
        self.i += 1
        return self.bufs[k], self.tiles[k], self.sems[k]


class Ctx:
    def __init__(self, nc):
        self.nc = nc
        self.P = Prog(nc)
        self.es = None
        self.pes = contextlib.ExitStack()
        self.uid = 0
        self.sem_pool = {False: [], True: []}
        self.sem_used = {False: [], True: []}
        self.PS = [nc.alloc_psum_tensor(f"psb{i}", [128, 512], F32) for i in range(8)]
        self.TPS = [Tile(f"ps{i}") for i in range(8)]
        for t_ in self.TPS:
            t_.excl = True
        self.psi = [0, 0]
        self.din = {}
        self.dout = {}
        self.dscr = {}

    def inp(self, name, shape, dt=F32):
        self.din[name] = self.nc.dram_tensor(name, list(shape), dt, kind="ExternalInput").ap()
        return self.din[name]

    def outp(self, name, shape, dt=F32):
        self.dout[name] = self.nc.dram_tensor(name, list(shape), dt, kind="ExternalOutput").ap()
        return self.dout[name]

    def scr(self, name, shape, dt):
        self.dscr[name] = self.nc.dram_tensor(name, list(shape), dt).ap()
        return self.dscr[name]

    def begin(self):
        self.es = contextlib.ExitStack()

    def end(self):
        self.P.flush()
        self.es.close()
        self.es = None
        for k in (False, True):
            self.sem_pool[k].extend(self.sem_used[k])
            self.sem_used[k] = []

    def sb(self, name, shape, dt):
        self.uid += 1
        return self.es.enter_context(self.nc.sbuf_tensor(f"{name}_{self.uid}", list(shape), dt))

    def psb(self, name, shape, dt):
        self.uid += 1
        return self.pes.enter_context(self.nc.sbuf_tensor(f"{name}_{self.uid}", list(shape), dt))

    def getsem(self, sw=False):
        if self.sem_pool[sw]:
            s = self.sem_pool[sw].pop()
        else:
            self.uid += 1
            s = self.P.dsem(f"ds{'w' if sw else 'h'}{self.uid}")
        self.sem_used[sw].append(s)
        return s

    def dump(self, name, ap, tiles):
        import os
        if not os.environ.get("DN_DEBUG") or name in self.dscr:
            return
        d = self.scr(name, list(ap.shape), ap.dtype)
        self.P.dma("sp", d, ap, self.getsem(), reads=tiles)

    def ps(self, g=0):
        k = g * 4 + self.psi[g] % 4
        self.psi[g] += 1
        return self.PS[k], self.TPS[k]


def rows_T(K, dst, T_dst, src2d, n, stage, T_stage, sem):
    P = K.P
    P.dma("sp", stage[0:n, :], src2d, sem, writes=[T_stage])
    ps, Tp = K.ps()
    P.op("pe", lambda e: e.transpose(ps[:, 0:n], stage[0:n, :], K.identf[0:n, 0:n]), reads=[T_stage, K.T_const], writes=[Tp])
    P.op("dve", lambda e: e.tensor_copy(dst, ps[:, 0:n]), reads=[Tp], writes=[T_dst])


def phase_consts(K):
    P = K.P
    K.begin()
    K.T_const = Tile("const")
    K.identf = K.psb("identf", [128, 128], F32)
    K.identb = K.psb("identb", [128, 128], BF16)
    K.onesf = K.psb("onesf", [128, 128], F32)
    K.onesb = K.psb("onesb", [128, 128], BF16)
    s = K.getsem()
    s2 = K.getsem(True)
    P.dma("sp", K.identf[:], K.din["ident"], s, writes=[K.T_const])
    P.dma("pool", K.identb[:], K.din["ident"], s2, writes=[K.T_const])
    P.op("dve", lambda e: e.memset(K.onesf[:], 1.0), writes=[K.T_const])
    P.op("dve", lambda e: e.memset(K.onesb[:], 1.0), writes=[K.T_const])
    K.end()


def phase_ada(K):
    nc, P = K.nc, K.P
    modrow_d = K.scr("modrow_d", [2, 2, 6 * D], F32)
    K.begin()
    cond = K.sb("cond", [2, D], F32)
    cs = K.sb("cs", [2, D], F32)
    condT = K.sb("condT", [128, KC, 2], BF16)
    brow = K.sb("brow", [2, 6 * D], F32)
    modrow = K.sb("modrow", [2, 6 * D], F32)
    T_cond, T_cs, T_condT, T_brow, T_modrow, T_mrd = P.tiles(6, "ada")
    s0, s1 = K.getsem(), K.getsem()
    wr = Ring(K, "adaw", [128, KC, 512], BF16, 3, sw=True)
    P.dma("sp", cond[:], K.din["cond"], s0, writes=[T_cond])
    P.op("act", lambda e: e.activation(out=cs[:], in_=cond[:], func=AF.Silu), reads=[T_cond], writes=[T_cs])
    ps, Tp = K.ps()
    for kc in range(KC):
        P.op("pe", lambda e, kc=kc, ps=ps: e.transpose(ps[:, kc * 2:kc * 2 + 2], cs[0:2, kc * 128:(kc + 1) * 128], K.identf[0:2, 0:2]),
             reads=[T_cs, K.T_const], writes=[Tp])
    P.op("dve", lambda e, ps=ps: e.tensor_copy(condT[:].rearrange("p k c -> p (k c)"), ps[:, 0:2 * KC]), reads=[Tp], writes=[T_condT])
    for l in range(2):
        P.dma("sp", brow[:], K.din["b_ada"][l:l + 1, :].partition_broadcast(2), s0, writes=[T_brow])
        for cg in range(24):
            wt, Tw, sw = wr.next()
            P.dma("pool", wt[:], K.din["w_ada"][l, :, cg * 512:(cg + 1) * 512].rearrange("(kc p) n -> p kc n", p=128), sw, writes=[Tw])
            ps, Tp = K.ps()
            for kc in range(KC):
                P.op("pe", lambda e, kc=kc, ps=ps, wt=wt: e.matmul(ps[0:2, :], condT[:, kc, :], wt[:, kc, :], start=(kc == 0), stop=(kc == KC - 1)),
                     reads=[T_condT, Tw], writes=[Tp])
            P.op("dve", lambda e, ps=ps, cg=cg: e.tensor_tensor(modrow[0:2, cg * 512:(cg + 1) * 512], ps[0:2, :], brow[0:2, cg * 512:(cg + 1) * 512], ALU.add),
                 reads=[Tp, T_brow], writes=[T_modrow])
        P.dma("sp", modrow_d[l], modrow[:], s1, reads=[T_modrow], writes=[T_mrd])
    K.end()
    K.begin()
    K.T_mod = Tile("mod")
    K.modF = [[K.psb(f"modF{l}{c}", [128, 96], F32) for c in range(2)] for l in range(2)]
    K.gsF = [[[K.psb(f"gsF{l}{w}{c}", [128, KC], F32) for c in range(2)] for w in range(2)] for l in range(2)]
    K.fnorm = K.psb("fnormF", [128, KC], F32)
    stage = K.sb("stg", [128, 128], F32)
    gF = K.sb("gF", [128, KC], F32)
    T_stage, T_g = P.tiles(2, "adaf")
    s0 = K.getsem()
    for l in range(2):
        for c in range(2):
            rows_T(K, K.modF[l][c][:], K.T_mod, modrow_d[l, c].rearrange("(r p) -> r p", p=128), 96, stage, T_stage, s0)
        for w, nm in enumerate(("norm_mix", "norm_mlp")):
            rows_T(K, gF[:], T_g, K.din[nm][l].rearrange("(r p) -> r p", p=128), KC, stage, T_stage, s0)
            for c in range(2):
                sc = K.modF[l][c][:, (1 + 3 * w) * KC:(2 + 3 * w) * KC]
                P.op("dve", lambda e, l=l, w=w, c=c, sc=sc: e.scalar_tensor_tensor(K.gsF[l][w][c][:], sc, 1.0, gF[:], ALU.add, ALU.mult),
                     reads=[K.T_mod, T_g], writes=[K.T_mod])
    K.end()


class NormT:
    def __init__(self, K, n=2):
        self.K = K
        self.ss = Ring(K, "nss", [128, 2], F32, n)
        self.xn = Ring(K, "nxn", [128, D], BF16, n)

    def run(self, xt, T_x, gs, shift, dst_fn, T_dst_fn):
        K = self.K
        P = K.P
        ss, T_ss, _ = self.ss.next()
        xn, T_xn, _ = self.xn.next()
        P.op("act", lambda e: e.activation(out=xn[:], in_=xt, func=AF.Square, accum_out=ss[:, 0:1]), reads=[T_x], writes=[T_xn, T_ss])
        import os
        NTL = int(os.environ.get("NT_LEVEL", "9"))
        if NTL < 2:
            return
        P.op("act", lambda e: e.activation(out=ss[:, 1:2], in_=ss[:, 0:1], func=AF.Sqrt, bias=EPS, scale=1.0 / D), reads=[T_ss], writes=[T_ss])
        P.op("dve", lambda e: e.reciprocal(ss[:, 1:2], ss[:, 1:2]), reads=[T_ss], writes=[T_ss])
        if NTL < 3:
            return
        P.op("act", lambda e: e.activation(out=xn[:], in_=xt, func=AF.Identity, scale=ss[:, 1:2]), reads=[T_x, T_ss], writes=[T_xn])
        if NTL < 4:
            return
        for half in range(2):
            ps, Tp = K.ps()
            psb = ps[:].bitcast(BF16)
            for j in range(8):
                kc = half * 8 + j
                P.op("pe", lambda e, j=j, kc=kc, psb=psb: e.transpose(psb[:, j * 128:(j + 1) * 128], xn[:, kc * 128:(kc + 1) * 128], K.identb[:]),
                     reads=[T_xn, K.T_const], writes=[Tp])
            for j in range(8):
                kc = half * 8 + j
                if True:
                    P.op("dve", lambda e, j=j, kc=kc, psb=psb: e.tensor_scalar(dst_fn(kc), psb[:, j * 128:(j + 1) * 128], gs[:, kc:kc + 1], shift[:, kc:kc + 1], ALU.mult, ALU.add),
                         reads=[Tp, K.T_mod], writes=[T_dst_fn(kc)])
                else:
                    P.op("act", lambda e, j=j, kc=kc, psb=psb: e.activation(out=dst_fn(kc), in_=psb[:, j * 128:(j + 1) * 128], func=AF.Identity, scale=gs[:, kc:kc + 1], bias=shift[:, kc:kc + 1]),
                         reads=[Tp, K.T_mod], writes=[T_dst_fn(kc)])


def token_groups(n_tb, breaks=()):
    out = []
    pts = [0] + list(breaks) + [n_tb]
    for a, b in zip(pts[:-1], pts[1:]):
        t = a
        while t < b:
            m = min(4, b - t)
            out.append((t, m))
            t += m
    return out


def phase_l0_inproj(K, which_pass):
    nc, P = K.nc, K.P
    if which_pass == 1:
        K.scr("qaT_d", [1024, TOK1], BF16)
        K.scr("kaT_d", [1024, TOK1], BF16)
        K.scr("va_d", [TOK1, 1024], BF16)
        K.scr("qkvT_d", [3072, TOKB], F32)
        K.scr("z_d", [TOK1, 1024], F32)
        K.scr("gates_d", [TOKB, 32], F32)
        ntb = NG1
        srcs = [(K.din["xp"][g * 128:(g + 1) * 128, :], 0) for g in range(NPT)] + \
               [(K.din["xs"][g * 128:(g + 1) * 128, :], 1) for g in range(NS1)]
        tok0 = 0
        groups = token_groups(NG1, breaks=(NPT,))
    else:
        ntb = NS2
        srcs = [(K.din["xs"][(NS1 + g) * 128:(NS1 + g + 1) * 128, :], 1) for g in range(NS2)]
        tok0 = TOK1
        groups = token_groups(NS2)
    K.begin()
    hT = K.sb("hT", [128, KC, ntb * 128], BF16)
    T_h = [[Tile(f"h{g}_{kc}") for kc in range(KC)] for g in range(ntb)]
    xr = Ring(K, "xin", [128, D], F32, 3)
    nt = NormT(K)
    for g, (src, c) in enumerate(srcs):
        xt, T_x, sx = xr.next()
        P.dma("sp", xt[:], src, sx, writes=[T_x])
        nt.run(xt[:], T_x, K.gsF[0][0][c], K.modF[0][c][:, 0:KC],
               lambda kc, g=g: hT[:, kc, g * 128:(g + 1) * 128], lambda kc, g=g: T_h[g][kc])
    wr = Ring(K, "w0", [128, KC, 512], BF16, 3, sw=True)
    st32 = Ring(K, "st32", [128, 512], F32, 3)
    st16 = Ring(K, "st16", [128, 512], BF16, 3)
    W = K.din["ab_w_in"]
    evi = [0]

    def evac(dst, src, T_src, T_dst):
        evi[0] += 1
        if evi[0] % 2:
            P.op("dve", lambda e: e.tensor_copy(dst, src), reads=[T_src], writes=[T_dst])
        else:
            P.op("act", lambda e: e.copy(dst, src), reads=[T_src], writes=[T_dst])

    def fm(wt, Tw, ncol_blocks, dst_d, row0, f32):
        for sub in range(ncol_blocks):
            for (t0, m) in groups:
                n = m * 128
                ps, Tp = K.ps()
                for kc in range(KC):
                    P.op("pe", lambda e, kc=kc, ps=ps, sub=sub, t0=t0, n=n: e.matmul(ps[:, 0:n], wt[:, kc, sub * 128:(sub + 1) * 128], hT[:, kc, t0 * 128:t0 * 128 + n], start=(kc == 0), stop=(kc == KC - 1)),
                         reads=[Tw] + [T_h[t0 + i][kc] for i in range(m)], writes=[Tp])
                sg, Ts, ss_ = (st32 if f32 else st16).next()
                evac(sg[:, 0:n], ps[:, 0:n], Tp, Ts)
                P.dma("sp", dst_d[row0 + sub * 128:row0 + (sub + 1) * 128, tok0 + t0 * 128:tok0 + t0 * 128 + n], sg[:, 0:n], ss_, reads=[Ts])

    def tm(wt, Tw, ncols, tbs, dests):
        for g in tbs:
            ps, Tp = K.ps()
            for kc in range(KC):
                P.op("pe", lambda e, kc=kc, ps=ps, g=g: e.matmul(ps[:, 0:ncols], hT[:, kc, g * 128:(g + 1) * 128], wt[:, kc, 0:ncols], start=(kc == 0), stop=(kc == KC - 1)),
                     reads=[Tw, T_h[g][kc]], writes=[Tp])
            for (dfn, f32) in dests:
                d = dfn(g)
                if d is None:
                    continue
                sg, Ts, ss_ = (st32 if f32 else st16).next()
                evac(sg[:, 0:ncols], ps[:, 0:ncols], Tp, Ts)
                P.dma("sp", d, sg[:, 0:ncols], ss_, reads=[Ts])

    wg32 = K.sb("wg32", [128, KC, 32], F32)
    T_wg32 = Tile("wg32")
    s_wg = K.getsem()

    def loadw(src, ncols=512):
        wt, Tw, sw = wr.next()
        if ncols == 512:
            P.dma("pool", wt[:, :, 0:ncols], src.rearrange("(kc p) n -> p kc n", p=128), sw, writes=[Tw])
        else:
            P.dma("sp", wg32[:], src.rearrange("(kc p) n -> p kc n", p=128), s_wg, writes=[T_wg32])
            P.op("dve", lambda e, wt=wt: e.tensor_copy(wt[:, :, 0:ncols], wg32[:]), reads=[T_wg32], writes=[Tw])
        return wt, Tw

    S = K.dscr
    if which_pass == 1:
        import os
        for t in range(int(os.environ.get('KDBG_NT', '14'))):
            wt, Tw = loadw(W[:, t * 512:(t + 1) * 512])
            if t < 2:
                fm(wt, Tw, 4, S["qaT_d"], t * 512, False)
            elif t < 4:
                fm(wt, Tw, 4, S["kaT_d"], (t - 2) * 512, False)
                tm(wt, Tw, 512, range(NPT), [(lambda g, t=t: K.dout["nak"][g * 128:(g + 1) * 128, (t - 2) * 512:(t - 1) * 512], True)])
            elif t < 6:
                tm(wt, Tw, 512, range(NG1), [(lambda g, t=t: S["va_d"][g * 128:(g + 1) * 128, (t - 4) * 512:(t - 3) * 512], False),
                                            (lambda g, t=t: K.dout["nav"][g * 128:(g + 1) * 128, (t - 4) * 512:(t - 3) * 512] if g < NPT else None, True)])
            elif t < 12:
                fm(wt, Tw, 4, S["qkvT_d"], (t - 6) * 512, True)
            else:
                tm(wt, Tw, 512, range(NG1), [(lambda g, t=t: S["z_d"][g * 128:(g + 1) * 128, (t - 12) * 512:(t - 11) * 512], True)])
        if int(os.environ.get('KDBG_G', '1')):
          wt, Tw = loadw(K.din["w_gates"], 32)
          tm(wt, Tw, 32, range(NG1), [(lambda g: S["gates_d"][g * 128:(g + 1) * 128, :], True)])
    else:
        for t in range(8, 12):
            wt, Tw = loadw(W[:, t * 512:(t + 1) * 512])
            fm(wt, Tw, 4, S["qkvT_d"], (t - 6) * 512, True)
        wt, Tw = loadw(K.din["w_gates"], 32)
        tm(wt, Tw, 32, range(NS2), [(lambda g: S["gates_d"][tok0 + g * 128:tok0 + (g + 1) * 128, :], True)])
    K.end()


NCAT = NPT + SEXT
TOKC = NCAT * 128


def attn_core(K, S_list, nq, rhs_q, out_ap, T_out, scale, extra_den=None, pools=None, g4=False):
    P = K.P
    q_ap, q_tiles = rhs_q
    M = out_ap.shape[0]
    psn, Tn = K.ps(1)
    psd, Td = K.ps(1)
    pt_ring, rec_ring = pools
    n = len(S_list)
    v3 = (lambda ap: ap.rearrange("p (g t) -> p g t", g=4)) if g4 else (lambda ap: ap)
    def s_mm(i):
        lk, tk = S_list[i][0], S_list[i][1]
        pss, Ts = K.ps(0)
        P.op("pe", lambda e, pss=pss, lk=lk: e.matmul(v3(pss[:, 0:nq]), lk, q_ap, start=True, stop=True), reads=tk + q_tiles, writes=[Ts])
        return pss, Ts

    nxt = s_mm(0)
    for i, (lk, tk, lv, tv, mask, tm_, _) in enumerate(S_list):
        pss, Ts = nxt
        if i + 1 < n:
            nxt = s_mm(i + 1)
        pt, Tpt, _ = pt_ring.next()
        P.op("act", lambda e, pss=pss, pt=pt: e.activation(out=pt[:, 0:nq], in_=pss[:, 0:nq], func=AF.Exp, scale=scale), reads=[Ts], writes=[Tpt])
        if mask is not None:
            P.op("pool", lambda e, pt=pt, mask=mask: e.tensor_tensor(v3(pt[:, 0:nq]), v3(pt[:, 0:nq]), mask, ALU.mult), reads=[Tpt] + tm_, writes=[Tpt])
        P.op("pe", lambda e, pt=pt, lv=lv, i=i: e.matmul(psn[0:M, 0:nq], lv, pt[:, 0:nq], start=(i == 0), stop=(i == n - 1)), reads=tv + [Tpt], writes=[Tn])
        P.op("pe", lambda e, pt=pt, i=i: e.matmul(psd[0:M, 0:nq], K.onesb[:, 0:M], pt[:, 0:nq], start=(i == 0), stop=(i == n - 1)), reads=[K.T_const, Tpt], writes=[Td])
    rec, Trec, _ = rec_ring.next()
    if extra_den is not None:
        ed, ted = extra_den
        if g4:
            P.op("dve", lambda e: e.tensor_tensor(v3(rec[0:M, 0:nq]), v3(psd[0:M, 0:nq]), ed, ALU.add), reads=[Td] + ted, writes=[Trec])
        else:
            P.op("dve", lambda e: e.tensor_scalar_add(rec[0:M, 0:nq], psd[0:M, 0:nq], ed), reads=[Td] + ted, writes=[Trec])
        P.op("dve", lambda e: e.reciprocal(rec[0:M, 0:nq], rec[0:M, 0:nq]), reads=[Trec], writes=[Trec])
    else:
        P.op("dve", lambda e: e.reciprocal(rec[0:M, 0:nq], psd[0:M, 0:nq]), reads=[Td], writes=[Trec])
    P.op("dve", lambda e: e.tensor_tensor(out_ap, v3(psn[0:M, 0:nq]), v3(rec[0:M, 0:nq]), ALU.mult), reads=[Tn, Trec], writes=[T_out])


def phase_attn_a(K):
    nc, P = K.nc, K.P
    S = K.dscr
    catT = K.scr("catT_d", [D, TOKC], BF16)
    scale = 128 ** -0.5
    K.begin()
    pt_ring = Ring(K, "pt", [128, 256], BF16, 4)
    rec_ring = Ring(K, "rec", [128, 256], F32, 2)
    pools = (pt_ring, rec_ring)
    qr = Ring(K, "cq", [128, 8, 256], BF16, 2)
    kr = Ring(K, "ck", [128, 8, 256], BF16, 2)
    vr = Ring(K, "cv", [128, 2, 1024], BF16, 2)
    orr = Ring(K, "co", [128, 8, 256], BF16, 2)
    for s in range(4):
        qt, Tq, sq = qr.next()
        kt, Tk, sk = kr.next()
        vt, Tv, sv = vr.next()
        ot, To, so = orr.next()
        P.dma("sp", qt[:], S["qaT_d"][:, s * 256:(s + 1) * 256].rearrange("(h p) t -> p h t", p=128), sq, writes=[Tq])
        P.dma("sp", kt[:], S["kaT_d"][:, s * 256:(s + 1) * 256].rearrange("(h p) t -> p h t", p=128), sk, writes=[Tk])
        P.dma("sp", vt[:], S["va_d"][s * 256:(s + 1) * 256, :].rearrange("(c p) f -> p c f", p=128), sv, writes=[Tv])
        for h in range(8):
            sl = [(kt[:, h, c * 128:(c + 1) * 128], [Tk], vt[:, c, h * 128:(h + 1) * 128], [Tv], None, [], 0) for c in range(2)]
            attn_core(K, sl, 256, (qt[:, h, :], [Tq]), ot[:, h, :], To, scale, pools=pools)
        P.dma("sp", catT[0:1024, s * 256:(s + 1) * 256].rearrange("(h p) t -> p h t", p=128), ot[:], so, reads=[To])
    ck_tm = K.sb("ck_tm", [128, 2, 1024], BF16)
    cvt = K.sb("cvt", [128, 2, 1024], BF16)
    ckT = K.sb("ckT", [128, 8, 256], BF16)
    T_ck, T_cv, T_ckT, T_eb = P.tiles(4, "na")
    sw0 = K.getsem(True)
    P.dma("pool", ck_tm[:], K.din["cache_a_k"].rearrange("(c p) f -> p c f", p=128), sw0, writes=[T_ck])
    P.dma("pool", cvt[:], K.din["cache_a_v"].rearrange("(c p) f -> p c f", p=128), sw0, writes=[T_cv])
    for c in range(2):
        ps, Tp = K.ps(0)
        psb = ps[:].bitcast(BF16)
        for h in range(8):
            P.op("pe", lambda e, c=c, h=h, psb=psb: e.transpose(psb[:, h * 128:(h + 1) * 128], ck_tm[:, c, h * 128:(h + 1) * 128], K.identb[:]), reads=[T_ck, K.T_const], writes=[Tp])
        P.op("dve", lambda e, c=c, psb=psb: e.tensor_copy(ckT[:, :, c * 128:(c + 1) * 128], psb.rearrange("p (h k) -> p h k", h=8)), reads=[Tp], writes=[T_ckT])
    EB = K.sb("EB", [128, 2, 8, 6, 256], BF16)
    mr = Ring(K, "mstage", [128, 6, 256], F32, 2)
    for cl in range(2):
        for h in range(8):
            mt, Tm, sm = mr.next()
            P.dma("sp", mt[:], K.din["na_mask"][cl, h].rearrange("c k q -> k c q"), sm, writes=[Tm])
            P.op("act", lambda e, cl=cl, h=h, mt=mt: e.activation(out=EB[:, cl, h, :, :], in_=mt[:], func=AF.Exp), reads=[Tm], writes=[T_eb])
    NQ = SEXT * 128
    NK = NS1 * 128
    qh = Ring(K, "nq", [128, NQ], BF16, 2)
    kh = Ring(K, "nk", [128, NK], BF16, 2)
    vh = Ring(K, "nv", [128, NS1, 128], BF16, 2)
    oh = Ring(K, "no", [128, NQ], BF16, 2)
    for h in range(8):
        qt, Tq, sq = qh.next()
        kt, Tk, sk = kh.next()
        vt, Tv, sv = vh.next()
        ot, To, so = oh.next()
        P.dma("sp", qt[:], S["qaT_d"][h * 128:(h + 1) * 128, 1024:1024 + NQ], sq, writes=[Tq])
        P.dma("sp", kt[:], S["kaT_d"][h * 128:(h + 1) * 128, 1024:1024 + NK], sk, writes=[Tk])
        P.dma("sp", vt[:], S["va_d"][1024:1024 + NK, h * 128:(h + 1) * 128].rearrange("(c p) f -> p c f", p=128), sv, writes=[Tv])
        for i in range(9):
            nq = 256 if i < 8 else 128
            cl = 0 if i == 0 else 1
            base = 0 if i == 0 else (i - 1) * 256
            nch = 6 if i < 8 else 5
            sl = []
            for ch in range(nch):
                t0 = base + ch * 128
                sl.append((kt[:, t0:t0 + 128], [Tk], vt[:, t0 // 128, :], [Tv], EB[:, cl, h, ch, 0:nq], [T_eb], 0))
            for c in range(2):
                sl.append((ckT[:, h, c * 128:(c + 1) * 128], [T_ckT], cvt[:, c, h * 128:(h + 1) * 128], [T_cv], None, [], 0))
            attn_core(K, sl, nq, (qt[:, i * 256:i * 256 + nq], [Tq]), ot[:, i * 256:i * 256 + nq], To, scale, pools=pools)
        P.dma("sp", catT[h * 128:(h + 1) * 128, 1024:1024 + NQ], ot[:], so, reads=[To])
    K.end()


def phase_dn_prep(K):
    nc, P = K.nc, K.P
    S = K.dscr
    K.scr("qnT_d", [1024, TOK1], BF16)
    K.scr("knT_d", [1024, TOKB], BF16)
    K.scr("ktm_d", [TOKB, 1024], BF16)
    K.scr("vtm_d", [TOKB, 1024], BF16)
    K.begin()
    cwF = K.sb("cwF", [128, 3, 24], F32)
    stage = K.sb("cstg", [128, 128], F32)
    T_cw, T_stage = P.tiles(2, "cw")
    s0 = K.getsem()
    for j in range(3):
        rows_T(K, cwF[:, j, :], T_cw, K.din["conv_w"][j].rearrange("(r p) -> r p", p=128), 24, stage, T_stage, s0)
    pieces = [(s * 256, 256, True, True, 256) for s in range(4)]
    for i in range(8):
        nqv = 512 if i < 4 else (256 if i == 4 else 0)
        pieces.append((1024 + i * 512, 512, i == 0, i == 7, nqv))
    xr = Ring(K, "dx", [128, 514], F32, 3)
    yr = Ring(K, "dy", [128, 512], F32, 2)
    sqr = Ring(K, "dsq", [128, 512], F32, 2)
    rsr = Ring(K, "drs", [128, 512], F32, 2)
    ynr = Ring(K, "dyn", [128, 512], BF16, 3)
    tmr = Ring(K, "dtm", [128, 4, 128], BF16, 3)
    for fb in range(24):
        kind, h = fb // 8, fb % 8
        for (tok0, n0, le, re_, nq) in pieces:
            n = nq if kind == 0 else n0
            if n == 0:
                continue
            re2 = re_ and n == n0
            x, Tx, sx = xr.next()
            a = 1 if le else 0
            b = n + 1 if re2 else n + 2
            if le:
                P.op("pool", lambda e, x=x: e.memset(x[:, 0:1], 0.0), writes=[Tx])
            if re2:
                P.op("pool", lambda e, x=x, n=n: e.memset(x[:, n + 1:n + 2], 0.0), writes=[Tx])
            P.dma("sp", x[:, a:b], S["qkvT_d"][fb * 128:(fb + 1) * 128, tok0 - 1 + a:tok0 - 1 + b], sx, writes=[Tx])
            y, Ty, _ = yr.next()
            P.op("pool", lambda e, x=x, y=y, n=n, fb=fb: e.tensor_scalar_mul(y[:, 0:n], x[:, 0:n], cwF[:, 0, fb:fb + 1]), reads=[Tx, T_cw], writes=[Ty])
            P.op("dve", lambda e, x=x, y=y, n=n, fb=fb: e.scalar_tensor_tensor(y[:, 0:n], x[:, 1:n + 1], cwF[:, 1, fb:fb + 1], y[:, 0:n], ALU.mult, ALU.add), reads=[Tx, T_cw, Ty], writes=[Ty])
            P.op("dve", lambda e, x=x, y=y, n=n, fb=fb: e.scalar_tensor_tensor(y[:, 0:n], x[:, 2:n + 2], cwF[:, 2, fb:fb + 1], y[:, 0:n], ALU.mult, ALU.add), reads=[Tx, T_cw, Ty], writes=[Ty])
            P.op("act", lambda e, y=y, n=n: e.activation(out=y[:, 0:n], in_=y[:, 0:n], func=AF.Silu), reads=[Ty], writes=[Ty])
            yn, Tyn, syn = ynr.next()
            if kind < 2:
                sq, Tsq, _ = sqr.next()
                rs, Trs, _ = rsr.next()
                P.op("pool", lambda e, y=y, sq=sq, n=n: e.tensor_tensor(sq[:, 0:n], y[:, 0:n], y[:, 0:n], ALU.mult), reads=[Ty], writes=[Tsq])
                ps, Tp = K.ps(0)
                P.op("pe", lambda e, ps=ps, sq=sq, n=n: e.matmul(ps[:, 0:n], K.onesf[:], sq[:, 0:n], start=True, stop=True), reads=[Tsq, K.T_const], writes=[Tp])
                P.op("act", lambda e, ps=ps, rs=rs, n=n: e.activation(out=rs[:, 0:n], in_=ps[:, 0:n], func=AF.Sqrt, bias=EPS, scale=1.0), reads=[Tp], writes=[Trs])
                P.op("dve", lambda e, rs=rs, n=n: e.reciprocal(rs[:, 0:n], rs[:, 0:n]), reads=[Trs], writes=[Trs])
                cc = 128 ** -0.5 if kind == 0 else 1.0
                P.op("dve", lambda e, y=y, rs=rs, yn=yn, n=n, cc=cc: e.scalar_tensor_tensor(yn[:, 0:n], y[:, 0:n], cc, rs[:, 0:n], ALU.mult, ALU.mult), reads=[Ty, Trs], writes=[Tyn])
                dst = S["qnT_d"] if kind == 0 else S["knT_d"]
                P.dma("sp", dst[h * 128:(h + 1) * 128, tok0:tok0 + n], yn[:, 0:n], syn, reads=[Tyn])
            else:
                P.op("pool", lambda e, y=y, yn=yn, n=n: e.tensor_copy(yn[:, 0:n], y[:, 0:n]), reads=[Ty], writes=[Tyn])
            if kind >= 1:
                nb = n // 128
                ps, Tp = K.ps(0)
                psb = ps[:].bitcast(BF16)
                for j in range(nb):
                    P.op("pe", lambda e, j=j, psb=psb, yn=yn: e.transpose(psb[:, j * 128:(j + 1) * 128], yn[:, j * 128:(j + 1) * 128], K.identb[:]), reads=[Tyn, K.T_const], writes=[Tp])
                tm_, Ttm, stm = tmr.next()
                P.op("act", lambda e, psb=psb, tm_=tm_, nb=nb: e.copy(tm_[:, 0:nb, :], psb[:, 0:nb * 128].rearrange("p (j f) -> p j f", f=128)), reads=[Tp], writes=[Ttm])
                dst = S["ktm_d"] if kind == 1 else S["vtm_d"]
                P.dma("sp", dst[tok0:tok0 + n, h * 128:(h + 1) * 128].rearrange("(j p) f -> p j f", p=128), tm_[:, 0:nb, :], stm, reads=[Ttm])
    K.end()


def phase_dn_scan(K):
    nc, P = K.nc, K.P
    S = K.dscr
    K.scr("of_d", [TOKC, 1024], F32)
    catT = S["catT_d"]
    K.begin()
    H = 8
    tri = K.sb("tri", [128, 4, 128], F32)
    dtb = K.sb("dtb", [128, 16], F32)
    nea = K.sb("nea", [128, 16], F32)
    gon = K.sb("gon", [128, 128], F32)
    T_c = Tile("dnc")
    sc = K.getsem()
    P.dma("sp", tri[:], K.din["tri"].rearrange("m k c -> k m c"), sc, writes=[T_c])
    P.dma("sp", nea[:], K.din["alog_dt"][0:1, :].partition_broadcast(128), sc, writes=[T_c])
    P.dma("sp", dtb[:], K.din["alog_dt"][1:2, :].partition_broadcast(128), sc, writes=[T_c])
    P.dma("sp", gon[:], K.din["onorm"].partition_broadcast(128), sc, writes=[T_c])
    P.op("act", lambda e: e.activation(out=nea[:], in_=nea[:], func=AF.Exp), reads=[T_c], writes=[T_c])
    P.op("dve", lambda e: e.tensor_scalar_mul(nea[:], nea[:], -1.0), reads=[T_c], writes=[T_c])
    LM, UM, SLM, SUM = 0, 1, 2, 3
    St = K.sb("St", [128, H, 128], F32)
    Sb = K.sb("Sb", [128, H, 128], BF16)
    T_S, T_Sb = P.tiles(2, "S")
    T_of = {}
    big = lambda name, dt, n=2: Ring(K, name, [128, H, 128], dt, n)
    r_kT, r_qT, r_k, r_v = big("lkT", BF16), big("lqT", BF16), big("lk", BF16), big("lv", BF16)
    r_Rg, r_Rb = big("Rg", F32, 1), big("Rb", BF16, 1)
    r_diff, r_x1, r_e1, r_e2i, r_e2s = big("diff", F32, 1), big("x1", F32, 1), big("e1", F32, 1), big("e2i", F32, 1), big("e2s", F32, 1)
    r_kbT, r_egb, r_qd = big("kbT", BF16, 1), big("egb", BF16, 1), big("qd", BF16, 1)
    r_N, r_M, r_X, r_Y = big("N", F32, 2), big("M", F32, 2), big("X", F32, 2), big("Y", F32, 2)
    r_Xb = big("Xb", BF16, 1)
    r_qk, r_vb, r_kbg, r_kd = big("qk", BF16, 1), big("vb", BF16, 1), big("kbg", BF16, 1), big("kdc", BF16, 1)
    r_u, r_wT, r_vn, r_o = big("u", F32, 1), big("wT", BF16, 1), big("vn", BF16, 1), big("o", F32, 2)
    r_z, r_sq, r_on, r_obT = Ring(K, "z", [128, 1024], F32, 1), big("osq", F32, 1), big("on", BF16, 1), big("obT", BF16, 2)
    r_st = Ring(K, "ost", [128, 16], F32, 2)

    def v8(ap):
        return ap.rearrange("p (h f) -> p h f", h=H)

    def bc_h(ap2):
        return ap2.unsqueeze(2).to_broadcast([128, H, 128])

    def bc_m(m):
        return tri[:, m, :].unsqueeze(1).to_broadcast([128, H, 128])

    def mm8(lhs_fn, rhs_fn, reads, g, extra=None):
        b0, T0 = K.ps(g)
        b1, T1 = K.ps(g)
        banks = ((b0, T0), (b1, T1))
        for h in range(H):
            b, Tb = banks[h // 4]
            o_ = b[:, (h % 4) * 128:(h % 4 + 1) * 128]
            if extra is None:
                P.op("pe", lambda e, h=h, o_=o_: e.matmul(o_, lhs_fn(h), rhs_fn(h), start=True, stop=True), reads=reads, writes=[Tb])
            else:
                l2, r2, reads2 = extra
                P.op("pe", lambda e, h=h, o_=o_: e.matmul(o_, lhs_fn(h), rhs_fn(h), start=True, stop=False), reads=reads, writes=[Tb])
                P.op("pe", lambda e, h=h, o_=o_: e.matmul(o_, l2(h), r2(h), start=False, stop=True), reads=reads2, writes=[Tb])
        return banks

    def ev(eng, banks, fn, reads, writes):
        for i, (b, Tb) in enumerate(banks):
            bv = b[:].rearrange("p (h f) -> p h f", h=4)
            P.op(eng, lambda e, bv=bv, i=i: fn(e, bv, slice(4 * i, 4 * i + 4)), reads=[Tb] + reads, writes=writes)

    def gates(tokd0, nch):
        G = {}
        graw = K.sb("graw", [128, nch, 32], F32)
        beta = K.sb("beta", [128, nch, 16], F32)
        g = K.sb("gg", [128, nch, 16], F32)
        Tg = Tile("gates")
        sg = K.getsem()
        P.dma("sp", graw[:], S["gates_d"][tokd0:tokd0 + nch * 128, :].rearrange("(c p) g -> p c g", p=128), sg, writes=[Tg])
        P.op("act", lambda e: e.activation(out=beta[:], in_=graw[:, :, 0:16], func=AF.Sigmoid), reads=[Tg], writes=[Tg])
        P.op("dve", lambda e: e.tensor_tensor(g[:], graw[:, :, 16:32], dtb[:].unsqueeze(1).to_broadcast([128, nch, 16]), ALU.add), reads=[Tg, T_c], writes=[Tg])
        P.op("act", lambda e: e.activation(out=g[:], in_=g[:], func=AF.Exp), reads=[Tg], writes=[Tg])
        P.op("act", lambda e: e.activation(out=g[:], in_=g[:], func=AF.Ln, bias=1.0, scale=1.0), reads=[Tg], writes=[Tg])
        P.op("dve", lambda e: e.tensor_tensor(g[:], g[:], nea[:].unsqueeze(1).to_broadcast([128, nch, 16]), ALU.mult), reads=[Tg, T_c], writes=[Tg])
        G["beta"], G["T"] = beta, Tg
        for dr in range(2):
            gc = K.sb(f"gc{dr}", [128, nch, 8], F32)
            gl = K.sb(f"gl{dr}", [128, nch, 8], F32)
            eg = K.sb(f"eg{dr}", [128, nch, 8], F32)
            bg = K.sb(f"bg{dr}", [128, nch, 8], F32)
            kd = K.sb(f"kd{dr}", [128, nch, 8], F32)
            cd = K.sb(f"cd{dr}", [128, nch, 8], F32)
            tr = UM if dr == 0 else LM
            ps, Tp = K.ps(0)
            gsl = g[:, :, dr * 8:(dr + 1) * 8]
            P.op("pe", lambda e, ps=ps, tr=tr, gsl=gsl: e.matmul(ps[:, 0:nch * 8].rearrange("p (c h) -> p c h", h=8), tri[:, tr, :], gsl, start=True, stop=True), reads=[Tg, T_c], writes=[Tp])
            P.op("dve", lambda e, ps=ps, gc=gc: e.tensor_copy(gc[:].rearrange("p c h -> p (c h)"), ps[:, 0:nch * 8]), reads=[Tp], writes=[Tg])
            ps2, Tp2 = K.ps(0)
            P.op("pe", lambda e, ps2=ps2, gsl=gsl: e.matmul(ps2[:, 0:nch * 8].rearrange("p (c h) -> p c h", h=8), K.onesf[:], gsl, start=True, stop=True), reads=[Tg, K.T_const], writes=[Tp2])
            P.op("dve", lambda e, ps2=ps2, gl=gl: e.tensor_copy(gl[:].rearrange("p c h -> p (c h)"), ps2[:, 0:nch * 8]), reads=[Tp2], writes=[Tg])
            P.op("act", lambda e, eg=eg, gc=gc: e.activation(out=eg[:], in_=gc[:], func=AF.Exp), reads=[Tg], writes=[Tg])
            P.op("dve", lambda e, bg=bg, eg=eg, dr=dr: e.tensor_tensor(bg[:], eg[:], beta[:, :, dr * 8:(dr + 1) * 8], ALU.mult), reads=[Tg], writes=[Tg])
            P.op("dve", lambda e, kd=kd, gl=gl, gc=gc: e.tensor_tensor(kd[:], gl[:], gc[:], ALU.subtract), reads=[Tg], writes=[Tg])
            P.op("act", lambda e, kd=kd: e.activation(out=kd[:], in_=kd[:], func=AF.Exp), reads=[Tg], writes=[Tg])
            P.op("act", lambda e, cd=cd, gl=gl: e.activation(out=cd[:], in_=gl[:], func=AF.Exp), reads=[Tg], writes=[Tg])
            G[dr] = dict(gc=gc, eg=eg, bg=bg, kd=kd, cd=cd)
        return G

    def chunk(G, tokd0, ch, dr, full, final, tokc0):
        t0 = tokd0 + ch * 128
        Tg = G["T"]
        gd = G[dr]
        beta_c = G["beta"][:, ch, dr * 8:(dr + 1) * 8]
        gc_c = gd["gc"][:, ch, :]
        mL, mU, mSL, mSU = (LM, UM, SLM, SUM) if dr == 0 else (UM, LM, SUM, SLM)
        kT, TkT, s1 = r_kT.next()
        ktm, Tk, s2 = r_k.next()
        vtm, Tv, s3 = r_v.next()
        P.dma("sp", kT[:], S["knT_d"][:, t0:t0 + 128].rearrange("(h p) t -> p h t", p=128), s1, writes=[TkT])
        P.dma("sp", ktm[:].rearrange("p h f -> p (h f)"), S["ktm_d"][t0:t0 + 128, :], s2, writes=[Tk])
        P.dma("sp", vtm[:].rearrange("p h f -> p (h f)"), S["vtm_d"][t0:t0 + 128, :], s3, writes=[Tv])
        if full:
            qT, TqT, s4 = r_qT.next()
            P.dma("sp", qT[:], S["qnT_d"][:, t0:t0 + 128].rearrange("(h p) t -> p h t", p=128), s4, writes=[TqT])
        Rg, TRg, _ = r_Rg.next()
        Rb, TRb, _ = r_Rb.next()
        idb = K.identf[:].unsqueeze(1).to_broadcast([128, H, 128])
        P.op("dve", lambda e: e.tensor_tensor(Rg[:], bc_h(gc_c), idb, ALU.mult), reads=[Tg, K.T_const], writes=[TRg])
        P.op("pool", lambda e: e.tensor_tensor(Rb[:], bc_h(beta_c), idb, ALU.mult), reads=[Tg, K.T_const], writes=[TRb])
        gcb = mm8(lambda h: K.onesf[:], lambda h: Rg[:, h, :], [TRg, K.T_const], 0)
        btb = mm8(lambda h: K.onesb[:], lambda h: Rb[:, h, :], [TRb, K.T_const], 0)
        diff, Tdiff, _ = r_diff.next()
        ev("dve", gcb, lambda e, bv, hs: e.tensor_tensor(diff[:, hs, :], bc_h(gc_c)[:, hs, :], bv, ALU.subtract), [Tg], [Tdiff])
        kbT, TkbT, _ = r_kbT.next()
        ev("dve", btb, lambda e, bv, hs: e.tensor_tensor(kbT[:, hs, :], kT[:, hs, :], bv, ALU.mult), [TkT], [TkbT])
        if full:
            egb, Tegb, _ = r_egb.next()
            qd, Tqd, _ = r_qd.next()
            ev("act", gcb, lambda e, bv, hs: e.activation(out=egb[:, hs, :], in_=bv, func=AF.Exp), [], [Tegb])
            P.op("pool", lambda e: e.tensor_tensor(qd[:], qT[:], egb[:], ALU.mult), reads=[TqT, Tegb], writes=[Tqd])
        x1, Tx1, _ = r_x1.next()
        e1, Te1, _ = r_e1.next()
        e2i, Te2i, _ = r_e2i.next()
        e2s, Te2s, _ = r_e2s.next()
        P.op("dve", lambda e: e.tensor_tensor(x1[:], diff[:], bc_m(mL), ALU.mult), reads=[Tdiff, T_c], writes=[Tx1])
        P.op("act", lambda e: e.activation(out=e1[:], in_=x1[:], func=AF.Exp), reads=[Tx1], writes=[Te1])
        P.op("pool", lambda e: e.tensor_tensor(e1[:], e1[:], bc_m(mSL), ALU.mult), reads=[Te1, T_c], writes=[Te1])
        P.op("dve", lambda e: e.tensor_tensor(x1[:], diff[:], bc_m(mU), ALU.mult), reads=[Tdiff, T_c, Te1], writes=[Tx1])
        P.op("act", lambda e: e.activation(out=e2i[:], in_=x1[:], func=AF.Exp, scale=-1.0), reads=[Tx1], writes=[Te2i])
        P.op("pool", lambda e: e.tensor_tensor(e2s[:], e2i[:], bc_m(mSU), ALU.mult), reads=[Te2i, T_c], writes=[Te2s])
        P.op("pool", lambda e: e.tensor_tensor(e2i[:], e2i[:], bc_m(mU), ALU.mult), reads=[Te2i, T_c, Te2s], writes=[Te2i])
        a1 = mm8(lambda h: kbT[:, h, :], lambda h: kT[:, h, :], [TkbT, TkT], 0)
        a2 = mm8(lambda h: kT[:, h, :], lambda h: kbT[:, h, :], [TkbT, TkT], 1)
        Mj, TM, _ = r_M.next()
        Nj, TN, _ = r_N.next()
        ev("dve", a1, lambda e, bv, hs, Mj=Mj: e.tensor_tensor(Mj[:, hs, :], bv, e1[:, hs, :], ALU.mult), [Te1], [TM])
        ev("dve", a2, lambda e, bv, hs, Nj=Nj: e.tensor_tensor(Nj[:, hs, :], bv, e2s[:, hs, :], ALU.mult), [Te2s], [TN])
        if full:
            a3 = mm8(lambda h: kT[:, h, :], lambda h: qT[:, h, :], [TkT, TqT], 0)
            qk, Tqk, _ = r_qk.next()
            ev("dve", a3, lambda e, bv, hs: e.tensor_tensor(qk[:, hs, :], bv, e2i[:, hs, :], ALU.mult), [Te2i], [Tqk])
        X, TX, _ = r_X.next()
        Y, TY, _ = r_Y.next()
        idbb = K.identf[:].unsqueeze(1).to_broadcast([128, H, 128])
        P.op("pool", lambda e, X=X, Nj=Nj: e.tensor_tensor(X[:], idbb, Nj[:], ALU.subtract), reads=[TN, K.T_const], writes=[TX])
        P.op("pool", lambda e, Y=Y, Mj=Mj: e.tensor_tensor(Y[:], idbb, Mj[:], ALU.subtract), reads=[TM, K.T_const], writes=[TY])
        for j in range(1, 7):
            last = j == 6
            Nn, TNn, _ = r_N.next()
            pn = mm8(lambda h, Mj=Mj: Mj[:, h, :], lambda h, Nj=Nj: Nj[:, h, :], [TM, TN], 0)
            ev("act", pn, lambda e, bv, hs, Nn=Nn: e.copy(Nn[:, hs, :], bv), [], [TNn])
            if not last:
                Mn, TMn, _ = r_M.next()
                pm = mm8(lambda h, Nj=Nj: Nj[:, h, :], lambda h, Mj=Mj: Mj[:, h, :], [TM, TN], 0)
                ev("act", pm, lambda e, bv, hs, Mn=Mn: e.copy(Mn[:, hs, :], bv), [], [TMn])
            Xn, TXn, _ = r_X.next()
            px = mm8(lambda h, Y=Y: Y[:, h, :], lambda h, Nn=Nn: Nn[:, h, :], [TY, TNn], 1)
            ev("dve", px, lambda e, bv, hs, Xn=Xn, X=X: e.tensor_tensor(Xn[:, hs, :], bv, X[:, hs, :], ALU.add), [TX], [TXn])
            if not last:
                Yn, TYn, _ = r_Y.next()
                py = mm8(lambda h, X=X: X[:, h, :], lambda h, Mn=Mn: Mn[:, h, :], [TX, TMn], 1)
                ev("dve", py, lambda e, bv, hs, Yn=Yn, Y=Y: e.tensor_tensor(Yn[:, hs, :], bv, Y[:, hs, :], ALU.add), [TY], [TYn])
                Mj, TM, Y, TY = Mn, TMn, Yn, TYn
            Nj, TN, X, TX = Nn, TNn, Xn, TXn
        K.dump("dbg_diff", diff[:], [Tdiff]); K.dump("dbg_e1", e1[:], [Te1]); K.dump("dbg_e2i", e2i[:], [Te2i])
        K.dump("dbg_kbT", kbT[:], [TkbT]); K.dump("dbg_X", X[:], [TX]); K.dump("dbg_gc", gd["gc"][:], [Tg]); K.dump("dbg_beta", G["beta"][:], [Tg])
        K.dump("dbg_kd", gd["kd"][:], [Tg]); K.dump("dbg_cd", gd["cd"][:], [Tg])
        vb, Tvb, _ = r_vb.next()
        kbg, Tkbg, _ = r_kbg.next()
        kdc, Tkdc, _ = r_kd.next()
        P.op("pool", lambda e: e.tensor_tensor(vb[:], vtm[:], bc_h(beta_c), ALU.mult), reads=[Tv, Tg], writes=[Tvb])
        P.op("pool", lambda e: e.tensor_tensor(kbg[:], ktm[:], bc_h(gd["bg"][:, ch, :]), ALU.mult), reads=[Tk, Tg], writes=[Tkbg])
        P.op("pool", lambda e: e.tensor_tensor(kdc[:], ktm[:], bc_h(gd["kd"][:, ch, :]), ALU.mult), reads=[Tk, Tg], writes=[Tkdc])
        Xb, TXb, _ = r_Xb.next()
        P.op("act", lambda e, X=X: e.copy(Xb[:], X[:]), reads=[TX], writes=[TXb])
        pu = mm8(lambda h: Xb[:, h, :], lambda h: vb[:, h, :], [TXb, Tvb], 0)
        u, Tu, _ = r_u.next()
        ev("act", pu, lambda e, bv, hs: e.copy(u[:, hs, :], bv), [], [Tu])
        pw = mm8(lambda h: kbg[:, h, :], lambda h: Xb[:, h, :], [TXb, Tkbg], 0)
        wT, TwT, _ = r_wT.next()
        ev("act", pw, lambda e, bv, hs: e.copy(wT[:, hs, :], bv), [], [TwT])
        pws = mm8(lambda h: wT[:, h, :], lambda h: Sb[:, h, :], [TwT, T_Sb], 1)
        vn, Tvn, _ = r_vn.next()
        ev("dve", pws, lambda e, bv, hs: e.tensor_tensor(vn[:, hs, :], u[:, hs, :], bv, ALU.subtract), [Tu], [Tvn])
        if full:
            po = mm8(lambda h: qd[:, h, :], lambda h: Sb[:, h, :], [Tqd, T_Sb], 1,
                     extra=(lambda h: qk[:, h, :], lambda h: vn[:, h, :], [Tqk, Tvn]))
            o, To, so = r_o.next()
            if not final:
                ev("act", po, lambda e, bv, hs: e.copy(o[:, hs, :], bv), [], [To])
                P.dma("act", S["of_d"][tokc0 + ch * 128:tokc0 + (ch + 1) * 128, :], o[:].rearrange("p h f -> p (h f)"), so, reads=[To], writes=[T_of.setdefault(tokc0 + ch * 128, Tile("of"))])
            else:
                P.dma("sp", o[:].rearrange("p h f -> p (h f)"), S["of_d"][tokc0 + ch * 128:tokc0 + (ch + 1) * 128, :], so, reads=[T_of[tokc0 + ch * 128]], writes=[To])
                ev("dve", po, lambda e, bv, hs: e.tensor_tensor(o[:, hs, :], o[:, hs, :], bv, ALU.add), [To], [To])
        K.dump("dbg_u", u[:], [Tu]); K.dump("dbg_wT", wT[:], [TwT]); K.dump("dbg_vn", vn[:], [Tvn])
        if full:
            K.dump("dbg_o", o[:], [To]); K.dump("dbg_qk", qk[:], [Tqk])
        pds = mm8(lambda h: kdc[:, h, :], lambda h: vn[:, h, :], [Tkdc, Tvn], 1)
        for h in range(H):
            b, Tb = pds[h // 4]
            P.op("dve", lambda e, h=h, b=b: e.scalar_tensor_tensor(St[:, h, :], St[:, h, :], gd["cd"][:, ch, h:h + 1], b[:, (h % 4) * 128:(h % 4 + 1) * 128], ALU.mult, ALU.add),
                 reads=[Tb, Tg, T_S], writes=[T_S])
        P.op("act", lambda e: e.copy(Sb[:], St[:]), reads=[T_S], writes=[T_Sb])
        if full and final:
            z, Tz, sz = r_z.next()
            P.dma("sp", z[:], S["z_d"][tokc0 + ch * 128:tokc0 + (ch + 1) * 128, :], sz, writes=[Tz])
            sq, Tsq, _ = r_sq.next()
            st, Tst, _ = r_st.next()
            on, Ton, _ = r_on.next()
            P.op("pool", lambda e: e.tensor_tensor(sq[:], o[:], o[:], ALU.mult), reads=[To], writes=[Tsq])
            P.op("dve", lambda e: e.tensor_reduce(out=st[:, 0:8], in_=sq[:], axis=AX.X, op=ALU.add), reads=[Tsq], writes=[Tst])
            P.op("act", lambda e: e.activation(out=st[:, 8:16], in_=st[:, 0:8], func=AF.Sqrt, bias=EPS, scale=1.0 / 128), reads=[Tst], writes=[Tst])
            P.op("dve", lambda e: e.reciprocal(st[:, 8:16], st[:, 8:16]), reads=[Tst], writes=[Tst])
            P.op("act", lambda e: e.activation(out=z[:], in_=z[:], func=AF.Silu), reads=[Tz], writes=[Tz])
            P.op("dve", lambda e: e.tensor_tensor(sq[:], o[:], bc_h(st[:, 8:16]), ALU.mult), reads=[To, Tst, Tsq], writes=[Tsq])
            P.op("pool", lambda e: e.tensor_tensor(sq[:], sq[:], gon[:].unsqueeze(1).to_broadcast([128, H, 128]), ALU.mult), reads=[Tsq, T_c], writes=[Tsq])
            P.op("dve", lambda e: e.tensor_tensor(on[:], sq[:], v8(z[:]), ALU.mult), reads=[Tsq, Tz], writes=[Ton])
            ps, Tp = K.ps(0)
            psb = ps[:].bitcast(BF16)
            for h in range(H):
                P.op("pe", lambda e, h=h, psb=psb: e.transpose(psb[:, h * 128:(h + 1) * 128], on[:, h, :], K.identb[:]), reads=[Ton, K.T_const], writes=[Tp])
            obT, TobT, sob = r_obT.next()
            P.op("act", lambda e, psb=psb: e.copy(obT[:].rearrange("p h f -> p (h f)"), psb), reads=[Tp], writes=[TobT])
            P.dma("act", catT[1024:2048, tokc0 + ch * 128:tokc0 + (ch + 1) * 128].rearrange("(h p) t -> p h t", p=128), obT[:], sob, reads=[TobT])

    def set_state(src):
        ss_ = K.getsem()
        if src is None:
            P.op("pool", lambda e: e.memset(St[:], 0.0), reads=[T_Sb], writes=[T_S])
        else:
            P.dma("sp", St[:], src.rearrange("h k v -> k h v"), ss_, reads=[T_Sb], writes=[T_S])
        P.op("act", lambda e: e.copy(Sb[:], St[:]), reads=[T_S], writes=[T_Sb])

    def save_state(dst):
        ss_ = K.getsem()
        P.dma("sp", dst.rearrange("h k v -> k h v"), St[:], ss_, reads=[T_S])

    import os
    which = os.environ.get("DN_SEQS", "ps")
    if "p" in which:
      for s in range(4):
        G = gates(s * 256, 2)
        set_state(None)
        for ch in (0, 1):
            chunk(G, s * 256, ch, 0, True, False, s * 256)
        save_state(K.dout["nbf"][s])
        set_state(None)
        for ch in (1, 0):
            chunk(G, s * 256, ch, 1, True, True, s * 256)
        save_state(K.dout["nbb"][s])
    if "s" in which:
        G = gates(1024, 32)
        set_state(K.din["s0"][0])
        for ch in range(SEXT):
            chunk(G, 1024, ch, 0, True, False, 1024)
        set_state(K.din["s0"][1])
        for ch in range(31, -1, -1):
            chunk(G, 1024, ch, 1, ch < SEXT, True, 1024)
    K.end()


def phase_mlp(K, l):
    nc, P = K.nc, K.P
    S = K.dscr
    ntb = NCAT if l == 0 else NPT + 16
    groups = token_groups(ntb, breaks=(NPT,))
    if l == 0:
        x1_d = K.scr("x1_d", [TOKC, D], F32)
        oT_d, Wo, W1, W2 = S["catT_d"], K.din["ab_w_out"], K.din["w_mlp_in"][0], K.din["w_mlp_out"][0]
        xsrc = lambda g: K.din["xp"][g * 128:(g + 1) * 128, :] if g < NPT else K.din["xs"][(g - NPT) * 128:(g - NPT + 1) * 128, :]
    else:
        oT_d, Wo, W1, W2 = S["o1T_d"], K.din["c_w_out"], K.din["w_mlp_in"][1], K.din["w_mlp_out"][1]
        xsrc = lambda g: S["x1_d"][g * 128:(g + 1) * 128, :]
    K.begin()
    actT = K.sb("actT", [128, KC, 512], BF16)
    T_act = [[Tile() for kc in range(KC)] for i in range(4)]
    xres = K.sb("xres", [128, 4, D], F32)
    T_x = [Tile() for i in range(4)]
    uT = K.sb("uT", [128, 64, 512], BF16)
    T_u = [Tile() for fc in range(64)]
    gate = [K.sb(f"gate{i}", [128, D], F32) for i in range(2)]
    T_gate = Tile("gate")
    wr = Ring(K, "wm", [128, KC, 512], BF16, 2, sw=True)
    tr = Ring(K, "tg", [128, 512], F32, 2)
    rr = Ring(K, "rl", [128, 512], F32, 2)
    nt = NormT(K)
    sx = [K.getsem() for i in range(4)]
    sg, so = K.getsem(), K.getsem()
    if l == 1:
        fng = K.sb("fng", [128, D], F32)
        fss = Ring(K, "fss", [128, 2], F32, 2)
        fjk = K.sb("fjk", [128, D], BF16)
        T_fng, T_fjk = P.tiles(2, "fn")
        P.dma("sp", fng[:], K.din["final_norm"].partition_broadcast(128), sg, writes=[T_fng])
    cur_c = None
    for (t0, m) in groups:
        n = m * 128
        c = 0 if t0 < NPT else 1
        if c != cur_c:
            P.dma("sp", gate[0][:], S["modrow_d"][l, c:c + 1, 2 * D:3 * D].partition_broadcast(128), sg, writes=[T_gate])
            P.dma("sp", gate[1][:], S["modrow_d"][l, c:c + 1, 5 * D:6 * D].partition_broadcast(128), sg, writes=[T_gate])
            cur_c = c
        P.dma("sp", actT[:, :, 0:n], oT_d[:, t0 * 128:t0 * 128 + n].rearrange("(fc p) t -> p fc t", p=128), so,
              writes=[T_act[i][kc] for i in range(m) for kc in range(KC)])
        for i in range(m):
            P.dma("sp", xres[:, i, :], xsrc(t0 + i), sx[i], writes=[T_x[i]])

        def second(Wsrc, nfq, lhs_fn, lhs_tiles_fn, gi):
            for dg in range(4):
                banks = [K.ps(1) for i in range(m)]
                for fq in range(nfq):
                    wt, Tw, sw = wr.next()
                    P.dma("pool", wt[:], Wsrc[fq * 2048:(fq + 1) * 2048, dg * 512:(dg + 1) * 512].rearrange("(kc p) n -> p kc n", p=128), sw, writes=[Tw])
                    for i in range(m):
                        b, Tb = banks[i]
                        for kc in range(KC):
                            fc = fq * KC + kc
                            P.op("pe", lambda e, b=b, i=i, fc=fc, kc=kc, wt=wt: e.matmul(b[:, :], lhs_fn(fc, i), wt[:, kc, :], start=(fc == 0), stop=(fc == nfq * KC - 1)),
                                 reads=[Tw] + lhs_tiles_fn(fc, i), writes=[Tb])
                for i in range(m):
                    b, Tb = banks[i]
                    tt, Tt, _ = tr.next()
                    P.op("dve", lambda e, b=b, tt=tt, dg=dg: e.tensor_tensor(tt[:], b[:, :], gate[gi][:, dg * 512:(dg + 1) * 512], ALU.mult), reads=[Tb, T_gate], writes=[Tt])
                    P.op("dve", lambda e, i=i, tt=tt, dg=dg: e.tensor_tensor(xres[:, i, dg * 512:(dg + 1) * 512], xres[:, i, dg * 512:(dg + 1) * 512], tt[:], ALU.add), reads=[Tt, T_x[i]], writes=[T_x[i]])

        second(Wo, 1, lambda fc, i: actT[:, fc, i * 128:(i + 1) * 128], lambda fc, i: [T_act[i][fc]], 0)
        for i in range(m):
            nt.run(xres[:, i, :], T_x[i], K.gsF[l][1][c], K.modF[l][c][:, 3 * KC:4 * KC],
                   lambda kc, i=i: actT[:, kc, i * 128:(i + 1) * 128], lambda kc, i=i: T_act[i][kc])
        for fg in range(16):
            wt, Tw, sw = wr.next()
            P.dma("pool", wt[:], W1[:, fg * 512:(fg + 1) * 512].rearrange("(kc p) n -> p kc n", p=128), sw, writes=[Tw])
            for sub in range(4):
                fc = fg * 4 + sub
                ps, Tp = K.ps(0)
                for kc in range(KC):
                    P.op("pe", lambda e, ps=ps, kc=kc, sub=sub, wt=wt, n=n: e.matmul(ps[:, 0:n], wt[:, kc, sub * 128:(sub + 1) * 128], actT[:, kc, 0:n], start=(kc == 0), stop=(kc == KC - 1)),
                         reads=[Tw] + [T_act[i][kc] for i in range(m)], writes=[Tp])
                r, Tr, _ = rr.next()
                P.op("act", lambda e, ps=ps, r=r, n=n: e.activation(out=r[:, 0:n], in_=ps[:, 0:n], func=AF.Relu), reads=[Tp], writes=[Tr])
                P.op("dve", lambda e, r=r, fc=fc, n=n: e.tensor_tensor(uT[:, fc, 0:n], r[:, 0:n], r[:, 0:n], ALU.mult), reads=[Tr], writes=[T_u[fc]])
        second(W2, 4, lambda fc, i: uT[:, fc, i * 128:(i + 1) * 128], lambda fc, i: [T_u[fc]], 1)
        for i in range(m):
            g = t0 + i
            if l == 0:
                P.dma("sp", x1_d[g * 128:(g + 1) * 128, :], xres[:, i, :], sx[i], reads=[T_x[i]])
            else:
                ss, Tss, _ = fss.next()
                P.op("act", lambda e, i=i, ss=ss: e.activation(out=fjk[:], in_=xres[:, i, :], func=AF.Square, accum_out=ss[:, 0:1]), reads=[T_x[i]], writes=[T_fjk, Tss])
                P.op("act", lambda e, ss=ss: e.activation(out=ss[:, 1:2], in_=ss[:, 0:1], func=AF.Sqrt, bias=EPS, scale=1.0 / D), reads=[Tss], writes=[Tss])
                P.op("dve", lambda e, ss=ss: e.reciprocal(ss[:, 1:2], ss[:, 1:2]), reads=[Tss], writes=[Tss])
                P.op("dve", lambda e, i=i, ss=ss: e.scalar_tensor_tensor(xres[:, i, :], xres[:, i, :], ss[:, 1:2], fng[:], ALU.mult, ALU.mult), reads=[T_x[i], Tss, T_fng], writes=[T_x[i]])
                dst = K.dout["y_p"][g * 128:(g + 1) * 128, :] if g < NPT else K.dout["y_s"][(g - NPT) * 128:(g - NPT + 1) * 128, :]
                P.dma("sp", dst, xres[:, i, :], sx[i], reads=[T_x[i]])
    K.end()


def phase_l1_inproj(K):
    nc, P = K.nc, K.P
    S = K.dscr
    q1T = K.scr("q1T_d", [D, TOKC], BF16)
    k1T = K.scr("k1T_d", [256, TOKC], BF16)
    v1 = K.scr("v1_d", [TOKC, 256], BF16)
    K.begin()
    ntb = NCAT
    groups = token_groups(ntb, breaks=(NPT,))
    hT = K.sb("h1T", [128, KC, ntb * 128], BF16)
    T_h = [[Tile() for kc in range(KC)] for g in range(ntb)]
    xr = Ring(K, "x1in", [128, D], F32, 2)
    nt = NormT(K)
    for g in range(ntb):
        c = 0 if g < NPT else 1
        xt, T_x, sx = xr.next()
        P.dma("sp", xt[:], S["x1_d"][g * 128:(g + 1) * 128, :], sx, writes=[T_x])
        nt.run(xt[:], T_x, K.gsF[1][0][c], K.modF[1][c][:, 0:KC],
               lambda kc, g=g: hT[:, kc, g * 128:(g + 1) * 128], lambda kc, g=g: T_h[g][kc])
    cosT = K.sb("cosT", [128, SEXT * 128], F32)
    sinT = K.sb("sinT", [128, SEXT * 128], F32)
    perm = K.sb("perm", [128, 128], F32)
    T_rc = Tile("ropec")
    sr = K.getsem()
    P.dma("sp", cosT[:], K.din["rope_cos"], sr, writes=[T_rc])
    P.dma("sp", sinT[:], K.din["rope_sin"], sr, writes=[T_rc])
    P.dma("sp", perm[:], K.din["rope_perm"], sr, writes=[T_rc])
    wr = Ring(K, "w1q", [128, KC, 512], BF16, 2, sw=True)
    q32r = Ring(K, "q32", [128, 512], F32, 2)
    t1r = Ring(K, "rt1", [128, 512], F32, 2)
    st16 = Ring(K, "s16", [128, 512], BF16, 3)
    st32 = Ring(K, "s32", [128, 512], F32, 2)
    W = K.din["c_w_qkv"]
    for t in range(5):
        wt, Tw, sw = wr.next()
        P.dma("pool", wt[:], W[:, t * 512:(t + 1) * 512].rearrange("(kc p) n -> p kc n", p=128), sw, writes=[Tw])
        nsub = 4 if t < 4 else 2
        for sub in range(nsub):
            dst, row0 = (q1T, t * 512 + sub * 128) if t < 4 else (k1T, sub * 128)
            for (t0, m) in groups:
                n = m * 128
                ps, Tp = K.ps(0)
                for kc in range(KC):
                    P.op("pe", lambda e, ps=ps, kc=kc, sub=sub, wt=wt, t0=t0, n=n: e.matmul(ps[:, 0:n], wt[:, kc, sub * 128:(sub + 1) * 128], hT[:, kc, t0 * 128:t0 * 128 + n], start=(kc == 0), stop=(kc == KC - 1)),
                         reads=[Tw] + [T_h[t0 + i][kc] for i in range(m)], writes=[Tp])
                sg, Ts, ss_ = st16.next()
                if t0 < NPT:
                    P.op("act", lambda e, ps=ps, sg=sg, n=n: e.copy(sg[:, 0:n], ps[:, 0:n]), reads=[Tp], writes=[Ts])
                else:
                    s0 = (t0 - NPT) * 128
                    q32, Tq, _ = q32r.next()
                    t1, Tt1, _ = t1r.next()
                    P.op("act", lambda e, ps=ps, q32=q32, n=n: e.copy(q32[:, 0:n], ps[:, 0:n]), reads=[Tp], writes=[Tq])
                    ps2, Tp2 = K.ps(1)
                    P.op("pe", lambda e, ps2=ps2, q32=q32, n=n: e.matmul(ps2[:, 0:n], perm[:], q32[:, 0:n], start=True, stop=True), reads=[Tq, T_rc], writes=[Tp2])
                    P.op("pool", lambda e, q32=q32, t1=t1, n=n, s0=s0: e.tensor_tensor(t1[:, 0:n], q32[:, 0:n], cosT[:, s0:s0 + n], ALU.mult), reads=[Tq, T_rc], writes=[Tt1])
                    P.op("dve", lambda e, ps2=ps2, q32=q32, n=n, s0=s0: e.tensor_tensor(q32[:, 0:n], ps2[:, 0:n], sinT[:, s0:s0 + n], ALU.mult), reads=[Tp2, T_rc, Tt1], writes=[Tq])
                    P.op("dve", lambda e, q32=q32, t1=t1, sg=sg, n=n: e.tensor_tensor(sg[:, 0:n], q32[:, 0:n], t1[:, 0:n], ALU.add), reads=[Tq, Tt1], writes=[Ts])
                P.dma("sp", dst[row0:row0 + 128, t0 * 128:t0 * 128 + n], sg[:, 0:n], ss_, reads=[Ts])
        if t == 4:
            for g in range(ntb):
                ps, Tp = K.ps(0)
                for kc in range(KC):
                    P.op("pe", lambda e, ps=ps, kc=kc, g=g, wt=wt: e.matmul(ps[:, :], hT[:, kc, g * 128:(g + 1) * 128], wt[:, kc, :], start=(kc == 0), stop=(kc == KC - 1)),
                         reads=[Tw, T_h[g][kc]], writes=[Tp])
                sg, Ts, ss_ = st16.next()
                P.op("act", lambda e, ps=ps, sg=sg: e.copy(sg[:, 0:256], ps[:, 256:512]), reads=[Tp], writes=[Ts])
                P.dma("sp", v1[g * 128:(g + 1) * 128, :], sg[:, 0:256], ss_, reads=[Ts])
                if g < NPT:
                    s32, Ts32, ss32 = st32.next()
                    P.op("dve", lambda e, ps=ps, s32=s32: e.tensor_copy(s32[:], ps[:, :]), reads=[Tp], writes=[Ts32])
                    P.dma("sp", K.dout["nck"][g * 128:(g + 1) * 128, :], s32[:, 0:256], ss32, reads=[Ts32])
                    P.dma("sp", K.dout["ncv"][g * 128:(g + 1) * 128, :], s32[:, 256:512], ss32, reads=[Ts32])
    K.end()


def phase_attn_c(K):
    nc, P = K.nc, K.P
    S = K.dscr
    o1T = K.scr("o1T_d", [D, TOKC], BF16)
    scale = 64 ** -0.5
    K.begin()
    pt_ring = Ring(K, "pt", [128, 512], BF16, 4)
    rec_ring = Ring(K, "rec", [128, 512], F32, 2)
    pools = (pt_ring, rec_ring)
    snk = K.sb("snk", [128, 32], F32)
    trib = K.sb("trib", [128, 2, 128], BF16)
    ck_tm = K.sb("cck", [128, 2, 256], BF16)
    cvt = K.sb("ccv", [128, 2, 256], BF16)
    ckT = K.sb("cckT", [64, 4, 256], BF16)
    T_c, T_ck, T_cv, T_ckT = P.tiles(4, "ac")
    s0, sw0 = K.getsem(), K.getsem(True)
    P.dma("sp", snk[:], K.din["c_sink"].partition_broadcast(128), s0, writes=[T_c])
    P.op("act", lambda e: e.activation(out=snk[:], in_=snk[:], func=AF.Exp), reads=[T_c], writes=[T_c])
    P.dma("pool", trib[:], K.din["tri"][0:2].rearrange("m k c -> k m c"), sw0, writes=[T_c])
    P.dma("pool", ck_tm[:], K.din["cache_c_k"].rearrange("(c p) f -> p c f", p=128), sw0, writes=[T_ck])
    P.dma("pool", cvt[:], K.din["cache_c_v"].rearrange("(c p) f -> p c f", p=128), sw0, writes=[T_cv])
    for c in range(2):
        ps, Tp = K.ps(0)
        psb = ps[:].bitcast(BF16)
        for kh in range(4):
            P.op("pe", lambda e, c=c, kh=kh, psb=psb: e.transpose(psb[0:64, kh * 128:(kh + 1) * 128], ck_tm[:, c, kh * 64:(kh + 1) * 64], K.identb[:]), reads=[T_ck, K.T_const], writes=[Tp])
        P.op("dve", lambda e, c=c, psb=psb: e.tensor_copy(ckT[:, :, c * 128:(c + 1) * 128], psb[0:64, 0:512].rearrange("p (h k) -> p h k", h=4)), reads=[Tp], writes=[T_ckT])
    qr = Ring(K, "pq", [64, 8, 256], BF16, 2)
    kr = Ring(K, "pk", [64, 256], BF16, 2)
    vr = Ring(K, "pv", [128, 2, 64], BF16, 2)
    orr = Ring(K, "po", [64, 8, 256], BF16, 2)
    for kh in range(4):
        for s in range(4):
            qt, Tq, sq = qr.next()
            kt, Tk, sk = kr.next()
            vt, Tv, sv = vr.next()
            ot, To, so = orr.next()
            P.dma("sp", qt[:], S["q1T_d"][kh * 512:(kh + 1) * 512, s * 256:(s + 1) * 256].rearrange("(g d) t -> d g t", d=64), sq, writes=[Tq])
            P.dma("sp", kt[:], S["k1T_d"][kh * 64:(kh + 1) * 64, s * 256:(s + 1) * 256], sk, writes=[Tk])
            P.dma("sp", vt[:], S["v1_d"][s * 256:(s + 1) * 256, kh * 64:(kh + 1) * 64].rearrange("(c p) f -> p c f", p=128), sv, writes=[Tv])
            for g in range(8):
                sl = [(kt[:, c * 128:(c + 1) * 128], [Tk], vt[:, c, :], [Tv], None, [], 0) for c in range(2)]
                hq = kh * 8 + g
                attn_core(K, sl, 256, (qt[:, g, :], [Tq]), ot[:, g, :], To, scale, extra_den=(snk[0:64, hq:hq + 1], [T_c]), pools=pools)
            P.dma("sp", o1T[kh * 512:(kh + 1) * 512, s * 256:(s + 1) * 256].rearrange("(g d) t -> d g t", d=64), ot[:], so, reads=[To])
    NT = SEXT * 128
    kh_k = Ring(K, "sk", [64, NT], BF16, 2)
    kh_v = Ring(K, "sv", [128, SEXT, 64], BF16, 2)
    kh_q = Ring(K, "sq", [64, 8, 2048], BF16, 1)
    kh_o = Ring(K, "so", [64, 8, 2048], BF16, 1)
    for kh in range(4):
        kt, Tk, sk = kh_k.next()
        vt, Tv, sv = kh_v.next()
        qt, Tq, sq = kh_q.next()
        ot, To, so = kh_o.next()
        P.dma("sp", kt[:], S["k1T_d"][kh * 64:(kh + 1) * 64, 1024:1024 + NT], sk, writes=[Tk])
        P.dma("sp", vt[:], S["v1_d"][1024:1024 + NT, kh * 64:(kh + 1) * 64].rearrange("(c p) f -> p c f", p=128), sv, writes=[Tv])
        P.dma("sp", qt[:], S["q1T_d"][kh * 512:(kh + 1) * 512, 1024:1024 + 2048].rearrange("(g d) t -> d g t", d=64), sq, writes=[Tq])
        for i in range(16):
            for gh in range(2):
                sl = []
                for j in (i - 1, i, i + 1):
                    if j < 0 or j > 16:
                        continue
                    mask = None
                    if j == i - 1:
                        mask = trib[:, 0, :].unsqueeze(1).to_broadcast([128, 4, 128])
                    elif j == i + 1:
                        mask = trib[:, 1, :].unsqueeze(1).to_broadcast([128, 4, 128])
                    sl.append((kt[:, j * 128:(j + 1) * 128], [Tk], vt[:, j, :], [Tv], mask, [T_c], 0))
                for c in range(2):
                    sl.append((ckT[:, kh, c * 128:(c + 1) * 128], [T_ckT], cvt[:, c, kh * 64:(kh + 1) * 64], [T_cv], None, [], 0))
                hq0 = kh * 8 + gh * 4
                ed = snk[0:64, hq0:hq0 + 4].unsqueeze(2).to_broadcast([64, 4, 128])
                attn_core(K, sl, 512, (qt[:, gh * 4:gh * 4 + 4, i * 128:(i + 1) * 128], [Tq]), ot[:, gh * 4:gh * 4 + 4, i * 128:(i + 1) * 128], To, scale,
                          extra_den=(ed, [T_c]), pools=pools, g4=True)
        P.dma("sp", o1T[kh * 512:(kh + 1) * 512, 1024:1024 + 2048].rearrange("(g d) t -> d g t", d=64), ot[:], so, reads=[To])
    K.end()


def declare_io(K):
    K.inp("ident", [128, 128])
    K.inp("cond", [2, D])
    K.inp("xp", [NPT * 128, D])
    K.inp("xs", [4096, D])
    K.inp("w_ada", [2, D, 6 * D])
    K.inp("b_ada", [2, 6 * D])
    K.inp("norm_mix", [2, D])
    K.inp("norm_mlp", [2, D])
    K.inp("ab_w_in", [D, 7200])
    K.inp("w_gates", [D, 32])
    K.inp("cache_a_k", [256, 1024])
    K.inp("cache_a_v", [256, 1024])
    K.inp("na_mask", [2, 8, 6, 128, 256])
    K.inp("conv_w", [3, 3072])
    K.inp("alog_dt", [2, 16])
    K.inp("tri", [4, 128, 128])
    K.inp("s0", [2, 8, 128, 128])
    K.inp("onorm", [1, 128])
    K.inp("ab_w_out", [D, D])
    K.inp("w_mlp_in", [2, D, 4 * D])
    K.inp("w_mlp_out", [2, 4 * D, D])
    K.inp("c_w_qkv", [D, 2560])
    K.inp("c_w_out", [D, D])
    K.inp("cache_c_k", [256, 256])
    K.inp("cache_c_v", [256, 256])
    K.inp("c_sink", [1, 32])
    K.inp("final_norm", [1, D])
    K.inp("rope_cos", [128, SEXT * 128])
    K.inp("rope_sin", [128, SEXT * 128])
    K.inp("rope_perm", [128, 128])
    K.outp("nck", [NPT * 128, 256])
    K.outp("ncv", [NPT * 128, 256])
    K.outp("y_p", [NPT * 128, D])
    K.outp("y_s", [2048, D])
    K.outp("nbf", [4, 8, 128, 128])
    K.outp("nbb", [4, 8, 128, 128])
    K.outp("nak", [NPT * 128, 1024])
    K.outp("nav", [NPT * 128, 1024])


def build(stop=99, debug=()):
    nc = bass.Bass("TRN2", target_bir_lowering=False)
    K = Ctx(nc)
    declare_io(K)
    phases = [phase_consts, phase_ada, lambda K: phase_l0_inproj(K, 1), lambda K: phase_l0_inproj(K, 2), phase_attn_a, phase_dn_prep, phase_dn_scan, lambda K: phase_mlp(K, 0), phase_l1_inproj, phase_attn_c, lambda K: phase_mlp(K, 1)]
    for i, ph in enumerate(phases):
        if i >= stop:
            break
        ph(K)
    if debug:
        K.begin()
        s = K.getsem()
        for name in debug:
            src = K.dscr[name]
            o = K.outp("dbg_" + name, src.shape, src.dtype)
            nr = src.shape[0]
            step = max(1, min(nr, (1 << 20) // (src.shape[1] * 4)))
            for r0 in range(0, nr, step):
                K.P.dma("sp", o[r0:min(nr, r0 + step)], src[r0:min(nr, r0 + step)], s)
        K.end()
    K.pes.close()
    return nc, K


def na_mask_host(rel_bias, flip):
    out = np.full((2, 8, 6, 128, 256), -30000.0, np.float32)
    qq = np.arange(256)
    kk = np.arange(768)
    for cl in range(2):
        qr = (0 if cl == 0 else 12) + qq // 64
        qc = qq % 64
        kr = (0 if cl == 0 else 8) + kk // 64
        kc = kk % 64
        if flip:
            qr, qc, kr, kc = 63 - qr, 63 - qc, 63 - kr, 63 - kc
        rs = np.clip(qr - 4, 0, 56)
        cs = np.clip(qc - 8, 0, 48)
        vr = (kr[:, None] >= rs[None, :]) & (kr[:, None] < rs[None, :] + 8)
        vc = (kc[:, None] >= cs[None, :]) & (kc[:, None] < cs[None, :] + 16)
        valid = vr & vc
        dr = np.clip(kr[:, None] - qr[None, :] + 7, 0, 14)
        dc = np.clip(kc[:, None] - qc[None, :] + 15, 0, 30)
        for h in range(8):
            b = rel_bias[h][dr, dc]
            m = np.where(valid, b, np.float32(-30000.0)).astype(np.float32)
            out[cl, h] = m.reshape(6, 128, 256)
    return out


def tri_host():
    i = np.arange(128)[:, None]
    j = np.arange(128)[None, :]
    return np.stack([(j <= i), (j >= i), (j < i), (j > i)]).astype(np.float32)


def rope_host(flip):
    nf = 16
    inv = (10000.0 ** (-np.arange(nf, dtype=np.float32) / nf)).astype(np.float32)
    tloc = np.arange(SEXT * 128)
    tok = (4095 - tloc) if flip else tloc
    pos = np.stack([tok // 64, tok % 64], 0).astype(np.float32)
    d = np.arange(128) % 64
    a, b, fidx = d // 32, (d % 32) // 16, d % 16
    ang = pos[a, :] * inv[fidx][:, None]
    cos = np.cos(ang).astype(np.float32)
    sin = np.sin(ang).astype(np.float32)
    perm = np.zeros((128, 128), np.float32)
    for m in range(128):
        bm = (m % 32) // 16
        partner = m + 16 if bm == 0 else m - 16
        perm[partner, m] = -1.0 if bm == 0 else 1.0
    return cos, sin, perm


def host_inputs(inputs, c):
    seq, flip = c // 2, c % 2
    f = lambda a: np.ascontiguousarray(a, dtype=np.float32)
    xp = inputs["x_prompt"][4 * c:4 * c + 4]
    xs = inputs["x_sample"][seq]
    if flip:
        xp = xp[:, ::-1]
        xs = xs[::-1]
    wg = inputs["ab_w_in"][0][:, 7168:7200]
    if flip:
        wg = np.concatenate([wg[:, 8:16], wg[:, 0:8], wg[:, 24:32], wg[:, 16:24]], axis=1)
    m = {
        "ident": np.eye(128, dtype=np.float32),
        "cond": f(np.stack([inputs["c_ctx"], inputs["c"][seq]])),
        "xp": f(xp.reshape(NPT * 128, D)),
        "xs": f(xs),
        "w_ada": inputs["w_ada"], "b_ada": inputs["b_ada"],
        "norm_mix": inputs["norm_mix"], "norm_mlp": inputs["norm_mlp"],
        "ab_w_in": inputs["ab_w_in"][0], "w_gates": f(wg),
        "cache_a_k": f(inputs["cache_a_k"][seq, 0].reshape(256, 1024)),
        "cache_a_v": f(inputs["cache_a_v"][seq, 0].reshape(256, 1024)),
        "na_mask": na_mask_host(inputs["a_rel_bias"][0], flip),
        "ab_w_out": inputs["ab_w_out"][0], "w_mlp_in": inputs["w_mlp_in"], "w_mlp_out": inputs["w_mlp_out"],
        "c_w_qkv": inputs["c_w_qkv"][0], "c_w_out": inputs["c_w_out"][0],
        "cache_c_k": f(inputs["cache_c_k"][seq, 0].reshape(256, 256)), "cache_c_v": f(inputs["cache_c_v"][seq, 0].reshape(256, 256)),
        "c_sink": f(inputs["c_sink"][0].reshape(1, 32)), "final_norm": f(inputs["final_norm"].reshape(1, D)),
        "rope_cos": rope_host(flip)[0], "rope_sin": rope_host(flip)[1], "rope_perm": rope_host(flip)[2],
        "conv_w": f(inputs["b_conv"][0][::-1] if flip else inputs["b_conv"][0]),
        "alog_dt": f(np.stack([(inputs["b_a_log"][0][::-1] if flip else inputs["b_a_log"][0]).reshape(16),
                               (inputs["b_dt_bias"][0][::-1] if flip else inputs["b_dt_bias"][0]).reshape(16)])),
        "tri": tri_host(),
        "s0": f(np.stack([inputs["state_b_bwd"][seq, 0], inputs["state_b_fwd"][seq, 0]]) if flip else
                np.stack([inputs["state_b_fwd"][seq, 0], inputs["state_b_bwd"][seq, 0]])),
        "onorm": f(inputs["b_out_norm"][0].reshape(1, 128)),
    }
    return m


_CACHE = {}


def kernel(**inputs):
    inputs = {k: np.asarray(v) for k, v in inputs.items()}
    if "nc" not in _CACHE:
        _CACHE["nc"] = build()
    nc, K = _CACHE["nc"]
    maps = [host_inputs(inputs, c) for c in range(8)]
    res = run_bass_kernel_spmd(nc, maps, core_ids=list(range(8))).results
    f32 = np.float32
    y_p = np.zeros((32, 256, D), f32)
    y_s = np.zeros((4, 4096, D), f32)
    nak = np.zeros((32, 1, 256, 8, 128), f32)
    nav = np.zeros((32, 1, 256, 8, 128), f32)
    nbf = np.zeros((32, 1, 8, 128, 128), f32)
    nbb = np.zeros((32, 1, 8, 128, 128), f32)
    nck = np.zeros((32, 1, 256, 4, 64), f32)
    ncv = np.zeros((32, 1, 256, 4, 64), f32)
    for c in range(8):
        seq, flip = c // 2, c % 2
        r = {k: np.asarray(v, dtype=f32) for k, v in res[c].items()}
        fl = (lambda a: a[:, ::-1]) if flip else (lambda a: a)
        sl = slice(4 * c, 4 * c + 4)
        y_p[sl] = fl(r["y_p"].reshape(4, 256, D))
        if flip:
            y_s[seq, 2048:4096] = r["y_s"][::-1]
        else:
            y_s[seq, 0:2048] = r["y_s"]
        nak[sl, 0] = fl(r["nak"].reshape(4, 256, 8, 128))
        nav[sl, 0] = fl(r["nav"].reshape(4, 256, 8, 128))
        nck[sl, 0] = fl(r["nck"].reshape(4, 256, 4, 64))
        ncv[sl, 0] = fl(r["ncv"].reshape(4, 256, 4, 64))
        if flip:
            nbf[sl, 0], nbb[sl, 0] = r["nbb"], r["nbf"]
        else:
            nbf[sl, 0], nbb[sl, 0] = r["nbf"], r["nbb"]
    return (y_p, y_s, nak, nav, nbf, nbb, nck, ncv)
```

```python
import contextlib
import numpy as np
import concourse.bass as bass
import concourse.mybir as mybir

F32 = mybir.dt.float32
BF16 = mybir.dt.bfloat16
I32 = mybir.dt.int32
AF = mybir.ActivationFunctionType
ALU = mybir.AluOpType
AX = mybir.AxisListType

ENGS = ("pe", "act", "dve", "pool", "sp")
HANDLES = {"pe": "tensor", "act": "scalar", "dve": "vector", "pool": "gpsimd", "sp": "sync"}
SEM_LIMIT = 16000


class Tile:
    __slots__ = ("name", "writer", "readers", "excl")

    def __init__(self, name=""):
        self.name = name
        self.writer = None
        self.readers = {}
        self.excl = False


class DSem:
    def __init__(self, prog, name):
        self.prog = prog
        self.name = name
        self.gen = 0
        self.h = prog.nc.alloc_semaphore(name=name)
        self.count = 0
        self.last = None

    def bump(self):
        if self.count + 16 > SEM_LIMIT:
            self.gen += 1
            self.h = self.prog.nc.alloc_semaphore(name=f"{self.name}_g{self.gen}")
            self.count = 0
        self.count += 16
        return self.h, self.count


class Ins:
    __slots__ = ("eng", "fn", "deps", "sem", "count", "needed", "is_dma", "epoch")

    def __init__(self, eng, fn, is_dma=False):
        self.eng = eng
        self.fn = fn
        self.deps = []
        self.sem = None
        self.count = None
        self.needed = False
        self.is_dma = is_dma
        self.epoch = 0


class Prog:
    def __init__(self, nc):
        self.nc = nc
        self.lists = {e: [] for e in ENGS}
        self.esem = {e: nc.alloc_semaphore(name=f"es_{e}_0") for e in ENGS}
        self.esem_gen = {e: 0 for e in ENGS}
        self.ecount = {e: 0 for e in ENGS}
        self.known = {e: {} for e in ENGS}
        self.epoch = 0
        self.n_ins = 0
        self.dsems = []
        self.last_ins = {e: None for e in ENGS}

    def tile(self, name=""):
        return Tile(name)

    def tiles(self, n, name=""):
        return [Tile(f"{name}{i}") for i in range(n)]

    def dsem(self, name):
        d = DSem(self, name)
        self.dsems.append(d)
        return d

    def _add(self, eng, fn, reads, writes, dsem=None):
        ins = Ins(eng, fn, is_dma=dsem is not None)
        ins.epoch = self.epoch
        deps = []
        for t in reads:
            if t.writer is not None:
                deps.append(t.writer)
            if t.excl:
                deps.extend(r for r in t.readers.values() if r.eng != eng)
        for t in writes:
            if t.writer is not None:
                deps.append(t.writer)
            deps.extend(t.readers.values())
        if dsem is not None:
            if dsem.last is not None:
                deps.append(dsem.last)
            ins.sem, ins.count = dsem.bump()
            ins.needed = True
            dsem.last = ins
        out = []
        seen = set()
        for d in deps:
            if d is ins or id(d) in seen:
                continue
            seen.add(id(d))
            if d.epoch < self.epoch:
                continue
            if eng == "pe" and d.eng == "pe" and not d.is_dma and dsem is None:
                continue
            out.append(d)
        ins.deps = out
        for d in out:
            d.needed = True
        for t in reads:
            key = (eng, dsem.name) if dsem is not None else eng
            t.readers[key] = ins
        for t in writes:
            t.writer = ins
            t.readers = {}
        self.lists[eng].append(ins)
        self.last_ins[eng] = ins
        self.n_ins += 1
        return ins

    def op(self, eng, fn, reads=(), writes=()):
        return self._add(eng, fn, list(reads), list(writes))

    def dma(self, eng, out, in_, dsem, reads=(), writes=()):
        return self._add(eng, lambda e: e.dma_start(out=out, in_=in_), list(reads), list(writes), dsem=dsem)

    def barrier(self):
        deps = [i for i in self.last_ins.values() if i is not None]
        deps += [d.last for d in self.dsems if d.last is not None]
        deps = [d for d in deps if d.epoch == self.epoch]
        for d in deps:
            d.needed = True
        for e in ENGS:
            ins = Ins(e, None)
            ins.epoch = self.epoch
            ins.deps = list(deps)
            self.lists[e].append(ins)

    def flush(self, final=False):
        self.barrier()
        for e in ENGS:
            for ins in self.lists[e]:
                if ins.is_dma or ins.fn is None:
                    continue
                if ins.needed:
                    if self.ecount[e] + 1 > SEM_LIMIT:
                        self.esem_gen[e] += 1
                        self.esem[e] = self.nc.alloc_semaphore(name=f"es_{e}_{self.esem_gen[e]}")
                        self.ecount[e] = 0
                    self.ecount[e] += 1
                    ins.sem = self.esem[e]
                    ins.count = self.ecount[e]
        prog = self

        def run(e, h):
            known = prog.known[e]
            for ins in prog.lists[e]:
                need = {}
                for d in ins.deps:
                    k = id(d.sem)
                    if known.get(k, 0) >= d.count:
                        continue
                    if k not in need or need[k][1] < d.count:
                        need[k] = (d.sem, d.count)
                for k, (s, v) in need.items():
                    h.wait_ge(s, v)
                    known[k] = v
                if ins.fn is None:
                    continue
                bi = ins.fn(h)
                if ins.is_dma:
                    bi.then_inc(ins.sem, 16)
                elif ins.needed:
                    bi.then_inc(ins.sem, 1)

        with self.nc.Block() as block:
            for e in ENGS:
                if not self.lists[e]:
                    continue
                dec = getattr(block, HANDLES[e])

                def mk(e):
                    def _f(h):
                        run(e, h)
                    return _f
                dec(mk(e))
        self.lists = {e: [] for e in ENGS}
        self.epoch += 1

from concourse.bass_utils import run_bass_kernel_spmd

D = 2048
KC = 16
EPS = 1e-6
NPT = 8
NS1 = 19
NG1 = NPT + NS1
NS2 = 13
TOK1 = NG1 * 128
TOKB = (NG1 + NS2) * 128
SEXT = 17


class Ring:
    def __init__(self, K, name, shape, dt, n, sw=False):
        self.bufs = [K.sb(f"{name}{i}", shape, dt) for i in range(n)]
        self.tiles = [Tile(f"{name}{i}") for i in range(n)]
        self.sems = [K.getsem(sw) for i in range(n)]
        self.i = 0

    def next(self):
        k = self.i % len(self.bufs)
        self.i += 1
        return self.bufs[k], self.tiles[k], self.sems[k]


class Ctx:
    def __init__(self, nc):
        self.nc = nc
        self.P = Prog(nc)
        self.es = None
        self.pes = contextlib.ExitStack()
        self.uid = 0
        self.sem_pool = {False: [], True: []}
        self.sem_used = {False: [], True: []}
        self.PS = [nc.alloc_psum_tensor(f"psb{i}", [128, 512], F32) for i in range(8)]
        self.TPS = [Tile(f"ps{i}") for i in range(8)]
        for t_ in self.TPS:
            t_.excl = True
        self.psi = [0, 0]
        self.depth = 0
        self.din = {}
        self.dout = {}
        self.dscr = {}

    def inp(self, name, shape, dt=F32):
        self.din[name] = self.nc.dram_tensor(name, list(shape), dt, kind="ExternalInput").ap()
        return self.din[name]

    def outp(self, name, shape, dt=F32):
        self.dout[name] = self.nc.dram_tensor(name, list(shape), dt, kind="ExternalOutput").ap()
        return self.dout[name]

    def scr(self, name, shape, dt):
        self.dscr[name] = self.nc.dram_tensor(name, list(shape), dt).ap()
        return self.dscr[name]

    def begin(self):
        if self.es is not None:
            self.depth += 1
            return
        self.es = contextlib.ExitStack()

    def end(self):
        if self.depth > 0:
            self.depth -= 1
            return
        self.P.flush()
        self.es.close()
        self.es = None
        for k in (False, True):
            self.sem_pool[k].extend(self.sem_used[k])
            self.sem_used[k] = []

    def sb(self, name, shape, dt):
        self.uid += 1
        return self.es.enter_context(self.nc.sbuf_tensor(f"{name}_{self.uid}", list(shape), dt))

    def psb(self, name, shape, dt):
        self.uid += 1
        return self.pes.enter_context(self.nc.sbuf_tensor(f"{name}_{self.uid}", list(shape), dt))

    def getsem(self, sw=False):
        if self.sem_pool[sw]:
            s = self.sem_pool[sw].pop()
        else:
            self.uid += 1
            s = self.P.dsem(f"ds{'w' if sw else 'h'}{self.uid}")
        self.sem_used[sw].append(s)
        return s

    def dump(self, name, ap, tiles):
        import os
        if not os.environ.get("DN_DEBUG") or name in self.dscr:
            return
        d = self.scr(name, list(ap.shape), ap.dtype)
        self.P.dma("sp", d, ap, self.getsem(), reads=tiles)

    def ps(self, g=0):
        k = g * 4 + self.psi[g] % 4
        self.psi[g] += 1
        return self.PS[k], self.TPS[k]


def rows_T(K, dst, T_dst, src2d, n, stage, T_stage, sem):
    P = K.P
    P.dma("sp", stage[0:n, :], src2d, sem, writes=[T_stage])
    ps, Tp = K.ps()
    P.op("pe", lambda e: e.transpose(ps[:, 0:n], stage[0:n, :], K.identf[0:n, 0:n]), reads=[T_stage, K.T_const], writes=[Tp])
    P.op("dve", lambda e: e.tensor_copy(dst, ps[:, 0:n]), reads=[Tp], writes=[T_dst])


def phase_consts(K):
    P = K.P
    K.begin()
    K.T_const = Tile("const")
    K.identf = K.psb("identf", [128, 128], F32)
    K.identb = K.psb("identb", [128, 128], BF16)
    K.onesf = K.psb("onesf", [128, 128], F32)
    K.onesb = K.psb("onesb", [128, 128], BF16)
    s = K.getsem()
    s2 = K.getsem(True)
    P.dma("sp", K.identf[:], K.din["ident"], s, writes=[K.T_const])
    P.dma("pool", K.identb[:], K.din["ident"], s2, writes=[K.T_const])
    P.op("dve", lambda e: e.memset(K.onesf[:], 1.0), writes=[K.T_const])
    P.op("dve", lambda e: e.memset(K.onesb[:], 1.0), writes=[K.T_const])
    K.end()


def phase_ada(K):
    nc, P = K.nc, K.P
    modrow_d = K.scr("modrow_d", [2, 2, 6 * D], F32)
    K.begin()
    cond = K.sb("cond", [2, D], F32)
    cs = K.sb("cs", [2, D], F32)
    condT = K.sb("condT", [128, KC, 2], BF16)
    brow = K.sb("brow", [2, 6 * D], F32)
    modrow = K.sb("modrow", [2, 6 * D], F32)
    T_cond, T_cs, T_condT, T_brow, T_modrow, T_mrd = P.tiles(6, "ada")
    s0, s1 = K.getsem(), K.getsem()
    wr = Ring(K, "adaw", [128, KC, 512], BF16, 3, sw=True)
    P.dma("sp", cond[:], K.din["cond"], s0, writes=[T_cond])
    P.op("act", lambda e: e.activation(out=cs[:], in_=cond[:], func=AF.Silu), reads=[T_cond], writes=[T_cs])
    ps, Tp = K.ps()
    for kc in range(KC):
        P.op("pe", lambda e, kc=kc, ps=ps: e.transpose(ps[:, kc * 2:kc * 2 + 2], cs[0:2, kc * 128:(kc + 1) * 128], K.identf[0:2, 0:2]),
             reads=[T_cs, K.T_const], writes=[Tp])
    P.op("dve", lambda e, ps=ps: e.tensor_copy(condT[:].rearrange("p k c -> p (k c)"), ps[:, 0:2 * KC]), reads=[Tp], writes=[T_condT])
    for l in range(2):
        P.dma("sp", brow[:], K.din["b_ada"][l:l + 1, :].partition_broadcast(2), s0, writes=[T_brow])
        for cg in range(24):
            wt, Tw, sw = wr.next()
            P.dma("pool", wt[:], K.din["w_ada"][l, :, cg * 512:(cg + 1) * 512].rearrange("(kc p) n -> p kc n", p=128), sw, writes=[Tw])
            ps, Tp = K.ps()
            for kc in range(KC):
                P.op("pe", lambda e, kc=kc, ps=ps, wt=wt: e.matmul(ps[0:2, :], condT[:, kc, :], wt[:, kc, :], start=(kc == 0), stop=(kc == KC - 1)),
                     reads=[T_condT, Tw], writes=[Tp])
            P.op("dve", lambda e, ps=ps, cg=cg: e.tensor_tensor(modrow[0:2, cg * 512:(cg + 1) * 512], ps[0:2, :], brow[0:2, cg * 512:(cg + 1) * 512], ALU.add),
                 reads=[Tp, T_brow], writes=[T_modrow])
        P.dma("sp", modrow_d[l], modrow[:], s1, reads=[T_modrow], writes=[T_mrd])
    K.end()
    K.begin()
    K.T_mod = Tile("mod")
    K.modF = [[K.psb(f"modF{l}{c}", [128, 96], F32) for c in range(2)] for l in range(2)]
    K.gsF = [[[K.psb(f"gsF{l}{w}{c}", [128, KC], F32) for c in range(2)] for w in range(2)] for l in range(2)]
    K.fnorm = K.psb("fnormF", [128, KC], F32)
    stage = K.sb("stg", [128, 128], F32)
    gF = K.sb("gF", [128, KC], F32)
    T_stage, T_g = P.tiles(2, "adaf")
    s0 = K.getsem()
    for l in range(2):
        for c in range(2):
            rows_T(K, K.modF[l][c][:], K.T_mod, modrow_d[l, c].rearrange("(r p) -> r p", p=128), 96, stage, T_stage, s0)
        for w, nm in enumerate(("norm_mix", "norm_mlp")):
            rows_T(K, gF[:], T_g, K.din[nm][l].rearrange("(r p) -> r p", p=128), KC, stage, T_stage, s0)
            for c in range(2):
                sc = K.modF[l][c][:, (1 + 3 * w) * KC:(2 + 3 * w) * KC]
                P.op("dve", lambda e, l=l, w=w, c=c, sc=sc: e.scalar_tensor_tensor(K.gsF[l][w][c][:], sc, 1.0, gF[:], ALU.add, ALU.mult),
                     reads=[K.T_mod, T_g], writes=[K.T_mod])
    K.end()


class NormT:
    def __init__(self, K, n=2):
        self.K = K
        self.ss = Ring(K, "nss", [128, 2], F32, n)
        self.xn = Ring(K, "nxn", [128, D], BF16, n)

    def run(self, xt, T_x, gs, shift, dst_fn, T_dst_fn):
        K = self.K
        P = K.P
        ss, T_ss, _ = self.ss.next()
        xn, T_xn, _ = self.xn.next()
        P.op("act", lambda e: e.activation(out=xn[:], in_=xt, func=AF.Square, accum_out=ss[:, 0:1]), reads=[T_x], writes=[T_xn, T_ss])
        import os
        NTL = int(os.environ.get("NT_LEVEL", "9"))
        if NTL < 2:
            return
        P.op("act", lambda e: e.activation(out=ss[:, 1:2], in_=ss[:, 0:1], func=AF.Sqrt, bias=EPS, scale=1.0 / D), reads=[T_ss], writes=[T_ss])
        P.op("dve", lambda e: e.reciprocal(ss[:, 1:2], ss[:, 1:2]), reads=[T_ss], writes=[T_ss])
        if NTL < 3:
            return
        P.op("act", lambda e: e.activation(out=xn[:], in_=xt, func=AF.Identity, scale=ss[:, 1:2]), reads=[T_x, T_ss], writes=[T_xn])
        if NTL < 4:
            return
        for half in range(2):
            ps, Tp = K.ps()
            psb = ps[:].bitcast(BF16)
            for j in range(8):
                kc = half * 8 + j
                P.op("pe", lambda e, j=j, kc=kc, psb=psb: e.transpose(psb[:, j * 128:(j + 1) * 128], xn[:, kc * 128:(kc + 1) * 128], K.identb[:]),
                     reads=[T_xn, K.T_const], writes=[Tp])
            for j in range(8):
                kc = half * 8 + j
                if True:
                    P.op("dve", lambda e, j=j, kc=kc, psb=psb: e.tensor_scalar(dst_fn(kc), psb[:, j * 128:(j + 1) * 128], gs[:, kc:kc + 1], shift[:, kc:kc + 1], ALU.mult, ALU.add),
                         reads=[Tp, K.T_mod], writes=[T_dst_fn(kc)])
                else:
                    P.op("act", lambda e, j=j, kc=kc, psb=psb: e.activation(out=dst_fn(kc), in_=psb[:, j * 128:(j + 1) * 128], func=AF.Identity, scale=gs[:, kc:kc + 1], bias=shift[:, kc:kc + 1]),
                         reads=[Tp, K.T_mod], writes=[T_dst_fn(kc)])


def token_groups(n_tb, breaks=()):
    out = []
    pts = [0] + list(breaks) + [n_tb]
    for a, b in zip(pts[:-1], pts[1:]):
        t = a
        while t < b:
            m = min(4, b - t)
            out.append((t, m))
            t += m
    return out


def phase_l0_inproj(K, which_pass):
    nc, P = K.nc, K.P
    if which_pass == 1:
        K.scr("qaT_d", [1024, TOK1], BF16)
        K.scr("kaT_d", [1024, TOK1], BF16)
        K.scr("va_d", [TOK1, 1024], BF16)
        K.scr("qkvT_d", [3072, TOKB], F32)
        K.scr("z_d", [TOK1, 1024], F32)
        K.scr("gates_d", [TOKB, 32], F32)
        ntb = NG1
        srcs = [(K.din["xp"][g * 128:(g + 1) * 128, :], 0) for g in range(NPT)] + \
               [(K.din["xs"][g * 128:(g + 1) * 128, :], 1) for g in range(NS1)]
        tok0 = 0
        groups = token_groups(NG1, breaks=(NPT,))
    else:
        ntb = NS2
        srcs = [(K.din["xs"][(NS1 + g) * 128:(NS1 + g + 1) * 128, :], 1) for g in range(NS2)]
        tok0 = TOK1
        groups = token_groups(NS2)
    K.begin()
    hT = K.sb("hT", [128, KC, ntb * 128], BF16)
    T_h = [[Tile(f"h{g}_{kc}") for kc in range(KC)] for g in range(ntb)]
    xr = Ring(K, "xin", [128, D], F32, 3)
    nt = NormT(K)
    for g, (src, c) in enumerate(srcs):
        xt, T_x, sx = xr.next()
        P.dma("sp", xt[:], src, sx, writes=[T_x])
        nt.run(xt[:], T_x, K.gsF[0][0][c], K.modF[0][c][:, 0:KC],
               lambda kc, g=g: hT[:, kc, g * 128:(g + 1) * 128], lambda kc, g=g: T_h[g][kc])
    wr = Ring(K, "w0", [128, KC, 512], BF16, 3, sw=True)
    st32 = Ring(K, "st32", [128, 512], F32, 3)
    st16 = Ring(K, "st16", [128, 512], BF16, 3)
    W = K.din["ab_w_in"]
    evi = [0]

    def evac(dst, src, T_src, T_dst):
        evi[0] += 1
        if evi[0] % 2:
            P.op("dve", lambda e: e.tensor_copy(dst, src), reads=[T_src], writes=[T_dst])
        else:
            P.op("act", lambda e: e.copy(dst, src), reads=[T_src], writes=[T_dst])

    def fm(wt, Tw, ncol_blocks, dst_d, row0, f32):
        for sub in range(ncol_blocks):
            for (t0, m) in groups:
                n = m * 128
                ps, Tp = K.ps()
                for kc in range(KC):
                    P.op("pe", lambda e, kc=kc, ps=ps, sub=sub, t0=t0, n=n: e.matmul(ps[:, 0:n], wt[:, kc, sub * 128:(sub + 1) * 128], hT[:, kc, t0 * 128:t0 * 128 + n], start=(kc == 0), stop=(kc == KC - 1)),
                         reads=[Tw] + [T_h[t0 + i][kc] for i in range(m)], writes=[Tp])
                sg, Ts, ss_ = (st32 if f32 else st16).next()
                evac(sg[:, 0:n], ps[:, 0:n], Tp, Ts)
                P.dma("sp", dst_d[row0 + sub * 128:row0 + (sub + 1) * 128, tok0 + t0 * 128:tok0 + t0 * 128 + n], sg[:, 0:n], ss_, reads=[Ts])

    def tm(wt, Tw, ncols, tbs, dests):
        for g in tbs:
            ps, Tp = K.ps()
            for kc in range(KC):
                P.op("pe", lambda e, kc=kc, ps=ps, g=g: e.matmul(ps[:, 0:ncols], hT[:, kc, g * 128:(g + 1) * 128], wt[:, kc, 0:ncols], start=(kc == 0), stop=(kc == KC - 1)),
                     reads=[Tw, T_h[g][kc]], writes=[Tp])
            for (dfn, f32) in dests:
                d = dfn(g)
                if d is None:
                    continue
                sg, Ts, ss_ = (st32 if f32 else st16).next()
                evac(sg[:, 0:ncols], ps[:, 0:ncols], Tp, Ts)
                P.dma("sp", d, sg[:, 0:ncols], ss_, reads=[Ts])

    wg32 = K.sb("wg32", [128, KC, 32], F32)
    T_wg32 = Tile("wg32")
    s_wg = K.getsem()

    def loadw(src, ncols=512):
        wt, Tw, sw = wr.next()
        if ncols == 512:
            P.dma("pool", wt[:, :, 0:ncols], src.rearrange("(kc p) n -> p kc n", p=128), sw, writes=[Tw])
        else:
            P.dma("sp", wg32[:], src.rearrange("(kc p) n -> p kc n", p=128), s_wg, writes=[T_wg32])
            P.op("dve", lambda e, wt=wt: e.tensor_copy(wt[:, :, 0:ncols], wg32[:]), reads=[T_wg32], writes=[Tw])
        return wt, Tw

    S = K.dscr
    if which_pass == 1:
        import os
        for t in range(int(os.environ.get('KDBG_NT', '14'))):
            wt, Tw = loadw(W[:, t * 512:(t + 1) * 512])
            if t < 2:
                fm(wt, Tw, 4, S["qaT_d"], t * 512, False)
            elif t < 4:
                fm(wt, Tw, 4, S["kaT_d"], (t - 2) * 512, False)
                tm(wt, Tw, 512, range(NPT), [(lambda g, t=t: K.dout["nak"][g * 128:(g + 1) * 128, (t - 2) * 512:(t - 1) * 512], True)])
            elif t < 6:
                tm(wt, Tw, 512, range(NG1), [(lambda g, t=t: S["va_d"][g * 128:(g + 1) * 128, (t - 4) * 512:(t - 3) * 512], False),
                                            (lambda g, t=t: K.dout["nav"][g * 128:(g + 1) * 128, (t - 4) * 512:(t - 3) * 512] if g < NPT else None, True)])
            elif t < 12:
                fm(wt, Tw, 4, S["qkvT_d"], (t - 6) * 512, True)
            else:
                tm(wt, Tw, 512, range(NG1), [(lambda g, t=t: S["z_d"][g * 128:(g + 1) * 128, (t - 12) * 512:(t - 11) * 512], True)])
        if int(os.environ.get('KDBG_G', '1')):
          wt, Tw = loadw(K.din["w_gates"], 32)
          tm(wt, Tw, 32, range(NG1), [(lambda g: S["gates_d"][g * 128:(g + 1) * 128, :], True)])
    else:
        for t in range(8, 12):
            wt, Tw = loadw(W[:, t * 512:(t + 1) * 512])
            fm(wt, Tw, 4, S["qkvT_d"], (t - 6) * 512, True)
        wt, Tw = loadw(K.din["w_gates"], 32)
        tm(wt, Tw, 32, range(NS2), [(lambda g: S["gates_d"][tok0 + g * 128:tok0 + (g + 1) * 128, :], True)])
    K.end()


NCAT = NPT + SEXT
TOKC = NCAT * 128


def attn_core(K, S_list, nq, rhs_q, out_ap, T_out, scale, extra_den=None, pools=None, g4=False):
    P = K.P
    q_ap, q_tiles = rhs_q
    M = out_ap.shape[0]
    psn, Tn = K.ps(1)
    psd, Td = K.ps(1)
    pt_ring, rec_ring = pools
    n = len(S_list)
    v3 = (lambda ap: ap.rearrange("p (g t) -> p g t", g=4)) if g4 else (lambda ap: ap)
    def s_mm(i):
        lk, tk = S_list[i][0], S_list[i][1]
        pss, Ts = K.ps(0)
        P.op("pe", lambda e, pss=pss, lk=lk: e.matmul(v3(pss[:, 0:nq]), lk, q_ap, start=True, stop=True), reads=tk + q_tiles, writes=[Ts])
        return pss, Ts

    nxt = s_mm(0)
    for i, (lk, tk, lv, tv, mask, tm_, _) in enumerate(S_list):
        pss, Ts = nxt
        if i + 1 < n:
            nxt = s_mm(i + 1)
        pt, Tpt, _ = pt_ring.next()
        P.op("act", lambda e, pss=pss, pt=pt: e.activation(out=pt[:, 0:nq], in_=pss[:, 0:nq], func=AF.Exp, scale=scale), reads=[Ts], writes=[Tpt])
        if mask is not None:
            P.op("pool", lambda e, pt=pt, mask=mask: e.tensor_tensor(v3(pt[:, 0:nq]), v3(pt[:, 0:nq]), mask, ALU.mult), reads=[Tpt] + tm_, writes=[Tpt])
        P.op("pe", lambda e, pt=pt, lv=lv, i=i: e.matmul(psn[0:M, 0:nq], lv, pt[:, 0:nq], start=(i == 0), stop=(i == n - 1)), reads=tv + [Tpt], writes=[Tn])
        P.op("pe", lambda e, pt=pt, i=i: e.matmul(psd[0:M, 0:nq], K.onesb[:, 0:M], pt[:, 0:nq], start=(i == 0), stop=(i == n - 1)), reads=[K.T_const, Tpt], writes=[Td])
    rec, Trec, _ = rec_ring.next()
    if extra_den is not None:
        ed, ted = extra_den
        if g4:
            P.op("dve", lambda e: e.tensor_tensor(v3(rec[0:M, 0:nq]), v3(psd[0:M, 0:nq]), ed, ALU.add), reads=[Td] + ted, writes=[Trec])
        else:
            P.op("dve", lambda e: e.tensor_scalar_add(rec[0:M, 0:nq], psd[0:M, 0:nq], ed), reads=[Td] + ted, writes=[Trec])
        P.op("dve", lambda e: e.reciprocal(rec[0:M, 0:nq], rec[0:M, 0:nq]), reads=[Trec], writes=[Trec])
    else:
        P.op("dve", lambda e: e.reciprocal(rec[0:M, 0:nq], psd[0:M, 0:nq]), reads=[Td], writes=[Trec])
    P.op("dve", lambda e: e.tensor_tensor(out_ap, v3(psn[0:M, 0:nq]), v3(rec[0:M, 0:nq]), ALU.mult), reads=[Tn, Trec], writes=[T_out])


def phase_attn_a(K):
    nc, P = K.nc, K.P
    S = K.dscr
    catT = K.scr("catT_d", [D, TOKC], BF16)
    scale = 128 ** -0.5
    K.begin()
    pt_ring = Ring(K, "pt", [128, 256], BF16, 4)
    rec_ring = Ring(K, "rec", [128, 256], F32, 2)
    pools = (pt_ring, rec_ring)
    qr = Ring(K, "cq", [128, 8, 256], BF16, 2)
    kr = Ring(K, "ck", [128, 8, 256], BF16, 2)
    vr = Ring(K, "cv", [128, 2, 1024], BF16, 2)
    orr = Ring(K, "co", [128, 8, 256], BF16, 2)
    for s in range(4):
        qt, Tq, sq = qr.next()
        kt, Tk, sk = kr.next()
        vt, Tv, sv = vr.next()
        ot, To, so = orr.next()
        P.dma("sp", qt[:], S["qaT_d"][:, s * 256:(s + 1) * 256].rearrange("(h p) t -> p h t", p=128), sq, writes=[Tq])
        P.dma("sp", kt[:], S["kaT_d"][:, s * 256:(s + 1) * 256].rearrange("(h p) t -> p h t", p=128), sk, writes=[Tk])
        P.dma("sp", vt[:], S["va_d"][s * 256:(s + 1) * 256, :].rearrange("(c p) f -> p c f", p=128), sv, writes=[Tv])
        for h in range(8):
            sl = [(kt[:, h, c * 128:(c + 1) * 128], [Tk], vt[:, c, h * 128:(h + 1) * 128], [Tv], None, [], 0) for c in range(2)]
            attn_core(K, sl, 256, (qt[:, h, :], [Tq]), ot[:, h, :], To, scale, pools=pools)
        P.dma("sp", catT[0:1024, s * 256:(s + 1) * 256].rearrange("(h p) t -> p h t", p=128), ot[:], so, reads=[To])
    ck_tm = K.sb("ck_tm", [128, 2, 1024], BF16)
    cvt = K.sb("cvt", [128, 2, 1024], BF16)
    ckT = K.sb("ckT", [128, 8, 256], BF16)
    T_ck, T_cv, T_ckT, T_eb = P.tiles(4, "na")
    sw0 = K.getsem(True)
    P.dma("pool", ck_tm[:], K.din["cache_a_k"].rearrange("(c p) f -> p c f", p=128), sw0, writes=[T_ck])
    P.dma("pool", cvt[:], K.din["cache_a_v"].rearrange("(c p) f -> p c f", p=128), sw0, writes=[T_cv])
    for c in range(2):
        ps, Tp = K.ps(0)
        psb = ps[:].bitcast(BF16)
        for h in range(8):
            P.op("pe", lambda e, c=c, h=h, psb=psb: e.transpose(psb[:, h * 128:(h + 1) * 128], ck_tm[:, c, h * 128:(h + 1) * 128], K.identb[:]), reads=[T_ck, K.T_const], writes=[Tp])
        P.op("dve", lambda e, c=c, psb=psb: e.tensor_copy(ckT[:, :, c * 128:(c + 1) * 128], psb.rearrange("p (h k) -> p h k", h=8)), reads=[Tp], writes=[T_ckT])
    EB = K.sb("EB", [128, 2, 8, 6, 256], BF16)
    mr = Ring(K, "mstage", [128, 6, 256], F32, 2)
    for cl in range(2):
        for h in range(8):
            mt, Tm, sm = mr.next()
            P.dma("sp", mt[:], K.din["na_mask"][cl, h].rearrange("c k q -> k c q"), sm, writes=[Tm])
            P.op("act", lambda e, cl=cl, h=h, mt=mt: e.activation(out=EB[:, cl, h, :, :], in_=mt[:], func=AF.Exp), reads=[Tm], writes=[T_eb])
    NQ = SEXT * 128
    NK = NS1 * 128
    qh = Ring(K, "nq", [128, NQ], BF16, 2)
    kh = Ring(K, "nk", [128, NK], BF16, 2)
    vh = Ring(K, "nv", [128, NS1, 128], BF16, 2)
    oh = Ring(K, "no", [128, NQ], BF16, 2)
    for h in range(8):
        qt, Tq, sq = qh.next()
        kt, Tk, sk = kh.next()
        vt, Tv, sv = vh.next()
        ot, To, so = oh.next()
        P.dma("sp", qt[:], S["qaT_d"][h * 128:(h + 1) * 128, 1024:1024 + NQ], sq, writes=[Tq])
        P.dma("sp", kt[:], S["kaT_d"][h * 128:(h + 1) * 128, 1024:1024 + NK], sk, writes=[Tk])
        P.dma("sp", vt[:], S["va_d"][1024:1024 + NK, h * 128:(h + 1) * 128].rearrange("(c p) f -> p c f", p=128), sv, writes=[Tv])
        for i in range(9):
            nq = 256 if i < 8 else 128
            cl = 0 if i == 0 else 1
            base = 0 if i == 0 else (i - 1) * 256
            nch = 6 if i < 8 else 5
            sl = []
            for ch in range(nch):
                t0 = base + ch * 128
                sl.append((kt[:, t0:t0 + 128], [Tk], vt[:, t0 // 128, :], [Tv], EB[:, cl, h, ch, 0:nq], [T_eb], 0))
            for c in range(2):
                sl.append((ckT[:, h, c * 128:(c + 1) * 128], [T_ckT], cvt[:, c, h * 128:(h + 1) * 128], [T_cv], None, [], 0))
            attn_core(K, sl, nq, (qt[:, i * 256:i * 256 + nq], [Tq]), ot[:, i * 256:i * 256 + nq], To, scale, pools=pools)
        P.dma("sp", catT[h * 128:(h + 1) * 128, 1024:1024 + NQ], ot[:], so, reads=[To])
    K.end()


def phase_dn_prep(K):
    nc, P = K.nc, K.P
    S = K.dscr
    K.scr("qnT_d", [1024, TOK1], BF16)
    K.scr("knT_d", [1024, TOKB], BF16)
    K.scr("ktm_d", [TOKB, 1024], BF16)
    K.scr("vtm_d", [TOKB, 1024], BF16)
    K.begin()
    cwF = K.sb("cwF", [128, 3, 24], F32)
    stage = K.sb("cstg", [128, 128], F32)
    T_cw, T_stage = P.tiles(2, "cw")
    s0 = K.getsem()
    for j in range(3):
        rows_T(K, cwF[:, j, :], T_cw, K.din["conv_w"][j].rearrange("(r p) -> r p", p=128), 24, stage, T_stage, s0)
    pieces = [(s * 256, 256, True, True, 256) for s in range(4)]
    for i in range(8):
        nqv = 512 if i < 4 else (256 if i == 4 else 0)
        pieces.append((1024 + i * 512, 512, i == 0, i == 7, nqv))
    xr = Ring(K, "dx", [128, 514], F32, 3)
    yr = Ring(K, "dy", [128, 512], F32, 2)
    sqr = Ring(K, "dsq", [128, 512], F32, 2)
    rsr = Ring(K, "drs", [128, 512], F32, 2)
    ynr = Ring(K, "dyn", [128, 512], BF16, 3)
    tmr = Ring(K, "dtm", [128, 4, 128], BF16, 3)
    for fb in range(24):
        kind, h = fb // 8, fb % 8
        for (tok0, n0, le, re_, nq) in pieces:
            n = nq if kind == 0 else n0
            if n == 0:
                continue
            re2 = re_ and n == n0
            x, Tx, sx = xr.next()
            a = 1 if le else 0
            b = n + 1 if re2 else n + 2
            if le:
                P.op("pool", lambda e, x=x: e.memset(x[:, 0:1], 0.0), writes=[Tx])
            if re2:
                P.op("pool", lambda e, x=x, n=n: e.memset(x[:, n + 1:n + 2], 0.0), writes=[Tx])
            P.dma("sp", x[:, a:b], S["qkvT_d"][fb * 128:(fb + 1) * 128, tok0 - 1 + a:tok0 - 1 + b], sx, writes=[Tx])
            y, Ty, _ = yr.next()
            P.op("pool", lambda e, x=x, y=y, n=n, fb=fb: e.tensor_scalar_mul(y[:, 0:n], x[:, 0:n], cwF[:, 0, fb:fb + 1]), reads=[Tx, T_cw], writes=[Ty])
            P.op("dve", lambda e, x=x, y=y, n=n, fb=fb: e.scalar_tensor_tensor(y[:, 0:n], x[:, 1:n + 1], cwF[:, 1, fb:fb + 1], y[:, 0:n], ALU.mult, ALU.add), reads=[Tx, T_cw, Ty], writes=[Ty])
            P.op("dve", lambda e, x=x, y=y, n=n, fb=fb: e.scalar_tensor_tensor(y[:, 0:n], x[:, 2:n + 2], cwF[:, 2, fb:fb + 1], y[:, 0:n], ALU.mult, ALU.add), reads=[Tx, T_cw, Ty], writes=[Ty])
            P.op("act", lambda e, y=y, n=n: e.activation(out=y[:, 0:n], in_=y[:, 0:n], func=AF.Silu), reads=[Ty], writes=[Ty])
            yn, Tyn, syn = ynr.next()
            if kind < 2:
                sq, Tsq, _ = sqr.next()
                rs, Trs, _ = rsr.next()
                P.op("pool", lambda e, y=y, sq=sq, n=n: e.tensor_tensor(sq[:, 0:n], y[:, 0:n], y[:, 0:n], ALU.mult), reads=[Ty], writes=[Tsq])
                ps, Tp = K.ps(0)
                P.op("pe", lambda e, ps=ps, sq=sq, n=n: e.matmul(ps[:, 0:n], K.onesf[:], sq[:, 0:n], start=True, stop=True), reads=[Tsq, K.T_const], writes=[Tp])
                P.op("act", lambda e, ps=ps, rs=rs, n=n: e.activation(out=rs[:, 0:n], in_=ps[:, 0:n], func=AF.Sqrt, bias=EPS, scale=1.0), reads=[Tp], writes=[Trs])
                P.op("dve", lambda e, rs=rs, n=n: e.reciprocal(rs[:, 0:n], rs[:, 0:n]), reads=[Trs], writes=[Trs])
                cc = 128 ** -0.5 if kind == 0 else 1.0
                P.op("dve", lambda e, y=y, rs=rs, yn=yn, n=n, cc=cc: e.scalar_tensor_tensor(yn[:, 0:n], y[:, 0:n], cc, rs[:, 0:n], ALU.mult, ALU.mult), reads=[Ty, Trs], writes=[Tyn])
                dst = S["qnT_d"] if kind == 0 else S["knT_d"]
                P.dma("sp", dst[h * 128:(h + 1) * 128, tok0:tok0 + n], yn[:, 0:n], syn, reads=[Tyn])
            else:
                P.op("pool", lambda e, y=y, yn=yn, n=n: e.tensor_copy(yn[:, 0:n], y[:, 0:n]), reads=[Ty], writes=[Tyn])
            if kind >= 1:
                nb = n // 128
                ps, Tp = K.ps(0)
                psb = ps[:].bitcast(BF16)
                for j in range(nb):
                    P.op("pe", lambda e, j=j, psb=psb, yn=yn: e.transpose(psb[:, j * 128:(j + 1) * 128], yn[:, j * 128:(j + 1) * 128], K.identb[:]), reads=[Tyn, K.T_const], writes=[Tp])
                tm_, Ttm, stm = tmr.next()
                P.op("act", lambda e, psb=psb, tm_=tm_, nb=nb: e.copy(tm_[:, 0:nb, :], psb[:, 0:nb * 128].rearrange("p (j f) -> p j f", f=128)), reads=[Tp], writes=[Ttm])
                dst = S["ktm_d"] if kind == 1 else S["vtm_d"]
                P.dma("sp", dst[tok0:tok0 + n, h * 128:(h + 1) * 128].rearrange("(j p) f -> p j f", p=128), tm_[:, 0:nb, :], stm, reads=[Ttm])
    K.end()


def phase_dn_scan(K):
    nc, P = K.nc, K.P
    S = K.dscr
    K.scr("of_d", [TOKC, 1024], F32)
    catT = S["catT_d"]
    K.begin()
    H = 8
    tri = K.sb("tri", [128, 4, 128], F32)
    dtb = K.sb("dtb", [128, 16], F32)
    nea = K.sb("nea", [128, 16], F32)
    gon = K.sb("gon", [128, 128], F32)
    T_c = Tile("dnc")
    sc = K.getsem()
    P.dma("sp", tri[:], K.din["tri"].rearrange("m k c -> k m c"), sc, writes=[T_c])
    P.dma("sp", nea[:], K.din["alog_dt"][0:1, :].partition_broadcast(128), sc, writes=[T_c])
    P.dma("sp", dtb[:], K.din["alog_dt"][1:2, :].partition_broadcast(128), sc, writes=[T_c])
    P.dma("sp", gon[:], K.din["onorm"].partition_broadcast(128), sc, writes=[T_c])
    P.op("act", lambda e: e.activation(out=nea[:], in_=nea[:], func=AF.Exp), reads=[T_c], writes=[T_c])
    P.op("dve", lambda e: e.tensor_scalar_mul(nea[:], nea[:], -1.0), reads=[T_c], writes=[T_c])
    LM, UM, SLM, SUM = 0, 1, 2, 3
    St = K.sb("St", [128, H, 128], F32)
    Sb = K.sb("Sb", [128, H, 128], BF16)
    T_S, T_Sb = P.tiles(2, "S")
    T_of = {}
    big = lambda name, dt, n=2: Ring(K, name, [128, H, 128], dt, n)
    r_kT, r_qT, r_k, r_v = big("lkT", BF16), big("lqT", BF16), big("lk", BF16), big("lv", BF16)
    r_Rg, r_Rb = big("Rg", F32, 1), big("Rb", BF16, 1)
    r_diff, r_x1, r_e1, r_e2i, r_e2s = big("diff", F32, 1), big("x1", F32, 1), big("e1", F32, 1), big("e2i", F32, 1), big("e2s", F32, 1)
    r_kbT, r_egb, r_qd = big("kbT", BF16, 1), big("egb", BF16, 1), big("qd", BF16, 1)
    r_N, r_M, r_X, r_Y = big("N", F32, 2), big("M", F32, 2), big("X", F32, 2), big("Y", F32, 2)
    r_Xb = big("Xb", BF16, 1)
    r_qk, r_vb, r_kbg, r_kd = big("qk", BF16, 1), big("vb", BF16, 1), big("kbg", BF16, 1), big("kdc", BF16, 1)
    r_u, r_wT, r_vn, r_o = big("u", F32, 1), big("wT", BF16, 1), big("vn", BF16, 1), big("o", F32, 2)
    r_z, r_sq, r_on, r_obT = Ring(K, "z", [128, 1024], F32, 1), big("osq", F32, 1), big("on", BF16, 1), big("obT", BF16, 2)
    r_st = Ring(K, "ost", [128, 16], F32, 2)

    def v8(ap):
        return ap.rearrange("p (h f) -> p h f", h=H)

    def bc_h(ap2):
        return ap2.unsqueeze(2).to_broadcast([128, H, 128])

    def bc_m(m):
        return tri[:, m, :].unsqueeze(1).to_broadcast([128, H, 128])

    def mm8(lhs_fn, rhs_fn, reads, g, extra=None):
        b0, T0 = K.ps(g)
        b1, T1 = K.ps(g)
        banks = ((b0, T0), (b1, T1))
        for h in range(H):
            b, Tb = banks[h // 4]
            o_ = b[:, (h % 4) * 128:(h % 4 + 1) * 128]
            if extra is None:
                P.op("pe", lambda e, h=h, o_=o_: e.matmul(o_, lhs_fn(h), rhs_fn(h), start=True, stop=True), reads=reads, writes=[Tb])
            else:
                l2, r2, reads2 = extra
                P.op("pe", lambda e, h=h, o_=o_: e.matmul(o_, lhs_fn(h), rhs_fn(h), start=True, stop=False), reads=reads, writes=[Tb])
                P.op("pe", lambda e, h=h, o_=o_: e.matmul(o_, l2(h), r2(h), start=False, stop=True), reads=reads2, writes=[Tb])
        return banks

    def ev(eng, banks, fn, reads, writes):
        for i, (b, Tb) in enumerate(banks):
            bv = b[:].rearrange("p (h f) -> p h f", h=4)
            P.op(eng, lambda e, bv=bv, i=i: fn(e, bv, slice(4 * i, 4 * i + 4)), reads=[Tb] + reads, writes=writes)

    def gates(tokd0, nch):
        G = {}
        graw = K.sb("graw", [128, nch, 32], F32)
        beta = K.sb("beta", [128, nch, 16], F32)
        g = K.sb("gg", [128, nch, 16], F32)
        Tg = Tile("gates")
        sg = K.getsem()
        P.dma("sp", graw[:], S["gates_d"][tokd0:tokd0 + nch * 128, :].rearrange("(c p) g -> p c g", p=128), sg, writes=[Tg])
        P.op("act", lambda e: e.activation(out=beta[:], in_=graw[:, :, 0:16], func=AF.Sigmoid), reads=[Tg], writes=[Tg])
        P.op("dve", lambda e: e.tensor_tensor(g[:], graw[:, :, 16:32], dtb[:].unsqueeze(1).to_broadcast([128, nch, 16]), ALU.add), reads=[Tg, T_c], writes=[Tg])
        P.op("act", lambda e: e.activation(out=g[:], in_=g[:], func=AF.Exp), reads=[Tg], writes=[Tg])
        P.op("act", lambda e: e.activation(out=g[:], in_=g[:], func=AF.Ln, bias=1.0, scale=1.0), reads=[Tg], writes=[Tg])
        P.op("dve", lambda e: e.tensor_tensor(g[:], g[:], nea[:].unsqueeze(1).to_broadcast([128, nch, 16]), ALU.mult), reads=[Tg, T_c], writes=[Tg])
        G["beta"], G["T"] = beta, Tg
        for dr in range(2):
            gc = K.sb(f"gc{dr}", [128, nch, 8], F32)
            gl = K.sb(f"gl{dr}", [128, nch, 8], F32)
            eg = K.sb(f"eg{dr}", [128, nch, 8], F32)
            bg = K.sb(f"bg{dr}", [128, nch, 8], F32)
            kd = K.sb(f"kd{dr}", [128, nch, 8], F32)
            cd = K.sb(f"cd{dr}", [128, nch, 8], F32)
            tr = UM if dr == 0 else LM
            ps, Tp = K.ps(0)
            gsl = g[:, :, dr * 8:(dr + 1) * 8]
            P.op("pe", lambda e, ps=ps, tr=tr, gsl=gsl: e.matmul(ps[:, 0:nch * 8].rearrange("p (c h) -> p c h", h=8), tri[:, tr, :], gsl, start=True, stop=True), reads=[Tg, T_c], writes=[Tp])
            P.op("dve", lambda e, ps=ps, gc=gc: e.tensor_copy(gc[:].rearrange("p c h -> p (c h)"), ps[:, 0:nch * 8]), reads=[Tp], writes=[Tg])
            ps2, Tp2 = K.ps(0)
            P.op("pe", lambda e, ps2=ps2, gsl=gsl: e.matmul(ps2[:, 0:nch * 8].rearrange("p (c h) -> p c h", h=8), K.onesf[:], gsl, start=True, stop=True), reads=[Tg, K.T_const], writes=[Tp2])
            P.op("dve", lambda e, ps2=ps2, gl=gl: e.tensor_copy(gl[:].rearrange("p c h -> p (c h)"), ps2[:, 0:nch * 8]), reads=[Tp2], writes=[Tg])
            P.op("act", lambda e, eg=eg, gc=gc: e.activation(out=eg[:], in_=gc[:], func=AF.Exp), reads=[Tg], writes=[Tg])
            P.op("dve", lambda e, bg=bg, eg=eg, dr=dr: e.tensor_tensor(bg[:], eg[:], beta[:, :, dr * 8:(dr + 1) * 8], ALU.mult), reads=[Tg], writes=[Tg])
            P.op("dve", lambda e, kd=kd, gl=gl, gc=gc: e.tensor_tensor(kd[:], gl[:], gc[:], ALU.subtract), reads=[Tg], writes=[Tg])
            P.op("act", lambda e, kd=kd: e.activation(out=kd[:], in_=kd[:], func=AF.Exp), reads=[Tg], writes=[Tg])
            P.op("act", lambda e, cd=cd, gl=gl: e.activation(out=cd[:], in_=gl[:], func=AF.Exp), reads=[Tg], writes=[Tg])
            G[dr] = dict(gc=gc, eg=eg, bg=bg, kd=kd, cd=cd)
        return G

    def chunk(G, tokd0, ch, dr, full, final, tokc0):
        t0 = tokd0 + ch * 128
        Tg = G["T"]
        gd = G[dr]
        beta_c = G["beta"][:, ch, dr * 8:(dr + 1) * 8]
        gc_c = gd["gc"][:, ch, :]
        mL, mU, mSL, mSU = (LM, UM, SLM, SUM) if dr == 0 else (UM, LM, SUM, SLM)
        kT, TkT, s1 = r_kT.next()
        ktm, Tk, s2 = r_k.next()
        vtm, Tv, s3 = r_v.next()
        P.dma("sp", kT[:], S["knT_d"][:, t0:t0 + 128].rearrange("(h p) t -> p h t", p=128), s1, writes=[TkT])
        P.dma("sp", ktm[:].rearrange("p h f -> p (h f)"), S["ktm_d"][t0:t0 + 128, :], s2, writes=[Tk])
        P.dma("sp", vtm[:].rearrange("p h f -> p (h f)"), S["vtm_d"][t0:t0 + 128, :], s3, writes=[Tv])
        if full:
            qT, TqT, s4 = r_qT.next()
            P.dma("sp", qT[:], S["qnT_d"][:, t0:t0 + 128].rearrange("(h p) t -> p h t", p=128), s4, writes=[TqT])
        Rg, TRg, _ = r_Rg.next()
        Rb, TRb, _ = r_Rb.next()
        idb = K.identf[:].unsqueeze(1).to_broadcast([128, H, 128])
        P.op("dve", lambda e: e.tensor_tensor(Rg[:], bc_h(gc_c), idb, ALU.mult), reads=[Tg, K.T_const], writes=[TRg])
        P.op("pool", lambda e: e.tensor_tensor(Rb[:], bc_h(beta_c), idb, ALU.mult), reads=[Tg, K.T_const], writes=[TRb])
        gcb = mm8(lambda h: K.onesf[:], lambda h: Rg[:, h, :], [TRg, K.T_const], 0)
        btb = mm8(lambda h: K.onesb[:], lambda h: Rb[:, h, :], [TRb, K.T_const], 0)
        diff, Tdiff, _ = r_diff.next()
        ev("dve", gcb, lambda e, bv, hs: e.tensor_tensor(diff[:, hs, :], bc_h(gc_c)[:, hs, :], bv, ALU.subtract), [Tg], [Tdiff])
        kbT, TkbT, _ = r_kbT.next()
        ev("dve", btb, lambda e, bv, hs: e.tensor_tensor(kbT[:, hs, :], kT[:, hs, :], bv, ALU.mult), [TkT], [TkbT])
        if full:
            egb, Tegb, _ = r_egb.next()
            qd, Tqd, _ = r_qd.next()
            ev("act", gcb, lambda e, bv, hs: e.activation(out=egb[:, hs, :], in_=bv, func=AF.Exp), [], [Tegb])
            P.op("pool", lambda e: e.tensor_tensor(qd[:], qT[:], egb[:], ALU.mult), reads=[TqT, Tegb], writes=[Tqd])
        x1, Tx1, _ = r_x1.next()
        e1, Te1, _ = r_e1.next()
        e2i, Te2i, _ = r_e2i.next()
        e2s, Te2s, _ = r_e2s.next()
        P.op("dve", lambda e: e.tensor_tensor(x1[:], diff[:], bc_m(mL), ALU.mult), reads=[Tdiff, T_c], writes=[Tx1])
        P.op("act", lambda e: e.activation(out=e1[:], in_=x1[:], func=AF.Exp), reads=[Tx1], writes=[Te1])
        P.op("pool", lambda e: e.tensor_tensor(e1[:], e1[:], bc_m(mSL), ALU.mult), reads=[Te1, T_c], writes=[Te1])
        P.op("dve", lambda e: e.tensor_tensor(x1[:], diff[:], bc_m(mU), ALU.mult), reads=[Tdiff, T_c, Te1], writes=[Tx1])
        P.op("act", lambda e: e.activation(out=e2i[:], in_=x1[:], func=AF.Exp, scale=-1.0), reads=[Tx1], writes=[Te2i])
        P.op("pool", lambda e: e.tensor_tensor(e2s[:], e2i[:], bc_m(mSU), ALU.mult), reads=[Te2i, T_c], writes=[Te2s])
        P.op("pool", lambda e: e.tensor_tensor(e2i[:], e2i[:], bc_m(mU), ALU.mult), reads=[Te2i, T_c, Te2s], writes=[Te2i])
        a1 = mm8(lambda h: kbT[:, h, :], lambda h: kT[:, h, :], [TkbT, TkT], 0)
        a2 = mm8(lambda h: kT[:, h, :], lambda h: kbT[:, h, :], [TkbT, TkT], 1)
        Mj, TM, _ = r_M.next()
        Nj, TN, _ = r_N.next()
        ev("dve", a1, lambda e, bv, hs, Mj=Mj: e.tensor_tensor(Mj[:, hs, :], bv, e1[:, hs, :], ALU.mult), [Te1], [TM])
        ev("dve", a2, lambda e, bv, hs, Nj=Nj: e.tensor_tensor(Nj[:, hs, :], bv, e2s[:, hs, :], ALU.mult), [Te2s], [TN])
        if full:
            a3 = mm8(lambda h: kT[:, h, :], lambda h: qT[:, h, :], [TkT, TqT], 0)
            qk, Tqk, _ = r_qk.next()
            ev("dve", a3, lambda e, bv, hs: e.tensor_tensor(qk[:, hs, :], bv, e2i[:, hs, :], ALU.mult), [Te2i], [Tqk])
        X, TX, _ = r_X.next()
        Y, TY, _ = r_Y.next()
        idbb = K.identf[:].unsqueeze(1).to_broadcast([128, H, 128])
        P.op("pool", lambda e, X=X, Nj=Nj: e.tensor_tensor(X[:], idbb, Nj[:], ALU.subtract), reads=[TN, K.T_const], writes=[TX])
        P.op("pool", lambda e, Y=Y, Mj=Mj: e.tensor_tensor(Y[:], idbb, Mj[:], ALU.subtract), reads=[TM, K.T_const], writes=[TY])
        for j in range(1, 7):
            last = j == 6
            Nn, TNn, _ = r_N.next()
            pn = mm8(lambda h, Mj=Mj: Mj[:, h, :], lambda h, Nj=Nj: Nj[:, h, :], [TM, TN], 0)
            ev("act", pn, lambda e, bv, hs, Nn=Nn: e.copy(Nn[:, hs, :], bv), [], [TNn])
            if not last:
                Mn, TMn, _ = r_M.next()
                pm = mm8(lambda h, Nj=Nj: Nj[:, h, :], lambda h, Mj=Mj: Mj[:, h, :], [TM, TN], 0)
                ev("act", pm, lambda e, bv, hs, Mn=Mn: e.copy(Mn[:, hs, :], bv), [], [TMn])
            Xn, TXn, _ = r_X.next()
            px = mm8(lambda h, Y=Y: Y[:, h, :], lambda h, Nn=Nn: Nn[:, h, :], [TY, TNn], 1)
            ev("dve", px, lambda e, bv, hs, Xn=Xn, X=X: e.tensor_tensor(Xn[:, hs, :], bv, X[:, hs, :], ALU.add), [TX], [TXn])
            if not last:
                Yn, TYn, _ = r_Y.next()
                py = mm8(lambda h, X=X: X[:, h, :], lambda h, Mn=Mn: Mn[:, h, :], [TX, TMn], 1)
                ev("dve", py, lambda e, bv, hs, Yn=Yn, Y=Y: e.tensor_tensor(Yn[:, hs, :], bv, Y[:, hs, :], ALU.add), [TY], [TYn])
                Mj, TM, Y, TY = Mn, TMn, Yn, TYn
            Nj, TN, X, TX = Nn, TNn, Xn, TXn
        K.dump("dbg_diff", diff[:], [Tdiff]); K.dump("dbg_e1", e1[:], [Te1]); K.dump("dbg_e2i", e2i[:], [Te2i])
        K.dump("dbg_kbT", kbT[:], [TkbT]); K.dump("dbg_X", X[:], [TX]); K.dump("dbg_gc", gd["gc"][:], [Tg]); K.dump("dbg_beta", G["beta"][:], [Tg])
        K.dump("dbg_kd", gd["kd"][:], [Tg]); K.dump("dbg_cd", gd["cd"][:], [Tg])
        vb, Tvb, _ = r_vb.next()
        kbg, Tkbg, _ = r_kbg.next()
        kdc, Tkdc, _ = r_kd.next()
        P.op("pool", lambda e: e.tensor_tensor(vb[:], vtm[:], bc_h(beta_c), ALU.mult), reads=[Tv, Tg], writes=[Tvb])
        P.op("pool", lambda e: e.tensor_tensor(kbg[:], ktm[:], bc_h(gd["bg"][:, ch, :]), ALU.mult), reads=[Tk, Tg], writes=[Tkbg])
        P.op("pool", lambda e: e.tensor_tensor(kdc[:], ktm[:], bc_h(gd["kd"][:, ch, :]), ALU.mult), reads=[Tk, Tg], writes=[Tkdc])
        Xb, TXb, _ = r_Xb.next()
        P.op("act", lambda e, X=X: e.copy(Xb[:], X[:]), reads=[TX], writes=[TXb])
        pu = mm8(lambda h: Xb[:, h, :], lambda h: vb[:, h, :], [TXb, Tvb], 0)
        u, Tu, _ = r_u.next()
        ev("act", pu, lambda e, bv, hs: e.copy(u[:, hs, :], bv), [], [Tu])
        pw = mm8(lambda h: kbg[:, h, :], lambda h: Xb[:, h, :], [TXb, Tkbg], 0)
        wT, TwT, _ = r_wT.next()
        ev("act", pw, lambda e, bv, hs: e.copy(wT[:, hs, :], bv), [], [TwT])
        pws = mm8(lambda h: wT[:, h, :], lambda h: Sb[:, h, :], [TwT, T_Sb], 1)
        vn, Tvn, _ = r_vn.next()
        ev("dve", pws, lambda e, bv, hs: e.tensor_tensor(vn[:, hs, :], u[:, hs, :], bv, ALU.subtract), [Tu], [Tvn])
        if full:
            po = mm8(lambda h: qd[:, h, :], lambda h: Sb[:, h, :], [Tqd, T_Sb], 1,
                     extra=(lambda h: qk[:, h, :], lambda h: vn[:, h, :], [Tqk, Tvn]))
            o, To, so = r_o.next()
            if not final:
                ev("act", po, lambda e, bv, hs: e.copy(o[:, hs, :], bv), [], [To])
                P.dma("act", S["of_d"][tokc0 + ch * 128:tokc0 + (ch + 1) * 128, :], o[:].rearrange("p h f -> p (h f)"), so, reads=[To], writes=[T_of.setdefault(tokc0 + ch * 128, Tile("of"))])
            else:
                P.dma("sp", o[:].rearrange("p h f -> p (h f)"), S["of_d"][tokc0 + ch * 128:tokc0 + (ch + 1) * 128, :], so, reads=[T_of[tokc0 + ch * 128]], writes=[To])
                ev("dve", po, lambda e, bv, hs: e.tensor_tensor(o[:, hs, :], o[:, hs, :], bv, ALU.add), [To], [To])
        K.dump("dbg_u", u[:], [Tu]); K.dump("dbg_wT", wT[:], [TwT]); K.dump("dbg_vn", vn[:], [Tvn])
        if full:
            K.dump("dbg_o", o[:], [To]); K.dump("dbg_qk", qk[:], [Tqk])
        pds = mm8(lambda h: kdc[:, h, :], lambda h: vn[:, h, :], [Tkdc, Tvn], 1)
        for h in range(H):
            b, Tb = pds[h // 4]
            P.op("dve", lambda e, h=h, b=b: e.scalar_tensor_tensor(St[:, h, :], St[:, h, :], gd["cd"][:, ch, h:h + 1], b[:, (h % 4) * 128:(h % 4 + 1) * 128], ALU.mult, ALU.add),
                 reads=[Tb, Tg, T_S], writes=[T_S])
        P.op("act", lambda e: e.copy(Sb[:], St[:]), reads=[T_S], writes=[T_Sb])
        if full and final:
            z, Tz, sz = r_z.next()
            P.dma("sp", z[:], S["z_d"][tokc0 + ch * 128:tokc0 + (ch + 1) * 128, :], sz, writes=[Tz])
            sq, Tsq, _ = r_sq.next()
            st, Tst, _ = r_st.next()
            on, Ton, _ = r_on.next()
            P.op("pool", lambda e: e.tensor_tensor(sq[:], o[:], o[:], ALU.mult), reads=[To], writes=[Tsq])
            P.op("dve", lambda e: e.tensor_reduce(out=st[:, 0:8], in_=sq[:], axis=AX.X, op=ALU.add), reads=[Tsq], writes=[Tst])
            P.op("act", lambda e: e.activation(out=st[:, 8:16], in_=st[:, 0:8], func=AF.Sqrt, bias=EPS, scale=1.0 / 128), reads=[Tst], writes=[Tst])
            P.op("dve", lambda e: e.reciprocal(st[:, 8:16], st[:, 8:16]), reads=[Tst], writes=[Tst])
            P.op("act", lambda e: e.activation(out=z[:], in_=z[:], func=AF.Silu), reads=[Tz], writes=[Tz])
            P.op("dve", lambda e: e.tensor_tensor(sq[:], o[:], bc_h(st[:, 8:16]), ALU.mult), reads=[To, Tst, Tsq], writes=[Tsq])
            P.op("pool", lambda e: e.tensor_tensor(sq[:], sq[:], gon[:].unsqueeze(1).to_broadcast([128, H, 128]), ALU.mult), reads=[Tsq, T_c], writes=[Tsq])
            P.op("dve", lambda e: e.tensor_tensor(on[:], sq[:], v8(z[:]), ALU.mult), reads=[Tsq, Tz], writes=[Ton])
            ps, Tp = K.ps(0)
            psb = ps[:].bitcast(BF16)
            for h in range(H):
                P.op("pe", lambda e, h=h, psb=psb: e.transpose(psb[:, h * 128:(h + 1) * 128], on[:, h, :], K.identb[:]), reads=[Ton, K.T_const], writes=[Tp])
            obT, TobT, sob = r_obT.next()
            P.op("act", lambda e, psb=psb: e.copy(obT[:].rearrange("p h f -> p (h f)"), psb), reads=[Tp], writes=[TobT])
            P.dma("act", catT[1024:2048, tokc0 + ch * 128:tokc0 + (ch + 1) * 128].rearrange("(h p) t -> p h t", p=128), obT[:], sob, reads=[TobT])

    def set_state(src):
        ss_ = K.getsem()
        if src is None:
            P.op("pool", lambda e: e.memset(St[:], 0.0), reads=[T_Sb], writes=[T_S])
        else:
            P.dma("sp", St[:], src.rearrange("h k v -> k h v"), ss_, reads=[T_Sb], writes=[T_S])
        P.op("act", lambda e: e.copy(Sb[:], St[:]), reads=[T_S], writes=[T_Sb])

    def save_state(dst):
        ss_ = K.getsem()
        P.dma("sp", dst.rearrange("h k v -> k h v"), St[:], ss_, reads=[T_S])

    import os
    which = os.environ.get("DN_SEQS", "ps")
    if "p" in which:
      for s in range(4):
        G = gates(s * 256, 2)
        set_state(None)
        for ch in (0, 1):
            chunk(G, s * 256, ch, 0, True, False, s * 256)
        save_state(K.dout["nbf"][s])
        set_state(None)
        for ch in (1, 0):
            chunk(G, s * 256, ch, 1, True, True, s * 256)
        save_state(K.dout["nbb"][s])
    if "s" in which:
        G = gates(1024, 32)
        set_state(K.din["s0"][0])
        for ch in range(SEXT):
            chunk(G, 1024, ch, 0, True, False, 1024)
        set_state(K.din["s0"][1])
        for ch in range(31, -1, -1):
            chunk(G, 1024, ch, 1, ch < SEXT, True, 1024)
    K.end()


def phase_mlp(K, l):
    nc, P = K.nc, K.P
    S = K.dscr
    ntb = NCAT if l == 0 else NPT + 16
    groups = token_groups(ntb, breaks=(NPT,))
    if l == 0:
        x1_d = K.scr("x1_d", [TOKC, D], F32)
        oT_d, Wo, W1, W2 = S["catT_d"], K.din["ab_w_out"], K.din["w_mlp_in"][0], K.din["w_mlp_out"][0]
        xsrc = lambda g: K.din["xp"][g * 128:(g + 1) * 128, :] if g < NPT else K.din["xs"][(g - NPT) * 128:(g - NPT + 1) * 128, :]
    else:
        oT_d, Wo, W1, W2 = S["o1T_d"], K.din["c_w_out"], K.din["w_mlp_in"][1], K.din["w_mlp_out"][1]
        xsrc = lambda g: S["x1_d"][g * 128:(g + 1) * 128, :]
    K.begin()
    actT = K.sb("actT", [128, KC, 512], BF16)
    T_act = [[Tile() for kc in range(KC)] for i in range(4)]
    xres = K.sb("xres", [128, 4, D], F32)
    T_x = [Tile() for i in range(4)]
    uT = K.sb("uT", [128, 64, 512], BF16)
    T_u = [Tile() for fc in range(64)]
    gate = [K.sb(f"gate{i}", [128, D], F32) for i in range(2)]
    T_gate = Tile("gate")
    wr = Ring(K, "wm", [128, KC, 512], BF16, 3, sw=True)
    tr = Ring(K, "tg", [128, 512], F32, 1)
    rr = Ring(K, "rl", [128, 512], F32, 2)
    nt = NormT(K)
    sx = [K.getsem() for i in range(4)]
    sg, so = K.getsem(), K.getsem()
    if l == 1:
        fng = K.sb("fng", [128, D], F32)
        fss = Ring(K, "fss", [128, 2], F32, 2)
        T_fng = Tile("fng")
        P.dma("sp", fng[:], K.din["final_norm"].partition_broadcast(128), sg, writes=[T_fng])
    cur_c = None
    for (t0, m) in groups:
        n = m * 128
        c = 0 if t0 < NPT else 1
        if c != cur_c:
            P.dma("sp", gate[0][:], S["modrow_d"][l, c:c + 1, 2 * D:3 * D].partition_broadcast(128), sg, writes=[T_gate])
            P.dma("sp", gate[1][:], S["modrow_d"][l, c:c + 1, 5 * D:6 * D].partition_broadcast(128), sg, writes=[T_gate])
            cur_c = c
        P.dma("sp", actT[:, :, 0:n], oT_d[:, t0 * 128:t0 * 128 + n].rearrange("(fc p) t -> p fc t", p=128), so,
              writes=[T_act[i][kc] for i in range(m) for kc in range(KC)])
        for i in range(m):
            P.dma("sp", xres[:, i, :], xsrc(t0 + i), sx[i], writes=[T_x[i]])

        def second(Wsrc, nfq, lhs_fn, lhs_tiles_fn, gi):
            for dg in range(4):
                banks = [K.ps(1 - dg % 2) for i in range(m)]
                for fq in range(nfq):
                    wt, Tw, sw = wr.next()
                    P.dma("pool", wt[:], Wsrc[fq * 2048:(fq + 1) * 2048, dg * 512:(dg + 1) * 512].rearrange("(kc p) n -> p kc n", p=128), sw, writes=[Tw])
                    for i in range(m):
                        b, Tb = banks[i]
                        for kc in range(KC):
                            fc = fq * KC + kc
                            P.op("pe", lambda e, b=b, i=i, fc=fc, kc=kc, wt=wt: e.matmul(b[:, :], lhs_fn(fc, i), wt[:, kc, :], start=(fc == 0), stop=(fc == nfq * KC - 1)),
                                 reads=[Tw] + lhs_tiles_fn(fc, i), writes=[Tb])
                for i in range(m):
                    b, Tb = banks[i]
                    tt, Tt, _ = tr.next()
                    P.op("dve", lambda e, b=b, tt=tt, dg=dg: e.tensor_tensor(tt[:], b[:, :], gate[gi][:, dg * 512:(dg + 1) * 512], ALU.mult), reads=[Tb, T_gate], writes=[Tt])
                    P.op("dve", lambda e, i=i, tt=tt, dg=dg: e.tensor_tensor(xres[:, i, dg * 512:(dg + 1) * 512], xres[:, i, dg * 512:(dg + 1) * 512], tt[:], ALU.add), reads=[Tt, T_x[i]], writes=[T_x[i]])

        second(Wo, 1, lambda fc, i: actT[:, fc, i * 128:(i + 1) * 128], lambda fc, i: [T_act[i][fc]], 0)
        for i in range(m):
            nt.run(xres[:, i, :], T_x[i], K.gsF[l][1][c], K.modF[l][c][:, 3 * KC:4 * KC],
                   lambda kc, i=i: actT[:, kc, i * 128:(i + 1) * 128], lambda kc, i=i: T_act[i][kc])
        for fg in range(16):
            wt, Tw, sw = wr.next()
            P.dma("pool", wt[:], W1[:, fg * 512:(fg + 1) * 512].rearrange("(kc p) n -> p kc n", p=128), sw, writes=[Tw])
            for sub in range(4):
                fc = fg * 4 + sub
                ps, Tp = K.ps(0)
                for kc in range(KC):
                    P.op("pe", lambda e, ps=ps, kc=kc, sub=sub, wt=wt, n=n: e.matmul(ps[:, 0:n], wt[:, kc, sub * 128:(sub + 1) * 128], actT[:, kc, 0:n], start=(kc == 0), stop=(kc == KC - 1)),
                         reads=[Tw] + [T_act[i][kc] for i in range(m)], writes=[Tp])
                r, Tr, _ = rr.next()
                P.op("act", lambda e, ps=ps, r=r, n=n: e.activation(out=r[:, 0:n], in_=ps[:, 0:n], func=AF.Relu), reads=[Tp], writes=[Tr])
                P.op("dve", lambda e, r=r, fc=fc, n=n: e.tensor_tensor(uT[:, fc, 0:n], r[:, 0:n], r[:, 0:n], ALU.mult), reads=[Tr], writes=[T_u[fc]])
        second(W2, 4, lambda fc, i: uT[:, fc, i * 128:(i + 1) * 128], lambda fc, i: [T_u[fc]], 1)
        for i in range(m):
            g = t0 + i
            if l == 0:
                P.dma("sp", x1_d[g * 128:(g + 1) * 128, :], xres[:, i, :], sx[i], reads=[T_x[i]])
            else:
                ss, Tss, _ = fss.next()
                fjk, T_fjk, _ = nt.xn.next()
                P.op("act", lambda e, i=i, ss=ss, fjk=fjk: e.activation(out=fjk[:], in_=xres[:, i, :], func=AF.Square, accum_out=ss[:, 0:1]), reads=[T_x[i]], writes=[T_fjk, Tss])
                P.op("act", lambda e, ss=ss: e.activation(out=ss[:, 1:2], in_=ss[:, 0:1], func=AF.Sqrt, bias=EPS, scale=1.0 / D), reads=[Tss], writes=[Tss])
                P.op("dve", lambda e, ss=ss: e.reciprocal(ss[:, 1:2], ss[:, 1:2]), reads=[Tss], writes=[Tss])
                P.op("dve", lambda e, i=i, ss=ss: e.scalar_tensor_tensor(xres[:, i, :], xres[:, i, :], ss[:, 1:2], fng[:], ALU.mult, ALU.mult), reads=[T_x[i], Tss, T_fng], writes=[T_x[i]])
                dst = K.dout["y_p"][g * 128:(g + 1) * 128, :] if g < NPT else K.dout["y_s"][(g - NPT) * 128:(g - NPT + 1) * 128, :]
                P.dma("sp", dst, xres[:, i, :], sx[i], reads=[T_x[i]])
    K.end()


def phase_l1_inproj(K):
    nc, P = K.nc, K.P
    S = K.dscr
    q1T = K.scr("q1T_d", [D, TOKC], BF16)
    k1T = K.scr("k1T_d", [256, TOKC], BF16)
    v1 = K.scr("v1_d", [TOKC, 256], BF16)
    K.begin()
    ntb = NCAT
    groups = token_groups(ntb, breaks=(NPT,))
    hT = K.sb("h1T", [128, KC, ntb * 128], BF16)
    T_h = [[Tile() for kc in range(KC)] for g in range(ntb)]
    xr = Ring(K, "x1in", [128, D], F32, 2)
    nt = NormT(K)
    for g in range(ntb):
        c = 0 if g < NPT else 1
        xt, T_x, sx = xr.next()
        P.dma("sp", xt[:], S["x1_d"][g * 128:(g + 1) * 128, :], sx, writes=[T_x])
        nt.run(xt[:], T_x, K.gsF[1][0][c], K.modF[1][c][:, 0:KC],
               lambda kc, g=g: hT[:, kc, g * 128:(g + 1) * 128], lambda kc, g=g: T_h[g][kc])
    cosT = K.sb("cosT", [128, SEXT * 128], F32)
    sinT = K.sb("sinT", [128, SEXT * 128], F32)
    perm = K.sb("perm", [128, 128], F32)
    T_rc = Tile("ropec")
    sr = K.getsem()
    P.dma("sp", cosT[:], K.din["rope_cos"], sr, writes=[T_rc])
    P.dma("sp", sinT[:], K.din["rope_sin"], sr, writes=[T_rc])
    P.dma("sp", perm[:], K.din["rope_perm"], sr, writes=[T_rc])
    wr = Ring(K, "w1q", [128, KC, 512], BF16, 2, sw=True)
    q32r = Ring(K, "q32", [128, 512], F32, 2)
    t1r = Ring(K, "rt1", [128, 512], F32, 2)
    st16 = Ring(K, "s16", [128, 512], BF16, 3)
    st32 = Ring(K, "s32", [128, 512], F32, 2)
    W = K.din["c_w_qkv"]
    for t in range(5):
        wt, Tw, sw = wr.next()
        P.dma("pool", wt[:], W[:, t * 512:(t + 1) * 512].rearrange("(kc p) n -> p kc n", p=128), sw, writes=[Tw])
        nsub = 4 if t < 4 else 2
        for sub in range(nsub):
            dst, row0 = (q1T, t * 512 + sub * 128) if t < 4 else (k1T, sub * 128)
            for (t0, m) in groups:
                n = m * 128
                ps, Tp = K.ps(0)
                for kc in range(KC):
                    P.op("pe", lambda e, ps=ps, kc=kc, sub=sub, wt=wt, t0=t0, n=n: e.matmul(ps[:, 0:n], wt[:, kc, sub * 128:(sub + 1) * 128], hT[:, kc, t0 * 128:t0 * 128 + n], start=(kc == 0), stop=(kc == KC - 1)),
                         reads=[Tw] + [T_h[t0 + i][kc] for i in range(m)], writes=[Tp])
                sg, Ts, ss_ = st16.next()
                if t0 < NPT:
                    P.op("act", lambda e, ps=ps, sg=sg, n=n: e.copy(sg[:, 0:n], ps[:, 0:n]), reads=[Tp], writes=[Ts])
                else:
                    s0 = (t0 - NPT) * 128
                    q32, Tq, _ = q32r.next()
                    t1, Tt1, _ = t1r.next()
                    P.op("act", lambda e, ps=ps, q32=q32, n=n: e.copy(q32[:, 0:n], ps[:, 0:n]), reads=[Tp], writes=[Tq])
                    ps2, Tp2 = K.ps(1)
                    P.op("pe", lambda e, ps2=ps2, q32=q32, n=n: e.matmul(ps2[:, 0:n], perm[:], q32[:, 0:n], start=True, stop=True), reads=[Tq, T_rc], writes=[Tp2])
                    P.op("pool", lambda e, q32=q32, t1=t1, n=n, s0=s0: e.tensor_tensor(t1[:, 0:n], q32[:, 0:n], cosT[:, s0:s0 + n], ALU.mult), reads=[Tq, T_rc], writes=[Tt1])
                    P.op("dve", lambda e, ps2=ps2, q32=q32, n=n, s0=s0: e.tensor_tensor(q32[:, 0:n], ps2[:, 0:n], sinT[:, s0:s0 + n], ALU.mult), reads=[Tp2, T_rc, Tt1], writes=[Tq])
                    P.op("dve", lambda e, q32=q32, t1=t1, sg=sg, n=n: e.tensor_tensor(sg[:, 0:n], q32[:, 0:n], t1[:, 0:n], ALU.add), reads=[Tq, Tt1], writes=[Ts])
                P.dma("sp", dst[row0:row0 + 128, t0 * 128:t0 * 128 + n], sg[:, 0:n], ss_, reads=[Ts])
        if t == 4:
            for g in range(ntb):
                ps, Tp = K.ps(0)
                for kc in range(KC):
                    P.op("pe", lambda e, ps=ps, kc=kc, g=g, wt=wt: e.matmul(ps[:, :], hT[:, kc, g * 128:(g + 1) * 128], wt[:, kc, :], start=(kc == 0), stop=(kc == KC - 1)),
                         reads=[Tw, T_h[g][kc]], writes=[Tp])
                sg, Ts, ss_ = st16.next()
                P.op("act", lambda e, ps=ps, sg=sg: e.copy(sg[:, 0:256], ps[:, 256:512]), reads=[Tp], writes=[Ts])
                P.dma("sp", v1[g * 128:(g + 1) * 128, :], sg[:, 0:256], ss_, reads=[Ts])
                if g < NPT:
                    s32, Ts32, ss32 = st32.next()
                    P.op("dve", lambda e, ps=ps, s32=s32: e.tensor_copy(s32[:], ps[:, :]), reads=[Tp], writes=[Ts32])
                    P.dma("sp", K.dout["nck"][g * 128:(g + 1) * 128, :], s32[:, 0:256], ss32, reads=[Ts32])
                    P.dma("sp", K.dout["ncv"][g * 128:(g + 1) * 128, :], s32[:, 256:512], ss32, reads=[Ts32])
    K.end()


def phase_attn_c(K):
    nc, P = K.nc, K.P
    S = K.dscr
    o1T = K.scr("o1T_d", [D, TOKC], BF16)
    scale = 64 ** -0.5
    K.begin()
    pt_ring = Ring(K, "pt", [128, 512], BF16, 4)
    rec_ring = Ring(K, "rec", [128, 512], F32, 2)
    pools = (pt_ring, rec_ring)
    snk = K.sb("snk", [128, 32], F32)
    trib = K.sb("trib", [128, 2, 128], BF16)
    ck_tm = K.sb("cck", [128, 2, 256], BF16)
    cvt = K.sb("ccv", [128, 2, 256], BF16)
    ckT = K.sb("cckT", [64, 4, 256], BF16)
    T_c, T_ck, T_cv, T_ckT = P.tiles(4, "ac")
    s0, sw0 = K.getsem(), K.getsem(True)
    P.dma("sp", snk[:], K.din["c_sink"].partition_broadcast(128), s0, writes=[T_c])
    P.op("act", lambda e: e.activation(out=snk[:], in_=snk[:], func=AF.Exp), reads=[T_c], writes=[T_c])
    P.dma("pool", trib[:], K.din["tri"][0:2].rearrange("m k c -> k m c"), sw0, writes=[T_c])
    P.dma("pool", ck_tm[:], K.din["cache_c_k"].rearrange("(c p) f -> p c f", p=128), sw0, writes=[T_ck])
    P.dma("pool", cvt[:], K.din["cache_c_v"].rearrange("(c p) f -> p c f", p=128), sw0, writes=[T_cv])
    for c in range(2):
        ps, Tp = K.ps(0)
        psb = ps[:].bitcast(BF16)
        for kh in range(4):
            P.op("pe", lambda e, c=c, kh=kh, psb=psb: e.transpose(psb[0:64, kh * 128:(kh + 1) * 128], ck_tm[:, c, kh * 64:(kh + 1) * 64], K.identb[:]), reads=[T_ck, K.T_const], writes=[Tp])
        P.op("dve", lambda e, c=c, psb=psb: e.tensor_copy(ckT[:, :, c * 128:(c + 1) * 128], psb[0:64, 0:512].rearrange("p (h k) -> p h k", h=4)), reads=[Tp], writes=[T_ckT])
    qr = Ring(K, "pq", [64, 8, 256], BF16, 2)
    kr = Ring(K, "pk", [64, 256], BF16, 2)
    vr = Ring(K, "pv", [128, 2, 64], BF16, 2)
    orr = Ring(K, "po", [64, 8, 256], BF16, 2)
    for kh in range(4):
        for s in range(4):
            qt, Tq, sq = qr.next()
            kt, Tk, sk = kr.next()
            vt, Tv, sv = vr.next()
            ot, To, so = orr.next()
            P.dma("sp", qt[:], S["q1T_d"][kh * 512:(kh + 1) * 512, s * 256:(s + 1) * 256].rearrange("(g d) t -> d g t", d=64), sq, writes=[Tq])
            P.dma("sp", kt[:], S["k1T_d"][kh * 64:(kh + 1) * 64, s * 256:(s + 1) * 256], sk, writes=[Tk])
            P.dma("sp", vt[:], S["v1_d"][s * 256:(s + 1) * 256, kh * 64:(kh + 1) * 64].rearrange("(c p) f -> p c f", p=128), sv, writes=[Tv])
            for g in range(8):
                sl = [(kt[:, c * 128:(c + 1) * 128], [Tk], vt[:, c, :], [Tv], None, [], 0) for c in range(2)]
                hq = kh * 8 + g
                attn_core(K, sl, 256, (qt[:, g, :], [Tq]), ot[:, g, :], To, scale, extra_den=(snk[0:64, hq:hq + 1], [T_c]), pools=pools)
            P.dma("sp", o1T[kh * 512:(kh + 1) * 512, s * 256:(s + 1) * 256].rearrange("(g d) t -> d g t", d=64), ot[:], so, reads=[To])
    NT = SEXT * 128
    kh_k = Ring(K, "sk", [64, NT], BF16, 2)
    kh_v = Ring(K, "sv", [128, SEXT, 64], BF16, 2)
    kh_q = Ring(K, "sq", [64, 8, 2048], BF16, 1)
    kh_o = Ring(K, "so", [64, 8, 2048], BF16, 1)
    for kh in range(4):
        kt, Tk, sk = kh_k.next()
        vt, Tv, sv = kh_v.next()
        qt, Tq, sq = kh_q.next()
        ot, To, so = kh_o.next()
        P.dma("sp", kt[:], S["k1T_d"][kh * 64:(kh + 1) * 64, 1024:1024 + NT], sk, writes=[Tk])
        P.dma("sp", vt[:], S["v1_d"][1024:1024 + NT, kh * 64:(kh + 1) * 64].rearrange("(c p) f -> p c f", p=128), sv, writes=[Tv])
        P.dma("sp", qt[:], S["q1T_d"][kh * 512:(kh + 1) * 512, 1024:1024 + 2048].rearrange("(g d) t -> d g t", d=64), sq, writes=[Tq])
        for i in range(16):
            for gh in range(2):
                sl = []
                for j in (i - 1, i, i + 1):
                    if j < 0 or j > 16:
                        continue
                    mask = None
                    if j == i - 1:
                        mask = trib[:, 0, :].unsqueeze(1).to_broadcast([128, 4, 128])
                    elif j == i + 1:
                        mask = trib[:, 1, :].unsqueeze(1).to_broadcast([128, 4, 128])
                    sl.append((kt[:, j * 128:(j + 1) * 128], [Tk], vt[:, j, :], [Tv], mask, [T_c], 0))
                for c in range(2):
                    sl.append((ckT[:, kh, c * 128:(c + 1) * 128], [T_ckT], cvt[:, c, kh * 64:(kh + 1) * 64], [T_cv], None, [], 0))
                hq0 = kh * 8 + gh * 4
                ed = snk[0:64, hq0:hq0 + 4].unsqueeze(2).to_broadcast([64, 4, 128])
                attn_core(K, sl, 512, (qt[:, gh * 4:gh * 4 + 4, i * 128:(i + 1) * 128], [Tq]), ot[:, gh * 4:gh * 4 + 4, i * 128:(i + 1) * 128], To, scale,
                          extra_den=(ed, [T_c]), pools=pools, g4=True)
        P.dma("sp", o1T[kh * 512:(kh + 1) * 512, 1024:1024 + 2048].rearrange("(g d) t -> d g t", d=64), ot[:], so, reads=[To])
    K.end()


def phase_attn_a_and_prep(K):
    K.begin()
    phase_attn_a(K)
    phase_dn_prep(K)
    K.end()


def declare_io(K):
    K.inp("ident", [128, 128])
    K.inp("cond", [2, D])
    K.inp("xp", [NPT * 128, D])
    K.inp("xs", [4096, D])
    K.inp("w_ada", [2, D, 6 * D])
    K.inp("b_ada", [2, 6 * D])
    K.inp("norm_mix", [2, D])
    K.inp("norm_mlp", [2, D])
    K.inp("ab_w_in", [D, 7200])
    K.inp("w_gates", [D, 32])
    K.inp("cache_a_k", [256, 1024])
    K.inp("cache_a_v", [256, 1024])
    K.inp("na_mask", [2, 8, 6, 128, 256])
    K.inp("conv_w", [3, 3072])
    K.inp("alog_dt", [2, 16])
    K.inp("tri", [4, 128, 128])
    K.inp("s0", [2, 8, 128, 128])
    K.inp("onorm", [1, 128])
    K.inp("ab_w_out", [D, D])
    K.inp("w_mlp_in", [2, D, 4 * D])
    K.inp("w_mlp_out", [2, 4 * D, D])
    K.inp("c_w_qkv", [D, 2560])
    K.inp("c_w_out", [D, D])
    K.inp("cache_c_k", [256, 256])
    K.inp("cache_c_v", [256, 256])
    K.inp("c_sink", [1, 32])
    K.inp("final_norm", [1, D])
    K.inp("rope_cos", [128, SEXT * 128])
    K.inp("rope_sin", [128, SEXT * 128])
    K.inp("rope_perm", [128, 128])
    K.outp("nck", [NPT * 128, 256])
    K.outp("ncv", [NPT * 128, 256])
    K.outp("y_p", [NPT * 128, D])
    K.outp("y_s", [2048, D])
    K.outp("nbf", [4, 8, 128, 128])
    K.outp("nbb", [4, 8, 128, 128])
    K.outp("nak", [NPT * 128, 1024])
    K.outp("nav", [NPT * 128, 1024])


def build(stop=99, debug=()):
    nc = bass.Bass("TRN2", target_bir_lowering=False)
    K = Ctx(nc)
    declare_io(K)
    phases = [phase_consts, phase_ada, lambda K: phase_l0_inproj(K, 1), lambda K: phase_l0_inproj(K, 2), phase_attn_a_and_prep, phase_dn_scan, lambda K: phase_mlp(K, 0), phase_l1_inproj, phase_attn_c, lambda K: phase_mlp(K, 1)]
    for i, ph in enumerate(phases):
        if i >= stop:
            break
        ph(K)
    if debug:
        K.begin()
        s = K.getsem()
        for name in debug:
            src = K.dscr[name]
            o = K.outp("dbg_" + name, src.shape, src.dtype)
            nr = src.shape[0]
            step = max(1, min(nr, (1 << 20) // (src.shape[1] * 4)))
            for r0 in range(0, nr, step):
                K.P.dma("sp", o[r0:min(nr, r0 + step)], src[r0:min(nr, r0 + step)], s)
        K.end()
    K.pes.close()
    return nc, K


def na_mask_host(rel_bias, flip):
    out = np.full((2, 8, 6, 128, 256), -30000.0, np.float32)
    qq = np.arange(256)
    kk = np.arange(768)
    for cl in range(2):
        qr = (0 if cl == 0 else 12) + qq // 64
        qc = qq % 64
        kr = (0 if cl == 0 else 8) + kk // 64
        kc = kk % 64
        if flip:
            qr, qc, kr, kc = 63 - qr, 63 - qc, 63 - kr, 63 - kc
        rs = np.clip(qr - 4, 0, 56)
        cs = np.clip(qc - 8, 0, 48)
        vr = (kr[:, None] >= rs[None, :]) & (kr[:, None] < rs[None, :] + 8)
        vc = (kc[:, None] >= cs[None, :]) & (kc[:, None] < cs[None, :] + 16)
        valid = vr & vc
        dr = np.clip(kr[:, None] - qr[None, :] + 7, 0, 14)
        dc = np.clip(kc[:, None] - qc[None, :] + 15, 0, 30)
        for h in range(8):
            b = rel_bias[h][dr, dc]
            m = np.where(valid, b, np.float32(-30000.0)).astype(np.float32)
            out[cl, h] = m.reshape(6, 128, 256)
    return out


def tri_host():
    i = np.arange(128)[:, None]
    j = np.arange(128)[None, :]
    return np.stack([(j <= i), (j >= i), (j < i), (j > i)]).astype(np.float32)


def rope_host(flip):
    nf = 16
    inv = (10000.0 ** (-np.arange(nf, dtype=np.float32) / nf)).astype(np.float32)
    tloc = np.arange(SEXT * 128)
    tok = (4095 - tloc) if flip else tloc
    pos = np.stack([tok // 64, tok % 64], 0).astype(np.float32)
    d = np.arange(128) % 64
    a, b, fidx = d // 32, (d % 32) // 16, d % 16
    ang = pos[a, :] * inv[fidx][:, None]
    cos = np.cos(ang).astype(np.float32)
    sin = np.sin(ang).astype(np.float32)
    perm = np.zeros((128, 128), np.float32)
    for m in range(128):
        bm = (m % 32) // 16
        partner = m + 16 if bm == 0 else m - 16
        perm[partner, m] = -1.0 if bm == 0 else 1.0
    return cos, sin, perm


def host_inputs(inputs, c):
    seq, flip = c // 2, c % 2
    f = lambda a: np.ascontiguousarray(a, dtype=np.float32)
    xp = inputs["x_prompt"][4 * c:4 * c + 4]
    xs = inputs["x_sample"][seq]
    if flip:
        xp = xp[:, ::-1]
        xs = xs[::-1]
    wg = inputs["ab_w_in"][0][:, 7168:7200]
    if flip:
        wg = np.concatenate([wg[:, 8:16], wg[:, 0:8], wg[:, 24:32], wg[:, 16:24]], axis=1)
    m = {
        "ident": np.eye(128, dtype=np.float32),
        "cond": f(np.stack([inputs["c_ctx"], inputs["c"][seq]])),
        "xp": f(xp.reshape(NPT * 128, D)),
        "xs": f(xs),
        "w_ada": inputs["w_ada"], "b_ada": inputs["b_ada"],
        "norm_mix": inputs["norm_mix"], "norm_mlp": inputs["norm_mlp"],
        "ab_w_in": inputs["ab_w_in"][0], "w_gates": f(wg),
        "cache_a_k": f(inputs["cache_a_k"][seq, 0].reshape(256, 1024)),
        "cache_a_v": f(inputs["cache_a_v"][seq, 0].reshape(256, 1024)),
        "na_mask": na_mask_host(inputs["a_rel_bias"][0], flip),
        "ab_w_out": inputs["ab_w_out"][0], "w_mlp_in": inputs["w_mlp_in"], "w_mlp_out": inputs["w_mlp_out"],
        "c_w_qkv": inputs["c_w_qkv"][0], "c_w_out": inputs["c_w_out"][0],
        "cache_c_k": f(inputs["cache_c_k"][seq, 0].reshape(256, 256)), "cache_c_v": f(inputs["cache_c_v"][seq, 0].reshape(256, 256)),
        "c_sink": f(inputs["c_sink"][0].reshape(1, 32)), "final_norm": f(inputs["final_norm"].reshape(1, D)),
        "rope_cos": rope_host(flip)[0], "rope_sin": rope_host(flip)[1], "rope_perm": rope_host(flip)[2],
        "conv_w": f(inputs["b_conv"][0][::-1] if flip else inputs["b_conv"][0]),
        "alog_dt": f(np.stack([(inputs["b_a_log"][0][::-1] if flip else inputs["b_a_log"][0]).reshape(16),
                               (inputs["b_dt_bias"][0][::-1] if flip else inputs["b_dt_bias"][0]).reshape(16)])),
        "tri": tri_host(),
        "s0": f(np.stack([inputs["state_b_bwd"][seq, 0], inputs["state_b_fwd"][seq, 0]]) if flip else
                np.stack([inputs["state_b_fwd"][seq, 0], inputs["state_b_bwd"][seq, 0]])),
        "onorm": f(inputs["b_out_norm"][0].reshape(1, 128)),
    }
    return m


_CACHE = {}


def kernel(**inputs):
    inputs = {k: np.asarray(v) for k, v in inputs.items()}
    if "nc" not in _CACHE:
        _CACHE["nc"] = build()
    nc, K = _CACHE["nc"]
    maps = [host_inputs(inputs, c) for c in range(8)]
    res = run_bass_kernel_spmd(nc, maps, core_ids=list(range(8))).results
    f32 = np.float32
    y_p = np.zeros((32, 256, D), f32)
    y_s = np.zeros((4, 4096, D), f32)
    nak = np.zeros((32, 1, 256, 8, 128), f32)
    nav = np.zeros((32, 1, 256, 8, 128), f32)
    nbf = np.zeros((32, 1, 8, 128, 128), f32)
    nbb = np.zeros((32, 1, 8, 128, 128), f32)
    nck = np.zeros((32, 1, 256, 4, 64), f32)
    ncv = np.zeros((32, 1, 256, 4, 64), f32)
    for c in range(8):
        seq, flip = c // 2, c % 2
        r = {k: np.asarray(v, dtype=f32) for k, v in res[c].items()}
        fl = (lambda a: a[:, ::-1]) if flip else (lambda a: a)
        sl = slice(4 * c, 4 * c + 4)
        y_p[sl] = fl(r["y_p"].reshape(4, 256, D))
        if flip:
            y_s[seq, 2048:4096] = r["y_s"][::-1]
        else:
            y_s[seq, 0:2048] = r["y_s"]
        nak[sl, 0] = fl(r["nak"].reshape(4, 256, 8, 128))
        nav[sl, 0] = fl(r["nav"].reshape(4, 256, 8, 128))
        nck[sl, 0] = fl(r["nck"].reshape(4, 256, 4, 64))
        ncv[sl, 0] = fl(r["ncv"].reshape(4, 256, 4, 64))
        if flip:
            nbf[sl, 0], nbb[sl, 0] = r["nbb"], r["nbf"]
        else:
            nbf[sl, 0], nbb[sl, 0] = r["nbf"], r["nbb"]
    return (y_p, y_s, nak, nav, nbf, nbb, nck, ncv)
```

```python
import contextlib
import numpy as np
import concourse.bass as bass
import concourse.mybir as mybir

F32 = mybir.dt.float32
BF16 = mybir.dt.bfloat16
I32 = mybir.dt.int32
AF = mybir.ActivationFunctionType
ALU = mybir.AluOpType
AX = mybir.AxisListType

ENGS = ("pe", "act", "dve", "pool", "sp")
HANDLES = {"pe": "tensor", "act": "scalar", "dve": "vector", "pool": "gpsimd", "sp": "sync"}
SEM_LIMIT = 16000


class Tile:
    __slots__ = ("name", "writer", "readers", "excl")

    def __init__(self, name=""):
        self.name = name
        self.writer = None
        self.readers = {}
        self.excl = False


class DSem:
    def __init__(self, prog, name):
        self.prog = prog
        self.name = name
        self.gen = 0
        self.h = prog.nc.alloc_semaphore(name=name)
        self.count = 0
        self.last = None

    def bump(self):
        if self.count + 16 > SEM_LIMIT:
            self.gen += 1
            self.h = self.prog.nc.alloc_semaphore(name=f"{self.name}_g{self.gen}")
            self.count = 0
        self.count += 16
        return self.h, self.count


class Ins:
    __slots__ = ("eng", "fn", "deps", "sem", "count", "needed", "is_dma", "epoch")

    def __init__(self, eng, fn, is_dma=False):
        self.eng = eng
        self.fn = fn
        self.deps = []
        self.sem = None
        self.count = None
        self.needed = False
        self.is_dma = is_dma
        self.epoch = 0


class Prog:
    def __init__(self, nc):
        self.nc = nc
        self.lists = {e: [] for e in ENGS}
        self.esem = {e: nc.alloc_semaphore(name=f"es_{e}_0") for e in ENGS}
        self.esem_gen = {e: 0 for e in ENGS}
        self.ecount = {e: 0 for e in ENGS}
        self.known = {e: {} for e in ENGS}
        self.epoch = 0
        self.n_ins = 0
        self.dsems = []
        self.last_ins = {e: None for e in ENGS}

    def tile(self, name=""):
        return Tile(name)

    def tiles(self, n, name=""):
        return [Tile(f"{name}{i}") for i in range(n)]

    def dsem(self, name):
        d = DSem(self, name)
        self.dsems.append(d)
        return d

    def _add(self, eng, fn, reads, writes, dsem=None):
        ins = Ins(eng, fn, is_dma=dsem is not None)
        ins.epoch = self.epoch
        deps = []
        for t in reads:
            if t.writer is not None:
                deps.append(t.writer)
            if t.excl:
                deps.extend(r for r in t.readers.values() if r.eng != eng)
        for t in writes:
            if t.writer is not None:
                deps.append(t.writer)
            deps.extend(t.readers.values())
        if dsem is not None:
            if dsem.last is not None:
                deps.append(dsem.last)
            ins.sem, ins.count = dsem.bump()
            ins.needed = True
            dsem.last = ins
        out = []
        seen = set()
        for d in deps:
            if d is ins or id(d) in seen:
                continue
            seen.add(id(d))
            if d.epoch < self.epoch:
                continue
            if eng == "pe" and d.eng == "pe" and not d.is_dma and dsem is None:
                continue
            out.append(d)
        ins.deps = out
        for d in out:
            d.needed = True
        for t in reads:
            key = (eng, dsem.name) if dsem is not None else eng
            t.readers[key] = ins
        for t in writes:
            t.writer = ins
            t.readers = {}
        self.lists[eng].append(ins)
        self.last_ins[eng] = ins
        self.n_ins += 1
        return ins

    def op(self, eng, fn, reads=(), writes=()):
        return self._add(eng, fn, list(reads), list(writes))

    def dma(self, eng, out, in_, dsem, reads=(), writes=()):
        return self._add(eng, lambda e: e.dma_start(out=out, in_=in_), list(reads), list(writes), dsem=dsem)

    def barrier(self):
        deps = [i for i in self.last_ins.values() if i is not None]
        deps += [d.last for d in self.dsems if d.last is not None]
        deps = [d for d in deps if d.epoch == self.epoch]
        for d in deps:
            d.needed = True
        for e in ENGS:
            ins = Ins(e, None)
            ins.epoch = self.epoch
            ins.deps = list(deps)
            self.lists[e].append(ins)

    def flush(self, final=False):
        self.barrier()
        for e in ENGS:
            for ins in self.lists[e]:
                if ins.is_dma or ins.fn is None:
                    continue
                if ins.needed:
                    if self.ecount[e] + 1 > SEM_LIMIT:
                        self.esem_gen[e] += 1
                        self.esem[e] = self.nc.alloc_semaphore(name=f"es_{e}_{self.esem_gen[e]}")
                        self.ecount[e] = 0
                    self.ecount[e] += 1
                    ins.sem = self.esem[e]
                    ins.count = self.ecount[e]
        prog = self

        def run(e, h):
            known = prog.known[e]
            for ins in prog.lists[e]:
                need = {}
                for d in ins.deps:
                    k = id(d.sem)
                    if known.get(k, 0) >= d.count:
                        continue
                    if k not in need or need[k][1] < d.count:
                        need[k] = (d.sem, d.count)
                for k, (s, v) in need.items():
                    h.wait_ge(s, v)
                    known[k] = v
                if ins.fn is None:
                    continue
                bi = ins.fn(h)
                if ins.is_dma:
                    bi.then_inc(ins.sem, 16)
                elif ins.needed:
                    bi.then_inc(ins.sem, 1)

        with self.nc.Block() as block:
            for e in ENGS:
                if not self.lists[e]:
                    continue
                dec = getattr(block, HANDLES[e])

                def mk(e):
                    def _f(h):
                        run(e, h)
                    return _f
                dec(mk(e))
        self.lists = {e: [] for e in ENGS}
        self.epoch += 1

from concourse.bass_utils import run_bass_kernel_spmd

D = 2048
KC = 16
EPS = 1e-6
NPT = 8
NS1 = 19
NG1 = NPT + NS1
NS2 = 13
TOK1 = NG1 * 128
TOKB = (NG1 + NS2) * 128
SEXT = 17


class Ring:
    def __init__(self, K, name, shape, dt, n, sw=False):
        self.bufs = [K.sb(f"{name}{i}", shape, dt) for i in range(n)]
        self.tiles = [Tile(f"{name}{i}") for i in range(n)]
        self.sems = [K.getsem(sw) for i in range(n)]
        self.i = 0

    def next(self):
        k = self.i % len(self.bufs)
        self.i += 1
        return self.bufs[k], self.tiles[k], self.sems[k]


class Ctx:
    def __init__(self, nc):
        self.nc = nc
        self.P = Prog(nc)
        self.es = None
        self.pes = contextlib.ExitStack()
        self.uid = 0
        self.sem_pool = {False: [], True: []}
        self.sem_used = {False: [], True: []}
        self.PS = [nc.alloc_psum_tensor(f"psb{i}", [128, 512], F32) for i in range(8)]
        self.TPS = [Tile(f"ps{i}") for i in range(8)]
        for t_ in self.TPS:
            t_.excl = True
        self.psi = [0, 0]
        self.depth = 0
        self.din = {}
        self.dout = {}
        self.dscr = {}

    def inp(self, name, shape, dt=F32):
        self.din[name] = self.nc.dram_tensor(name, list(shape), dt, kind="ExternalInput").ap()
        return self.din[name]

    def outp(self, name, shape, dt=F32):
        self.dout[name] = self.nc.dram_tensor(name, list(shape), dt, kind="ExternalOutput").ap()
        return self.dout[name]

    def scr(self, name, shape, dt):
        self.dscr[name] = self.nc.dram_tensor(name, list(shape), dt).ap()
        return self.dscr[name]

    def begin(self):
        if self.es is not None:
            self.depth += 1
            return
        self.es = contextlib.ExitStack()

    def end(self):
        if self.depth > 0:
            self.depth -= 1
            return
        self.P.flush()
        self.es.close()
        self.es = None
        for k in (False, True):
            self.sem_pool[k].extend(self.sem_used[k])
            self.sem_used[k] = []

    def sb(self, name, shape, dt):
        self.uid += 1
        return self.es.enter_context(self.nc.sbuf_tensor(f"{name}_{self.uid}", list(shape), dt))

    def psb(self, name, shape, dt):
        self.uid += 1
        return self.pes.enter_context(self.nc.sbuf_tensor(f"{name}_{self.uid}", list(shape), dt))

    def getsem(self, sw=False):
        if self.sem_pool[sw]:
            s = self.sem_pool[sw].pop()
        else:
            self.uid += 1
            s = self.P.dsem(f"ds{'w' if sw else 'h'}{self.uid}")
        self.sem_used[sw].append(s)
        return s

    def dump(self, name, ap, tiles):
        import os
        if not os.environ.get("DN_DEBUG") or name in self.dscr:
            return
        d = self.scr(name, list(ap.shape), ap.dtype)
        self.P.dma("sp", d, ap, self.getsem(), reads=tiles)

    def ps(self, g=0):
        k = g * 4 + self.psi[g] % 4
        self.psi[g] += 1
        return self.PS[k], self.TPS[k]


def rows_T(K, dst, T_dst, src2d, n, stage, T_stage, sem):
    P = K.P
    P.dma("sp", stage[0:n, :], src2d, sem, writes=[T_stage])
    ps, Tp = K.ps()
    P.op("pe", lambda e: e.transpose(ps[:, 0:n], stage[0:n, :], K.identf[0:n, 0:n]), reads=[T_stage, K.T_const], writes=[Tp])
    P.op("dve", lambda e: e.tensor_copy(dst, ps[:, 0:n]), reads=[Tp], writes=[T_dst])


def phase_consts(K):
    P = K.P
    K.begin()
    K.T_const = Tile("const")
    K.identf = K.psb("identf", [128, 128], F32)
    K.identb = K.psb("identb", [128, 128], BF16)
    K.onesf = K.psb("onesf", [128, 128], F32)
    K.onesb = K.psb("onesb", [128, 128], BF16)
    s = K.getsem()
    s2 = K.getsem(True)
    P.dma("sp", K.identf[:], K.din["ident"], s, writes=[K.T_const])
    P.dma("pool", K.identb[:], K.din["ident"], s2, writes=[K.T_const])
    P.op("dve", lambda e: e.memset(K.onesf[:], 1.0), writes=[K.T_const])
    P.op("dve", lambda e: e.memset(K.onesb[:], 1.0), writes=[K.T_const])
    K.end()


def phase_ada(K):
    nc, P = K.nc, K.P
    modrow_d = K.scr("modrow_d", [2, 2, 6 * D], F32)
    K.begin()
    cond = K.sb("cond", [2, D], F32)
    cs = K.sb("cs", [2, D], F32)
    condT = K.sb("condT", [128, KC, 2], BF16)
    brow = K.sb("brow", [2, 6 * D], F32)
    modrow = K.sb("modrow", [2, 6 * D], F32)
    T_cond, T_cs, T_condT, T_brow, T_modrow, T_mrd = P.tiles(6, "ada")
    s0, s1 = K.getsem(), K.getsem()
    wr = Ring(K, "adaw", [128, KC, 512], BF16, 3, sw=True)
    P.dma("sp", cond[:], K.din["cond"], s0, writes=[T_cond])
    P.op("act", lambda e: e.activation(out=cs[:], in_=cond[:], func=AF.Silu), reads=[T_cond], writes=[T_cs])
    ps, Tp = K.ps()
    for kc in range(KC):
        P.op("pe", lambda e, kc=kc, ps=ps: e.transpose(ps[:, kc * 2:kc * 2 + 2], cs[0:2, kc * 128:(kc + 1) * 128], K.identf[0:2, 0:2]),
             reads=[T_cs, K.T_const], writes=[Tp])
    P.op("dve", lambda e, ps=ps: e.tensor_copy(condT[:].rearrange("p k c -> p (k c)"), ps[:, 0:2 * KC]), reads=[Tp], writes=[T_condT])
    for l in range(2):
        P.dma("sp", brow[:], K.din["b_ada"][l:l + 1, :].partition_broadcast(2), s0, writes=[T_brow])
        for cg in range(24):
            wt, Tw, sw = wr.next()
            P.dma("pool", wt[:], K.din["w_ada"][l, :, cg * 512:(cg + 1) * 512].rearrange("(kc p) n -> p kc n", p=128), sw, writes=[Tw])
            ps, Tp = K.ps()
            for kc in range(KC):
                P.op("pe", lambda e, kc=kc, ps=ps, wt=wt: e.matmul(ps[0:2, :], condT[:, kc, :], wt[:, kc, :], start=(kc == 0), stop=(kc == KC - 1)),
                     reads=[T_condT, Tw], writes=[Tp])
            P.op("dve", lambda e, ps=ps, cg=cg: e.tensor_tensor(modrow[0:2, cg * 512:(cg + 1) * 512], ps[0:2, :], brow[0:2, cg * 512:(cg + 1) * 512], ALU.add),
                 reads=[Tp, T_brow], writes=[T_modrow])
        P.dma("sp", modrow_d[l], modrow[:], s1, reads=[T_modrow], writes=[T_mrd])
    K.end()
    K.begin()
    K.T_mod = Tile("mod")
    K.modF = [[K.psb(f"modF{l}{c}", [128, 96], F32) for c in range(2)] for l in range(2)]
    K.gsF = [[[K.psb(f"gsF{l}{w}{c}", [128, KC], F32) for c in range(2)] for w in range(2)] for l in range(2)]
    K.fnorm = K.psb("fnormF", [128, KC], F32)
    stage = K.sb("stg", [128, 128], F32)
    gF = K.sb("gF", [128, KC], F32)
    T_stage, T_g = P.tiles(2, "adaf")
    s0 = K.getsem()
    for l in range(2):
        for c in range(2):
            rows_T(K, K.modF[l][c][:], K.T_mod, modrow_d[l, c].rearrange("(r p) -> r p", p=128), 96, stage, T_stage, s0)
        for w, nm in enumerate(("norm_mix", "norm_mlp")):
            rows_T(K, gF[:], T_g, K.din[nm][l].rearrange("(r p) -> r p", p=128), KC, stage, T_stage, s0)
            for c in range(2):
                sc = K.modF[l][c][:, (1 + 3 * w) * KC:(2 + 3 * w) * KC]
                P.op("dve", lambda e, l=l, w=w, c=c, sc=sc: e.scalar_tensor_tensor(K.gsF[l][w][c][:], sc, 1.0, gF[:], ALU.add, ALU.mult),
                     reads=[K.T_mod, T_g], writes=[K.T_mod])
    K.end()


class NormT:
    def __init__(self, K, n=2):
        self.K = K
        self.ss = Ring(K, "nss", [128, 2], F32, n)
        self.xn = Ring(K, "nxn", [128, D], BF16, n)

    def run(self, xt, T_x, gs, shift, dst_fn, T_dst_fn):
        K = self.K
        P = K.P
        ss, T_ss, _ = self.ss.next()
        xn, T_xn, _ = self.xn.next()
        P.op("act", lambda e: e.activation(out=xn[:], in_=xt, func=AF.Square, accum_out=ss[:, 0:1]), reads=[T_x], writes=[T_xn, T_ss])
        import os
        NTL = int(os.environ.get("NT_LEVEL", "9"))
        if NTL < 2:
            return
        P.op("act", lambda e: e.activation(out=ss[:, 1:2], in_=ss[:, 0:1], func=AF.Sqrt, bias=EPS, scale=1.0 / D), reads=[T_ss], writes=[T_ss])
        P.op("dve", lambda e: e.reciprocal(ss[:, 1:2], ss[:, 1:2]), reads=[T_ss], writes=[T_ss])
        if NTL < 3:
            return
        P.op("act", lambda e: e.activation(out=xn[:], in_=xt, func=AF.Identity, scale=ss[:, 1:2]), reads=[T_x, T_ss], writes=[T_xn])
        if NTL < 4:
            return
        for half in range(2):
            ps, Tp = K.ps()
            psb = ps[:].bitcast(BF16)
            for j in range(8):
                kc = half * 8 + j
                P.op("pe", lambda e, j=j, kc=kc, psb=psb: e.transpose(psb[:, j * 128:(j + 1) * 128], xn[:, kc * 128:(kc + 1) * 128], K.identb[:]),
                     reads=[T_xn, K.T_const], writes=[Tp])
            for j in range(8):
                kc = half * 8 + j
                if True:
                    P.op("dve", lambda e, j=j, kc=kc, psb=psb: e.tensor_scalar(dst_fn(kc), psb[:, j * 128:(j + 1) * 128], gs[:, kc:kc + 1], shift[:, kc:kc + 1], ALU.mult, ALU.add),
                         reads=[Tp, K.T_mod], writes=[T_dst_fn(kc)])
                else:
                    P.op("act", lambda e, j=j, kc=kc, psb=psb: e.activation(out=dst_fn(kc), in_=psb[:, j * 128:(j + 1) * 128], func=AF.Identity, scale=gs[:, kc:kc + 1], bias=shift[:, kc:kc + 1]),
                         reads=[Tp, K.T_mod], writes=[T_dst_fn(kc)])


def token_groups(n_tb, breaks=()):
    out = []
    pts = [0] + list(breaks) + [n_tb]
    for a, b in zip(pts[:-1], pts[1:]):
        t = a
        while t < b:
            m = min(4, b - t)
            out.append((t, m))
            t += m
    return out


def phase_l0_inproj(K, which_pass):
    nc, P = K.nc, K.P
    if which_pass == 1:
        K.scr("qaT_d", [1024, TOK1], BF16)
        K.scr("kaT_d", [1024, TOK1], BF16)
        K.scr("va_d", [TOK1, 1024], BF16)
        K.scr("qkvT_d", [3072, TOKB], F32)
        K.scr("z_d", [TOK1, 1024], F32)
        K.scr("gates_d", [TOKB, 32], F32)
        ntb = NG1
        srcs = [(K.din["xp"][g * 128:(g + 1) * 128, :], 0) for g in range(NPT)] + \
               [(K.din["xs"][g * 128:(g + 1) * 128, :], 1) for g in range(NS1)]
        tok0 = 0
        groups = token_groups(NG1, breaks=(NPT,))
    else:
        ntb = NS2
        srcs = [(K.din["xs"][(NS1 + g) * 128:(NS1 + g + 1) * 128, :], 1) for g in range(NS2)]
        tok0 = TOK1
        groups = token_groups(NS2)
    K.begin()
    hT = K.sb("hT", [128, KC, ntb * 128], BF16)
    T_h = [[Tile(f"h{g}_{kc}") for kc in range(KC)] for g in range(ntb)]
    xr = Ring(K, "xin", [128, D], F32, 3)
    nt = NormT(K)
    for g, (src, c) in enumerate(srcs):
        xt, T_x, sx = xr.next()
        P.dma("sp", xt[:], src, sx, writes=[T_x])
        nt.run(xt[:], T_x, K.gsF[0][0][c], K.modF[0][c][:, 0:KC],
               lambda kc, g=g: hT[:, kc, g * 128:(g + 1) * 128], lambda kc, g=g: T_h[g][kc])
    wr = Ring(K, "w0", [128, KC, 512], BF16, 3, sw=True)
    st32 = Ring(K, "st32", [128, 512], F32, 3)
    st16 = Ring(K, "st16", [128, 512], BF16, 3)
    W = K.din["ab_w_in"]
    evi = [0]

    def evac(dst, src, T_src, T_dst):
        evi[0] += 1
        if evi[0] % 2:
            P.op("dve", lambda e: e.tensor_copy(dst, src), reads=[T_src], writes=[T_dst])
        else:
            P.op("act", lambda e: e.copy(dst, src), reads=[T_src], writes=[T_dst])

    def fm(wt, Tw, ncol_blocks, dst_d, row0, f32):
        for sub in range(ncol_blocks):
            for (t0, m) in groups:
                n = m * 128
                ps, Tp = K.ps()
                for kc in range(KC):
                    P.op("pe", lambda e, kc=kc, ps=ps, sub=sub, t0=t0, n=n: e.matmul(ps[:, 0:n], wt[:, kc, sub * 128:(sub + 1) * 128], hT[:, kc, t0 * 128:t0 * 128 + n], start=(kc == 0), stop=(kc == KC - 1)),
                         reads=[Tw] + [T_h[t0 + i][kc] for i in range(m)], writes=[Tp])
                sg, Ts, ss_ = (st32 if f32 else st16).next()
                evac(sg[:, 0:n], ps[:, 0:n], Tp, Ts)
                P.dma("sp", dst_d[row0 + sub * 128:row0 + (sub + 1) * 128, tok0 + t0 * 128:tok0 + t0 * 128 + n], sg[:, 0:n], ss_, reads=[Ts])

    def tm(wt, Tw, ncols, tbs, dests):
        for g in tbs:
            ps, Tp = K.ps()
            for kc in range(KC):
                P.op("pe", lambda e, kc=kc, ps=ps, g=g: e.matmul(ps[:, 0:ncols], hT[:, kc, g * 128:(g + 1) * 128], wt[:, kc, 0:ncols], start=(kc == 0), stop=(kc == KC - 1)),
                     reads=[Tw, T_h[g][kc]], writes=[Tp])
            for (dfn, f32) in dests:
                d = dfn(g)
                if d is None:
                    continue
                sg, Ts, ss_ = (st32 if f32 else st16).next()
                evac(sg[:, 0:ncols], ps[:, 0:ncols], Tp, Ts)
                P.dma("sp", d, sg[:, 0:ncols], ss_, reads=[Ts])

    wg32 = K.sb("wg32", [128, KC, 32], F32)
    T_wg32 = Tile("wg32")
    s_wg = K.getsem()

    def loadw(src, ncols=512):
        wt, Tw, sw = wr.next()
        if ncols == 512:
            P.dma("pool", wt[:, :, 0:ncols], src.rearrange("(kc p) n -> p kc n", p=128), sw, writes=[Tw])
        else:
            P.dma("sp", wg32[:], src.rearrange("(kc p) n -> p kc n", p=128), s_wg, writes=[T_wg32])
            P.op("dve", lambda e, wt=wt: e.tensor_copy(wt[:, :, 0:ncols], wg32[:]), reads=[T_wg32], writes=[Tw])
        return wt, Tw

    S = K.dscr
    if which_pass == 1:
        import os
        for t in range(int(os.environ.get('KDBG_NT', '14'))):
            wt, Tw = loadw(W[:, t * 512:(t + 1) * 512])
            if t < 2:
                fm(wt, Tw, 4, S["qaT_d"], t * 512, False)
            elif t < 4:
                fm(wt, Tw, 4, S["kaT_d"], (t - 2) * 512, False)
                tm(wt, Tw, 512, range(NPT), [(lambda g, t=t: K.dout["nak"][g * 128:(g + 1) * 128, (t - 2) * 512:(t - 1) * 512], True)])
            elif t < 6:
                tm(wt, Tw, 512, range(NG1), [(lambda g, t=t: S["va_d"][g * 128:(g + 1) * 128, (t - 4) * 512:(t - 3) * 512], False),
                                            (lambda g, t=t: K.dout["nav"][g * 128:(g + 1) * 128, (t - 4) * 512:(t - 3) * 512] if g < NPT else None, True)])
            elif t < 12:
                fm(wt, Tw, 4, S["qkvT_d"], (t - 6) * 512, True)
            else:
                tm(wt, Tw, 512, range(NG1), [(lambda g, t=t: S["z_d"][g * 128:(g + 1) * 128, (t - 12) * 512:(t - 11) * 512], True)])
        if int(os.environ.get('KDBG_G', '1')):
          wt, Tw = loadw(K.din["w_gates"], 32)
          tm(wt, Tw, 32, range(NG1), [(lambda g: S["gates_d"][g * 128:(g + 1) * 128, :], True)])
    else:
        for t in range(8, 12):
            wt, Tw = loadw(W[:, t * 512:(t + 1) * 512])
            fm(wt, Tw, 4, S["qkvT_d"], (t - 6) * 512, True)
        wt, Tw = loadw(K.din["w_gates"], 32)
        tm(wt, Tw, 32, range(NS2), [(lambda g: S["gates_d"][tok0 + g * 128:tok0 + (g + 1) * 128, :], True)])
    K.end()


NCAT = NPT + SEXT
TOKC = NCAT * 128


def attn_core(K, S_list, nq, rhs_q, out_ap, T_out, scale, extra_den=None, pools=None, g4=False):
    P = K.P
    q_ap, q_tiles = rhs_q
    M = out_ap.shape[0]
    psn, Tn = K.ps(1)
    psd, Td = K.ps(1)
    pt_ring, rec_ring = pools
    n = len(S_list)
    v3 = (lambda ap: ap.rearrange("p (g t) -> p g t", g=4)) if g4 else (lambda ap: ap)
    def s_mm(i):
        lk, tk = S_list[i][0], S_list[i][1]
        pss, Ts = K.ps(0)
        P.op("pe", lambda e, pss=pss, lk=lk: e.matmul(v3(pss[:, 0:nq]), lk, q_ap, start=True, stop=True), reads=tk + q_tiles, writes=[Ts])
        return pss, Ts

    nxt = s_mm(0)
    for i, (lk, tk, lv, tv, mask, tm_, _) in enumerate(S_list):
        pss, Ts = nxt
        if i + 1 < n:
            nxt = s_mm(i + 1)
        pt, Tpt, _ = pt_ring.next()
        P.op("act", lambda e, pss=pss, pt=pt: e.activation(out=pt[:, 0:nq], in_=pss[:, 0:nq], func=AF.Exp, scale=scale), reads=[Ts], writes=[Tpt])
        if mask is not None:
            P.op("pool", lambda e, pt=pt, mask=mask: e.tensor_tensor(v3(pt[:, 0:nq]), v3(pt[:, 0:nq]), mask, ALU.mult), reads=[Tpt] + tm_, writes=[Tpt])
        P.op("pe", lambda e, pt=pt, lv=lv, i=i: e.matmul(psn[0:M, 0:nq], lv, pt[:, 0:nq], start=(i == 0), stop=(i == n - 1)), reads=tv + [Tpt], writes=[Tn])
        P.op("pe", lambda e, pt=pt, i=i: e.matmul(psd[0:M, 0:nq], K.onesb[:, 0:M], pt[:, 0:nq], start=(i == 0), stop=(i == n - 1)), reads=[K.T_const, Tpt], writes=[Td])
    rec, Trec, _ = rec_ring.next()
    if extra_den is not None:
        ed, ted = extra_den
        if g4:
            P.op("dve", lambda e: e.tensor_tensor(v3(rec[0:M, 0:nq]), v3(psd[0:M, 0:nq]), ed, ALU.add), reads=[Td] + ted, writes=[Trec])
        else:
            P.op("dve", lambda e: e.tensor_scalar_add(rec[0:M, 0:nq], psd[0:M, 0:nq], ed), reads=[Td] + ted, writes=[Trec])
        P.op("dve", lambda e: e.reciprocal(rec[0:M, 0:nq], rec[0:M, 0:nq]), reads=[Trec], writes=[Trec])
    else:
        P.op("dve", lambda e: e.reciprocal(rec[0:M, 0:nq], psd[0:M, 0:nq]), reads=[Td], writes=[Trec])
    P.op("dve", lambda e: e.tensor_tensor(out_ap, v3(psn[0:M, 0:nq]), v3(rec[0:M, 0:nq]), ALU.mult), reads=[Tn, Trec], writes=[T_out])


def phase_attn_a(K):
    nc, P = K.nc, K.P
    S = K.dscr
    catT = K.scr("catT_d", [D, TOKC], BF16)
    scale = 128 ** -0.5
    K.begin()
    pt_ring = Ring(K, "pt", [128, 256], BF16, 4)
    rec_ring = Ring(K, "rec", [128, 256], F32, 2)
    pools = (pt_ring, rec_ring)
    qr = Ring(K, "cq", [128, 8, 256], BF16, 2)
    kr = Ring(K, "ck", [128, 8, 256], BF16, 2)
    vr = Ring(K, "cv", [128, 2, 1024], BF16, 2)
    orr = Ring(K, "co", [128, 8, 256], BF16, 2)
    for s in range(4):
        qt, Tq, sq = qr.next()
        kt, Tk, sk = kr.next()
        vt, Tv, sv = vr.next()
        ot, To, so = orr.next()
        P.dma("sp", qt[:], S["qaT_d"][:, s * 256:(s + 1) * 256].rearrange("(h p) t -> p h t", p=128), sq, writes=[Tq])
        P.dma("sp", kt[:], S["kaT_d"][:, s * 256:(s + 1) * 256].rearrange("(h p) t -> p h t", p=128), sk, writes=[Tk])
        P.dma("sp", vt[:], S["va_d"][s * 256:(s + 1) * 256, :].rearrange("(c p) f -> p c f", p=128), sv, writes=[Tv])
        for h in range(8):
            sl = [(kt[:, h, c * 128:(c + 1) * 128], [Tk], vt[:, c, h * 128:(h + 1) * 128], [Tv], None, [], 0) for c in range(2)]
            attn_core(K, sl, 256, (qt[:, h, :], [Tq]), ot[:, h, :], To, scale, pools=pools)
        P.dma("sp", catT[0:1024, s * 256:(s + 1) * 256].rearrange("(h p) t -> p h t", p=128), ot[:], so, reads=[To])
    ck_tm = K.sb("ck_tm", [128, 2, 1024], BF16)
    cvt = K.sb("cvt", [128, 2, 1024], BF16)
    ckT = K.sb("ckT", [128, 8, 256], BF16)
    T_ck, T_cv, T_ckT, T_eb = P.tiles(4, "na")
    sw0 = K.getsem(True)
    P.dma("pool", ck_tm[:], K.din["cache_a_k"].rearrange("(c p) f -> p c f", p=128), sw0, writes=[T_ck])
    P.dma("pool", cvt[:], K.din["cache_a_v"].rearrange("(c p) f -> p c f", p=128), sw0, writes=[T_cv])
    for c in range(2):
        ps, Tp = K.ps(0)
        psb = ps[:].bitcast(BF16)
        for h in range(8):
            P.op("pe", lambda e, c=c, h=h, psb=psb: e.transpose(psb[:, h * 128:(h + 1) * 128], ck_tm[:, c, h * 128:(h + 1) * 128], K.identb[:]), reads=[T_ck, K.T_const], writes=[Tp])
        P.op("dve", lambda e, c=c, psb=psb: e.tensor_copy(ckT[:, :, c * 128:(c + 1) * 128], psb.rearrange("p (h k) -> p h k", h=8)), reads=[Tp], writes=[T_ckT])
    EB = K.sb("EB", [128, 2, 8, 6, 256], BF16)
    mr = Ring(K, "mstage", [128, 6, 256], F32, 2)
    for cl in range(2):
        for h in range(8):
            mt, Tm, sm = mr.next()
            P.dma("sp", mt[:], K.din["na_mask"][cl, h].rearrange("c k q -> k c q"), sm, writes=[Tm])
            P.op("act", lambda e, cl=cl, h=h, mt=mt: e.activation(out=EB[:, cl, h, :, :], in_=mt[:], func=AF.Exp), reads=[Tm], writes=[T_eb])
    NQ = SEXT * 128
    NK = NS1 * 128
    qh = Ring(K, "nq", [128, NQ], BF16, 2)
    kh = Ring(K, "nk", [128, NK], BF16, 2)
    vh = Ring(K, "nv", [128, NS1, 128], BF16, 2)
    oh = Ring(K, "no", [128, NQ], BF16, 2)
    for h in range(8):
        qt, Tq, sq = qh.next()
        kt, Tk, sk = kh.next()
        vt, Tv, sv = vh.next()
        ot, To, so = oh.next()
        P.dma("sp", qt[:], S["qaT_d"][h * 128:(h + 1) * 128, 1024:1024 + NQ], sq, writes=[Tq])
        P.dma("sp", kt[:], S["kaT_d"][h * 128:(h + 1) * 128, 1024:1024 + NK], sk, writes=[Tk])
        P.dma("sp", vt[:], S["va_d"][1024:1024 + NK, h * 128:(h + 1) * 128].rearrange("(c p) f -> p c f", p=128), sv, writes=[Tv])
        for i in range(9):
            nq = 256 if i < 8 else 128
            cl = 0 if i == 0 else 1
            base = 0 if i == 0 else (i - 1) * 256
            nch = 6 if i < 8 else 5
            sl = []
            for ch in range(nch):
                t0 = base + ch * 128
                sl.append((kt[:, t0:t0 + 128], [Tk], vt[:, t0 // 128, :], [Tv], EB[:, cl, h, ch, 0:nq], [T_eb], 0))
            for c in range(2):
                sl.append((ckT[:, h, c * 128:(c + 1) * 128], [T_ckT], cvt[:, c, h * 128:(h + 1) * 128], [T_cv], None, [], 0))
            attn_core(K, sl, nq, (qt[:, i * 256:i * 256 + nq], [Tq]), ot[:, i * 256:i * 256 + nq], To, scale, pools=pools)
        P.dma("sp", catT[h * 128:(h + 1) * 128, 1024:1024 + NQ], ot[:], so, reads=[To])
    K.end()


def phase_dn_prep(K):
    nc, P = K.nc, K.P
    S = K.dscr
    K.scr("qnT_d", [1024, TOK1], BF16)
    K.scr("knT_d", [1024, TOKB], BF16)
    K.scr("ktm_d", [TOKB, 1024], BF16)
    K.scr("vtm_d", [TOKB, 1024], BF16)
    K.begin()
    cwF = K.sb("cwF", [128, 3, 24], F32)
    stage = K.sb("cstg", [128, 128], F32)
    T_cw, T_stage = P.tiles(2, "cw")
    s0 = K.getsem()
    for j in range(3):
        rows_T(K, cwF[:, j, :], T_cw, K.din["conv_w"][j].rearrange("(r p) -> r p", p=128), 24, stage, T_stage, s0)
    pieces = [(s * 256, 256, True, True, 256) for s in range(4)]
    for i in range(8):
        nqv = 512 if i < 4 else (256 if i == 4 else 0)
        pieces.append((1024 + i * 512, 512, i == 0, i == 7, nqv))
    xr = Ring(K, "dx", [128, 514], F32, 3)
    yr = Ring(K, "dy", [128, 512], F32, 2)
    sqr = Ring(K, "dsq", [128, 512], F32, 2)
    rsr = Ring(K, "drs", [128, 512], F32, 2)
    ynr = Ring(K, "dyn", [128, 512], BF16, 3)
    tmr = Ring(K, "dtm", [128, 4, 128], BF16, 3)
    for fb in range(24):
        kind, h = fb // 8, fb % 8
        for (tok0, n0, le, re_, nq) in pieces:
            n = nq if kind == 0 else n0
            if n == 0:
                continue
            re2 = re_ and n == n0
            x, Tx, sx = xr.next()
            a = 1 if le else 0
            b = n + 1 if re2 else n + 2
            if le:
                P.op("pool", lambda e, x=x: e.memset(x[:, 0:1], 0.0), writes=[Tx])
            if re2:
                P.op("pool", lambda e, x=x, n=n: e.memset(x[:, n + 1:n + 2], 0.0), writes=[Tx])
            P.dma("sp", x[:, a:b], S["qkvT_d"][fb * 128:(fb + 1) * 128, tok0 - 1 + a:tok0 - 1 + b], sx, writes=[Tx])
            y, Ty, _ = yr.next()
            P.op("pool", lambda e, x=x, y=y, n=n, fb=fb: e.tensor_scalar_mul(y[:, 0:n], x[:, 0:n], cwF[:, 0, fb:fb + 1]), reads=[Tx, T_cw], writes=[Ty])
            P.op("dve", lambda e, x=x, y=y, n=n, fb=fb: e.scalar_tensor_tensor(y[:, 0:n], x[:, 1:n + 1], cwF[:, 1, fb:fb + 1], y[:, 0:n], ALU.mult, ALU.add), reads=[Tx, T_cw, Ty], writes=[Ty])
            P.op("dve", lambda e, x=x, y=y, n=n, fb=fb: e.scalar_tensor_tensor(y[:, 0:n], x[:, 2:n + 2], cwF[:, 2, fb:fb + 1], y[:, 0:n], ALU.mult, ALU.add), reads=[Tx, T_cw, Ty], writes=[Ty])
            P.op("act", lambda e, y=y, n=n: e.activation(out=y[:, 0:n], in_=y[:, 0:n], func=AF.Silu), reads=[Ty], writes=[Ty])
            yn, Tyn, syn = ynr.next()
            if kind < 2:
                sq, Tsq, _ = sqr.next()
                rs, Trs, _ = rsr.next()
                P.op("pool", lambda e, y=y, sq=sq, n=n: e.tensor_tensor(sq[:, 0:n], y[:, 0:n], y[:, 0:n], ALU.mult), reads=[Ty], writes=[Tsq])
                ps, Tp = K.ps(0)
                P.op("pe", lambda e, ps=ps, sq=sq, n=n: e.matmul(ps[:, 0:n], K.onesf[:], sq[:, 0:n], start=True, stop=True), reads=[Tsq, K.T_const], writes=[Tp])
                P.op("act", lambda e, ps=ps, rs=rs, n=n: e.activation(out=rs[:, 0:n], in_=ps[:, 0:n], func=AF.Sqrt, bias=EPS, scale=1.0), reads=[Tp], writes=[Trs])
                P.op("dve", lambda e, rs=rs, n=n: e.reciprocal(rs[:, 0:n], rs[:, 0:n]), reads=[Trs], writes=[Trs])
                cc = 128 ** -0.5 if kind == 0 else 1.0
                P.op("dve", lambda e, y=y, rs=rs, yn=yn, n=n, cc=cc: e.scalar_tensor_tensor(yn[:, 0:n], y[:, 0:n], cc, rs[:, 0:n], ALU.mult, ALU.mult), reads=[Ty, Trs], writes=[Tyn])
                dst = S["qnT_d"] if kind == 0 else S["knT_d"]
                P.dma("sp", dst[h * 128:(h + 1) * 128, tok0:tok0 + n], yn[:, 0:n], syn, reads=[Tyn])
            else:
                P.op("pool", lambda e, y=y, yn=yn, n=n: e.tensor_copy(yn[:, 0:n], y[:, 0:n]), reads=[Ty], writes=[Tyn])
            if kind >= 1:
                nb = n // 128
                ps, Tp = K.ps(0)
                psb = ps[:].bitcast(BF16)
                for j in range(nb):
                    P.op("pe", lambda e, j=j, psb=psb, yn=yn: e.transpose(psb[:, j * 128:(j + 1) * 128], yn[:, j * 128:(j + 1) * 128], K.identb[:]), reads=[Tyn, K.T_const], writes=[Tp])
                tm_, Ttm, stm = tmr.next()
                P.op("act", lambda e, psb=psb, tm_=tm_, nb=nb: e.copy(tm_[:, 0:nb, :], psb[:, 0:nb * 128].rearrange("p (j f) -> p j f", f=128)), reads=[Tp], writes=[Ttm])
                dst = S["ktm_d"] if kind == 1 else S["vtm_d"]
                P.dma("sp", dst[tok0:tok0 + n, h * 128:(h + 1) * 128].rearrange("(j p) f -> p j f", p=128), tm_[:, 0:nb, :], stm, reads=[Ttm])
    K.end()


def phase_dn_scan(K):
    nc, P = K.nc, K.P
    S = K.dscr
    K.scr("of_d", [TOKC, 1024], F32)
    catT = S["catT_d"]
    K.begin()
    H = 8
    tri = K.sb("tri", [128, 4, 128], F32)
    dtb = K.sb("dtb", [128, 16], F32)
    nea = K.sb("nea", [128, 16], F32)
    gon = K.sb("gon", [128, 128], F32)
    T_c = Tile("dnc")
    sc = K.getsem()
    P.dma("sp", tri[:], K.din["tri"].rearrange("m k c -> k m c"), sc, writes=[T_c])
    P.dma("sp", nea[:], K.din["alog_dt"][0:1, :].partition_broadcast(128), sc, writes=[T_c])
    P.dma("sp", dtb[:], K.din["alog_dt"][1:2, :].partition_broadcast(128), sc, writes=[T_c])
    P.dma("sp", gon[:], K.din["onorm"].partition_broadcast(128), sc, writes=[T_c])
    P.op("act", lambda e: e.activation(out=nea[:], in_=nea[:], func=AF.Exp), reads=[T_c], writes=[T_c])
    P.op("dve", lambda e: e.tensor_scalar_mul(nea[:], nea[:], -1.0), reads=[T_c], writes=[T_c])
    LM, UM, SLM, SUM = 0, 1, 2, 3
    St = K.sb("St", [128, H, 128], F32)
    Sb = K.sb("Sb", [128, H, 128], BF16)
    T_S, T_Sb = P.tiles(2, "S")
    T_of = {}
    big = lambda name, dt, n=2: Ring(K, name, [128, H, 128], dt, n)
    r_kT, r_qT, r_k, r_v = big("lkT", BF16), big("lqT", BF16), big("lk", BF16), big("lv", BF16)
    r_Rg, r_Rb = big("Rg", F32, 1), big("Rb", BF16, 1)
    r_diff, r_x1, r_e1, r_e2i, r_e2s = big("diff", F32, 1), big("x1", F32, 1), big("e1", F32, 1), big("e2i", F32, 1), big("e2s", F32, 1)
    r_kbT, r_egb, r_qd = big("kbT", BF16, 1), big("egb", BF16, 1), big("qd", BF16, 2)
    r_N, r_M, r_X, r_Y = big("N", F32, 2), big("M", F32, 2), big("X", F32, 2), big("Y", F32, 2)
    r_Xb = big("Xb", BF16, 1)
    r_qk, r_vb, r_kbg, r_kd = big("qk", BF16, 2), big("vb", BF16, 1), big("kbg", BF16, 1), big("kdc", BF16, 2)
    r_u, r_wT, r_vn, r_o = big("u", F32, 2), big("wT", BF16, 2), big("vn", BF16, 1), big("o", F32, 2)
    r_z, r_sq, r_on, r_obT = Ring(K, "z", [128, 1024], F32, 1), big("osq", F32, 1), big("on", BF16, 1), big("obT", BF16, 2)
    r_st = Ring(K, "ost", [128, 16], F32, 2)

    def v8(ap):
        return ap.rearrange("p (h f) -> p h f", h=H)

    def bc_h(ap2):
        return ap2.unsqueeze(2).to_broadcast([128, H, 128])

    def bc_m(m):
        return tri[:, m, :].unsqueeze(1).to_broadcast([128, H, 128])

    def mm8(lhs_fn, rhs_fn, reads, g, extra=None):
        b0, T0 = K.ps(g)
        b1, T1 = K.ps(g)
        banks = ((b0, T0), (b1, T1))
        for h in range(H):
            b, Tb = banks[h // 4]
            o_ = b[:, (h % 4) * 128:(h % 4 + 1) * 128]
            if extra is None:
                P.op("pe", lambda e, h=h, o_=o_: e.matmul(o_, lhs_fn(h), rhs_fn(h), start=True, stop=True), reads=reads, writes=[Tb])
            else:
                l2, r2, reads2 = extra
                P.op("pe", lambda e, h=h, o_=o_: e.matmul(o_, lhs_fn(h), rhs_fn(h), start=True, stop=False), reads=reads, writes=[Tb])
                P.op("pe", lambda e, h=h, o_=o_: e.matmul(o_, l2(h), r2(h), start=False, stop=True), reads=reads2, writes=[Tb])
        return banks

    def ev(eng, banks, fn, reads, writes):
        for i, (b, Tb) in enumerate(banks):
            bv = b[:].rearrange("p (h f) -> p h f", h=4)
            P.op(eng, lambda e, bv=bv, i=i: fn(e, bv, slice(4 * i, 4 * i + 4)), reads=[Tb] + reads, writes=writes)

    def gates(tokd0, nch):
        G = {}
        graw = K.sb("graw", [128, nch, 32], F32)
        beta = K.sb("beta", [128, nch, 16], F32)
        g = K.sb("gg", [128, nch, 16], F32)
        Tg = Tile("gates")
        sg = K.getsem()
        P.dma("sp", graw[:], S["gates_d"][tokd0:tokd0 + nch * 128, :].rearrange("(c p) g -> p c g", p=128), sg, writes=[Tg])
        P.op("act", lambda e: e.activation(out=beta[:], in_=graw[:, :, 0:16], func=AF.Sigmoid), reads=[Tg], writes=[Tg])
        P.op("dve", lambda e: e.tensor_tensor(g[:], graw[:, :, 16:32], dtb[:].unsqueeze(1).to_broadcast([128, nch, 16]), ALU.add), reads=[Tg, T_c], writes=[Tg])
        P.op("act", lambda e: e.activation(out=g[:], in_=g[:], func=AF.Exp), reads=[Tg], writes=[Tg])
        P.op("act", lambda e: e.activation(out=g[:], in_=g[:], func=AF.Ln, bias=1.0, scale=1.0), reads=[Tg], writes=[Tg])
        P.op("dve", lambda e: e.tensor_tensor(g[:], g[:], nea[:].unsqueeze(1).to_broadcast([128, nch, 16]), ALU.mult), reads=[Tg, T_c], writes=[Tg])
        G["beta"], G["T"] = beta, Tg
        for dr in range(2):
            gc = K.sb(f"gc{dr}", [128, nch, 8], F32)
            gl = K.sb(f"gl{dr}", [128, nch, 8], F32)
            eg = K.sb(f"eg{dr}", [128, nch, 8], F32)
            bg = K.sb(f"bg{dr}", [128, nch, 8], F32)
            kd = K.sb(f"kd{dr}", [128, nch, 8], F32)
            cd = K.sb(f"cd{dr}", [128, nch, 8], F32)
            tr = UM if dr == 0 else LM
            ps, Tp = K.ps(0)
            gsl = g[:, :, dr * 8:(dr + 1) * 8]
            P.op("pe", lambda e, ps=ps, tr=tr, gsl=gsl: e.matmul(ps[:, 0:nch * 8].rearrange("p (c h) -> p c h", h=8), tri[:, tr, :], gsl, start=True, stop=True), reads=[Tg, T_c], writes=[Tp])
            P.op("dve", lambda e, ps=ps, gc=gc: e.tensor_copy(gc[:].rearrange("p c h -> p (c h)"), ps[:, 0:nch * 8]), reads=[Tp], writes=[Tg])
            ps2, Tp2 = K.ps(0)
            P.op("pe", lambda e, ps2=ps2, gsl=gsl: e.matmul(ps2[:, 0:nch * 8].rearrange("p (c h) -> p c h", h=8), K.onesf[:], gsl, start=True, stop=True), reads=[Tg, K.T_const], writes=[Tp2])
            P.op("dve", lambda e, ps2=ps2, gl=gl: e.tensor_copy(gl[:].rearrange("p c h -> p (c h)"), ps2[:, 0:nch * 8]), reads=[Tp2], writes=[Tg])
            P.op("act", lambda e, eg=eg, gc=gc: e.activation(out=eg[:], in_=gc[:], func=AF.Exp), reads=[Tg], writes=[Tg])
            P.op("dve", lambda e, bg=bg, eg=eg, dr=dr: e.tensor_tensor(bg[:], eg[:], beta[:, :, dr * 8:(dr + 1) * 8], ALU.mult), reads=[Tg], writes=[Tg])
            P.op("dve", lambda e, kd=kd, gl=gl, gc=gc: e.tensor_tensor(kd[:], gl[:], gc[:], ALU.subtract), reads=[Tg], writes=[Tg])
            P.op("act", lambda e, kd=kd: e.activation(out=kd[:], in_=kd[:], func=AF.Exp), reads=[Tg], writes=[Tg])
            P.op("act", lambda e, cd=cd, gl=gl: e.activation(out=cd[:], in_=gl[:], func=AF.Exp), reads=[Tg], writes=[Tg])
            G[dr] = dict(gc=gc, eg=eg, bg=bg, kd=kd, cd=cd)
        return G

    def prep(C):
        G, tokd0, ch, dr, full = C['G'], C['tokd0'], C['ch'], C['dr'], C['full']
        t0 = tokd0 + ch * 128
        Tg = G["T"]
        gd = G[dr]
        beta_c = G["beta"][:, ch, dr * 8:(dr + 1) * 8]
        gc_c = gd["gc"][:, ch, :]
        mL, mU, mSL, mSU = (LM, UM, SLM, SUM) if dr == 0 else (UM, LM, SUM, SLM)
        kT, TkT, s1 = r_kT.next()
        ktm, Tk, s2 = r_k.next()
        vtm, Tv, s3 = r_v.next()
        P.dma("sp", kT[:], S["knT_d"][:, t0:t0 + 128].rearrange("(h p) t -> p h t", p=128), s1, writes=[TkT])
        P.dma("sp", ktm[:].rearrange("p h f -> p (h f)"), S["ktm_d"][t0:t0 + 128, :], s2, writes=[Tk])
        P.dma("sp", vtm[:].rearrange("p h f -> p (h f)"), S["vtm_d"][t0:t0 + 128, :], s3, writes=[Tv])
        if full:
            qT, TqT, s4 = r_qT.next()
            P.dma("sp", qT[:], S["qnT_d"][:, t0:t0 + 128].rearrange("(h p) t -> p h t", p=128), s4, writes=[TqT])
        yield
        Rg, TRg, _ = r_Rg.next()
        Rb, TRb, _ = r_Rb.next()
        idb = K.identf[:].unsqueeze(1).to_broadcast([128, H, 128])
        P.op("dve", lambda e: e.tensor_tensor(Rg[:], bc_h(gc_c), idb, ALU.mult), reads=[Tg, K.T_const], writes=[TRg])
        P.op("pool", lambda e: e.tensor_tensor(Rb[:], bc_h(beta_c), idb, ALU.mult), reads=[Tg, K.T_const], writes=[TRb])
        gcb = mm8(lambda h: K.onesf[:], lambda h: Rg[:, h, :], [TRg, K.T_const], 0)
        btb = mm8(lambda h: K.onesb[:], lambda h: Rb[:, h, :], [TRb, K.T_const], 0)
        diff, Tdiff, _ = r_diff.next()
        ev("dve", gcb, lambda e, bv, hs: e.tensor_tensor(diff[:, hs, :], bc_h(gc_c)[:, hs, :], bv, ALU.subtract), [Tg], [Tdiff])
        kbT, TkbT, _ = r_kbT.next()
        ev("dve", btb, lambda e, bv, hs: e.tensor_tensor(kbT[:, hs, :], kT[:, hs, :], bv, ALU.mult), [TkT], [TkbT])
        if full:
            egb, Tegb, _ = r_egb.next()
            qd, Tqd, _ = r_qd.next()
            ev("act", gcb, lambda e, bv, hs: e.activation(out=egb[:, hs, :], in_=bv, func=AF.Exp), [], [Tegb])
            P.op("pool", lambda e: e.tensor_tensor(qd[:], qT[:], egb[:], ALU.mult), reads=[TqT, Tegb], writes=[Tqd])
        yield
        x1, Tx1, _ = r_x1.next()
        e1, Te1, _ = r_e1.next()
        e2i, Te2i, _ = r_e2i.next()
        e2s, Te2s, _ = r_e2s.next()
        P.op("dve", lambda e: e.tensor_tensor(x1[:], diff[:], bc_m(mL), ALU.mult), reads=[Tdiff, T_c], writes=[Tx1])
        P.op("act", lambda e: e.activation(out=e1[:], in_=x1[:], func=AF.Exp), reads=[Tx1], writes=[Te1])
        P.op("pool", lambda e: e.tensor_tensor(e1[:], e1[:], bc_m(mSL), ALU.mult), reads=[Te1, T_c], writes=[Te1])
        P.op("dve", lambda e: e.tensor_tensor(x1[:], diff[:], bc_m(mU), ALU.mult), reads=[Tdiff, T_c, Te1], writes=[Tx1])
        P.op("act", lambda e: e.activation(out=e2i[:], in_=x1[:], func=AF.Exp, scale=-1.0), reads=[Tx1], writes=[Te2i])
        P.op("pool", lambda e: e.tensor_tensor(e2s[:], e2i[:], bc_m(mSU), ALU.mult), reads=[Te2i, T_c], writes=[Te2s])
        P.op("pool", lambda e: e.tensor_tensor(e2i[:], e2i[:], bc_m(mU), ALU.mult), reads=[Te2i, T_c, Te2s], writes=[Te2i])
        yield
        a1 = mm8(lambda h: kbT[:, h, :], lambda h: kT[:, h, :], [TkbT, TkT], 0)
        a2 = mm8(lambda h: kT[:, h, :], lambda h: kbT[:, h, :], [TkbT, TkT], 1)
        Mj, TM, _ = r_M.next()
        Nj, TN, _ = r_N.next()
        ev("dve", a1, lambda e, bv, hs, Mj=Mj: e.tensor_tensor(Mj[:, hs, :], bv, e1[:, hs, :], ALU.mult), [Te1], [TM])
        ev("dve", a2, lambda e, bv, hs, Nj=Nj: e.tensor_tensor(Nj[:, hs, :], bv, e2s[:, hs, :], ALU.mult), [Te2s], [TN])
        if full:
            a3 = mm8(lambda h: kT[:, h, :], lambda h: qT[:, h, :], [TkT, TqT], 0)
            qk, Tqk, _ = r_qk.next()
            ev("dve", a3, lambda e, bv, hs: e.tensor_tensor(qk[:, hs, :], bv, e2i[:, hs, :], ALU.mult), [Te2i], [Tqk])
        yield
        X, TX, _ = r_X.next()
        Y, TY, _ = r_Y.next()
        idbb = K.identf[:].unsqueeze(1).to_broadcast([128, H, 128])
        P.op("pool", lambda e, X=X, Nj=Nj: e.tensor_tensor(X[:], idbb, Nj[:], ALU.subtract), reads=[TN, K.T_const], writes=[TX])
        P.op("pool", lambda e, Y=Y, Mj=Mj: e.tensor_tensor(Y[:], idbb, Mj[:], ALU.subtract), reads=[TM, K.T_const], writes=[TY])
        for j in range(1, 7):
            last = j == 6
            Nn, TNn, _ = r_N.next()
            pn = mm8(lambda h, Mj=Mj: Mj[:, h, :], lambda h, Nj=Nj: Nj[:, h, :], [TM, TN], 0)
            ev("act", pn, lambda e, bv, hs, Nn=Nn: e.copy(Nn[:, hs, :], bv), [], [TNn])
            if not last:
                Mn, TMn, _ = r_M.next()
                pm = mm8(lambda h, Nj=Nj: Nj[:, h, :], lambda h, Mj=Mj: Mj[:, h, :], [TM, TN], 0)
                ev("act", pm, lambda e, bv, hs, Mn=Mn: e.copy(Mn[:, hs, :], bv), [], [TMn])
            Xn, TXn, _ = r_X.next()
            px = mm8(lambda h, Y=Y: Y[:, h, :], lambda h, Nn=Nn: Nn[:, h, :], [TY, TNn], 1)
            ev("dve", px, lambda e, bv, hs, Xn=Xn, X=X: e.tensor_tensor(Xn[:, hs, :], bv, X[:, hs, :], ALU.add), [TX], [TXn])
            if not last:
                Yn, TYn, _ = r_Y.next()
                py = mm8(lambda h, X=X: X[:, h, :], lambda h, Mn=Mn: Mn[:, h, :], [TX, TMn], 1)
                ev("dve", py, lambda e, bv, hs, Yn=Yn, Y=Y: e.tensor_tensor(Yn[:, hs, :], bv, Y[:, hs, :], ALU.add), [TY], [TYn])
                Mj, TM, Y, TY = Mn, TMn, Yn, TYn
            Nj, TN, X, TX = Nn, TNn, Xn, TXn
            yield
        vb, Tvb, _ = r_vb.next()
        kbg, Tkbg, _ = r_kbg.next()
        kdc, Tkdc, _ = r_kd.next()
        P.op("pool", lambda e: e.tensor_tensor(vb[:], vtm[:], bc_h(beta_c), ALU.mult), reads=[Tv, Tg], writes=[Tvb])
        P.op("pool", lambda e: e.tensor_tensor(kbg[:], ktm[:], bc_h(gd["bg"][:, ch, :]), ALU.mult), reads=[Tk, Tg], writes=[Tkbg])
        P.op("pool", lambda e: e.tensor_tensor(kdc[:], ktm[:], bc_h(gd["kd"][:, ch, :]), ALU.mult), reads=[Tk, Tg], writes=[Tkdc])
        Xb, TXb, _ = r_Xb.next()
        P.op("act", lambda e, X=X: e.copy(Xb[:], X[:]), reads=[TX], writes=[TXb])
        pu = mm8(lambda h: Xb[:, h, :], lambda h: vb[:, h, :], [TXb, Tvb], 0)
        u, Tu, _ = r_u.next()
        ev("act", pu, lambda e, bv, hs: e.copy(u[:, hs, :], bv), [], [Tu])
        pw = mm8(lambda h: kbg[:, h, :], lambda h: Xb[:, h, :], [TXb, Tkbg], 0)
        wT, TwT, _ = r_wT.next()
        ev("act", pw, lambda e, bv, hs: e.copy(wT[:, hs, :], bv), [], [TwT])
        C.update(u=u, Tu=Tu, wT=wT, TwT=TwT, kdc=kdc, Tkdc=Tkdc, gd=gd, Tg=Tg)
        if full:
            C.update(qd=qd, Tqd=Tqd, qk=qk, Tqk=Tqk)
        yield


    def scan(C):
        ch, full, final, tokc0 = C['ch'], C['full'], C['final'], C['tokc0']
        u, Tu, wT, TwT, kdc, Tkdc, gd, Tg = C['u'], C['Tu'], C['wT'], C['TwT'], C['kdc'], C['Tkdc'], C['gd'], C['Tg']
        if full:
            qd, Tqd, qk, Tqk = C['qd'], C['Tqd'], C['qk'], C['Tqk']
        if C.get('pre') is not None:
            C['pre']()
        pws = mm8(lambda h: wT[:, h, :], lambda h: Sb[:, h, :], [TwT, T_Sb], 1)
        vn, Tvn, _ = r_vn.next()
        ev("dve", pws, lambda e, bv, hs: e.tensor_tensor(vn[:, hs, :], u[:, hs, :], bv, ALU.subtract), [Tu], [Tvn])
        yield
        if full:
            po = mm8(lambda h: qd[:, h, :], lambda h: Sb[:, h, :], [Tqd, T_Sb], 1,
                     extra=(lambda h: qk[:, h, :], lambda h: vn[:, h, :], [Tqk, Tvn]))
            o, To, so = r_o.next()
            if not final:
                ev("act", po, lambda e, bv, hs: e.copy(o[:, hs, :], bv), [], [To])
                P.dma("act", S["of_d"][tokc0 + ch * 128:tokc0 + (ch + 1) * 128, :], o[:].rearrange("p h f -> p (h f)"), so, reads=[To], writes=[T_of.setdefault(tokc0 + ch * 128, Tile("of"))])
            else:
                P.dma("sp", o[:].rearrange("p h f -> p (h f)"), S["of_d"][tokc0 + ch * 128:tokc0 + (ch + 1) * 128, :], so, reads=[T_of[tokc0 + ch * 128]], writes=[To])
                ev("dve", po, lambda e, bv, hs: e.tensor_tensor(o[:, hs, :], o[:, hs, :], bv, ALU.add), [To], [To])
        yield
        pds = mm8(lambda h: kdc[:, h, :], lambda h: vn[:, h, :], [Tkdc, Tvn], 1)
        for h in range(H):
            b, Tb = pds[h // 4]
            P.op("dve", lambda e, h=h, b=b: e.scalar_tensor_tensor(St[:, h, :], St[:, h, :], gd["cd"][:, ch, h:h + 1], b[:, (h % 4) * 128:(h % 4 + 1) * 128], ALU.mult, ALU.add),
                 reads=[Tb, Tg, T_S], writes=[T_S])
        P.op("act", lambda e: e.copy(Sb[:], St[:]), reads=[T_S], writes=[T_Sb])
        yield
        if full and final:
            z, Tz, sz = r_z.next()
            P.dma("sp", z[:], S["z_d"][tokc0 + ch * 128:tokc0 + (ch + 1) * 128, :], sz, writes=[Tz])
            sq, Tsq, _ = r_sq.next()
            st, Tst, _ = r_st.next()
            on, Ton, _ = r_on.next()
            P.op("pool", lambda e: e.tensor_tensor(sq[:], o[:], o[:], ALU.mult), reads=[To], writes=[Tsq])
            P.op("dve", lambda e: e.tensor_reduce(out=st[:, 0:8], in_=sq[:], axis=AX.X, op=ALU.add), reads=[Tsq], writes=[Tst])
            P.op("act", lambda e: e.activation(out=st[:, 8:16], in_=st[:, 0:8], func=AF.Sqrt, bias=EPS, scale=1.0 / 128), reads=[Tst], writes=[Tst])
            P.op("dve", lambda e: e.reciprocal(st[:, 8:16], st[:, 8:16]), reads=[Tst], writes=[Tst])
            P.op("act", lambda e: e.activation(out=z[:], in_=z[:], func=AF.Silu), reads=[Tz], writes=[Tz])
            P.op("dve", lambda e: e.tensor_tensor(sq[:], o[:], bc_h(st[:, 8:16]), ALU.mult), reads=[To, Tst, Tsq], writes=[Tsq])
            P.op("pool", lambda e: e.tensor_tensor(sq[:], sq[:], gon[:].unsqueeze(1).to_broadcast([128, H, 128]), ALU.mult), reads=[Tsq, T_c], writes=[Tsq])
            P.op("dve", lambda e: e.tensor_tensor(on[:], sq[:], v8(z[:]), ALU.mult), reads=[Tsq, Tz], writes=[Ton])
            yield
            ps, Tp = K.ps(0)
            psb = ps[:].bitcast(BF16)
            for h in range(H):
                P.op("pe", lambda e, h=h, psb=psb: e.transpose(psb[:, h * 128:(h + 1) * 128], on[:, h, :], K.identb[:]), reads=[Ton, K.T_const], writes=[Tp])
            obT, TobT, sob = r_obT.next()
            P.op("act", lambda e, psb=psb: e.copy(obT[:].rearrange("p h f -> p (h f)"), psb), reads=[Tp], writes=[TobT])
            P.dma("act", catT[1024:2048, tokc0 + ch * 128:tokc0 + (ch + 1) * 128].rearrange("(h p) t -> p h t", p=128), obT[:], sob, reads=[TobT])

        if C.get('post') is not None:
            C['post']()
        yield


    def set_state(src):
        ss_ = K.getsem()
        if src is None:
            P.op("pool", lambda e: e.memset(St[:], 0.0), reads=[T_Sb], writes=[T_S])
        else:
            P.dma("sp", St[:], src.rearrange("h k v -> k h v"), ss_, reads=[T_Sb], writes=[T_S])
        P.op("act", lambda e: e.copy(Sb[:], St[:]), reads=[T_S], writes=[T_Sb])

    def save_state(dst):
        ss_ = K.getsem()
        P.dma("sp", dst.rearrange("h k v -> k h v"), St[:], ss_, reads=[T_S])

    jobs = []

    def job(G, tokd0, ch, dr, full, final, tokc0, pre=None, post=None):
        jobs.append(dict(G=G, tokd0=tokd0, ch=ch, dr=dr, full=full, final=final, tokc0=tokc0, pre=pre, post=post))

    for s_ in range(4):
        G = gates(s_ * 256, 2)
        job(G, s_ * 256, 0, 0, True, False, s_ * 256, pre=lambda: set_state(None))
        job(G, s_ * 256, 1, 0, True, False, s_ * 256, post=lambda s_=s_: save_state(K.dout["nbf"][s_]))
        job(G, s_ * 256, 1, 1, True, True, s_ * 256, pre=lambda: set_state(None))
        job(G, s_ * 256, 0, 1, True, True, s_ * 256, post=lambda s_=s_: save_state(K.dout["nbb"][s_]))
    G = gates(1024, 32)
    for ch in range(SEXT):
        job(G, 1024, ch, 0, True, False, 1024, pre=(lambda: set_state(K.din["s0"][0])) if ch == 0 else None)
    for ch in range(31, -1, -1):
        job(G, 1024, ch, 1, ch < SEXT, True, 1024, pre=(lambda: set_state(K.din["s0"][1])) if ch == 31 else None)
    prev = None
    for C in jobs + [None]:
        gp = prep(C) if C is not None else iter(())
        gs = scan(prev) if prev is not None else iter(())
        a_done = b_done = False
        while not (a_done and b_done):
            if not a_done:
                try:
                    next(gp)
                except StopIteration:
                    a_done = True
            if not b_done:
                try:
                    next(gs)
                except StopIteration:
                    b_done = True
        prev = C
    K.end()


def phase_mlp(K, l):
    nc, P = K.nc, K.P
    S = K.dscr
    ntb = NCAT if l == 0 else NPT + 16
    groups = token_groups(ntb, breaks=(NPT,))
    if l == 0:
        x1_d = K.scr("x1_d", [TOKC, D], F32)
        oT_d, Wo, W1, W2 = S["catT_d"], K.din["ab_w_out"], K.din["w_mlp_in"][0], K.din["w_mlp_out"][0]
        xsrc = lambda g: K.din["xp"][g * 128:(g + 1) * 128, :] if g < NPT else K.din["xs"][(g - NPT) * 128:(g - NPT + 1) * 128, :]
    else:
        oT_d, Wo, W1, W2 = S["o1T_d"], K.din["c_w_out"], K.din["w_mlp_in"][1], K.din["w_mlp_out"][1]
        xsrc = lambda g: S["x1_d"][g * 128:(g + 1) * 128, :]
    K.begin()
    actT = K.sb("actT", [128, KC, 512], BF16)
    T_act = [[Tile() for kc in range(KC)] for i in range(4)]
    xres = K.sb("xres", [128, 4, D], F32)
    T_x = [Tile() for i in range(4)]
    uT = K.sb("uT", [128, 64, 512], BF16)
    T_u = [Tile() for fc in range(64)]
    gate = [K.sb(f"gate{i}", [128, D], F32) for i in range(2)]
    T_gate = Tile("gate")
    wr = Ring(K, "wm", [128, KC, 512], BF16, 3, sw=True)
    tr = Ring(K, "tg", [128, 512], F32, 1)
    rr = Ring(K, "rl", [128, 512], F32, 2)
    nt = NormT(K)
    sx = [K.getsem() for i in range(4)]
    sg, so = K.getsem(), K.getsem()
    if l == 1:
        fng = K.sb("fng", [128, D], F32)
        fss = Ring(K, "fss", [128, 2], F32, 2)
        T_fng = Tile("fng")
        P.dma("sp", fng[:], K.din["final_norm"].partition_broadcast(128), sg, writes=[T_fng])
    cur_c = None
    for (t0, m) in groups:
        n = m * 128
        c = 0 if t0 < NPT else 1
        if c != cur_c:
            P.dma("sp", gate[0][:], S["modrow_d"][l, c:c + 1, 2 * D:3 * D].partition_broadcast(128), sg, writes=[T_gate])
            P.dma("sp", gate[1][:], S["modrow_d"][l, c:c + 1, 5 * D:6 * D].partition_broadcast(128), sg, writes=[T_gate])
            cur_c = c
        P.dma("sp", actT[:, :, 0:n], oT_d[:, t0 * 128:t0 * 128 + n].rearrange("(fc p) t -> p fc t", p=128), so,
              writes=[T_act[i][kc] for i in range(m) for kc in range(KC)])
        for i in range(m):
            P.dma("sp", xres[:, i, :], xsrc(t0 + i), sx[i], writes=[T_x[i]])

        def second(Wsrc, nfq, lhs_fn, lhs_tiles_fn, gi):
            for dg in range(4):
                banks = [K.ps(1 - dg % 2) for i in range(m)]
                for fq in range(nfq):
                    wt, Tw, sw = wr.next()
                    P.dma("pool", wt[:], Wsrc[fq * 2048:(fq + 1) * 2048, dg * 512:(dg + 1) * 512].rearrange("(kc p) n -> p kc n", p=128), sw, writes=[Tw])
                    for i in range(m):
                        b, Tb = banks[i]
                        for kc in range(KC):
                            fc = fq * KC + kc
                            P.op("pe", lambda e, b=b, i=i, fc=fc, kc=kc, wt=wt: e.matmul(b[:, :], lhs_fn(fc, i), wt[:, kc, :], start=(fc == 0), stop=(fc == nfq * KC - 1)),
                                 reads=[Tw] + lhs_tiles_fn(fc, i), writes=[Tb])
                for i in range(m):
                    b, Tb = banks[i]
                    tt, Tt, _ = tr.next()
                    P.op("dve", lambda e, b=b, tt=tt, dg=dg: e.tensor_tensor(tt[:], b[:, :], gate[gi][:, dg * 512:(dg + 1) * 512], ALU.mult), reads=[Tb, T_gate], writes=[Tt])
                    P.op("dve", lambda e, i=i, tt=tt, dg=dg: e.tensor_tensor(xres[:, i, dg * 512:(dg + 1) * 512], xres[:, i, dg * 512:(dg + 1) * 512], tt[:], ALU.add), reads=[Tt, T_x[i]], writes=[T_x[i]])

        second(Wo, 1, lambda fc, i: actT[:, fc, i * 128:(i + 1) * 128], lambda fc, i: [T_act[i][fc]], 0)
        for i in range(m):
            nt.run(xres[:, i, :], T_x[i], K.gsF[l][1][c], K.modF[l][c][:, 3 * KC:4 * KC],
                   lambda kc, i=i: actT[:, kc, i * 128:(i + 1) * 128], lambda kc, i=i: T_act[i][kc])
        for fg in range(16):
            wt, Tw, sw = wr.next()
            P.dma("pool", wt[:], W1[:, fg * 512:(fg + 1) * 512].rearrange("(kc p) n -> p kc n", p=128), sw, writes=[Tw])
            for sub in range(4):
                fc = fg * 4 + sub
                ps, Tp = K.ps(0)
                for kc in range(KC):
                    P.op("pe", lambda e, ps=ps, kc=kc, sub=sub, wt=wt, n=n: e.matmul(ps[:, 0:n], wt[:, kc, sub * 128:(sub + 1) * 128], actT[:, kc, 0:n], start=(kc == 0), stop=(kc == KC - 1)),
                         reads=[Tw] + [T_act[i][kc] for i in range(m)], writes=[Tp])
                r, Tr, _ = rr.next()
                P.op("act", lambda e, ps=ps, r=r, n=n: e.activation(out=r[:, 0:n], in_=ps[:, 0:n], func=AF.Relu), reads=[Tp], writes=[Tr])
                P.op("dve", lambda e, r=r, fc=fc, n=n: e.tensor_tensor(uT[:, fc, 0:n], r[:, 0:n], r[:, 0:n], ALU.mult), reads=[Tr], writes=[T_u[fc]])
        second(W2, 4, lambda fc, i: uT[:, fc, i * 128:(i + 1) * 128], lambda fc, i: [T_u[fc]], 1)
        for i in range(m):
            g = t0 + i
            if l == 0:
                P.dma("sp", x1_d[g * 128:(g + 1) * 128, :], xres[:, i, :], sx[i], reads=[T_x[i]])
            else:
                ss, Tss, _ = fss.next()
                fjk, T_fjk, _ = nt.xn.next()
                P.op("act", lambda e, i=i, ss=ss, fjk=fjk: e.activation(out=fjk[:], in_=xres[:, i, :], func=AF.Square, accum_out=ss[:, 0:1]), reads=[T_x[i]], writes=[T_fjk, Tss])
                P.op("act", lambda e, ss=ss: e.activation(out=ss[:, 1:2], in_=ss[:, 0:1], func=AF.Sqrt, bias=EPS, scale=1.0 / D), reads=[Tss], writes=[Tss])
                P.op("dve", lambda e, ss=ss: e.reciprocal(ss[:, 1:2], ss[:, 1:2]), reads=[Tss], writes=[Tss])
                P.op("dve", lambda e, i=i, ss=ss: e.scalar_tensor_tensor(xres[:, i, :], xres[:, i, :], ss[:, 1:2], fng[:], ALU.mult, ALU.mult), reads=[T_x[i], Tss, T_fng], writes=[T_x[i]])
                dst = K.dout["y_p"][g * 128:(g + 1) * 128, :] if g < NPT else K.dout["y_s"][(g - NPT) * 128:(g - NPT + 1) * 128, :]
                P.dma("sp", dst, xres[:, i, :], sx[i], reads=[T_x[i]])
    K.end()


def phase_l1_inproj(K):
    nc, P = K.nc, K.P
    S = K.dscr
    q1T = K.scr("q1T_d", [D, TOKC], BF16)
    k1T = K.scr("k1T_d", [256, TOKC], BF16)
    v1 = K.scr("v1_d", [TOKC, 256], BF16)
    K.begin()
    ntb = NCAT
    groups = token_groups(ntb, breaks=(NPT,))
    hT = K.sb("h1T", [128, KC, ntb * 128], BF16)
    T_h = [[Tile() for kc in range(KC)] for g in range(ntb)]
    xr = Ring(K, "x1in", [128, D], F32, 2)
    nt = NormT(K)
    for g in range(ntb):
        c = 0 if g < NPT else 1
        xt, T_x, sx = xr.next()
        P.dma("sp", xt[:], S["x1_d"][g * 128:(g + 1) * 128, :], sx, writes=[T_x])
        nt.run(xt[:], T_x, K.gsF[1][0][c], K.modF[1][c][:, 0:KC],
               lambda kc, g=g: hT[:, kc, g * 128:(g + 1) * 128], lambda kc, g=g: T_h[g][kc])
    cosT = K.sb("cosT", [128, SEXT * 128], F32)
    sinT = K.sb("sinT", [128, SEXT * 128], F32)
    perm = K.sb("perm", [128, 128], F32)
    T_rc = Tile("ropec")
    sr = K.getsem()
    P.dma("sp", cosT[:], K.din["rope_cos"], sr, writes=[T_rc])
    P.dma("sp", sinT[:], K.din["rope_sin"], sr, writes=[T_rc])
    P.dma("sp", perm[:], K.din["rope_perm"], sr, writes=[T_rc])
    wr = Ring(K, "w1q", [128, KC, 512], BF16, 2, sw=True)
    q32r = Ring(K, "q32", [128, 512], F32, 2)
    t1r = Ring(K, "rt1", [128, 512], F32, 2)
    st16 = Ring(K, "s16", [128, 512], BF16, 3)
    st32 = Ring(K, "s32", [128, 512], F32, 2)
    W = K.din["c_w_qkv"]
    for t in range(5):
        wt, Tw, sw = wr.next()
        P.dma("pool", wt[:], W[:, t * 512:(t + 1) * 512].rearrange("(kc p) n -> p kc n", p=128), sw, writes=[Tw])
        nsub = 4 if t < 4 else 2
        for sub in range(nsub):
            dst, row0 = (q1T, t * 512 + sub * 128) if t < 4 else (k1T, sub * 128)
            for (t0, m) in groups:
                n = m * 128
                ps, Tp = K.ps(0)
                for kc in range(KC):
                    P.op("pe", lambda e, ps=ps, kc=kc, sub=sub, wt=wt, t0=t0, n=n: e.matmul(ps[:, 0:n], wt[:, kc, sub * 128:(sub + 1) * 128], hT[:, kc, t0 * 128:t0 * 128 + n], start=(kc == 0), stop=(kc == KC - 1)),
                         reads=[Tw] + [T_h[t0 + i][kc] for i in range(m)], writes=[Tp])
                sg, Ts, ss_ = st16.next()
                if t0 < NPT:
                    P.op("act", lambda e, ps=ps, sg=sg, n=n: e.copy(sg[:, 0:n], ps[:, 0:n]), reads=[Tp], writes=[Ts])
                else:
                    s0 = (t0 - NPT) * 128
                    q32, Tq, _ = q32r.next()
                    t1, Tt1, _ = t1r.next()
                    P.op("act", lambda e, ps=ps, q32=q32, n=n: e.copy(q32[:, 0:n], ps[:, 0:n]), reads=[Tp], writes=[Tq])
                    ps2, Tp2 = K.ps(1)
                    P.op("pe", lambda e, ps2=ps2, q32=q32, n=n: e.matmul(ps2[:, 0:n], perm[:], q32[:, 0:n], start=True, stop=True), reads=[Tq, T_rc], writes=[Tp2])
                    P.op("pool", lambda e, q32=q32, t1=t1, n=n, s0=s0: e.tensor_tensor(t1[:, 0:n], q32[:, 0:n], cosT[:, s0:s0 + n], ALU.mult), reads=[Tq, T_rc], writes=[Tt1])
                    P.op("dve", lambda e, ps2=ps2, q32=q32, n=n, s0=s0: e.tensor_tensor(q32[:, 0:n], ps2[:, 0:n], sinT[:, s0:s0 + n], ALU.mult), reads=[Tp2, T_rc, Tt1], writes=[Tq])
                    P.op("dve", lambda e, q32=q32, t1=t1, sg=sg, n=n: e.tensor_tensor(sg[:, 0:n], q32[:, 0:n], t1[:, 0:n], ALU.add), reads=[Tq, Tt1], writes=[Ts])
                P.dma("sp", dst[row0:row0 + 128, t0 * 128:t0 * 128 + n], sg[:, 0:n], ss_, reads=[Ts])
        if t == 4:
            for g in range(ntb):
                ps, Tp = K.ps(0)
                for kc in range(KC):
                    P.op("pe", lambda e, ps=ps, kc=kc, g=g, wt=wt: e.matmul(ps[:, :], hT[:, kc, g * 128:(g + 1) * 128], wt[:, kc, :], start=(kc == 0), stop=(kc == KC - 1)),
                         reads=[Tw, T_h[g][kc]], writes=[Tp])
                sg, Ts, ss_ = st16.next()
                P.op("act", lambda e, ps=ps, sg=sg: e.copy(sg[:, 0:256], ps[:, 256:512]), reads=[Tp], writes=[Ts])
                P.dma("sp", v1[g * 128:(g + 1) * 128, :], sg[:, 0:256], ss_, reads=[Ts])
                if g < NPT:
                    s32, Ts32, ss32 = st32.next()
                    P.op("dve", lambda e, ps=ps, s32=s32: e.tensor_copy(s32[:], ps[:, :]), reads=[Tp], writes=[Ts32])
                    P.dma("sp", K.dout["nck"][g * 128:(g + 1) * 128, :], s32[:, 0:256], ss32, reads=[Ts32])
                    P.dma("sp", K.dout["ncv"][g * 128:(g + 1) * 128, :], s32[:, 256:512], ss32, reads=[Ts32])
    K.end()


def phase_attn_c(K):
    nc, P = K.nc, K.P
    S = K.dscr
    o1T = K.scr("o1T_d", [D, TOKC], BF16)
    scale = 64 ** -0.5
    K.begin()
    pt_ring = Ring(K, "pt", [128, 512], BF16, 4)
    rec_ring = Ring(K, "rec", [128, 512], F32, 2)
    pools = (pt_ring, rec_ring)
    snk = K.sb("snk", [128, 32], F32)
    trib = K.sb("trib", [128, 2, 128], BF16)
    ck_tm = K.sb("cck", [128, 2, 256], BF16)
    cvt = K.sb("ccv", [128, 2, 256], BF16)
    ckT = K.sb("cckT", [64, 4, 256], BF16)
    T_c, T_ck, T_cv, T_ckT = P.tiles(4, "ac")
    s0, sw0 = K.getsem(), K.getsem(True)
    P.dma("sp", snk[:], K.din["c_sink"].partition_broadcast(128), s0, writes=[T_c])
    P.op("act", lambda e: e.activation(out=snk[:], in_=snk[:], func=AF.Exp), reads=[T_c], writes=[T_c])
    P.dma("pool", trib[:], K.din["tri"][0:2].rearrange("m k c -> k m c"), sw0, writes=[T_c])
    P.dma("pool", ck_tm[:], K.din["cache_c_k"].rearrange("(c p) f -> p c f", p=128), sw0, writes=[T_ck])
    P.dma("pool", cvt[:], K.din["cache_c_v"].rearrange("(c p) f -> p c f", p=128), sw0, writes=[T_cv])
    for c in range(2):
        ps, Tp = K.ps(0)
        psb = ps[:].bitcast(BF16)
        for kh in range(4):
            P.op("pe", lambda e, c=c, kh=kh, psb=psb: e.transpose(psb[0:64, kh * 128:(kh + 1) * 128], ck_tm[:, c, kh * 64:(kh + 1) * 64], K.identb[:]), reads=[T_ck, K.T_const], writes=[Tp])
        P.op("dve", lambda e, c=c, psb=psb: e.tensor_copy(ckT[:, :, c * 128:(c + 1) * 128], psb[0:64, 0:512].rearrange("p (h k) -> p h k", h=4)), reads=[Tp], writes=[T_ckT])
    qr = Ring(K, "pq", [64, 8, 256], BF16, 2)
    kr = Ring(K, "pk", [64, 256], BF16, 2)
    vr = Ring(K, "pv", [128, 2, 64], BF16, 2)
    orr = Ring(K, "po", [64, 8, 256], BF16, 2)
    for kh in range(4):
        for s in range(4):
            qt, Tq, sq = qr.next()
            kt, Tk, sk = kr.next()
            vt, Tv, sv = vr.next()
            ot, To, so = orr.next()
            P.dma("sp", qt[:], S["q1T_d"][kh * 512:(kh + 1) * 512, s * 256:(s + 1) * 256].rearrange("(g d) t -> d g t", d=64), sq, writes=[Tq])
            P.dma("sp", kt[:], S["k1T_d"][kh * 64:(kh + 1) * 64, s * 256:(s + 1) * 256], sk, writes=[Tk])
            P.dma("sp", vt[:], S["v1_d"][s * 256:(s + 1) * 256, kh * 64:(kh + 1) * 64].rearrange("(c p) f -> p c f", p=128), sv, writes=[Tv])
            for g in range(8):
                sl = [(kt[:, c * 128:(c + 1) * 128], [Tk], vt[:, c, :], [Tv], None, [], 0) for c in range(2)]
                hq = kh * 8 + g
                attn_core(K, sl, 256, (qt[:, g, :], [Tq]), ot[:, g, :], To, scale, extra_den=(snk[0:64, hq:hq + 1], [T_c]), pools=pools)
            P.dma("sp", o1T[kh * 512:(kh + 1) * 512, s * 256:(s + 1) * 256].rearrange("(g d) t -> d g t", d=64), ot[:], so, reads=[To])
    NT = SEXT * 128
    kh_k = Ring(K, "sk", [64, NT], BF16, 2)
    kh_v = Ring(K, "sv", [128, SEXT, 64], BF16, 2)
    kh_q = Ring(K, "sq", [64, 8, 2048], BF16, 1)
    kh_o = Ring(K, "so", [64, 8, 2048], BF16, 1)
    for kh in range(4):
        kt, Tk, sk = kh_k.next()
        vt, Tv, sv = kh_v.next()
        qt, Tq, sq = kh_q.next()
        ot, To, so = kh_o.next()
        P.dma("sp", kt[:], S["k1T_d"][kh * 64:(kh + 1) * 64, 1024:1024 + NT], sk, writes=[Tk])
        P.dma("sp", vt[:], S["v1_d"][1024:1024 + NT, kh * 64:(kh + 1) * 64].rearrange("(c p) f -> p c f", p=128), sv, writes=[Tv])
        P.dma("sp", qt[:], S["q1T_d"][kh * 512:(kh + 1) * 512, 1024:1024 + 2048].rearrange("(g d) t -> d g t", d=64), sq, writes=[Tq])
        for i in range(16):
            for gh in range(2):
                sl = []
                for j in (i - 1, i, i + 1):
                    if j < 0 or j > 16:
                        continue
                    mask = None
                    if j == i - 1:
                        mask = trib[:, 0, :].unsqueeze(1).to_broadcast([128, 4, 128])
                    elif j == i + 1:
                        mask = trib[:, 1, :].unsqueeze(1).to_broadcast([128, 4, 128])
                    sl.append((kt[:, j * 128:(j + 1) * 128], [Tk], vt[:, j, :], [Tv], mask, [T_c], 0))
                for c in range(2):
                    sl.append((ckT[:, kh, c * 128:(c + 1) * 128], [T_ckT], cvt[:, c, kh * 64:(kh + 1) * 64], [T_cv], None, [], 0))
                hq0 = kh * 8 + gh * 4
                ed = snk[0:64, hq0:hq0 + 4].unsqueeze(2).to_broadcast([64, 4, 128])
                attn_core(K, sl, 512, (qt[:, gh * 4:gh * 4 + 4, i * 128:(i + 1) * 128], [Tq]), ot[:, gh * 4:gh * 4 + 4, i * 128:(i + 1) * 128], To, scale,
                          extra_den=(ed, [T_c]), pools=pools, g4=True)
        P.dma("sp", o1T[kh * 512:(kh + 1) * 512, 1024:1024 + 2048].rearrange("(g d) t -> d g t", d=64), ot[:], so, reads=[To])
    K.end()


def phase_attn_a_and_prep(K):
    K.begin()
    phase_attn_a(K)
    phase_dn_prep(K)
    K.end()


def declare_io(K):
    K.inp("ident", [128, 128])
    K.inp("cond", [2, D])
    K.inp("xp", [NPT * 128, D])
    K.inp("xs", [4096, D])
    K.inp("w_ada", [2, D, 6 * D])
    K.inp("b_ada", [2, 6 * D])
    K.inp("norm_mix", [2, D])
    K.inp("norm_mlp", [2, D])
    K.inp("ab_w_in", [D, 7200])
    K.inp("w_gates", [D, 32])
    K.inp("cache_a_k", [256, 1024])
    K.inp("cache_a_v", [256, 1024])
    K.inp("na_mask", [2, 8, 6, 128, 256])
    K.inp("conv_w", [3, 3072])
    K.inp("alog_dt", [2, 16])
    K.inp("tri", [4, 128, 128])
    K.inp("s0", [2, 8, 128, 128])
    K.inp("onorm", [1, 128])
    K.inp("ab_w_out", [D, D])
    K.inp("w_mlp_in", [2, D, 4 * D])
    K.inp("w_mlp_out", [2, 4 * D, D])
    K.inp("c_w_qkv", [D, 2560])
    K.inp("c_w_out", [D, D])
    K.inp("cache_c_k", [256, 256])
    K.inp("cache_c_v", [256, 256])
    K.inp("c_sink", [1, 32])
    K.inp("final_norm", [1, D])
    K.inp("rope_cos", [128, SEXT * 128])
    K.inp("rope_sin", [128, SEXT * 128])
    K.inp("rope_perm", [128, 128])
    K.outp("nck", [NPT * 128, 256])
    K.outp("ncv", [NPT * 128, 256])
    K.outp("y_p", [NPT * 128, D])
    K.outp("y_s", [2048, D])
    K.outp("nbf", [4, 8, 128, 128])
    K.outp("nbb", [4, 8, 128, 128])
    K.outp("nak", [NPT * 128, 1024])
    K.outp("nav", [NPT * 128, 1024])


def build(stop=99, debug=()):
    nc = bass.Bass("TRN2", target_bir_lowering=False)
    K = Ctx(nc)
    declare_io(K)
    phases = [phase_consts, phase_ada, lambda K: phase_l0_inproj(K, 1), lambda K: phase_l0_inproj(K, 2), phase_attn_a_and_prep, phase_dn_scan, lambda K: phase_mlp(K, 0), phase_l1_inproj, phase_attn_c, lambda K: phase_mlp(K, 1)]
    for i, ph in enumerate(phases):
        if i >= stop:
            break
        ph(K)
    if debug:
        K.begin()
        s = K.getsem()
        for name in debug:
            src = K.dscr[name]
            o = K.outp("dbg_" + name, src.shape, src.dtype)
            nr = src.shape[0]
            step = max(1, min(nr, (1 << 20) // (src.shape[1] * 4)))
            for r0 in range(0, nr, step):
                K.P.dma("sp", o[r0:min(nr, r0 + step)], src[r0:min(nr, r0 + step)], s)
        K.end()
    K.pes.close()
    return nc, K


def na_mask_host(rel_bias, flip):
    out = np.full((2, 8, 6, 128, 256), -30000.0, np.float32)
    qq = np.arange(256)
    kk = np.arange(768)
    for cl in range(2):
        qr = (0 if cl == 0 else 12) + qq // 64
        qc = qq % 64
        kr = (0 if cl == 0 else 8) + kk // 64
        kc = kk % 64
        if flip:
            qr, qc, kr, kc = 63 - qr, 63 - qc, 63 - kr, 63 - kc
        rs = np.clip(qr - 4, 0, 56)
        cs = np.clip(qc - 8, 0, 48)
        vr = (kr[:, None] >= rs[None, :]) & (kr[:, None] < rs[None, :] + 8)
        vc = (kc[:, None] >= cs[None, :]) & (kc[:, None] < cs[None, :] + 16)
        valid = vr & vc
        dr = np.clip(kr[:, None] - qr[None, :] + 7, 0, 14)
        dc = np.clip(kc[:, None] - qc[None, :] + 15, 0, 30)
        for h in range(8):
            b = rel_bias[h][dr, dc]
            m = np.where(valid, b, np.float32(-30000.0)).astype(np.float32)
            out[cl, h] = m.reshape(6, 128, 256)
    return out


def tri_host():
    i = np.arange(128)[:, None]
    j = np.arange(128)[None, :]
    return np.stack([(j <= i), (j >= i), (j < i), (j > i)]).astype(np.float32)


def rope_host(flip):
    nf = 16
    inv = (10000.0 ** (-np.arange(nf, dtype=np.float32) / nf)).astype(np.float32)
    tloc = np.arange(SEXT * 128)
    tok = (4095 - tloc) if flip else tloc
    pos = np.stack([tok // 64, tok % 64], 0).astype(np.float32)
    d = np.arange(128) % 64
    a, b, fidx = d // 32, (d % 32) // 16, d % 16
    ang = pos[a, :] * inv[fidx][:, None]
    cos = np.cos(ang).astype(np.float32)
    sin = np.sin(ang).astype(np.float32)
    perm = np.zeros((128, 128), np.float32)
    for m in range(128):
        bm = (m % 32) // 16
        partner = m + 16 if bm == 0 else m - 16
        perm[partner, m] = -1.0 if bm == 0 else 1.0
    return cos, sin, perm


def host_inputs(inputs, c):
    seq, flip = c // 2, c % 2
    f = lambda a: np.ascontiguousarray(a, dtype=np.float32)
    xp = inputs["x_prompt"][4 * c:4 * c + 4]
    xs = inputs["x_sample"][seq]
    if flip:
        xp = xp[:, ::-1]
        xs = xs[::-1]
    wg = inputs["ab_w_in"][0][:, 7168:7200]
    if flip:
        wg = np.concatenate([wg[:, 8:16], wg[:, 0:8], wg[:, 24:32], wg[:, 16:24]], axis=1)
    m = {
        "ident": np.eye(128, dtype=np.float32),
        "cond": f(np.stack([inputs["c_ctx"], inputs["c"][seq]])),
        "xp": f(xp.reshape(NPT * 128, D)),
        "xs": f(xs),
        "w_ada": inputs["w_ada"], "b_ada": inputs["b_ada"],
        "norm_mix": inputs["norm_mix"], "norm_mlp": inputs["norm_mlp"],
        "ab_w_in": inputs["ab_w_in"][0], "w_gates": f(wg),
        "cache_a_k": f(inputs["cache_a_k"][seq, 0].reshape(256, 1024)),
        "cache_a_v": f(inputs["cache_a_v"][seq, 0].reshape(256, 1024)),
        "na_mask": na_mask_host(inputs["a_rel_bias"][0], flip),
        "ab_w_out": inputs["ab_w_out"][0], "w_mlp_in": inputs["w_mlp_in"], "w_mlp_out": inputs["w_mlp_out"],
        "c_w_qkv": inputs["c_w_qkv"][0], "c_w_out": inputs["c_w_out"][0],
        "cache_c_k": f(inputs["cache_c_k"][seq, 0].reshape(256, 256)), "cache_c_v": f(inputs["cache_c_v"][seq, 0].reshape(256, 256)),
        "c_sink": f(inputs["c_sink"][0].reshape(1, 32)), "final_norm": f(inputs["final_norm"].reshape(1, D)),
        "rope_cos": rope_host(flip)[0], "rope_sin": rope_host(flip)[1], "rope_perm": rope_host(flip)[2],
        "conv_w": f(inputs["b_conv"][0][::-1] if flip else inputs["b_conv"][0]),
        "alog_dt": f(np.stack([(inputs["b_a_log"][0][::-1] if flip else inputs["b_a_log"][0]).reshape(16),
                               (inputs["b_dt_bias"][0][::-1] if flip else inputs["b_dt_bias"][0]).reshape(16)])),
        "tri": tri_host(),
        "s0": f(np.stack([inputs["state_b_bwd"][seq, 0], inputs["state_b_fwd"][seq, 0]]) if flip else
                np.stack([inputs["state_b_fwd"][seq, 0], inputs["state_b_bwd"][seq, 0]])),
        "onorm": f(inputs["b_out_norm"][0].reshape(1, 128)),
    }
    return m


_CACHE = {}


def kernel(**inputs):
    inputs = {k: np.asarray(v) for k, v in inputs.items()}
    if "nc" not in _CACHE:
        _CACHE["nc"] = build()
    nc, K = _CACHE["nc"]
    maps = [host_inputs(inputs, c) for c in range(8)]
    res = run_bass_kernel_spmd(nc, maps, core_ids=list(range(8))).results
    f32 = np.float32
    y_p = np.zeros((32, 256, D), f32)
    y_s = np.zeros((4, 4096, D), f32)
    nak = np.zeros((32, 1, 256, 8, 128), f32)
    nav = np.zeros((32, 1, 256, 8, 128), f32)
    nbf = np.zeros((32, 1, 8, 128, 128), f32)
    nbb = np.zeros((32, 1, 8, 128, 128), f32)
    nck = np.zeros((32, 1, 256, 4, 64), f32)
    ncv = np.zeros((32, 1, 256, 4, 64), f32)
    for c in range(8):
        seq, flip = c // 2, c % 2
        r = {k: np.asarray(v, dtype=f32) for k, v in res[c].items()}
        fl = (lambda a: a[:, ::-1]) if flip else (lambda a: a)
        sl = slice(4 * c, 4 * c + 4)
        y_p[sl] = fl(r["y_p"].reshape(4, 256, D))
        if flip:
            y_s[seq, 2048:4096] = r["y_s"][::-1]
        else:
            y_s[seq, 0:2048] = r["y_s"]
        nak[sl, 0] = fl(r["nak"].reshape(4, 256, 8, 128))
        nav[sl, 0] = fl(r["nav"].reshape(4, 256, 8, 128))
        nck[sl, 0] = fl(r["nck"].reshape(4, 256, 4, 64))
        ncv[sl, 0] = fl(r["ncv"].reshape(4, 256, 4, 64))
        if flip:
            nbf[sl, 0], nbb[sl, 0] = r["nbb"], r["nbf"]
        else:
            nbf[sl, 0], nbb[sl, 0] = r["nbf"], r["nbb"]
    return (y_p, y_s, nak, nav, nbf, nbb, nck, ncv)
```

```python
import contextlib
import numpy as np
import concourse.bass as bass
import concourse.mybir as mybir

F32 = mybir.dt.float32
BF16 = mybir.dt.bfloat16
I32 = mybir.dt.int32
AF = mybir.ActivationFunctionType
ALU = mybir.AluOpType
AX = mybir.AxisListType

ENGS = ("pe", "act", "dve", "pool", "sp")
HANDLES = {"pe": "tensor", "act": "scalar", "dve": "vector", "pool": "gpsimd", "sp": "sync"}
SEM_LIMIT = 16000


class Tile:
    __slots__ = ("name", "writer", "readers", "excl")

    def __init__(self, name=""):
        self.name = name
        self.writer = None
        self.readers = {}
        self.excl = False


class DSem:
    def __init__(self, prog, name):
        self.prog = prog
        self.name = name
        self.gen = 0
        self.h = prog.nc.alloc_semaphore(name=name)
        self.count = 0
        self.last = None

    def bump(self):
        if self.count + 16 > SEM_LIMIT:
            self.gen += 1
            self.h = self.prog.nc.alloc_semaphore(name=f"{self.name}_g{self.gen}")
            self.count = 0
        self.count += 16
        return self.h, self.count


class Ins:
    __slots__ = ("eng", "fn", "deps", "sem", "count", "needed", "is_dma", "epoch")

    def __init__(self, eng, fn, is_dma=False):
        self.eng = eng
        self.fn = fn
        self.deps = []
        self.sem = None
        self.count = None
        self.needed = False
        self.is_dma = is_dma
        self.epoch = 0


class Prog:
    def __init__(self, nc):
        self.nc = nc
        self.lists = {e: [] for e in ENGS}
        self.esem = {e: nc.alloc_semaphore(name=f"es_{e}_0") for e in ENGS}
        self.esem_gen = {e: 0 for e in ENGS}
        self.ecount = {e: 0 for e in ENGS}
        self.known = {e: {} for e in ENGS}
        self.epoch = 0
        self.n_ins = 0
        self.dsems = []
        self.last_ins = {e: None for e in ENGS}

    def tile(self, name=""):
        return Tile(name)

    def tiles(self, n, name=""):
        return [Tile(f"{name}{i}") for i in range(n)]

    def dsem(self, name):
        d = DSem(self, name)
        self.dsems.append(d)
        return d

    def _add(self, eng, fn, reads, writes, dsem=None):
        ins = Ins(eng, fn, is_dma=dsem is not None)
        ins.epoch = self.epoch
        deps = []
        for t in reads:
            if t.writer is not None:
                deps.append(t.writer)
            if t.excl:
                deps.extend(r for r in t.readers.values() if r.eng != eng)
        for t in writes:
            if t.writer is not None:
                deps.append(t.writer)
            deps.extend(t.readers.values())
        if dsem is not None:
            if dsem.last is not None:
                deps.append(dsem.last)
            ins.sem, ins.count = dsem.bump()
            ins.needed = True
            dsem.last = ins
        out = []
        seen = set()
        for d in deps:
            if d is ins or id(d) in seen:
                continue
            seen.add(id(d))
            if d.epoch < self.epoch:
                continue
            if eng == "pe" and d.eng == "pe" and not d.is_dma and dsem is None:
                continue
            out.append(d)
        ins.deps = out
        for d in out:
            d.needed = True
        for t in reads:
            key = (eng, dsem.name) if dsem is not None else eng
            t.readers[key] = ins
        for t in writes:
            t.writer = ins
            t.readers = {}
        self.lists[eng].append(ins)
        self.last_ins[eng] = ins
        self.n_ins += 1
        return ins

    def op(self, eng, fn, reads=(), writes=()):
        return self._add(eng, fn, list(reads), list(writes))

    def dma(self, eng, out, in_, dsem, reads=(), writes=()):
        return self._add(eng, lambda e: e.dma_start(out=out, in_=in_), list(reads), list(writes), dsem=dsem)

    def barrier(self):
        deps = [i for i in self.last_ins.values() if i is not None]
        deps += [d.last for d in self.dsems if d.last is not None]
        deps = [d for d in deps if d.epoch == self.epoch]
        for d in deps:
            d.needed = True
        for e in ENGS:
            ins = Ins(e, None)
            ins.epoch = self.epoch
            ins.deps = list(deps)
            self.lists[e].append(ins)

    def flush(self, final=False):
        self.barrier()
        for e in ENGS:
            for ins in self.lists[e]:
                if ins.is_dma or ins.fn is None:
                    continue
                if ins.needed:
                    if self.ecount[e] + 1 > SEM_LIMIT:
                        self.esem_gen[e] += 1
                        self.esem[e] = self.nc.alloc_semaphore(name=f"es_{e}_{self.esem_gen[e]}")
                        self.ecount[e] = 0
                    self.ecount[e] += 1
                    ins.sem = self.esem[e]
                    ins.count = self.ecount[e]
        prog = self

        def run(e, h):
            known = prog.known[e]
            for ins in prog.lists[e]:
                need = {}
                for d in ins.deps:
                    k = id(d.sem)
                    if known.get(k, 0) >= d.count:
                        continue
                    if k not in need or need[k][1] < d.count:
                        need[k] = (d.sem, d.count)
                for k, (s, v) in need.items():
                    h.wait_ge(s, v)
                    known[k] = v
                if ins.fn is None:
                    continue
                bi = ins.fn(h)
                if ins.is_dma:
                    bi.then_inc(ins.sem, 16)
                elif ins.needed:
                    bi.then_inc(ins.sem, 1)

        with self.nc.Block() as block:
            for e in ENGS:
                if not self.lists[e]:
                    continue
                dec = getattr(block, HANDLES[e])

                def mk(e):
                    def _f(h):
                        run(e, h)
                    return _f
                dec(mk(e))
        self.lists = {e: [] for e in ENGS}
        self.epoch += 1

from concourse.bass_utils import run_bass_kernel_spmd

D = 2048
KC = 16
EPS = 1e-6
NPT = 8
NS1 = 19
NG1 = NPT + NS1
NS2 = 13
TOK1 = NG1 * 128
TOKB = (NG1 + NS2) * 128
SEXT = 17


class Ring:
    def __init__(self, K, name, shape, dt, n, sw=False):
        self.bufs = [K.sb(f"{name}{i}", shape, dt) for i in range(n)]
        self.tiles = [Tile(f"{name}{i}") for i in range(n)]
        self.sems = [K.getsem(sw) for i in range(n)]
        self.i = 0

    def next(self):
        k = self.i % len(self.bufs)
        self.i += 1
        return self.bufs[k], self.tiles[k], self.sems[k]


class Ctx:
    def __init__(self, nc):
        self.nc = nc
        self.P = Prog(nc)
        self.es = None
        self.pes = contextlib.ExitStack()
        self.uid = 0
        self.sem_pool = {False: [], True: []}
        self.sem_used = {False: [], True: []}
        self.PS = [nc.alloc_psum_tensor(f"psb{i}", [128, 512], F32) for i in range(8)]
        self.TPS = [Tile(f"ps{i}") for i in range(8)]
        for t_ in self.TPS:
            t_.excl = True
        self.psi = [0, 0]
        self.depth = 0
        self.din = {}
        self.dout = {}
        self.dscr = {}

    def inp(self, name, shape, dt=F32):
        self.din[name] = self.nc.dram_tensor(name, list(shape), dt, kind="ExternalInput").ap()
        return self.din[name]

    def outp(self, name, shape, dt=F32):
        self.dout[name] = self.nc.dram_tensor(name, list(shape), dt, kind="ExternalOutput").ap()
        return self.dout[name]

    def scr(self, name, shape, dt):
        self.dscr[name] = self.nc.dram_tensor(name, list(shape), dt).ap()
        return self.dscr[name]

    def begin(self):
        if self.es is not None:
            self.depth += 1
            return
        self.es = contextlib.ExitStack()

    def end(self):
        if self.depth > 0:
            self.depth -= 1
            return
        self.P.flush()
        self.es.close()
        self.es = None
        for k in (False, True):
            self.sem_pool[k].extend(self.sem_used[k])
            self.sem_used[k] = []

    def sb(self, name, shape, dt):
        self.uid += 1
        return self.es.enter_context(self.nc.sbuf_tensor(f"{name}_{self.uid}", list(shape), dt))

    def psb(self, name, shape, dt):
        self.uid += 1
        return self.pes.enter_context(self.nc.sbuf_tensor(f"{name}_{self.uid}", list(shape), dt))

    def getsem(self, sw=False):
        if self.sem_pool[sw]:
            s = self.sem_pool[sw].pop()
        else:
            self.uid += 1
            s = self.P.dsem(f"ds{'w' if sw else 'h'}{self.uid}")
        self.sem_used[sw].append(s)
        return s

    def dump(self, name, ap, tiles):
        import os
        if not os.environ.get("DN_DEBUG") or name in self.dscr:
            return
        d = self.scr(name, list(ap.shape), ap.dtype)
        self.P.dma("sp", d, ap, self.getsem(), reads=tiles)

    def ps(self, g=0):
        k = g * 4 + self.psi[g] % 4
        self.psi[g] += 1
        return self.PS[k], self.TPS[k]


def rows_T(K, dst, T_dst, src2d, n, stage, T_stage, sem):
    P = K.P
    P.dma("sp", stage[0:n, :], src2d, sem, writes=[T_stage])
    ps, Tp = K.ps()
    P.op("pe", lambda e: e.transpose(ps[:, 0:n], stage[0:n, :], K.identf[0:n, 0:n]), reads=[T_stage, K.T_const], writes=[Tp])
    P.op("dve", lambda e: e.tensor_copy(dst, ps[:, 0:n]), reads=[Tp], writes=[T_dst])


def phase_consts(K):
    P = K.P
    K.begin()
    K.T_const = Tile("const")
    K.identf = K.psb("identf", [128, 128], F32)
    K.identb = K.psb("identb", [128, 128], BF16)
    K.onesf = K.psb("onesf", [128, 128], F32)
    K.onesb = K.psb("onesb", [128, 128], BF16)
    s = K.getsem()
    s2 = K.getsem(True)
    P.dma("sp", K.identf[:], K.din["ident"], s, writes=[K.T_const])
    P.dma("pool", K.identb[:], K.din["ident"], s2, writes=[K.T_const])
    P.op("dve", lambda e: e.memset(K.onesf[:], 1.0), writes=[K.T_const])
    P.op("dve", lambda e: e.memset(K.onesb[:], 1.0), writes=[K.T_const])
    K.end()


def phase_ada(K):
    nc, P = K.nc, K.P
    modrow_d = K.scr("modrow_d", [2, 2, 6 * D], F32)
    K.begin()
    cond = K.sb("cond", [2, D], F32)
    cs = K.sb("cs", [2, D], F32)
    condT = K.sb("condT", [128, KC, 2], BF16)
    brow = K.sb("brow", [2, 6 * D], F32)
    modrow = K.sb("modrow", [2, 6 * D], F32)
    T_cond, T_cs, T_condT, T_brow, T_modrow, T_mrd = P.tiles(6, "ada")
    s0, s1 = K.getsem(), K.getsem()
    wr = Ring(K, "adaw", [128, KC, 512], BF16, 3, sw=True)
    P.dma("sp", cond[:], K.din["cond"], s0, writes=[T_cond])
    P.op("act", lambda e: e.activation(out=cs[:], in_=cond[:], func=AF.Silu), reads=[T_cond], writes=[T_cs])
    ps, Tp = K.ps()
    for kc in range(KC):
        P.op("pe", lambda e, kc=kc, ps=ps: e.transpose(ps[:, kc * 2:kc * 2 + 2], cs[0:2, kc * 128:(kc + 1) * 128], K.identf[0:2, 0:2]),
             reads=[T_cs, K.T_const], writes=[Tp])
    P.op("dve", lambda e, ps=ps: e.tensor_copy(condT[:].rearrange("p k c -> p (k c)"), ps[:, 0:2 * KC]), reads=[Tp], writes=[T_condT])
    for l in range(2):
        P.dma("sp", brow[:], K.din["b_ada"][l:l + 1, :].partition_broadcast(2), s0, writes=[T_brow])
        for cg in range(24):
            wt, Tw, sw = wr.next()
            P.dma("pool", wt[:], K.din["w_ada"][l, :, cg * 512:(cg + 1) * 512].rearrange("(kc p) n -> p kc n", p=128), sw, writes=[Tw])
            ps, Tp = K.ps()
            for kc in range(KC):
                P.op("pe", lambda e, kc=kc, ps=ps, wt=wt: e.matmul(ps[0:2, :], condT[:, kc, :], wt[:, kc, :], start=(kc == 0), stop=(kc == KC - 1)),
                     reads=[T_condT, Tw], writes=[Tp])
            P.op("dve", lambda e, ps=ps, cg=cg: e.tensor_tensor(modrow[0:2, cg * 512:(cg + 1) * 512], ps[0:2, :], brow[0:2, cg * 512:(cg + 1) * 512], ALU.add),
                 reads=[Tp, T_brow], writes=[T_modrow])
        P.dma("sp", modrow_d[l], modrow[:], s1, reads=[T_modrow], writes=[T_mrd])
    K.end()
    K.begin()
    K.T_mod = Tile("mod")
    K.modF = [[K.psb(f"modF{l}{c}", [128, 96], F32) for c in range(2)] for l in range(2)]
    K.gsF = [[[K.psb(f"gsF{l}{w}{c}", [128, KC], F32) for c in range(2)] for w in range(2)] for l in range(2)]
    K.fnorm = K.psb("fnormF", [128, KC], F32)
    stage = K.sb("stg", [128, 128], F32)
    gF = K.sb("gF", [128, KC], F32)
    T_stage, T_g = P.tiles(2, "adaf")
    s0 = K.getsem()
    for l in range(2):
        for c in range(2):
            rows_T(K, K.modF[l][c][:], K.T_mod, modrow_d[l, c].rearrange("(r p) -> r p", p=128), 96, stage, T_stage, s0)
        for w, nm in enumerate(("norm_mix", "norm_mlp")):
            rows_T(K, gF[:], T_g, K.din[nm][l].rearrange("(r p) -> r p", p=128), KC, stage, T_stage, s0)
            for c in range(2):
                sc = K.modF[l][c][:, (1 + 3 * w) * KC:(2 + 3 * w) * KC]
                P.op("dve", lambda e, l=l, w=w, c=c, sc=sc: e.scalar_tensor_tensor(K.gsF[l][w][c][:], sc, 1.0, gF[:], ALU.add, ALU.mult),
                     reads=[K.T_mod, T_g], writes=[K.T_mod])
    K.end()


class NormT:
    def __init__(self, K, n=2):
        self.K = K
        self.ss = Ring(K, "nss", [128, 2], F32, n)
        self.xn = Ring(K, "nxn", [128, D], BF16, n)

    def run(self, xt, T_x, gs, shift, dst_fn, T_dst_fn):
        K = self.K
        P = K.P
        ss, T_ss, _ = self.ss.next()
        xn, T_xn, _ = self.xn.next()
        P.op("act", lambda e: e.activation(out=xn[:], in_=xt, func=AF.Square, accum_out=ss[:, 0:1]), reads=[T_x], writes=[T_xn, T_ss])
        import os
        NTL = int(os.environ.get("NT_LEVEL", "9"))
        if NTL < 2:
            return
        P.op("act", lambda e: e.activation(out=ss[:, 1:2], in_=ss[:, 0:1], func=AF.Sqrt, bias=EPS, scale=1.0 / D), reads=[T_ss], writes=[T_ss])
        P.op("dve", lambda e: e.reciprocal(ss[:, 1:2], ss[:, 1:2]), reads=[T_ss], writes=[T_ss])
        if NTL < 3:
            return
        P.op("act", lambda e: e.activation(out=xn[:], in_=xt, func=AF.Identity, scale=ss[:, 1:2]), reads=[T_x, T_ss], writes=[T_xn])
        if NTL < 4:
            return
        for half in range(2):
            ps, Tp = K.ps()
            psb = ps[:].bitcast(BF16)
            for j in range(8):
                kc = half * 8 + j
                P.op("pe", lambda e, j=j, kc=kc, psb=psb: e.transpose(psb[:, j * 128:(j + 1) * 128], xn[:, kc * 128:(kc + 1) * 128], K.identb[:]),
                     reads=[T_xn, K.T_const], writes=[Tp])
            for j in range(8):
                kc = half * 8 + j
                if True:
                    P.op("dve", lambda e, j=j, kc=kc, psb=psb: e.tensor_scalar(dst_fn(kc), psb[:, j * 128:(j + 1) * 128], gs[:, kc:kc + 1], shift[:, kc:kc + 1], ALU.mult, ALU.add),
                         reads=[Tp, K.T_mod], writes=[T_dst_fn(kc)])
                else:
                    P.op("act", lambda e, j=j, kc=kc, psb=psb: e.activation(out=dst_fn(kc), in_=psb[:, j * 128:(j + 1) * 128], func=AF.Identity, scale=gs[:, kc:kc + 1], bias=shift[:, kc:kc + 1]),
                         reads=[Tp, K.T_mod], writes=[T_dst_fn(kc)])


def token_groups(n_tb, breaks=()):
    out = []
    pts = [0] + list(breaks) + [n_tb]
    for a, b in zip(pts[:-1], pts[1:]):
        t = a
        while t < b:
            m = min(4, b - t)
            out.append((t, m))
            t += m
    return out


def phase_l0_inproj(K, which_pass):
    nc, P = K.nc, K.P
    if which_pass == 1:
        K.scr("qaT_d", [1024, TOK1], BF16)
        K.scr("kaT_d", [1024, TOK1], BF16)
        K.scr("va_d", [TOK1, 1024], BF16)
        K.scr("qkvT_d", [3072, TOKB], F32)
        K.scr("z_d", [TOK1, 1024], F32)
        K.scr("gates_d", [TOKB, 32], F32)
        ntb = NG1
        srcs = [(K.din["xp"][g * 128:(g + 1) * 128, :], 0) for g in range(NPT)] + \
               [(K.din["xs"][g * 128:(g + 1) * 128, :], 1) for g in range(NS1)]
        tok0 = 0
        groups = token_groups(NG1, breaks=(NPT,))
    else:
        ntb = NS2
        srcs = [(K.din["xs"][(NS1 + g) * 128:(NS1 + g + 1) * 128, :], 1) for g in range(NS2)]
        tok0 = TOK1
        groups = token_groups(NS2)
    K.begin()
    hT = K.sb("hT", [128, KC, ntb * 128], BF16)
    T_h = [[Tile(f"h{g}_{kc}") for kc in range(KC)] for g in range(ntb)]
    xr = Ring(K, "xin", [128, D], F32, 3)
    nt = NormT(K)
    for g, (src, c) in enumerate(srcs):
        xt, T_x, sx = xr.next()
        P.dma("sp", xt[:], src, sx, writes=[T_x])
        nt.run(xt[:], T_x, K.gsF[0][0][c], K.modF[0][c][:, 0:KC],
               lambda kc, g=g: hT[:, kc, g * 128:(g + 1) * 128], lambda kc, g=g: T_h[g][kc])
    wr = Ring(K, "w0", [128, KC, 512], BF16, 3, sw=True)
    st32 = Ring(K, "st32", [128, 512], F32, 3)
    st16 = Ring(K, "st16", [128, 512], BF16, 3)
    W = K.din["ab_w_in"]
    evi = [0]

    def evac(dst, src, T_src, T_dst):
        evi[0] += 1
        if evi[0] % 2:
            P.op("dve", lambda e: e.tensor_copy(dst, src), reads=[T_src], writes=[T_dst])
        else:
            P.op("act", lambda e: e.copy(dst, src), reads=[T_src], writes=[T_dst])

    def fm(wt, Tw, ncol_blocks, dst_d, row0, f32):
        for sub in range(ncol_blocks):
            for (t0, m) in groups:
                n = m * 128
                ps, Tp = K.ps()
                for kc in range(KC):
                    P.op("pe", lambda e, kc=kc, ps=ps, sub=sub, t0=t0, n=n: e.matmul(ps[:, 0:n], wt[:, kc, sub * 128:(sub + 1) * 128], hT[:, kc, t0 * 128:t0 * 128 + n], start=(kc == 0), stop=(kc == KC - 1)),
                         reads=[Tw] + [T_h[t0 + i][kc] for i in range(m)], writes=[Tp])
                sg, Ts, ss_ = (st32 if f32 else st16).next()
                evac(sg[:, 0:n], ps[:, 0:n], Tp, Ts)
                P.dma("sp", dst_d[row0 + sub * 128:row0 + (sub + 1) * 128, tok0 + t0 * 128:tok0 + t0 * 128 + n], sg[:, 0:n], ss_, reads=[Ts])

    def tm(wt, Tw, ncols, tbs, dests):
        for g in tbs:
            ps, Tp = K.ps()
            for kc in range(KC):
                P.op("pe", lambda e, kc=kc, ps=ps, g=g: e.matmul(ps[:, 0:ncols], hT[:, kc, g * 128:(g + 1) * 128], wt[:, kc, 0:ncols], start=(kc == 0), stop=(kc == KC - 1)),
                     reads=[Tw, T_h[g][kc]], writes=[Tp])
            for (dfn, f32) in dests:
                d = dfn(g)
                if d is None:
                    continue
                sg, Ts, ss_ = (st32 if f32 else st16).next()
                evac(sg[:, 0:ncols], ps[:, 0:ncols], Tp, Ts)
                P.dma("sp", d, sg[:, 0:ncols], ss_, reads=[Ts])

    wg32 = K.sb("wg32", [128, KC, 32], F32)
    T_wg32 = Tile("wg32")
    s_wg = K.getsem()

    def loadw(src, ncols=512):
        wt, Tw, sw = wr.next()
        if ncols == 512:
            P.dma("pool", wt[:, :, 0:ncols], src.rearrange("(kc p) n -> p kc n", p=128), sw, writes=[Tw])
        else:
            P.dma("sp", wg32[:], src.rearrange("(kc p) n -> p kc n", p=128), s_wg, writes=[T_wg32])
            P.op("dve", lambda e, wt=wt: e.tensor_copy(wt[:, :, 0:ncols], wg32[:]), reads=[T_wg32], writes=[Tw])
        return wt, Tw

    S = K.dscr
    if which_pass == 1:
        import os
        for t in range(int(os.environ.get('KDBG_NT', '14'))):
            wt, Tw = loadw(W[:, t * 512:(t + 1) * 512])
            if t < 2:
                fm(wt, Tw, 4, S["qaT_d"], t * 512, False)
            elif t < 4:
                fm(wt, Tw, 4, S["kaT_d"], (t - 2) * 512, False)
                tm(wt, Tw, 512, range(NPT), [(lambda g, t=t: K.dout["nak"][g * 128:(g + 1) * 128, (t - 2) * 512:(t - 1) * 512], True)])
            elif t < 6:
                tm(wt, Tw, 512, range(NG1), [(lambda g, t=t: S["va_d"][g * 128:(g + 1) * 128, (t - 4) * 512:(t - 3) * 512], False),
                                            (lambda g, t=t: K.dout["nav"][g * 128:(g + 1) * 128, (t - 4) * 512:(t - 3) * 512] if g < NPT else None, True)])
            elif t < 12:
                fm(wt, Tw, 4, S["qkvT_d"], (t - 6) * 512, True)
            else:
                tm(wt, Tw, 512, range(NG1), [(lambda g, t=t: S["z_d"][g * 128:(g + 1) * 128, (t - 12) * 512:(t - 11) * 512], True)])
        if int(os.environ.get('KDBG_G', '1')):
          wt, Tw = loadw(K.din["w_gates"], 32)
          tm(wt, Tw, 32, range(NG1), [(lambda g: S["gates_d"][g * 128:(g + 1) * 128, :], True)])
    else:
        for t in range(8, 12):
            wt, Tw = loadw(W[:, t * 512:(t + 1) * 512])
            fm(wt, Tw, 4, S["qkvT_d"], (t - 6) * 512, True)
        wt, Tw = loadw(K.din["w_gates"], 32)
        tm(wt, Tw, 32, range(NS2), [(lambda g: S["gates_d"][tok0 + g * 128:tok0 + (g + 1) * 128, :], True)])
    K.end()


NCAT = NPT + SEXT
TOKC = NCAT * 128


def attn_core(K, S_list, nq, rhs_q, out_ap, T_out, scale, extra_den=None, pools=None, g4=False):
    P = K.P
    q_ap, q_tiles = rhs_q
    M = out_ap.shape[0]
    psn, Tn = K.ps(1)
    psd, Td = K.ps(1)
    pt_ring, rec_ring = pools
    n = len(S_list)
    v3 = (lambda ap: ap.rearrange("p (g t) -> p g t", g=4)) if g4 else (lambda ap: ap)
    def s_mm(i):
        lk, tk = S_list[i][0], S_list[i][1]
        pss, Ts = K.ps(0)
        P.op("pe", lambda e, pss=pss, lk=lk: e.matmul(v3(pss[:, 0:nq]), lk, q_ap, start=True, stop=True), reads=tk + q_tiles, writes=[Ts])
        return pss, Ts

    nxt = s_mm(0)
    for i, (lk, tk, lv, tv, mask, tm_, _) in enumerate(S_list):
        pss, Ts = nxt
        if i + 1 < n:
            nxt = s_mm(i + 1)
        pt, Tpt, _ = pt_ring.next()
        P.op("act", lambda e, pss=pss, pt=pt: e.activation(out=pt[:, 0:nq], in_=pss[:, 0:nq], func=AF.Exp, scale=scale), reads=[Ts], writes=[Tpt])
        if mask is not None:
            P.op("pool", lambda e, pt=pt, mask=mask: e.tensor_tensor(v3(pt[:, 0:nq]), v3(pt[:, 0:nq]), mask, ALU.mult), reads=[Tpt] + tm_, writes=[Tpt])
        P.op("pe", lambda e, pt=pt, lv=lv, i=i: e.matmul(psn[0:M, 0:nq], lv, pt[:, 0:nq], start=(i == 0), stop=(i == n - 1)), reads=tv + [Tpt], writes=[Tn])
        P.op("pe", lambda e, pt=pt, i=i: e.matmul(psd[0:M, 0:nq], K.onesb[:, 0:M], pt[:, 0:nq], start=(i == 0), stop=(i == n - 1)), reads=[K.T_const, Tpt], writes=[Td])
    rec, Trec, _ = rec_ring.next()
    if extra_den is not None:
        ed, ted = extra_den
        if g4:
            P.op("dve", lambda e: e.tensor_tensor(v3(rec[0:M, 0:nq]), v3(psd[0:M, 0:nq]), ed, ALU.add), reads=[Td] + ted, writes=[Trec])
        else:
            P.op("dve", lambda e: e.tensor_scalar_add(rec[0:M, 0:nq], psd[0:M, 0:nq], ed), reads=[Td] + ted, writes=[Trec])
        P.op("dve", lambda e: e.reciprocal(rec[0:M, 0:nq], rec[0:M, 0:nq]), reads=[Trec], writes=[Trec])
    else:
        P.op("dve", lambda e: e.reciprocal(rec[0:M, 0:nq], psd[0:M, 0:nq]), reads=[Td], writes=[Trec])
    P.op("dve", lambda e: e.tensor_tensor(out_ap, v3(psn[0:M, 0:nq]), v3(rec[0:M, 0:nq]), ALU.mult), reads=[Tn, Trec], writes=[T_out])


def phase_attn_a(K):
    nc, P = K.nc, K.P
    S = K.dscr
    catT = K.scr("catT_d", [D, TOKC], BF16)
    scale = 128 ** -0.5
    K.begin()
    pt_ring = Ring(K, "pt", [128, 256], BF16, 4)
    rec_ring = Ring(K, "rec", [128, 256], F32, 2)
    pools = (pt_ring, rec_ring)
    qr = Ring(K, "cq", [128, 8, 256], BF16, 2)
    kr = Ring(K, "ck", [128, 8, 256], BF16, 2)
    vr = Ring(K, "cv", [128, 2, 1024], BF16, 2)
    orr = Ring(K, "co", [128, 8, 256], BF16, 2)
    for s in range(4):
        qt, Tq, sq = qr.next()
        kt, Tk, sk = kr.next()
        vt, Tv, sv = vr.next()
        ot, To, so = orr.next()
        P.dma("sp", qt[:], S["qaT_d"][:, s * 256:(s + 1) * 256].rearrange("(h p) t -> p h t", p=128), sq, writes=[Tq])
        P.dma("sp", kt[:], S["kaT_d"][:, s * 256:(s + 1) * 256].rearrange("(h p) t -> p h t", p=128), sk, writes=[Tk])
        P.dma("sp", vt[:], S["va_d"][s * 256:(s + 1) * 256, :].rearrange("(c p) f -> p c f", p=128), sv, writes=[Tv])
        for h in range(8):
            sl = [(kt[:, h, c * 128:(c + 1) * 128], [Tk], vt[:, c, h * 128:(h + 1) * 128], [Tv], None, [], 0) for c in range(2)]
            attn_core(K, sl, 256, (qt[:, h, :], [Tq]), ot[:, h, :], To, scale, pools=pools)
        P.dma("sp", catT[0:1024, s * 256:(s + 1) * 256].rearrange("(h p) t -> p h t", p=128), ot[:], so, reads=[To])
    ck_tm = K.sb("ck_tm", [128, 2, 1024], BF16)
    cvt = K.sb("cvt", [128, 2, 1024], BF16)
    ckT = K.sb("ckT", [128, 8, 256], BF16)
    T_ck, T_cv, T_ckT, T_eb = P.tiles(4, "na")
    sw0 = K.getsem(True)
    P.dma("pool", ck_tm[:], K.din["cache_a_k"].rearrange("(c p) f -> p c f", p=128), sw0, writes=[T_ck])
    P.dma("pool", cvt[:], K.din["cache_a_v"].rearrange("(c p) f -> p c f", p=128), sw0, writes=[T_cv])
    for c in range(2):
        ps, Tp = K.ps(0)
        psb = ps[:].bitcast(BF16)
        for h in range(8):
            P.op("pe", lambda e, c=c, h=h, psb=psb: e.transpose(psb[:, h * 128:(h + 1) * 128], ck_tm[:, c, h * 128:(h + 1) * 128], K.identb[:]), reads=[T_ck, K.T_const], writes=[Tp])
        P.op("dve", lambda e, c=c, psb=psb: e.tensor_copy(ckT[:, :, c * 128:(c + 1) * 128], psb.rearrange("p (h k) -> p h k", h=8)), reads=[Tp], writes=[T_ckT])
    EB = K.sb("EB", [128, 2, 8, 6, 256], BF16)
    mr = Ring(K, "mstage", [128, 6, 256], F32, 2)
    for cl in range(2):
        for h in range(8):
            mt, Tm, sm = mr.next()
            P.dma("sp", mt[:], K.din["na_mask"][cl, h].rearrange("c k q -> k c q"), sm, writes=[Tm])
            P.op("act", lambda e, cl=cl, h=h, mt=mt: e.activation(out=EB[:, cl, h, :, :], in_=mt[:], func=AF.Exp), reads=[Tm], writes=[T_eb])
    NQ = SEXT * 128
    NK = NS1 * 128
    qh = Ring(K, "nq", [128, NQ], BF16, 2)
    kh = Ring(K, "nk", [128, NK], BF16, 2)
    vh = Ring(K, "nv", [128, NS1, 128], BF16, 2)
    oh = Ring(K, "no", [128, NQ], BF16, 2)
    for h in range(8):
        qt, Tq, sq = qh.next()
        kt, Tk, sk = kh.next()
        vt, Tv, sv = vh.next()
        ot, To, so = oh.next()
        P.dma("sp", qt[:], S["qaT_d"][h * 128:(h + 1) * 128, 1024:1024 + NQ], sq, writes=[Tq])
        P.dma("sp", kt[:], S["kaT_d"][h * 128:(h + 1) * 128, 1024:1024 + NK], sk, writes=[Tk])
        P.dma("sp", vt[:], S["va_d"][1024:1024 + NK, h * 128:(h + 1) * 128].rearrange("(c p) f -> p c f", p=128), sv, writes=[Tv])
        for i in range(9):
            nq = 256 if i < 8 else 128
            cl = 0 if i == 0 else 1
            base = 0 if i == 0 else (i - 1) * 256
            nch = 6 if i < 8 else 5
            sl = []
            for ch in range(nch):
                t0 = base + ch * 128
                sl.append((kt[:, t0:t0 + 128], [Tk], vt[:, t0 // 128, :], [Tv], EB[:, cl, h, ch, 0:nq], [T_eb], 0))
            for c in range(2):
                sl.append((ckT[:, h, c * 128:(c + 1) * 128], [T_ckT], cvt[:, c, h * 128:(h + 1) * 128], [T_cv], None, [], 0))
            attn_core(K, sl, nq, (qt[:, i * 256:i * 256 + nq], [Tq]), ot[:, i * 256:i * 256 + nq], To, scale, pools=pools)
        P.dma("sp", catT[h * 128:(h + 1) * 128, 1024:1024 + NQ], ot[:], so, reads=[To])
    K.end()


def phase_dn_prep(K):
    nc, P = K.nc, K.P
    S = K.dscr
    K.scr("qnT_d", [1024, TOK1], BF16)
    K.scr("knT_d", [1024, TOKB], BF16)
    K.scr("ktm_d", [TOKB, 1024], BF16)
    K.scr("vtm_d", [TOKB, 1024], BF16)
    K.begin()
    cwF = K.sb("cwF", [128, 3, 24], F32)
    stage = K.sb("cstg", [128, 128], F32)
    T_cw, T_stage = P.tiles(2, "cw")
    s0 = K.getsem()
    for j in range(3):
        rows_T(K, cwF[:, j, :], T_cw, K.din["conv_w"][j].rearrange("(r p) -> r p", p=128), 24, stage, T_stage, s0)
    pieces = [(s * 256, 256, True, True, 256) for s in range(4)]
    for i in range(8):
        nqv = 512 if i < 4 else (256 if i == 4 else 0)
        pieces.append((1024 + i * 512, 512, i == 0, i == 7, nqv))
    xr = Ring(K, "dx", [128, 514], F32, 3)
    yr = Ring(K, "dy", [128, 512], F32, 2)
    sqr = Ring(K, "dsq", [128, 512], F32, 2)
    rsr = Ring(K, "drs", [128, 512], F32, 2)
    ynr = Ring(K, "dyn", [128, 512], BF16, 3)
    tmr = Ring(K, "dtm", [128, 4, 128], BF16, 3)
    for fb in range(24):
        kind, h = fb // 8, fb % 8
        for (tok0, n0, le, re_, nq) in pieces:
            n = nq if kind == 0 else n0
            if n == 0:
                continue
            re2 = re_ and n == n0
            x, Tx, sx = xr.next()
            a = 1 if le else 0
            b = n + 1 if re2 else n + 2
            if le:
                P.op("pool", lambda e, x=x: e.memset(x[:, 0:1], 0.0), writes=[Tx])
            if re2:
                P.op("pool", lambda e, x=x, n=n: e.memset(x[:, n + 1:n + 2], 0.0), writes=[Tx])
            P.dma("sp", x[:, a:b], S["qkvT_d"][fb * 128:(fb + 1) * 128, tok0 - 1 + a:tok0 - 1 + b], sx, writes=[Tx])
            y, Ty, _ = yr.next()
            P.op("pool", lambda e, x=x, y=y, n=n, fb=fb: e.tensor_scalar_mul(y[:, 0:n], x[:, 0:n], cwF[:, 0, fb:fb + 1]), reads=[Tx, T_cw], writes=[Ty])
            P.op("dve", lambda e, x=x, y=y, n=n, fb=fb: e.scalar_tensor_tensor(y[:, 0:n], x[:, 1:n + 1], cwF[:, 1, fb:fb + 1], y[:, 0:n], ALU.mult, ALU.add), reads=[Tx, T_cw, Ty], writes=[Ty])
            P.op("dve", lambda e, x=x, y=y, n=n, fb=fb: e.scalar_tensor_tensor(y[:, 0:n], x[:, 2:n + 2], cwF[:, 2, fb:fb + 1], y[:, 0:n], ALU.mult, ALU.add), reads=[Tx, T_cw, Ty], writes=[Ty])
            P.op("act", lambda e, y=y, n=n: e.activation(out=y[:, 0:n], in_=y[:, 0:n], func=AF.Silu), reads=[Ty], writes=[Ty])
            yn, Tyn, syn = ynr.next()
            if kind < 2:
                sq, Tsq, _ = sqr.next()
                rs, Trs, _ = rsr.next()
                P.op("pool", lambda e, y=y, sq=sq, n=n: e.tensor_tensor(sq[:, 0:n], y[:, 0:n], y[:, 0:n], ALU.mult), reads=[Ty], writes=[Tsq])
                ps, Tp = K.ps(0)
                P.op("pe", lambda e, ps=ps, sq=sq, n=n: e.matmul(ps[:, 0:n], K.onesf[:], sq[:, 0:n], start=True, stop=True), reads=[Tsq, K.T_const], writes=[Tp])
                P.op("act", lambda e, ps=ps, rs=rs, n=n: e.activation(out=rs[:, 0:n], in_=ps[:, 0:n], func=AF.Sqrt, bias=EPS, scale=1.0), reads=[Tp], writes=[Trs])
                P.op("dve", lambda e, rs=rs, n=n: e.reciprocal(rs[:, 0:n], rs[:, 0:n]), reads=[Trs], writes=[Trs])
                cc = 128 ** -0.5 if kind == 0 else 1.0
                P.op("dve", lambda e, y=y, rs=rs, yn=yn, n=n, cc=cc: e.scalar_tensor_tensor(yn[:, 0:n], y[:, 0:n], cc, rs[:, 0:n], ALU.mult, ALU.mult), reads=[Ty, Trs], writes=[Tyn])
                dst = S["qnT_d"] if kind == 0 else S["knT_d"]
                P.dma("sp", dst[h * 128:(h + 1) * 128, tok0:tok0 + n], yn[:, 0:n], syn, reads=[Tyn])
            else:
                P.op("pool", lambda e, y=y, yn=yn, n=n: e.tensor_copy(yn[:, 0:n], y[:, 0:n]), reads=[Ty], writes=[Tyn])
            if kind >= 1:
                nb = n // 128
                ps, Tp = K.ps(0)
                psb = ps[:].bitcast(BF16)
                for j in range(nb):
                    P.op("pe", lambda e, j=j, psb=psb, yn=yn: e.transpose(psb[:, j * 128:(j + 1) * 128], yn[:, j * 128:(j + 1) * 128], K.identb[:]), reads=[Tyn, K.T_const], writes=[Tp])
                tm_, Ttm, stm = tmr.next()
                P.op("act", lambda e, psb=psb, tm_=tm_, nb=nb: e.copy(tm_[:, 0:nb, :], psb[:, 0:nb * 128].rearrange("p (j f) -> p j f", f=128)), reads=[Tp], writes=[Ttm])
                dst = S["ktm_d"] if kind == 1 else S["vtm_d"]
                P.dma("sp", dst[tok0:tok0 + n, h * 128:(h + 1) * 128].rearrange("(j p) f -> p j f", p=128), tm_[:, 0:nb, :], stm, reads=[Ttm])
    K.end()


def phase_dn_scan(K):
    nc, P = K.nc, K.P
    S = K.dscr
    K.scr("of_d", [TOKC, 1024], F32)
    catT = S["catT_d"]
    K.begin()
    H = 8
    tri = K.sb("tri", [128, 4, 128], F32)
    dtb = K.sb("dtb", [128, 16], F32)
    nea = K.sb("nea", [128, 16], F32)
    gon = K.sb("gon", [128, 128], F32)
    T_c = Tile("dnc")
    sc = K.getsem()
    P.dma("sp", tri[:], K.din["tri"].rearrange("m k c -> k m c"), sc, writes=[T_c])
    P.dma("sp", nea[:], K.din["alog_dt"][0:1, :].partition_broadcast(128), sc, writes=[T_c])
    P.dma("sp", dtb[:], K.din["alog_dt"][1:2, :].partition_broadcast(128), sc, writes=[T_c])
    P.dma("sp", gon[:], K.din["onorm"].partition_broadcast(128), sc, writes=[T_c])
    P.op("act", lambda e: e.activation(out=nea[:], in_=nea[:], func=AF.Exp), reads=[T_c], writes=[T_c])
    P.op("dve", lambda e: e.tensor_scalar_mul(nea[:], nea[:], -1.0), reads=[T_c], writes=[T_c])
    LM, UM, SLM, SUM = 0, 1, 2, 3
    St = K.sb("St", [128, H, 128], F32)
    Sb = K.sb("Sb", [128, H, 128], BF16)
    T_S, T_Sb = P.tiles(2, "S")
    T_of = {}
    big = lambda name, dt, n=2: Ring(K, name, [128, H, 128], dt, n)
    r_kT, r_qT, r_k, r_v = big("lkT", BF16), big("lqT", BF16), big("lk", BF16), big("lv", BF16)
    r_Rg, r_Rb = big("Rg", F32, 1), big("Rb", BF16, 1)
    r_diff, r_x1, r_e1, r_e2i, r_e2s = big("diff", F32, 1), big("x1", F32, 1), big("e1", F32, 1), big("e2i", F32, 1), big("e2s", F32, 1)
    r_kbT, r_egb, r_qd = big("kbT", BF16, 1), big("egb", BF16, 1), big("qd", BF16, 3)
    r_N, r_M, r_X, r_Y = big("N", F32, 2), big("M", F32, 2), big("X", F32, 2), big("Y", F32, 2)
    r_Xb = big("Xb", BF16, 1)
    r_N0, r_M0, r_X0, r_Y0 = big("N0", F32, 2), big("M0", F32, 2), big("X0", F32, 2), big("Y0", F32, 2)
    r_qk, r_vb, r_kbg, r_kd = big("qk", BF16, 3), big("vb", BF16, 1), big("kbg", BF16, 1), big("kdc", BF16, 2)
    r_u, r_wT, r_vn, r_o = big("u", F32, 2), big("wT", BF16, 2), big("vn", BF16, 1), big("o", F32, 2)
    r_z, r_sq, r_on, r_obT = Ring(K, "z", [128, 1024], F32, 1), big("osq", F32, 1), big("on", BF16, 1), big("obT", BF16, 2)
    r_st = Ring(K, "ost", [128, 16], F32, 2)

    def v8(ap):
        return ap.rearrange("p (h f) -> p h f", h=H)

    def bc_h(ap2):
        return ap2.unsqueeze(2).to_broadcast([128, H, 128])

    def bc_m(m):
        return tri[:, m, :].unsqueeze(1).to_broadcast([128, H, 128])

    def mm8(lhs_fn, rhs_fn, reads, g, extra=None):
        b0, T0 = K.ps(g)
        b1, T1 = K.ps(g)
        banks = ((b0, T0), (b1, T1))
        for h in range(H):
            b, Tb = banks[h // 4]
            o_ = b[:, (h % 4) * 128:(h % 4 + 1) * 128]
            if extra is None:
                P.op("pe", lambda e, h=h, o_=o_: e.matmul(o_, lhs_fn(h), rhs_fn(h), start=True, stop=True), reads=reads, writes=[Tb])
            else:
                l2, r2, reads2 = extra
                P.op("pe", lambda e, h=h, o_=o_: e.matmul(o_, lhs_fn(h), rhs_fn(h), start=True, stop=False), reads=reads, writes=[Tb])
                P.op("pe", lambda e, h=h, o_=o_: e.matmul(o_, l2(h), r2(h), start=False, stop=True), reads=reads2, writes=[Tb])
        return banks

    def ev(eng, banks, fn, reads, writes):
        for i, (b, Tb) in enumerate(banks):
            bv = b[:].rearrange("p (h f) -> p h f", h=4)
            P.op(eng, lambda e, bv=bv, i=i: fn(e, bv, slice(4 * i, 4 * i + 4)), reads=[Tb] + reads, writes=writes)

    def gates(tokd0, nch):
        G = {}
        graw = K.sb("graw", [128, nch, 32], F32)
        beta = K.sb("beta", [128, nch, 16], F32)
        g = K.sb("gg", [128, nch, 16], F32)
        Tg = Tile("gates")
        sg = K.getsem()
        P.dma("sp", graw[:], S["gates_d"][tokd0:tokd0 + nch * 128, :].rearrange("(c p) g -> p c g", p=128), sg, writes=[Tg])
        P.op("act", lambda e: e.activation(out=beta[:], in_=graw[:, :, 0:16], func=AF.Sigmoid), reads=[Tg], writes=[Tg])
        P.op("dve", lambda e: e.tensor_tensor(g[:], graw[:, :, 16:32], dtb[:].unsqueeze(1).to_broadcast([128, nch, 16]), ALU.add), reads=[Tg, T_c], writes=[Tg])
        P.op("act", lambda e: e.activation(out=g[:], in_=g[:], func=AF.Exp), reads=[Tg], writes=[Tg])
        P.op("act", lambda e: e.activation(out=g[:], in_=g[:], func=AF.Ln, bias=1.0, scale=1.0), reads=[Tg], writes=[Tg])
        P.op("dve", lambda e: e.tensor_tensor(g[:], g[:], nea[:].unsqueeze(1).to_broadcast([128, nch, 16]), ALU.mult), reads=[Tg, T_c], writes=[Tg])
        G["beta"], G["T"] = beta, Tg
        for dr in range(2):
            gc = K.sb(f"gc{dr}", [128, nch, 8], F32)
            gl = K.sb(f"gl{dr}", [128, nch, 8], F32)
            eg = K.sb(f"eg{dr}", [128, nch, 8], F32)
            bg = K.sb(f"bg{dr}", [128, nch, 8], F32)
            kd = K.sb(f"kd{dr}", [128, nch, 8], F32)
            cd = K.sb(f"cd{dr}", [128, nch, 8], F32)
            tr = UM if dr == 0 else LM
            ps, Tp = K.ps(0)
            gsl = g[:, :, dr * 8:(dr + 1) * 8]
            P.op("pe", lambda e, ps=ps, tr=tr, gsl=gsl: e.matmul(ps[:, 0:nch * 8].rearrange("p (c h) -> p c h", h=8), tri[:, tr, :], gsl, start=True, stop=True), reads=[Tg, T_c], writes=[Tp])
            P.op("dve", lambda e, ps=ps, gc=gc: e.tensor_copy(gc[:].rearrange("p c h -> p (c h)"), ps[:, 0:nch * 8]), reads=[Tp], writes=[Tg])
            ps2, Tp2 = K.ps(0)
            P.op("pe", lambda e, ps2=ps2, gsl=gsl: e.matmul(ps2[:, 0:nch * 8].rearrange("p (c h) -> p c h", h=8), K.onesf[:], gsl, start=True, stop=True), reads=[Tg, K.T_const], writes=[Tp2])
            P.op("dve", lambda e, ps2=ps2, gl=gl: e.tensor_copy(gl[:].rearrange("p c h -> p (c h)"), ps2[:, 0:nch * 8]), reads=[Tp2], writes=[Tg])
            P.op("act", lambda e, eg=eg, gc=gc: e.activation(out=eg[:], in_=gc[:], func=AF.Exp), reads=[Tg], writes=[Tg])
            P.op("dve", lambda e, bg=bg, eg=eg, dr=dr: e.tensor_tensor(bg[:], eg[:], beta[:, :, dr * 8:(dr + 1) * 8], ALU.mult), reads=[Tg], writes=[Tg])
            P.op("dve", lambda e, kd=kd, gl=gl, gc=gc: e.tensor_tensor(kd[:], gl[:], gc[:], ALU.subtract), reads=[Tg], writes=[Tg])
            P.op("act", lambda e, kd=kd: e.activation(out=kd[:], in_=kd[:], func=AF.Exp), reads=[Tg], writes=[Tg])
            P.op("act", lambda e, cd=cd, gl=gl: e.activation(out=cd[:], in_=gl[:], func=AF.Exp), reads=[Tg], writes=[Tg])
            G[dr] = dict(gc=gc, eg=eg, bg=bg, kd=kd, cd=cd)
        return G

    def prepA(C):
        G, tokd0, ch, dr, full = C['G'], C['tokd0'], C['ch'], C['dr'], C['full']
        t0 = tokd0 + ch * 128
        Tg = G["T"]
        gd = G[dr]
        beta_c = G["beta"][:, ch, dr * 8:(dr + 1) * 8]
        gc_c = gd["gc"][:, ch, :]
        mL, mU, mSL, mSU = (LM, UM, SLM, SUM) if dr == 0 else (UM, LM, SUM, SLM)
        kT, TkT, s1 = r_kT.next()
        ktm, Tk, s2 = r_k.next()
        vtm, Tv, s3 = r_v.next()
        P.dma("sp", kT[:], S["knT_d"][:, t0:t0 + 128].rearrange("(h p) t -> p h t", p=128), s1, writes=[TkT])
        P.dma("sp", ktm[:].rearrange("p h f -> p (h f)"), S["ktm_d"][t0:t0 + 128, :], s2, writes=[Tk])
        P.dma("sp", vtm[:].rearrange("p h f -> p (h f)"), S["vtm_d"][t0:t0 + 128, :], s3, writes=[Tv])
        if full:
            qT, TqT, s4 = r_qT.next()
            P.dma("sp", qT[:], S["qnT_d"][:, t0:t0 + 128].rearrange("(h p) t -> p h t", p=128), s4, writes=[TqT])
        yield
        Rg, TRg, _ = r_Rg.next()
        Rb, TRb, _ = r_Rb.next()
        idb = K.identf[:].unsqueeze(1).to_broadcast([128, H, 128])
        P.op("dve", lambda e: e.tensor_tensor(Rg[:], bc_h(gc_c), idb, ALU.mult), reads=[Tg, K.T_const], writes=[TRg])
        P.op("pool", lambda e: e.tensor_tensor(Rb[:], bc_h(beta_c), idb, ALU.mult), reads=[Tg, K.T_const], writes=[TRb])
        gcb = mm8(lambda h: K.onesf[:], lambda h: Rg[:, h, :], [TRg, K.T_const], 0)
        btb = mm8(lambda h: K.onesb[:], lambda h: Rb[:, h, :], [TRb, K.T_const], 0)
        diff, Tdiff, _ = r_diff.next()
        ev("dve", gcb, lambda e, bv, hs: e.tensor_tensor(diff[:, hs, :], bc_h(gc_c)[:, hs, :], bv, ALU.subtract), [Tg], [Tdiff])
        kbT, TkbT, _ = r_kbT.next()
        ev("dve", btb, lambda e, bv, hs: e.tensor_tensor(kbT[:, hs, :], kT[:, hs, :], bv, ALU.mult), [TkT], [TkbT])
        if full:
            egb, Tegb, _ = r_egb.next()
            qd, Tqd, _ = r_qd.next()
            ev("act", gcb, lambda e, bv, hs: e.activation(out=egb[:, hs, :], in_=bv, func=AF.Exp), [], [Tegb])
            P.op("pool", lambda e: e.tensor_tensor(qd[:], qT[:], egb[:], ALU.mult), reads=[TqT, Tegb], writes=[Tqd])
        yield
        x1, Tx1, _ = r_x1.next()
        e1, Te1, _ = r_e1.next()
        e2i, Te2i, _ = r_e2i.next()
        e2s, Te2s, _ = r_e2s.next()
        P.op("dve", lambda e: e.tensor_tensor(x1[:], diff[:], bc_m(mL), ALU.mult), reads=[Tdiff, T_c], writes=[Tx1])
        P.op("act", lambda e: e.activation(out=e1[:], in_=x1[:], func=AF.Exp), reads=[Tx1], writes=[Te1])
        P.op("pool", lambda e: e.tensor_tensor(e1[:], e1[:], bc_m(mSL), ALU.mult), reads=[Te1, T_c], writes=[Te1])
        P.op("dve", lambda e: e.tensor_tensor(x1[:], diff[:], bc_m(mU), ALU.mult), reads=[Tdiff, T_c, Te1], writes=[Tx1])
        P.op("act", lambda e: e.activation(out=e2i[:], in_=x1[:], func=AF.Exp, scale=-1.0), reads=[Tx1], writes=[Te2i])
        P.op("pool", lambda e: e.tensor_tensor(e2s[:], e2i[:], bc_m(mSU), ALU.mult), reads=[Te2i, T_c], writes=[Te2s])
        P.op("pool", lambda e: e.tensor_tensor(e2i[:], e2i[:], bc_m(mU), ALU.mult), reads=[Te2i, T_c, Te2s], writes=[Te2i])
        yield
        a1 = mm8(lambda h: kbT[:, h, :], lambda h: kT[:, h, :], [TkbT, TkT], 0)
        a2 = mm8(lambda h: kT[:, h, :], lambda h: kbT[:, h, :], [TkbT, TkT], 1)
        Mj, TM, _ = r_M0.next()
        Nj, TN, _ = r_N0.next()
        ev("dve", a1, lambda e, bv, hs, Mj=Mj: e.tensor_tensor(Mj[:, hs, :], bv, e1[:, hs, :], ALU.mult), [Te1], [TM])
        ev("dve", a2, lambda e, bv, hs, Nj=Nj: e.tensor_tensor(Nj[:, hs, :], bv, e2s[:, hs, :], ALU.mult), [Te2s], [TN])
        if full:
            a3 = mm8(lambda h: kT[:, h, :], lambda h: qT[:, h, :], [TkT, TqT], 0)
            qk, Tqk, _ = r_qk.next()
            ev("dve", a3, lambda e, bv, hs: e.tensor_tensor(qk[:, hs, :], bv, e2i[:, hs, :], ALU.mult), [Te2i], [Tqk])
        X, TX, _ = r_X0.next()
        Y, TY, _ = r_Y0.next()
        idbb = K.identf[:].unsqueeze(1).to_broadcast([128, H, 128])
        P.op("pool", lambda e, X=X, Nj=Nj: e.tensor_tensor(X[:], idbb, Nj[:], ALU.subtract), reads=[TN, K.T_const], writes=[TX])
        P.op("pool", lambda e, Y=Y, Mj=Mj: e.tensor_tensor(Y[:], idbb, Mj[:], ALU.subtract), reads=[TM, K.T_const], writes=[TY])
        C.update(Mj=Mj, TM=TM, Nj=Nj, TN=TN, X=X, TX=TX, Y=Y, TY=TY, vtm=vtm, Tv=Tv, ktm=ktm, Tk=Tk, beta_c=beta_c, gd=gd, Tg=Tg, kT=kT)
        if full:
            C.update(qd=qd, Tqd=Tqd, qk=qk, Tqk=Tqk)
        yield

    def prepB(C):
        ch, full = C['ch'], C['full']
        Mj, TM, Nj, TN, X, TX, Y, TY = C['Mj'], C['TM'], C['Nj'], C['TN'], C['X'], C['TX'], C['Y'], C['TY']
        vtm, Tv, ktm, Tk, beta_c, gd, Tg = C['vtm'], C['Tv'], C['ktm'], C['Tk'], C['beta_c'], C['gd'], C['Tg']
        for j in range(1, 7):
            last = j == 6
            Nn, TNn, _ = r_N.next()
            pn = mm8(lambda h, Mj=Mj: Mj[:, h, :], lambda h, Nj=Nj: Nj[:, h, :], [TM, TN], 0)
            ev("act", pn, lambda e, bv, hs, Nn=Nn: e.copy(Nn[:, hs, :], bv), [], [TNn])
            if not last:
                Mn, TMn, _ = r_M.next()
                pm = mm8(lambda h, Nj=Nj: Nj[:, h, :], lambda h, Mj=Mj: Mj[:, h, :], [TM, TN], 0)
                ev("act", pm, lambda e, bv, hs, Mn=Mn: e.copy(Mn[:, hs, :], bv), [], [TMn])
            Xn, TXn, _ = r_X.next()
            px = mm8(lambda h, Y=Y: Y[:, h, :], lambda h, Nn=Nn: Nn[:, h, :], [TY, TNn], 1)
            ev("dve", px, lambda e, bv, hs, Xn=Xn, X=X: e.tensor_tensor(Xn[:, hs, :], bv, X[:, hs, :], ALU.add), [TX], [TXn])
            if not last:
                Yn, TYn, _ = r_Y.next()
                py = mm8(lambda h, X=X: X[:, h, :], lambda h, Mn=Mn: Mn[:, h, :], [TX, TMn], 1)
                ev("dve", py, lambda e, bv, hs, Yn=Yn, Y=Y: e.tensor_tensor(Yn[:, hs, :], bv, Y[:, hs, :], ALU.add), [TY], [TYn])
                Mj, TM, Y, TY = Mn, TMn, Yn, TYn
            Nj, TN, X, TX = Nn, TNn, Xn, TXn
            yield
        vb, Tvb, _ = r_vb.next()
        kbg, Tkbg, _ = r_kbg.next()
        kdc, Tkdc, _ = r_kd.next()
        P.op("pool", lambda e: e.tensor_tensor(vb[:], vtm[:], bc_h(beta_c), ALU.mult), reads=[Tv, Tg], writes=[Tvb])
        P.op("pool", lambda e: e.tensor_tensor(kbg[:], ktm[:], bc_h(gd["bg"][:, ch, :]), ALU.mult), reads=[Tk, Tg], writes=[Tkbg])
        P.op("pool", lambda e: e.tensor_tensor(kdc[:], ktm[:], bc_h(gd["kd"][:, ch, :]), ALU.mult), reads=[Tk, Tg], writes=[Tkdc])
        Xb, TXb, _ = r_Xb.next()
        P.op("act", lambda e, X=X: e.copy(Xb[:], X[:]), reads=[TX], writes=[TXb])
        pu = mm8(lambda h: Xb[:, h, :], lambda h: vb[:, h, :], [TXb, Tvb], 0)
        u, Tu, _ = r_u.next()
        ev("act", pu, lambda e, bv, hs: e.copy(u[:, hs, :], bv), [], [Tu])
        pw = mm8(lambda h: kbg[:, h, :], lambda h: Xb[:, h, :], [TXb, Tkbg], 0)
        wT, TwT, _ = r_wT.next()
        ev("act", pw, lambda e, bv, hs: e.copy(wT[:, hs, :], bv), [], [TwT])
        C.update(u=u, Tu=Tu, wT=wT, TwT=TwT, kdc=kdc, Tkdc=Tkdc, gd=gd, Tg=Tg)
        yield


    def scan(C):
        ch, full, final, tokc0 = C['ch'], C['full'], C['final'], C['tokc0']
        u, Tu, wT, TwT, kdc, Tkdc, gd, Tg = C['u'], C['Tu'], C['wT'], C['TwT'], C['kdc'], C['Tkdc'], C['gd'], C['Tg']
        if full:
            qd, Tqd, qk, Tqk = C['qd'], C['Tqd'], C['qk'], C['Tqk']
        if C.get('pre') is not None:
            C['pre']()
        pws = mm8(lambda h: wT[:, h, :], lambda h: Sb[:, h, :], [TwT, T_Sb], 1)
        vn, Tvn, _ = r_vn.next()
        ev("dve", pws, lambda e, bv, hs: e.tensor_tensor(vn[:, hs, :], u[:, hs, :], bv, ALU.subtract), [Tu], [Tvn])
        yield
        if full:
            po = mm8(lambda h: qd[:, h, :], lambda h: Sb[:, h, :], [Tqd, T_Sb], 1,
                     extra=(lambda h: qk[:, h, :], lambda h: vn[:, h, :], [Tqk, Tvn]))
            o, To, so = r_o.next()
            if not final:
                ev("act", po, lambda e, bv, hs: e.copy(o[:, hs, :], bv), [], [To])
                P.dma("act", S["of_d"][tokc0 + ch * 128:tokc0 + (ch + 1) * 128, :], o[:].rearrange("p h f -> p (h f)"), so, reads=[To], writes=[T_of.setdefault(tokc0 + ch * 128, Tile("of"))])
            else:
                P.dma("sp", o[:].rearrange("p h f -> p (h f)"), S["of_d"][tokc0 + ch * 128:tokc0 + (ch + 1) * 128, :], so, reads=[T_of[tokc0 + ch * 128]], writes=[To])
                ev("dve", po, lambda e, bv, hs: e.tensor_tensor(o[:, hs, :], o[:, hs, :], bv, ALU.add), [To], [To])
        yield
        pds = mm8(lambda h: kdc[:, h, :], lambda h: vn[:, h, :], [Tkdc, Tvn], 1)
        for h in range(H):
            b, Tb = pds[h // 4]
            P.op("dve", lambda e, h=h, b=b: e.scalar_tensor_tensor(St[:, h, :], St[:, h, :], gd["cd"][:, ch, h:h + 1], b[:, (h % 4) * 128:(h % 4 + 1) * 128], ALU.mult, ALU.add),
                 reads=[Tb, Tg, T_S], writes=[T_S])
        P.op("act", lambda e: e.copy(Sb[:], St[:]), reads=[T_S], writes=[T_Sb])
        yield
        if full and final:
            z, Tz, sz = r_z.next()
            P.dma("sp", z[:], S["z_d"][tokc0 + ch * 128:tokc0 + (ch + 1) * 128, :], sz, writes=[Tz])
            sq, Tsq, _ = r_sq.next()
            st, Tst, _ = r_st.next()
            on, Ton, _ = r_on.next()
            P.op("pool", lambda e: e.tensor_tensor(sq[:], o[:], o[:], ALU.mult), reads=[To], writes=[Tsq])
            P.op("dve", lambda e: e.tensor_reduce(out=st[:, 0:8], in_=sq[:], axis=AX.X, op=ALU.add), reads=[Tsq], writes=[Tst])
            P.op("act", lambda e: e.activation(out=st[:, 8:16], in_=st[:, 0:8], func=AF.Sqrt, bias=EPS, scale=1.0 / 128), reads=[Tst], writes=[Tst])
            P.op("dve", lambda e: e.reciprocal(st[:, 8:16], st[:, 8:16]), reads=[Tst], writes=[Tst])
            P.op("act", lambda e: e.activation(out=z[:], in_=z[:], func=AF.Silu), reads=[Tz], writes=[Tz])
            P.op("dve", lambda e: e.tensor_tensor(sq[:], o[:], bc_h(st[:, 8:16]), ALU.mult), reads=[To, Tst, Tsq], writes=[Tsq])
            P.op("pool", lambda e: e.tensor_tensor(sq[:], sq[:], gon[:].unsqueeze(1).to_broadcast([128, H, 128]), ALU.mult), reads=[Tsq, T_c], writes=[Tsq])
            P.op("dve", lambda e: e.tensor_tensor(on[:], sq[:], v8(z[:]), ALU.mult), reads=[Tsq, Tz], writes=[Ton])
            yield
            ps, Tp = K.ps(0)
            psb = ps[:].bitcast(BF16)
            for h in range(H):
                P.op("pe", lambda e, h=h, psb=psb: e.transpose(psb[:, h * 128:(h + 1) * 128], on[:, h, :], K.identb[:]), reads=[Ton, K.T_const], writes=[Tp])
            obT, TobT, sob = r_obT.next()
            P.op("act", lambda e, psb=psb: e.copy(obT[:].rearrange("p h f -> p (h f)"), psb), reads=[Tp], writes=[TobT])
            P.dma("act", catT[1024:2048, tokc0 + ch * 128:tokc0 + (ch + 1) * 128].rearrange("(h p) t -> p h t", p=128), obT[:], sob, reads=[TobT])

        if C.get('post') is not None:
            C['post']()
        yield


    def set_state(src):
        ss_ = K.getsem()
        if src is None:
            P.op("pool", lambda e: e.memset(St[:], 0.0), reads=[T_Sb], writes=[T_S])
        else:
            P.dma("sp", St[:], src.rearrange("h k v -> k h v"), ss_, reads=[T_Sb], writes=[T_S])
        P.op("act", lambda e: e.copy(Sb[:], St[:]), reads=[T_S], writes=[T_Sb])

    def save_state(dst):
        ss_ = K.getsem()
        P.dma("sp", dst.rearrange("h k v -> k h v"), St[:], ss_, reads=[T_S])

    jobs = []

    def job(G, tokd0, ch, dr, full, final, tokc0, pre=None, post=None):
        jobs.append(dict(G=G, tokd0=tokd0, ch=ch, dr=dr, full=full, final=final, tokc0=tokc0, pre=pre, post=post))

    for s_ in range(4):
        G = gates(s_ * 256, 2)
        job(G, s_ * 256, 0, 0, True, False, s_ * 256, pre=lambda: set_state(None))
        job(G, s_ * 256, 1, 0, True, False, s_ * 256, post=lambda s_=s_: save_state(K.dout["nbf"][s_]))
        job(G, s_ * 256, 1, 1, True, True, s_ * 256, pre=lambda: set_state(None))
        job(G, s_ * 256, 0, 1, True, True, s_ * 256, post=lambda s_=s_: save_state(K.dout["nbb"][s_]))
    G = gates(1024, 32)
    for ch in range(SEXT):
        job(G, 1024, ch, 0, True, False, 1024, pre=(lambda: set_state(K.din["s0"][0])) if ch == 0 else None)
    for ch in range(31, -1, -1):
        job(G, 1024, ch, 1, ch < SEXT, True, 1024, pre=(lambda: set_state(K.din["s0"][1])) if ch == 31 else None)
    nj = len(jobs)
    for t in range(nj + 2):
        gens = []
        if t < nj:
            gens.append(prepA(jobs[t]))
        if 0 <= t - 1 < nj:
            gens.append(prepB(jobs[t - 1]))
        if 0 <= t - 2 < nj:
            gens.append(scan(jobs[t - 2]))
        while gens:
            for g_ in list(gens):
                try:
                    next(g_)
                except StopIteration:
                    gens.remove(g_)
    K.end()


def phase_mlp(K, l):
    nc, P = K.nc, K.P
    S = K.dscr
    ntb = NCAT if l == 0 else NPT + 16
    groups = token_groups(ntb, breaks=(NPT,))
    if l == 0:
        x1_d = K.scr("x1_d", [TOKC, D], F32)
        oT_d, Wo, W1, W2 = S["catT_d"], K.din["ab_w_out"], K.din["w_mlp_in"][0], K.din["w_mlp_out"][0]
        xsrc = lambda g: K.din["xp"][g * 128:(g + 1) * 128, :] if g < NPT else K.din["xs"][(g - NPT) * 128:(g - NPT + 1) * 128, :]
    else:
        oT_d, Wo, W1, W2 = S["o1T_d"], K.din["c_w_out"], K.din["w_mlp_in"][1], K.din["w_mlp_out"][1]
        xsrc = lambda g: S["x1_d"][g * 128:(g + 1) * 128, :]
    K.begin()
    actT = K.sb("actT", [128, KC, 512], BF16)
    T_act = [[Tile() for kc in range(KC)] for i in range(4)]
    xres = K.sb("xres", [128, 4, D], F32)
    T_x = [Tile() for i in range(4)]
    uT = K.sb("uT", [128, 64, 512], BF16)
    T_u = [Tile() for fc in range(64)]
    gate = [K.sb(f"gate{i}", [128, D], F32) for i in range(2)]
    T_gate = Tile("gate")
    wr = Ring(K, "wm", [128, KC, 512], BF16, 3, sw=True)
    tr = Ring(K, "tg", [128, 512], F32, 1)
    rr = Ring(K, "rl", [128, 512], F32, 2)
    nt = NormT(K)
    sx = [K.getsem() for i in range(4)]
    sg, so = K.getsem(), K.getsem()
    if l == 1:
        fng = K.sb("fng", [128, D], F32)
        fss = Ring(K, "fss", [128, 2], F32, 2)
        T_fng = Tile("fng")
        P.dma("sp", fng[:], K.din["final_norm"].partition_broadcast(128), sg, writes=[T_fng])
    cur_c = None
    for (t0, m) in groups:
        n = m * 128
        c = 0 if t0 < NPT else 1
        if c != cur_c:
            P.dma("sp", gate[0][:], S["modrow_d"][l, c:c + 1, 2 * D:3 * D].partition_broadcast(128), sg, writes=[T_gate])
            P.dma("sp", gate[1][:], S["modrow_d"][l, c:c + 1, 5 * D:6 * D].partition_broadcast(128), sg, writes=[T_gate])
            cur_c = c
        P.dma("sp", actT[:, :, 0:n], oT_d[:, t0 * 128:t0 * 128 + n].rearrange("(fc p) t -> p fc t", p=128), so,
              writes=[T_act[i][kc] for i in range(m) for kc in range(KC)])
        for i in range(m):
            P.dma("sp", xres[:, i, :], xsrc(t0 + i), sx[i], writes=[T_x[i]])

        def second(Wsrc, nfq, lhs_fn, lhs_tiles_fn, gi):
            for dg in range(4):
                banks = [K.ps(1 - dg % 2) for i in range(m)]
                for fq in range(nfq):
                    wt, Tw, sw = wr.next()
                    P.dma("pool", wt[:], Wsrc[fq * 2048:(fq + 1) * 2048, dg * 512:(dg + 1) * 512].rearrange("(kc p) n -> p kc n", p=128), sw, writes=[Tw])
                    for i in range(m):
                        b, Tb = banks[i]
                        for kc in range(KC):
                            fc = fq * KC + kc
                            P.op("pe", lambda e, b=b, i=i, fc=fc, kc=kc, wt=wt: e.matmul(b[:, :], lhs_fn(fc, i), wt[:, kc, :], start=(fc == 0), stop=(fc == nfq * KC - 1)),
                                 reads=[Tw] + lhs_tiles_fn(fc, i), writes=[Tb])
                for i in range(m):
                    b, Tb = banks[i]
                    tt, Tt, _ = tr.next()
                    P.op("dve", lambda e, b=b, tt=tt, dg=dg: e.tensor_tensor(tt[:], b[:, :], gate[gi][:, dg * 512:(dg + 1) * 512], ALU.mult), reads=[Tb, T_gate], writes=[Tt])
                    P.op("dve", lambda e, i=i, tt=tt, dg=dg: e.tensor_tensor(xres[:, i, dg * 512:(dg + 1) * 512], xres[:, i, dg * 512:(dg + 1) * 512], tt[:], ALU.add), reads=[Tt, T_x[i]], writes=[T_x[i]])

        second(Wo, 1, lambda fc, i: actT[:, fc, i * 128:(i + 1) * 128], lambda fc, i: [T_act[i][fc]], 0)
        for i in range(m):
            nt.run(xres[:, i, :], T_x[i], K.gsF[l][1][c], K.modF[l][c][:, 3 * KC:4 * KC],
                   lambda kc, i=i: actT[:, kc, i * 128:(i + 1) * 128], lambda kc, i=i: T_act[i][kc])
        for fg in range(16):
            wt, Tw, sw = wr.next()
            P.dma("pool", wt[:], W1[:, fg * 512:(fg + 1) * 512].rearrange("(kc p) n -> p kc n", p=128), sw, writes=[Tw])
            for sub in range(4):
                fc = fg * 4 + sub
                ps, Tp = K.ps(0)
                for kc in range(KC):
                    P.op("pe", lambda e, ps=ps, kc=kc, sub=sub, wt=wt, n=n: e.matmul(ps[:, 0:n], wt[:, kc, sub * 128:(sub + 1) * 128], actT[:, kc, 0:n], start=(kc == 0), stop=(kc == KC - 1)),
                         reads=[Tw] + [T_act[i][kc] for i in range(m)], writes=[Tp])
                r, Tr, _ = rr.next()
                P.op("act", lambda e, ps=ps, r=r, n=n: e.activation(out=r[:, 0:n], in_=ps[:, 0:n], func=AF.Relu), reads=[Tp], writes=[Tr])
                P.op("dve", lambda e, r=r, fc=fc, n=n: e.tensor_tensor(uT[:, fc, 0:n], r[:, 0:n], r[:, 0:n], ALU.mult), reads=[Tr], writes=[T_u[fc]])
        second(W2, 4, lambda fc, i: uT[:, fc, i * 128:(i + 1) * 128], lambda fc, i: [T_u[fc]], 1)
        for i in range(m):
            g = t0 + i
            if l == 0:
                P.dma("sp", x1_d[g * 128:(g + 1) * 128, :], xres[:, i, :], sx[i], reads=[T_x[i]])
            else:
                ss, Tss, _ = fss.next()
                fjk, T_fjk, _ = nt.xn.next()
                P.op("act", lambda e, i=i, ss=ss, fjk=fjk: e.activation(out=fjk[:], in_=xres[:, i, :], func=AF.Square, accum_out=ss[:, 0:1]), reads=[T_x[i]], writes=[T_fjk, Tss])
                P.op("act", lambda e, ss=ss: e.activation(out=ss[:, 1:2], in_=ss[:, 0:1], func=AF.Sqrt, bias=EPS, scale=1.0 / D), reads=[Tss], writes=[Tss])
                P.op("dve", lambda e, ss=ss: e.reciprocal(ss[:, 1:2], ss[:, 1:2]), reads=[Tss], writes=[Tss])
                P.op("dve", lambda e, i=i, ss=ss: e.scalar_tensor_tensor(xres[:, i, :], xres[:, i, :], ss[:, 1:2], fng[:], ALU.mult, ALU.mult), reads=[T_x[i], Tss, T_fng], writes=[T_x[i]])
                dst = K.dout["y_p"][g * 128:(g + 1) * 128, :] if g < NPT else K.dout["y_s"][(g - NPT) * 128:(g - NPT + 1) * 128, :]
                P.dma("sp", dst, xres[:, i, :], sx[i], reads=[T_x[i]])
    K.end()


def phase_l1_inproj(K):
    nc, P = K.nc, K.P
    S = K.dscr
    q1T = K.scr("q1T_d", [D, TOKC], BF16)
    k1T = K.scr("k1T_d", [256, TOKC], BF16)
    v1 = K.scr("v1_d", [TOKC, 256], BF16)
    K.begin()
    ntb = NCAT
    groups = token_groups(ntb, breaks=(NPT,))
    hT = K.sb("h1T", [128, KC, ntb * 128], BF16)
    T_h = [[Tile() for kc in range(KC)] for g in range(ntb)]
    xr = Ring(K, "x1in", [128, D], F32, 2)
    nt = NormT(K)
    for g in range(ntb):
        c = 0 if g < NPT else 1
        xt, T_x, sx = xr.next()
        P.dma("sp", xt[:], S["x1_d"][g * 128:(g + 1) * 128, :], sx, writes=[T_x])
        nt.run(xt[:], T_x, K.gsF[1][0][c], K.modF[1][c][:, 0:KC],
               lambda kc, g=g: hT[:, kc, g * 128:(g + 1) * 128], lambda kc, g=g: T_h[g][kc])
    cosT = K.sb("cosT", [128, SEXT * 128], F32)
    sinT = K.sb("sinT", [128, SEXT * 128], F32)
    perm = K.sb("perm", [128, 128], F32)
    T_rc = Tile("ropec")
    sr = K.getsem()
    P.dma("sp", cosT[:], K.din["rope_cos"], sr, writes=[T_rc])
    P.dma("sp", sinT[:], K.din["rope_sin"], sr, writes=[T_rc])
    P.dma("sp", perm[:], K.din["rope_perm"], sr, writes=[T_rc])
    wr = Ring(K, "w1q", [128, KC, 512], BF16, 2, sw=True)
    q32r = Ring(K, "q32", [128, 512], F32, 2)
    t1r = Ring(K, "rt1", [128, 512], F32, 2)
    st16 = Ring(K, "s16", [128, 512], BF16, 3)
    st32 = Ring(K, "s32", [128, 512], F32, 2)
    W = K.din["c_w_qkv"]
    for t in range(5):
        wt, Tw, sw = wr.next()
        P.dma("pool", wt[:], W[:, t * 512:(t + 1) * 512].rearrange("(kc p) n -> p kc n", p=128), sw, writes=[Tw])
        nsub = 4 if t < 4 else 2
        for sub in range(nsub):
            dst, row0 = (q1T, t * 512 + sub * 128) if t < 4 else (k1T, sub * 128)
            for (t0, m) in groups:
                n = m * 128
                ps, Tp = K.ps(0)
                for kc in range(KC):
                    P.op("pe", lambda e, ps=ps, kc=kc, sub=sub, wt=wt, t0=t0, n=n: e.matmul(ps[:, 0:n], wt[:, kc, sub * 128:(sub + 1) * 128], hT[:, kc, t0 * 128:t0 * 128 + n], start=(kc == 0), stop=(kc == KC - 1)),
                         reads=[Tw] + [T_h[t0 + i][kc] for i in range(m)], writes=[Tp])
                sg, Ts, ss_ = st16.next()
                if t0 < NPT:
                    P.op("act", lambda e, ps=ps, sg=sg, n=n: e.copy(sg[:, 0:n], ps[:, 0:n]), reads=[Tp], writes=[Ts])
                else:
                    s0 = (t0 - NPT) * 128
                    q32, Tq, _ = q32r.next()
                    t1, Tt1, _ = t1r.next()
                    P.op("act", lambda e, ps=ps, q32=q32, n=n: e.copy(q32[:, 0:n], ps[:, 0:n]), reads=[Tp], writes=[Tq])
                    ps2, Tp2 = K.ps(1)
                    P.op("pe", lambda e, ps2=ps2, q32=q32, n=n: e.matmul(ps2[:, 0:n], perm[:], q32[:, 0:n], start=True, stop=True), reads=[Tq, T_rc], writes=[Tp2])
                    P.op("pool", lambda e, q32=q32, t1=t1, n=n, s0=s0: e.tensor_tensor(t1[:, 0:n], q32[:, 0:n], cosT[:, s0:s0 + n], ALU.mult), reads=[Tq, T_rc], writes=[Tt1])
                    P.op("dve", lambda e, ps2=ps2, q32=q32, n=n, s0=s0: e.tensor_tensor(q32[:, 0:n], ps2[:, 0:n], sinT[:, s0:s0 + n], ALU.mult), reads=[Tp2, T_rc, Tt1], writes=[Tq])
                    P.op("dve", lambda e, q32=q32, t1=t1, sg=sg, n=n: e.tensor_tensor(sg[:, 0:n], q32[:, 0:n], t1[:, 0:n], ALU.add), reads=[Tq, Tt1], writes=[Ts])
                P.dma("sp", dst[row0:row0 + 128, t0 * 128:t0 * 128 + n], sg[:, 0:n], ss_, reads=[Ts])
        if t == 4:
            for g in range(ntb):
                ps, Tp = K.ps(0)
                for kc in range(KC):
                    P.op("pe", lambda e, ps=ps, kc=kc, g=g, wt=wt: e.matmul(ps[:, :], hT[:, kc, g * 128:(g + 1) * 128], wt[:, kc, :], start=(kc == 0), stop=(kc == KC - 1)),
                         reads=[Tw, T_h[g][kc]], writes=[Tp])
                sg, Ts, ss_ = st16.next()
                P.op("act", lambda e, ps=ps, sg=sg: e.copy(sg[:, 0:256], ps[:, 256:512]), reads=[Tp], writes=[Ts])
                P.dma("sp", v1[g * 128:(g + 1) * 128, :], sg[:, 0:256], ss_, reads=[Ts])
                if g < NPT:
                    s32, Ts32, ss32 = st32.next()
                    P.op("dve", lambda e, ps=ps, s32=s32: e.tensor_copy(s32[:], ps[:, :]), reads=[Tp], writes=[Ts32])
                    P.dma("sp", K.dout["nck"][g * 128:(g + 1) * 128, :], s32[:, 0:256], ss32, reads=[Ts32])
                    P.dma("sp", K.dout["ncv"][g * 128:(g + 1) * 128, :], s32[:, 256:512], ss32, reads=[Ts32])
    K.end()


def phase_attn_c(K):
    nc, P = K.nc, K.P
    S = K.dscr
    o1T = K.scr("o1T_d", [D, TOKC], BF16)
    scale = 64 ** -0.5
    K.begin()
    pt_ring = Ring(K, "pt", [128, 512], BF16, 4)
    rec_ring = Ring(K, "rec", [128, 512], F32, 2)
    pools = (pt_ring, rec_ring)
    snk = K.sb("snk", [128, 32], F32)
    trib = K.sb("trib", [128, 2, 128], BF16)
    ck_tm = K.sb("cck", [128, 2, 256], BF16)
    cvt = K.sb("ccv", [128, 2, 256], BF16)
    ckT = K.sb("cckT", [64, 4, 256], BF16)
    T_c, T_ck, T_cv, T_ckT = P.tiles(4, "ac")
    s0, sw0 = K.getsem(), K.getsem(True)
    P.dma("sp", snk[:], K.din["c_sink"].partition_broadcast(128), s0, writes=[T_c])
    P.op("act", lambda e: e.activation(out=snk[:], in_=snk[:], func=AF.Exp), reads=[T_c], writes=[T_c])
    P.dma("pool", trib[:], K.din["tri"][0:2].rearrange("m k c -> k m c"), sw0, writes=[T_c])
    P.dma("pool", ck_tm[:], K.din["cache_c_k"].rearrange("(c p) f -> p c f", p=128), sw0, writes=[T_ck])
    P.dma("pool", cvt[:], K.din["cache_c_v"].rearrange("(c p) f -> p c f", p=128), sw0, writes=[T_cv])
    for c in range(2):
        ps, Tp = K.ps(0)
        psb = ps[:].bitcast(BF16)
        for kh in range(4):
            P.op("pe", lambda e, c=c, kh=kh, psb=psb: e.transpose(psb[0:64, kh * 128:(kh + 1) * 128], ck_tm[:, c, kh * 64:(kh + 1) * 64], K.identb[:]), reads=[T_ck, K.T_const], writes=[Tp])
        P.op("dve", lambda e, c=c, psb=psb: e.tensor_copy(ckT[:, :, c * 128:(c + 1) * 128], psb[0:64, 0:512].rearrange("p (h k) -> p h k", h=4)), reads=[Tp], writes=[T_ckT])
    qr = Ring(K, "pq", [64, 8, 256], BF16, 2)
    kr = Ring(K, "pk", [64, 256], BF16, 2)
    vr = Ring(K, "pv", [128, 2, 64], BF16, 2)
    orr = Ring(K, "po", [64, 8, 256], BF16, 2)
    for kh in range(4):
        for s in range(4):
            qt, Tq, sq = qr.next()
            kt, Tk, sk = kr.next()
            vt, Tv, sv = vr.next()
            ot, To, so = orr.next()
            P.dma("sp", qt[:], S["q1T_d"][kh * 512:(kh + 1) * 512, s * 256:(s + 1) * 256].rearrange("(g d) t -> d g t", d=64), sq, writes=[Tq])
            P.dma("sp", kt[:], S["k1T_d"][kh * 64:(kh + 1) * 64, s * 256:(s + 1) * 256], sk, writes=[Tk])
            P.dma("sp", vt[:], S["v1_d"][s * 256:(s + 1) * 256, kh * 64:(kh + 1) * 64].rearrange("(c p) f -> p c f", p=128), sv, writes=[Tv])
            for g in range(8):
                sl = [(kt[:, c * 128:(c + 1) * 128], [Tk], vt[:, c, :], [Tv], None, [], 0) for c in range(2)]
                hq = kh * 8 + g
                attn_core(K, sl, 256, (qt[:, g, :], [Tq]), ot[:, g, :], To, scale, extra_den=(snk[0:64, hq:hq + 1], [T_c]), pools=pools)
            P.dma("sp", o1T[kh * 512:(kh + 1) * 512, s * 256:(s + 1) * 256].rearrange("(g d) t -> d g t", d=64), ot[:], so, reads=[To])
    NT = SEXT * 128
    kh_k = Ring(K, "sk", [64, NT], BF16, 2)
    kh_v = Ring(K, "sv", [128, SEXT, 64], BF16, 2)
    kh_q = Ring(K, "sq", [64, 8, 2048], BF16, 1)
    kh_o = Ring(K, "so", [64, 8, 2048], BF16, 1)
    for kh in range(4):
        kt, Tk, sk = kh_k.next()
        vt, Tv, sv = kh_v.next()
        qt, Tq, sq = kh_q.next()
        ot, To, so = kh_o.next()
        P.dma("sp", kt[:], S["k1T_d"][kh * 64:(kh + 1) * 64, 1024:1024 + NT], sk, writes=[Tk])
        P.dma("sp", vt[:], S["v1_d"][1024:1024 + NT, kh * 64:(kh + 1) * 64].rearrange("(c p) f -> p c f", p=128), sv, writes=[Tv])
        P.dma("sp", qt[:], S["q1T_d"][kh * 512:(kh + 1) * 512, 1024:1024 + 2048].rearrange("(g d) t -> d g t", d=64), sq, writes=[Tq])
        for i in range(16):
            for gh in range(2):
                sl = []
                for j in (i - 1, i, i + 1):
                    if j < 0 or j > 16:
                        continue
                    mask = None
                    if j == i - 1:
                        mask = trib[:, 0, :].unsqueeze(1).to_broadcast([128, 4, 128])
                    elif j == i + 1:
                        mask = trib[:, 1, :].unsqueeze(1).to_broadcast([128, 4, 128])
                    sl.append((kt[:, j * 128:(j + 1) * 128], [Tk], vt[:, j, :], [Tv], mask, [T_c], 0))
                for c in range(2):
                    sl.append((ckT[:, kh, c * 128:(c + 1) * 128], [T_ckT], cvt[:, c, kh * 64:(kh + 1) * 64], [T_cv], None, [], 0))
                hq0 = kh * 8 + gh * 4
                ed = snk[0:64, hq0:hq0 + 4].unsqueeze(2).to_broadcast([64, 4, 128])
                attn_core(K, sl, 512, (qt[:, gh * 4:gh * 4 + 4, i * 128:(i + 1) * 128], [Tq]), ot[:, gh * 4:gh * 4 + 4, i * 128:(i + 1) * 128], To, scale,
                          extra_den=(ed, [T_c]), pools=pools, g4=True)
        P.dma("sp", o1T[kh * 512:(kh + 1) * 512, 1024:1024 + 2048].rearrange("(g d) t -> d g t", d=64), ot[:], so, reads=[To])
    K.end()


def phase_attn_a_and_prep(K):
    K.begin()
    phase_attn_a(K)
    phase_dn_prep(K)
    K.end()


def declare_io(K):
    K.inp("ident", [128, 128])
    K.inp("cond", [2, D])
    K.inp("xp", [NPT * 128, D])
    K.inp("xs", [4096, D])
    K.inp("w_ada", [2, D, 6 * D])
    K.inp("b_ada", [2, 6 * D])
    K.inp("norm_mix", [2, D])
    K.inp("norm_mlp", [2, D])
    K.inp("ab_w_in", [D, 7200])
    K.inp("w_gates", [D, 32])
    K.inp("cache_a_k", [256, 1024])
    K.inp("cache_a_v", [256, 1024])
    K.inp("na_mask", [2, 8, 6, 128, 256])
    K.inp("conv_w", [3, 3072])
    K.inp("alog_dt", [2, 16])
    K.inp("tri", [4, 128, 128])
    K.inp("s0", [2, 8, 128, 128])
    K.inp("onorm", [1, 128])
    K.inp("ab_w_out", [D, D])
    K.inp("w_mlp_in", [2, D, 4 * D])
    K.inp("w_mlp_out", [2, 4 * D, D])
    K.inp("c_w_qkv", [D, 2560])
    K.inp("c_w_out", [D, D])
    K.inp("cache_c_k", [256, 256])
    K.inp("cache_c_v", [256, 256])
    K.inp("c_sink", [1, 32])
    K.inp("final_norm", [1, D])
    K.inp("rope_cos", [128, SEXT * 128])
    K.inp("rope_sin", [128, SEXT * 128])
    K.inp("rope_perm", [128, 128])
    K.outp("nck", [NPT * 128, 256])
    K.outp("ncv", [NPT * 128, 256])
    K.outp("y_p", [NPT * 128, D])
    K.outp("y_s", [2048, D])
    K.outp("nbf", [4, 8, 128, 128])
    K.outp("nbb", [4, 8, 128, 128])
    K.outp("nak", [NPT * 128, 1024])
    K.outp("nav", [NPT * 128, 1024])


def build(stop=99, debug=()):
    nc = bass.Bass("TRN2", target_bir_lowering=False)
    K = Ctx(nc)
    declare_io(K)
    phases = [phase_consts, phase_ada, lambda K: phase_l0_inproj(K, 1), lambda K: phase_l0_inproj(K, 2), phase_attn_a_and_prep, phase_dn_scan, lambda K: phase_mlp(K, 0), phase_l1_inproj, phase_attn_c, lambda K: phase_mlp(K, 1)]
    for i, ph in enumerate(phases):
        if i >= stop:
            break
        ph(K)
    if debug:
        K.begin()
        s = K.getsem()
        for name in debug:
            src = K.dscr[name]
            o = K.outp("dbg_" + name, src.shape, src.dtype)
            nr = src.shape[0]
            step = max(1, min(nr, (1 << 20) // (src.shape[1] * 4)))
            for r0 in range(0, nr, step):
                K.P.dma("sp", o[r0:min(nr, r0 + step)], src[r0:min(nr, r0 + step)], s)
        K.end()
    K.pes.close()
    return nc, K


def na_mask_host(rel_bias, flip):
    out = np.full((2, 8, 6, 128, 256), -30000.0, np.float32)
    qq = np.arange(256)
    kk = np.arange(768)
    for cl in range(2):
        qr = (0 if cl == 0 else 12) + qq // 64
        qc = qq % 64
        kr = (0 if cl == 0 else 8) + kk // 64
        kc = kk % 64
        if flip:
            qr, qc, kr, kc = 63 - qr, 63 - qc, 63 - kr, 63 - kc
        rs = np.clip(qr - 4, 0, 56)
        cs = np.clip(qc - 8, 0, 48)
        vr = (kr[:, None] >= rs[None, :]) & (kr[:, None] < rs[None, :] + 8)
        vc = (kc[:, None] >= cs[None, :]) & (kc[:, None] < cs[None, :] + 16)
        valid = vr & vc
        dr = np.clip(kr[:, None] - qr[None, :] + 7, 0, 14)
        dc = np.clip(kc[:, None] - qc[None, :] + 15, 0, 30)
        for h in range(8):
            b = rel_bias[h][dr, dc]
            m = np.where(valid, b, np.float32(-30000.0)).astype(np.float32)
            out[cl, h] = m.reshape(6, 128, 256)
    return out


def tri_host():
    i = np.arange(128)[:, None]
    j = np.arange(128)[None, :]
    return np.stack([(j <= i), (j >= i), (j < i), (j > i)]).astype(np.float32)


def rope_host(flip):
    nf = 16
    inv = (10000.0 ** (-np.arange(nf, dtype=np.float32) / nf)).astype(np.float32)
    tloc = np.arange(SEXT * 128)
    tok = (4095 - tloc) if flip else tloc
    pos = np.stack([tok // 64, tok % 64], 0).astype(np.float32)
    d = np.arange(128) % 64
    a, b, fidx = d // 32, (d % 32) // 16, d % 16
    ang = pos[a, :] * inv[fidx][:, None]
    cos = np.cos(ang).astype(np.float32)
    sin = np.sin(ang).astype(np.float32)
    perm = np.zeros((128, 128), np.float32)
    for m in range(128):
        bm = (m % 32) // 16
        partner = m + 16 if bm == 0 else m - 16
        perm[partner, m] = -1.0 if bm == 0 else 1.0
    return cos, sin, perm


def host_inputs(inputs, c):
    seq, flip = c // 2, c % 2
    f = lambda a: np.ascontiguousarray(a, dtype=np.float32)
    xp = inputs["x_prompt"][4 * c:4 * c + 4]
    xs = inputs["x_sample"][seq]
    if flip:
        xp = xp[:, ::-1]
        xs = xs[::-1]
    wg = inputs["ab_w_in"][0][:, 7168:7200]
    if flip:
        wg = np.concatenate([wg[:, 8:16], wg[:, 0:8], wg[:, 24:32], wg[:, 16:24]], axis=1)
    m = {
        "ident": np.eye(128, dtype=np.float32),
        "cond": f(np.stack([inputs["c_ctx"], inputs["c"][seq]])),
        "xp": f(xp.reshape(NPT * 128, D)),
        "xs": f(xs),
        "w_ada": inputs["w_ada"], "b_ada": inputs["b_ada"],
        "norm_mix": inputs["norm_mix"], "norm_mlp": inputs["norm_mlp"],
        "ab_w_in": inputs["ab_w_in"][0], "w_gates": f(wg),
        "cache_a_k": f(inputs["cache_a_k"][seq, 0].reshape(256, 1024)),
        "cache_a_v": f(inputs["cache_a_v"][seq, 0].reshape(256, 1024)),
        "na_mask": na_mask_host(inputs["a_rel_bias"][0], flip),
        "ab_w_out": inputs["ab_w_out"][0], "w_mlp_in": inputs["w_mlp_in"], "w_mlp_out": inputs["w_mlp_out"],
        "c_w_qkv": inputs["c_w_qkv"][0], "c_w_out": inputs["c_w_out"][0],
        "cache_c_k": f(inputs["cache_c_k"][seq, 0].reshape(256, 256)), "cache_c_v": f(inputs["cache_c_v"][seq, 0].reshape(256, 256)),
        "c_sink": f(inputs["c_sink"][0].reshape(1, 32)), "final_norm": f(inputs["final_norm"].reshape(1, D)),
        "rope_cos": rope_host(flip)[0], "rope_sin": rope_host(flip)[1], "rope_perm": rope_host(flip)[2],
        "conv_w": f(inputs["b_conv"][0][::-1] if flip else inputs["b_conv"][0]),
        "alog_dt": f(np.stack([(inputs["b_a_log"][0][::-1] if flip else inputs["b_a_log"][0]).reshape(16),
                               (inputs["b_dt_bias"][0][::-1] if flip else inputs["b_dt_bias"][0]).reshape(16)])),
        "tri": tri_host(),
        "s0": f(np.stack([inputs["state_b_bwd"][seq, 0], inputs["state_b_fwd"][seq, 0]]) if flip else
                np.stack([inputs["state_b_fwd"][seq, 0], inputs["state_b_bwd"][seq, 0]])),
        "onorm": f(inputs["b_out_norm"][0].reshape(1, 128)),
    }
    return m


_CACHE = {}


def kernel(**inputs):
    inputs = {k: np.asarray(v) for k, v in inputs.items()}
    if "nc" not in _CACHE:
        _CACHE["nc"] = build()
    nc, K = _CACHE["nc"]
    maps = [host_inputs(inputs, c) for c in range(8)]
    res = run_bass_kernel_spmd(nc, maps, core_ids=list(range(8))).results
    f32 = np.float32
    y_p = np.zeros((32, 256, D), f32)
    y_s = np.zeros((4, 4096, D), f32)
    nak = np.zeros((32, 1, 256, 8, 128), f32)
    nav = np.zeros((32, 1, 256, 8, 128), f32)
    nbf = np.zeros((32, 1, 8, 128, 128), f32)
    nbb = np.zeros((32, 1, 8, 128, 128), f32)
    nck = np.zeros((32, 1, 256, 4, 64), f32)
    ncv = np.zeros((32, 1, 256, 4, 64), f32)
    for c in range(8):
        seq, flip = c // 2, c % 2
        r = {k: np.asarray(v, dtype=f32) for k, v in res[c].items()}
        fl = (lambda a: a[:, ::-1]) if flip else (lambda a: a)
        sl = slice(4 * c, 4 * c + 4)
        y_p[sl] = fl(r["y_p"].reshape(4, 256, D))
        if flip:
            y_s[seq, 2048:4096] = r["y_s"][::-1]
        else:
            y_s[seq, 0:2048] = r["y_s"]
        nak[sl, 0] = fl(r["nak"].reshape(4, 256, 8, 128))
        nav[sl, 0] = fl(r["nav"].reshape(4, 256, 8, 128))
        nck[sl, 0] = fl(r["nck"].reshape(4, 256, 4, 64))
        ncv[sl, 0] = fl(r["ncv"].reshape(4, 256, 4, 64))
        if flip:
            nbf[sl, 0], nbb[sl, 0] = r["nbb"], r["nbf"]
        else:
            nbf[sl, 0], nbb[sl, 0] = r["nbf"], r["nbb"]
    return (y_p, y_s, nak, nav, nbf, nbb, nck, ncv)
```

```python
import contextlib
import numpy as np
import concourse.bass as bass
import concourse.mybir as mybir

F32 = mybir.dt.float32
BF16 = mybir.dt.bfloat16
I32 = mybir.dt.int32
AF = mybir.ActivationFunctionType
ALU = mybir.AluOpType
AX = mybir.AxisListType

ENGS = ("pe", "act", "dve", "pool", "sp")
HANDLES = {"pe": "tensor", "act": "scalar", "dve": "vector", "pool": "gpsimd", "sp": "sync"}
SEM_LIMIT = 16000


class Tile:
    __slots__ = ("name", "writer", "readers", "excl")

    def __init__(self, name=""):
        self.name = name
        self.writer = None
        self.readers = {}
        self.excl = False


class DSem:
    def __init__(self, prog, name):
        self.prog = prog
        self.name = name
        self.gen = 0
        self.h = prog.nc.alloc_semaphore(name=name)
        self.count = 0
        self.last = None

    def bump(self):
        if self.count + 16 > SEM_LIMIT:
            self.gen += 1
            self.h = self.prog.nc.alloc_semaphore(name=f"{self.name}_g{self.gen}")
            self.count = 0
        self.count += 16
        return self.h, self.count


class Ins:
    __slots__ = ("eng", "fn", "deps", "sem", "count", "needed", "is_dma", "epoch")

    def __init__(self, eng, fn, is_dma=False):
        self.eng = eng
        self.fn = fn
        self.deps = []
        self.sem = None
        self.count = None
        self.needed = False
        self.is_dma = is_dma
        self.epoch = 0


class Prog:
    def __init__(self, nc):
        self.nc = nc
        self.lists = {e: [] for e in ENGS}
        self.esem = {e: nc.alloc_semaphore(name=f"es_{e}_0") for e in ENGS}
        self.esem_gen = {e: 0 for e in ENGS}
        self.ecount = {e: 0 for e in ENGS}
        self.known = {e: {} for e in ENGS}
        self.epoch = 0
        self.n_ins = 0
        self.dsems = []
        self.last_ins = {e: None for e in ENGS}

    def tile(self, name=""):
        return Tile(name)

    def tiles(self, n, name=""):
        return [Tile(f"{name}{i}") for i in range(n)]

    def dsem(self, name):
        d = DSem(self, name)
        self.dsems.append(d)
        return d

    def _add(self, eng, fn, reads, writes, dsem=None):
        ins = Ins(eng, fn, is_dma=dsem is not None)
        ins.epoch = self.epoch
        deps = []
        for t in reads:
            if t.writer is not None:
                deps.append(t.writer)
            if t.excl:
                deps.extend(r for r in t.readers.values() if r.eng != eng)
        for t in writes:
            if t.writer is not None:
                deps.append(t.writer)
            deps.extend(t.readers.values())
        if dsem is not None:
            if dsem.last is not None:
                deps.append(dsem.last)
            ins.sem, ins.count = dsem.bump()
            ins.needed = True
            dsem.last = ins
        out = []
        seen = set()
        for d in deps:
            if d is ins or id(d) in seen:
                continue
            seen.add(id(d))
            if d.epoch < self.epoch:
                continue
            if eng == "pe" and d.eng == "pe" and not d.is_dma and dsem is None:
                continue
            out.append(d)
        ins.deps = out
        for d in out:
            d.needed = True
        for t in reads:
            key = (eng, dsem.name) if dsem is not None else eng
            t.readers[key] = ins
        for t in writes:
            t.writer = ins
            t.readers = {}
        self.lists[eng].append(ins)
        self.last_ins[eng] = ins
        self.n_ins += 1
        return ins

    def op(self, eng, fn, reads=(), writes=()):
        return self._add(eng, fn, list(reads), list(writes))

    def dma(self, eng, out, in_, dsem, reads=(), writes=()):
        return self._add(eng, lambda e: e.dma_start(out=out, in_=in_), list(reads), list(writes), dsem=dsem)

    def barrier(self):
        deps = [i for i in self.last_ins.values() if i is not None]
        deps += [d.last for d in self.dsems if d.last is not None]
        deps = [d for d in deps if d.epoch == self.epoch]
        for d in deps:
            d.needed = True
        for e in ENGS:
            ins = Ins(e, None)
            ins.epoch = self.epoch
            ins.deps = list(deps)
            self.lists[e].append(ins)

    def flush(self, final=False):
        self.barrier()
        for e in ENGS:
            for ins in self.lists[e]:
                if ins.is_dma or ins.fn is None:
                    continue
                if ins.needed:
                    if self.ecount[e] + 1 > SEM_LIMIT:
                        self.esem_gen[e] += 1
                        self.esem[e] = self.nc.alloc_semaphore(name=f"es_{e}_{self.esem_gen[e]}")
                        self.ecount[e] = 0
                    self.ecount[e] += 1
                    ins.sem = self.esem[e]
                    ins.count = self.ecount[e]
        prog = self

        def run(e, h):
            known = prog.known[e]
            for ins in prog.lists[e]:
                need = {}
                for d in ins.deps:
                    k = id(d.sem)
                    if known.get(k, 0) >= d.count:
                        continue
                    if k not in need or need[k][1] < d.count:
                        need[k] = (d.sem, d.count)
                for k, (s, v) in need.items():
                    h.wait_ge(s, v)
                    known[k] = v
                if ins.fn is None:
                    continue
                bi = ins.fn(h)
                if ins.is_dma:
                    bi.then_inc(ins.sem, 16)
                elif ins.needed:
                    bi.then_inc(ins.sem, 1)

        with self.nc.Block() as block:
            for e in ENGS:
                if not self.lists[e]:
                    continue
                dec = getattr(block, HANDLES[e])

                def mk(e):
                    def _f(h):
                        run(e, h)
                    return _f
                dec(mk(e))
        self.lists = {e: [] for e in ENGS}
        self.epoch += 1

from concourse.bass_utils import run_bass_kernel_spmd

D = 2048
KC = 16
EPS = 1e-6
NPT = 8
NS1 = 19
NG1 = NPT + NS1
NS2 = 13
TOK1 = NG1 * 128
TOKB = (NG1 + NS2) * 128
SEXT = 17


class Ring:
    def __init__(self, K, name, shape, dt, n, sw=False):
        self.bufs = [K.sb(f"{name}{i}", shape, dt) for i in range(n)]
        self.tiles = [Tile(f"{name}{i}") for i in range(n)]
        self.sems = [K.getsem(sw) for i in range(n)]
        self.i = 0

    def next(self):
        k = self.i % len(self.bufs)
        self.i += 1
        return self.bufs[k], self.tiles[k], self.sems[k]


class Ctx:
    def __init__(self, nc):
        self.nc = nc
        self.P = Prog(nc)
        self.es = None
        self.pes = contextlib.ExitStack()
        self.uid = 0
        self.sem_pool = {False: [], True: []}
        self.sem_used = {False: [], True: []}
        self.PS = [nc.alloc_psum_tensor(f"psb{i}", [128, 512], F32) for i in range(8)]
        self.TPS = [Tile(f"ps{i}") for i in range(8)]
        for t_ in self.TPS:
            t_.excl = True
        self.psi = [0, 0]
        self.depth = 0
        self.din = {}
        self.dout = {}
        self.dscr = {}

    def inp(self, name, shape, dt=F32):
        self.din[name] = self.nc.dram_tensor(name, list(shape), dt, kind="ExternalInput").ap()
        return self.din[name]

    def outp(self, name, shape, dt=F32):
        self.dout[name] = self.nc.dram_tensor(name, list(shape), dt, kind="ExternalOutput").ap()
        return self.dout[name]

    def scr(self, name, shape, dt):
        self.dscr[name] = self.nc.dram_tensor(name, list(shape), dt).ap()
        return self.dscr[name]

    def begin(self):
        if self.es is not None:
            self.depth += 1
            return
        self.es = contextlib.ExitStack()

    def end(self):
        if self.depth > 0:
            self.depth -= 1
            return
        self.P.flush()
        self.es.close()
        self.es = None
        for k in (False, True):
            self.sem_pool[k].extend(self.sem_used[k])
            self.sem_used[k] = []

    def sb(self, name, shape, dt):
        self.uid += 1
        return self.es.enter_context(self.nc.sbuf_tensor(f"{name}_{self.uid}", list(shape), dt))

    def psb(self, name, shape, dt):
        self.uid += 1
        return self.pes.enter_context(self.nc.sbuf_tensor(f"{name}_{self.uid}", list(shape), dt))

    def getsem(self, sw=False):
        if self.sem_pool[sw]:
            s = self.sem_pool[sw].pop()
        else:
            self.uid += 1
            s = self.P.dsem(f"ds{'w' if sw else 'h'}{self.uid}")
        self.sem_used[sw].append(s)
        return s

    def dump(self, name, ap, tiles):
        import os
        if not os.environ.get("DN_DEBUG") or name in self.dscr:
            return
        d = self.scr(name, list(ap.shape), ap.dtype)
        self.P.dma("sp", d, ap, self.getsem(), reads=tiles)

    def ps(self, g=0):
        k = g * 4 + self.psi[g] % 4
        self.psi[g] += 1
        return self.PS[k], self.TPS[k]


def rows_T(K, dst, T_dst, src2d, n, stage, T_stage, sem):
    P = K.P
    P.dma("sp", stage[0:n, :], src2d, sem, writes=[T_stage])
    ps, Tp = K.ps()
    P.op("pe", lambda e: e.transpose(ps[:, 0:n], stage[0:n, :], K.identf[0:n, 0:n]), reads=[T_stage, K.T_const], writes=[Tp])
    P.op("dve", lambda e: e.tensor_copy(dst, ps[:, 0:n]), reads=[Tp], writes=[T_dst])


def phase_consts(K):
    P = K.P
    K.begin()
    K.T_const = Tile("const")
    K.identf = K.psb("identf", [128, 128], F32)
    K.identb = K.psb("identb", [128, 128], BF16)
    K.onesf = K.psb("onesf", [128, 128], F32)
    K.onesb = K.psb("onesb", [128, 128], BF16)
    s = K.getsem()
    s2 = K.getsem(True)
    P.dma("sp", K.identf[:], K.din["ident"], s, writes=[K.T_const])
    P.dma("pool", K.identb[:], K.din["ident"], s2, writes=[K.T_const])
    P.op("dve", lambda e: e.memset(K.onesf[:], 1.0), writes=[K.T_const])
    P.op("dve", lambda e: e.memset(K.onesb[:], 1.0), writes=[K.T_const])
    K.end()


def phase_ada(K):
    nc, P = K.nc, K.P
    modrow_d = K.scr("modrow_d", [2, 2, 6 * D], F32)
    K.begin()
    cond = K.sb("cond", [2, D], F32)
    cs = K.sb("cs", [2, D], F32)
    condT = K.sb("condT", [128, KC, 2], BF16)
    brow = K.sb("brow", [2, 6 * D], F32)
    modrow = K.sb("modrow", [2, 6 * D], F32)
    T_cond, T_cs, T_condT, T_brow, T_modrow, T_mrd = P.tiles(6, "ada")
    s0, s1 = K.getsem(), K.getsem()
    wr = Ring(K, "adaw", [128, KC, 512], BF16, 3, sw=True)
    P.dma("sp", cond[:], K.din["cond"], s0, writes=[T_cond])
    P.op("act", lambda e: e.activation(out=cs[:], in_=cond[:], func=AF.Silu), reads=[T_cond], writes=[T_cs])
    ps, Tp = K.ps()
    for kc in range(KC):
        P.op("pe", lambda e, kc=kc, ps=ps: e.transpose(ps[:, kc * 2:kc * 2 + 2], cs[0:2, kc * 128:(kc + 1) * 128], K.identf[0:2, 0:2]),
             reads=[T_cs, K.T_const], writes=[Tp])
    P.op("dve", lambda e, ps=ps: e.tensor_copy(condT[:].rearrange("p k c -> p (k c)"), ps[:, 0:2 * KC]), reads=[Tp], writes=[T_condT])
    for l in range(2):
        P.dma("sp", brow[:], K.din["b_ada"][l:l + 1, :].partition_broadcast(2), s0, writes=[T_brow])
        for cg in range(24):
            wt, Tw, sw = wr.next()
            P.dma("pool", wt[:], K.din["w_ada"][l, :, cg * 512:(cg + 1) * 512].rearrange("(kc p) n -> p kc n", p=128), sw, writes=[Tw])
            ps, Tp = K.ps()
            for kc in range(KC):
                P.op("pe", lambda e, kc=kc, ps=ps, wt=wt: e.matmul(ps[0:2, :], condT[:, kc, :], wt[:, kc, :], start=(kc == 0), stop=(kc == KC - 1)),
                     reads=[T_condT, Tw], writes=[Tp])
            P.op("dve", lambda e, ps=ps, cg=cg: e.tensor_tensor(modrow[0:2, cg * 512:(cg + 1) * 512], ps[0:2, :], brow[0:2, cg * 512:(cg + 1) * 512], ALU.add),
                 reads=[Tp, T_brow], writes=[T_modrow])
        P.dma("sp", modrow_d[l], modrow[:], s1, reads=[T_modrow], writes=[T_mrd])
    K.end()
    K.begin()
    K.T_mod = Tile("mod")
    K.modF = [[K.psb(f"modF{l}{c}", [128, 96], F32) for c in range(2)] for l in range(2)]
    K.gsF = [[[K.psb(f"gsF{l}{w}{c}", [128, KC], F32) for c in range(2)] for w in range(2)] for l in range(2)]
    K.fnorm = K.psb("fnormF", [128, KC], F32)
    stage = K.sb("stg", [128, 128], F32)
    gF = K.sb("gF", [128, KC], F32)
    T_stage, T_g = P.tiles(2, "adaf")
    s0 = K.getsem()
    for l in range(2):
        for c in range(2):
            rows_T(K, K.modF[l][c][:], K.T_mod, modrow_d[l, c].rearrange("(r p) -> r p", p=128), 96, stage, T_stage, s0)
        for w, nm in enumerate(("norm_mix", "norm_mlp")):
            rows_T(K, gF[:], T_g, K.din[nm][l].rearrange("(r p) -> r p", p=128), KC, stage, T_stage, s0)
            for c in range(2):
                sc = K.modF[l][c][:, (1 + 3 * w) * KC:(2 + 3 * w) * KC]
                P.op("dve", lambda e, l=l, w=w, c=c, sc=sc: e.scalar_tensor_tensor(K.gsF[l][w][c][:], sc, 1.0, gF[:], ALU.add, ALU.mult),
                     reads=[K.T_mod, T_g], writes=[K.T_mod])
    K.end()


class NormT:
    def __init__(self, K, n=2):
        self.K = K
        self.ss = Ring(K, "nss", [128, 2], F32, n)
        self.xn = Ring(K, "nxn", [128, D], BF16, n)

    def run(self, xt, T_x, gs, shift, dst_fn, T_dst_fn):
        K = self.K
        P = K.P
        ss, T_ss, _ = self.ss.next()
        xn, T_xn, _ = self.xn.next()
        P.op("act", lambda e: e.activation(out=xn[:], in_=xt, func=AF.Square, accum_out=ss[:, 0:1]), reads=[T_x], writes=[T_xn, T_ss])
        import os
        NTL = int(os.environ.get("NT_LEVEL", "9"))
        if NTL < 2:
            return
        P.op("act", lambda e: e.activation(out=ss[:, 1:2], in_=ss[:, 0:1], func=AF.Sqrt, bias=EPS, scale=1.0 / D), reads=[T_ss], writes=[T_ss])
        P.op("dve", lambda e: e.reciprocal(ss[:, 1:2], ss[:, 1:2]), reads=[T_ss], writes=[T_ss])
        if NTL < 3:
            return
        P.op("act", lambda e: e.activation(out=xn[:], in_=xt, func=AF.Identity, scale=ss[:, 1:2]), reads=[T_x, T_ss], writes=[T_xn])
        if NTL < 4:
            return
        for half in range(2):
            ps, Tp = K.ps()
            psb = ps[:].bitcast(BF16)
            for j in range(8):
                kc = half * 8 + j
                P.op("pe", lambda e, j=j, kc=kc, psb=psb: e.transpose(psb[:, j * 128:(j + 1) * 128], xn[:, kc * 128:(kc + 1) * 128], K.identb[:]),
                     reads=[T_xn, K.T_const], writes=[Tp])
            for j in range(8):
                kc = half * 8 + j
                if True:
                    P.op("dve", lambda e, j=j, kc=kc, psb=psb: e.tensor_scalar(dst_fn(kc), psb[:, j * 128:(j + 1) * 128], gs[:, kc:kc + 1], shift[:, kc:kc + 1], ALU.mult, ALU.add),
                         reads=[Tp, K.T_mod], writes=[T_dst_fn(kc)])
                else:
                    P.op("act", lambda e, j=j, kc=kc, psb=psb: e.activation(out=dst_fn(kc), in_=psb[:, j * 128:(j + 1) * 128], func=AF.Identity, scale=gs[:, kc:kc + 1], bias=shift[:, kc:kc + 1]),
                         reads=[Tp, K.T_mod], writes=[T_dst_fn(kc)])


def token_groups(n_tb, breaks=()):
    out = []
    pts = [0] + list(breaks) + [n_tb]
    for a, b in zip(pts[:-1], pts[1:]):
        t = a
        while t < b:
            m = min(4, b - t)
            out.append((t, m))
            t += m
    return out


def phase_l0_inproj(K, which_pass):
    nc, P = K.nc, K.P
    if which_pass == 1:
        K.scr("qaT_d", [1024, TOK1], BF16)
        K.scr("kaT_d", [1024, TOK1], BF16)
        K.scr("va_d", [TOK1, 1024], BF16)
        K.scr("qkvT_d", [3072, TOKB], F32)
        K.scr("z_d", [TOK1, 1024], F32)
        K.scr("gates_d", [TOKB, 32], F32)
        ntb = NG1
        srcs = [(K.din["xp"][g * 128:(g + 1) * 128, :], 0) for g in range(NPT)] + \
               [(K.din["xs"][g * 128:(g + 1) * 128, :], 1) for g in range(NS1)]
        tok0 = 0
        groups = token_groups(NG1, breaks=(NPT,))
    else:
        ntb = NS2
        srcs = [(K.din["xs"][(NS1 + g) * 128:(NS1 + g + 1) * 128, :], 1) for g in range(NS2)]
        tok0 = TOK1
        groups = token_groups(NS2)
    K.begin()
    hT = K.sb("hT", [128, KC, ntb * 128], BF16)
    T_h = [[Tile(f"h{g}_{kc}") for kc in range(KC)] for g in range(ntb)]
    xr = Ring(K, "xin", [128, D], F32, 3)
    nt = NormT(K)
    for g, (src, c) in enumerate(srcs):
        xt, T_x, sx = xr.next()
        P.dma("sp", xt[:], src, sx, writes=[T_x])
        nt.run(xt[:], T_x, K.gsF[0][0][c], K.modF[0][c][:, 0:KC],
               lambda kc, g=g: hT[:, kc, g * 128:(g + 1) * 128], lambda kc, g=g: T_h[g][kc])
    wr = Ring(K, "w0", [128, KC, 512], BF16, 3, sw=True)
    st32 = Ring(K, "st32", [128, 512], F32, 3)
    st16 = Ring(K, "st16", [128, 512], BF16, 3)
    W = K.din["ab_w_in"]
    evi = [0]

    def evac(dst, src, T_src, T_dst):
        evi[0] += 1
        if evi[0] % 2:
            P.op("dve", lambda e: e.tensor_copy(dst, src), reads=[T_src], writes=[T_dst])
        else:
            P.op("act", lambda e: e.copy(dst, src), reads=[T_src], writes=[T_dst])

    def fm(wt, Tw, ncol_blocks, dst_d, row0, f32):
        for sub in range(ncol_blocks):
            for (t0, m) in groups:
                n = m * 128
                ps, Tp = K.ps()
                for kc in range(KC):
                    P.op("pe", lambda e, kc=kc, ps=ps, sub=sub, t0=t0, n=n: e.matmul(ps[:, 0:n], wt[:, kc, sub * 128:(sub + 1) * 128], hT[:, kc, t0 * 128:t0 * 128 + n], start=(kc == 0), stop=(kc == KC - 1)),
                         reads=[Tw] + [T_h[t0 + i][kc] for i in range(m)], writes=[Tp])
                sg, Ts, ss_ = (st32 if f32 else st16).next()
                evac(sg[:, 0:n], ps[:, 0:n], Tp, Ts)
                P.dma("sp", dst_d[row0 + sub * 128:row0 + (sub + 1) * 128, tok0 + t0 * 128:tok0 + t0 * 128 + n], sg[:, 0:n], ss_, reads=[Ts])

    def tm(wt, Tw, ncols, tbs, dests):
        for g in tbs:
            ps, Tp = K.ps()
            for kc in range(KC):
                P.op("pe", lambda e, kc=kc, ps=ps, g=g: e.matmul(ps[:, 0:ncols], hT[:, kc, g * 128:(g + 1) * 128], wt[:, kc, 0:ncols], start=(kc == 0), stop=(kc == KC - 1)),
                     reads=[Tw, T_h[g][kc]], writes=[Tp])
            for (dfn, f32) in dests:
                d = dfn(g)
                if d is None:
                    continue
                sg, Ts, ss_ = (st32 if f32 else st16).next()
                evac(sg[:, 0:ncols], ps[:, 0:ncols], Tp, Ts)
                P.dma("sp", d, sg[:, 0:ncols], ss_, reads=[Ts])

    wg32 = K.sb("wg32", [128, KC, 32], F32)
    T_wg32 = Tile("wg32")
    s_wg = K.getsem()

    def loadw(src, ncols=512):
        wt, Tw, sw = wr.next()
        if ncols == 512:
            P.dma("pool", wt[:, :, 0:ncols], src.rearrange("(kc p) n -> p kc n", p=128), sw, writes=[Tw])
        else:
            P.dma("sp", wg32[:], src.rearrange("(kc p) n -> p kc n", p=128), s_wg, writes=[T_wg32])
            P.op("dve", lambda e, wt=wt: e.tensor_copy(wt[:, :, 0:ncols], wg32[:]), reads=[T_wg32], writes=[Tw])
        return wt, Tw

    S = K.dscr
    if which_pass == 1:
        import os
        for t in range(int(os.environ.get('KDBG_NT', '14'))):
            wt, Tw = loadw(W[:, t * 512:(t + 1) * 512])
            if t < 2:
                fm(wt, Tw, 4, S["qaT_d"], t * 512, False)
            elif t < 4:
                fm(wt, Tw, 4, S["kaT_d"], (t - 2) * 512, False)
                tm(wt, Tw, 512, range(NPT), [(lambda g, t=t: K.dout["nak"][g * 128:(g + 1) * 128, (t - 2) * 512:(t - 1) * 512], True)])
            elif t < 6:
                tm(wt, Tw, 512, range(NG1), [(lambda g, t=t: S["va_d"][g * 128:(g + 1) * 128, (t - 4) * 512:(t - 3) * 512], False),
                                            (lambda g, t=t: K.dout["nav"][g * 128:(g + 1) * 128, (t - 4) * 512:(t - 3) * 512] if g < NPT else None, True)])
            elif t < 12:
                fm(wt, Tw, 4, S["qkvT_d"], (t - 6) * 512, True)
            else:
                tm(wt, Tw, 512, range(NG1), [(lambda g, t=t: S["z_d"][g * 128:(g + 1) * 128, (t - 12) * 512:(t - 11) * 512], True)])
        if int(os.environ.get('KDBG_G', '1')):
          wt, Tw = loadw(K.din["w_gates"], 32)
          tm(wt, Tw, 32, range(NG1), [(lambda g: S["gates_d"][g * 128:(g + 1) * 128, :], True)])
    else:
        for t in range(8, 12):
            wt, Tw = loadw(W[:, t * 512:(t + 1) * 512])
            fm(wt, Tw, 4, S["qkvT_d"], (t - 6) * 512, True)
        wt, Tw = loadw(K.din["w_gates"], 32)
        tm(wt, Tw, 32, range(NS2), [(lambda g: S["gates_d"][tok0 + g * 128:tok0 + (g + 1) * 128, :], True)])
    K.end()


NCAT = NPT + SEXT
TOKC = NCAT * 128


def attn_core(K, S_list, nq, rhs_q, out_ap, T_out, scale, extra_den=None, pools=None, g4=False):
    P = K.P
    q_ap, q_tiles = rhs_q
    M = out_ap.shape[0]
    psn, Tn = K.ps(1)
    psd, Td = K.ps(1)
    pt_ring, rec_ring = pools
    n = len(S_list)
    v3 = (lambda ap: ap.rearrange("p (g t) -> p g t", g=4)) if g4 else (lambda ap: ap)
    def s_mm(i):
        lk, tk = S_list[i][0], S_list[i][1]
        pss, Ts = K.ps(0)
        P.op("pe", lambda e, pss=pss, lk=lk: e.matmul(v3(pss[:, 0:nq]), lk, q_ap, start=True, stop=True), reads=tk + q_tiles, writes=[Ts])
        return pss, Ts

    nxt = s_mm(0)
    for i, (lk, tk, lv, tv, mask, tm_, _) in enumerate(S_list):
        pss, Ts = nxt
        if i + 1 < n:
            nxt = s_mm(i + 1)
        pt, Tpt, _ = pt_ring.next()
        P.op("act", lambda e, pss=pss, pt=pt: e.activation(out=pt[:, 0:nq], in_=pss[:, 0:nq], func=AF.Exp, scale=scale), reads=[Ts], writes=[Tpt])
        if mask is not None:
            P.op("pool", lambda e, pt=pt, mask=mask: e.tensor_tensor(v3(pt[:, 0:nq]), v3(pt[:, 0:nq]), mask, ALU.mult), reads=[Tpt] + tm_, writes=[Tpt])
        P.op("pe", lambda e, pt=pt, lv=lv, i=i: e.matmul(psn[0:M, 0:nq], lv, pt[:, 0:nq], start=(i == 0), stop=(i == n - 1)), reads=tv + [Tpt], writes=[Tn])
        P.op("pe", lambda e, pt=pt, i=i: e.matmul(psd[0:M, 0:nq], K.onesb[:, 0:M], pt[:, 0:nq], start=(i == 0), stop=(i == n - 1)), reads=[K.T_const, Tpt], writes=[Td])
    rec, Trec, _ = rec_ring.next()
    if extra_den is not None:
        ed, ted = extra_den
        if g4:
            P.op("dve", lambda e: e.tensor_tensor(v3(rec[0:M, 0:nq]), v3(psd[0:M, 0:nq]), ed, ALU.add), reads=[Td] + ted, writes=[Trec])
        else:
            P.op("dve", lambda e: e.tensor_scalar_add(rec[0:M, 0:nq], psd[0:M, 0:nq], ed), reads=[Td] + ted, writes=[Trec])
        P.op("dve", lambda e: e.reciprocal(rec[0:M, 0:nq], rec[0:M, 0:nq]), reads=[Trec], writes=[Trec])
    else:
        P.op("dve", lambda e: e.reciprocal(rec[0:M, 0:nq], psd[0:M, 0:nq]), reads=[Td], writes=[Trec])
    P.op("dve", lambda e: e.tensor_tensor(out_ap, v3(psn[0:M, 0:nq]), v3(rec[0:M, 0:nq]), ALU.mult), reads=[Tn, Trec], writes=[T_out])


def phase_attn_a(K):
    nc, P = K.nc, K.P
    S = K.dscr
    catT = K.scr("catT_d", [D, TOKC], BF16)
    scale = 128 ** -0.5
    K.begin()
    pt_ring = Ring(K, "pt", [128, 256], BF16, 4)
    rec_ring = Ring(K, "rec", [128, 256], F32, 2)
    pools = (pt_ring, rec_ring)
    qr = Ring(K, "cq", [128, 8, 256], BF16, 2)
    kr = Ring(K, "ck", [128, 8, 256], BF16, 2)
    vr = Ring(K, "cv", [128, 2, 1024], BF16, 2)
    orr = Ring(K, "co", [128, 8, 256], BF16, 2)
    def ld_p(s):
        qt, Tq, sq = qr.next()
        kt, Tk, sk = kr.next()
        vt, Tv, sv = vr.next()
        P.dma("sp", qt[:], S["qaT_d"][:, s * 256:(s + 1) * 256].rearrange("(h p) t -> p h t", p=128), sq, writes=[Tq])
        P.dma("sp", kt[:], S["kaT_d"][:, s * 256:(s + 1) * 256].rearrange("(h p) t -> p h t", p=128), sk, writes=[Tk])
        P.dma("sp", vt[:], S["va_d"][s * 256:(s + 1) * 256, :].rearrange("(c p) f -> p c f", p=128), sv, writes=[Tv])
        return qt, Tq, kt, Tk, vt, Tv

    nxt_p = ld_p(0)
    for s in range(4):
        qt, Tq, kt, Tk, vt, Tv = nxt_p
        if s + 1 < 4:
            nxt_p = ld_p(s + 1)
        ot, To, so = orr.next()
        for h in range(8):
            sl = [(kt[:, h, c * 128:(c + 1) * 128], [Tk], vt[:, c, h * 128:(h + 1) * 128], [Tv], None, [], 0) for c in range(2)]
            attn_core(K, sl, 256, (qt[:, h, :], [Tq]), ot[:, h, :], To, scale, pools=pools)
        P.dma("sp", catT[0:1024, s * 256:(s + 1) * 256].rearrange("(h p) t -> p h t", p=128), ot[:], so, reads=[To])
    ck_tm = K.sb("ck_tm", [128, 2, 1024], BF16)
    cvt = K.sb("cvt", [128, 2, 1024], BF16)
    ckT = K.sb("ckT", [128, 8, 256], BF16)
    T_ck, T_cv, T_ckT, T_eb = P.tiles(4, "na")
    sw0 = K.getsem(True)
    P.dma("pool", ck_tm[:], K.din["cache_a_k"].rearrange("(c p) f -> p c f", p=128), sw0, writes=[T_ck])
    P.dma("pool", cvt[:], K.din["cache_a_v"].rearrange("(c p) f -> p c f", p=128), sw0, writes=[T_cv])
    for c in range(2):
        ps, Tp = K.ps(0)
        psb = ps[:].bitcast(BF16)
        for h in range(8):
            P.op("pe", lambda e, c=c, h=h, psb=psb: e.transpose(psb[:, h * 128:(h + 1) * 128], ck_tm[:, c, h * 128:(h + 1) * 128], K.identb[:]), reads=[T_ck, K.T_const], writes=[Tp])
        P.op("dve", lambda e, c=c, psb=psb: e.tensor_copy(ckT[:, :, c * 128:(c + 1) * 128], psb.rearrange("p (h k) -> p h k", h=8)), reads=[Tp], writes=[T_ckT])
    EB = K.sb("EB", [128, 2, 8, 6, 256], BF16)
    mr = Ring(K, "mstage", [128, 6, 256], F32, 2)
    for cl in range(2):
        for h in range(8):
            mt, Tm, sm = mr.next()
            P.dma("sp", mt[:], K.din["na_mask"][cl, h].rearrange("c k q -> k c q"), sm, writes=[Tm])
            P.op("act", lambda e, cl=cl, h=h, mt=mt: e.activation(out=EB[:, cl, h, :, :], in_=mt[:], func=AF.Exp), reads=[Tm], writes=[T_eb])
    NQ = SEXT * 128
    NK = NS1 * 128
    qh = Ring(K, "nq", [128, NQ], BF16, 2)
    kh = Ring(K, "nk", [128, NK], BF16, 2)
    vh = Ring(K, "nv", [128, NS1, 128], BF16, 2)
    oh = Ring(K, "no", [128, NQ], BF16, 2)
    def ld_h(h):
        qt, Tq, sq = qh.next()
        kt, Tk, sk = kh.next()
        vt, Tv, sv = vh.next()
        P.dma("sp", qt[:], S["qaT_d"][h * 128:(h + 1) * 128, 1024:1024 + NQ], sq, writes=[Tq])
        P.dma("sp", kt[:], S["kaT_d"][h * 128:(h + 1) * 128, 1024:1024 + NK], sk, writes=[Tk])
        P.dma("sp", vt[:], S["va_d"][1024:1024 + NK, h * 128:(h + 1) * 128].rearrange("(c p) f -> p c f", p=128), sv, writes=[Tv])
        return qt, Tq, kt, Tk, vt, Tv

    nxt_h = ld_h(0)
    for h in range(8):
        qt, Tq, kt, Tk, vt, Tv = nxt_h
        if h + 1 < 8:
            nxt_h = ld_h(h + 1)
        ot, To, so = oh.next()
        for i in range(9):
            nq = 256 if i < 8 else 128
            cl = 0 if i == 0 else 1
            base = 0 if i == 0 else (i - 1) * 256
            nch = 6 if i < 8 else 5
            sl = []
            for ch in range(nch):
                t0 = base + ch * 128
                sl.append((kt[:, t0:t0 + 128], [Tk], vt[:, t0 // 128, :], [Tv], EB[:, cl, h, ch, 0:nq], [T_eb], 0))
            for c in range(2):
                sl.append((ckT[:, h, c * 128:(c + 1) * 128], [T_ckT], cvt[:, c, h * 128:(h + 1) * 128], [T_cv], None, [], 0))
            attn_core(K, sl, nq, (qt[:, i * 256:i * 256 + nq], [Tq]), ot[:, i * 256:i * 256 + nq], To, scale, pools=pools)
        P.dma("sp", catT[h * 128:(h + 1) * 128, 1024:1024 + NQ], ot[:], so, reads=[To])
    K.end()


def phase_dn_prep(K):
    nc, P = K.nc, K.P
    S = K.dscr
    K.scr("qnT_d", [1024, TOK1], BF16)
    K.scr("knT_d", [1024, TOKB], BF16)
    K.scr("ktm_d", [TOKB, 1024], BF16)
    K.scr("vtm_d", [TOKB, 1024], BF16)
    K.begin()
    cwF = K.sb("cwF", [128, 3, 24], F32)
    stage = K.sb("cstg", [128, 128], F32)
    T_cw, T_stage = P.tiles(2, "cw")
    s0 = K.getsem()
    for j in range(3):
        rows_T(K, cwF[:, j, :], T_cw, K.din["conv_w"][j].rearrange("(r p) -> r p", p=128), 24, stage, T_stage, s0)
    pieces = [(s * 256, 256, True, True, 256) for s in range(4)]
    for i in range(8):
        nqv = 512 if i < 4 else (256 if i == 4 else 0)
        pieces.append((1024 + i * 512, 512, i == 0, i == 7, nqv))
    xr = Ring(K, "dx", [128, 514], F32, 3)
    yr = Ring(K, "dy", [128, 512], F32, 2)
    sqr = Ring(K, "dsq", [128, 512], F32, 2)
    rsr = Ring(K, "drs", [128, 512], F32, 2)
    ynr = Ring(K, "dyn", [128, 512], BF16, 3)
    tmr = Ring(K, "dtm", [128, 4, 128], BF16, 3)
    for fb in range(24):
        kind, h = fb // 8, fb % 8
        for (tok0, n0, le, re_, nq) in pieces:
            n = nq if kind == 0 else n0
            if n == 0:
                continue
            re2 = re_ and n == n0
            x, Tx, sx = xr.next()
            a = 1 if le else 0
            b = n + 1 if re2 else n + 2
            if le:
                P.op("pool", lambda e, x=x: e.memset(x[:, 0:1], 0.0), writes=[Tx])
            if re2:
                P.op("pool", lambda e, x=x, n=n: e.memset(x[:, n + 1:n + 2], 0.0), writes=[Tx])
            P.dma("sp", x[:, a:b], S["qkvT_d"][fb * 128:(fb + 1) * 128, tok0 - 1 + a:tok0 - 1 + b], sx, writes=[Tx])
            y, Ty, _ = yr.next()
            P.op("pool", lambda e, x=x, y=y, n=n, fb=fb: e.tensor_scalar_mul(y[:, 0:n], x[:, 0:n], cwF[:, 0, fb:fb + 1]), reads=[Tx, T_cw], writes=[Ty])
            P.op("dve", lambda e, x=x, y=y, n=n, fb=fb: e.scalar_tensor_tensor(y[:, 0:n], x[:, 1:n + 1], cwF[:, 1, fb:fb + 1], y[:, 0:n], ALU.mult, ALU.add), reads=[Tx, T_cw, Ty], writes=[Ty])
            P.op("dve", lambda e, x=x, y=y, n=n, fb=fb: e.scalar_tensor_tensor(y[:, 0:n], x[:, 2:n + 2], cwF[:, 2, fb:fb + 1], y[:, 0:n], ALU.mult, ALU.add), reads=[Tx, T_cw, Ty], writes=[Ty])
            P.op("act", lambda e, y=y, n=n: e.activation(out=y[:, 0:n], in_=y[:, 0:n], func=AF.Silu), reads=[Ty], writes=[Ty])
            yn, Tyn, syn = ynr.next()
            if kind < 2:
                sq, Tsq, _ = sqr.next()
                rs, Trs, _ = rsr.next()
                P.op("pool", lambda e, y=y, sq=sq, n=n: e.tensor_tensor(sq[:, 0:n], y[:, 0:n], y[:, 0:n], ALU.mult), reads=[Ty], writes=[Tsq])
                ps, Tp = K.ps(0)
                P.op("pe", lambda e, ps=ps, sq=sq, n=n: e.matmul(ps[:, 0:n], K.onesf[:], sq[:, 0:n], start=True, stop=True), reads=[Tsq, K.T_const], writes=[Tp])
                P.op("act", lambda e, ps=ps, rs=rs, n=n: e.activation(out=rs[:, 0:n], in_=ps[:, 0:n], func=AF.Sqrt, bias=EPS, scale=1.0), reads=[Tp], writes=[Trs])
                P.op("dve", lambda e, rs=rs, n=n: e.reciprocal(rs[:, 0:n], rs[:, 0:n]), reads=[Trs], writes=[Trs])
                cc = 128 ** -0.5 if kind == 0 else 1.0
                P.op("dve", lambda e, y=y, rs=rs, yn=yn, n=n, cc=cc: e.scalar_tensor_tensor(yn[:, 0:n], y[:, 0:n], cc, rs[:, 0:n], ALU.mult, ALU.mult), reads=[Ty, Trs], writes=[Tyn])
                dst = S["qnT_d"] if kind == 0 else S["knT_d"]
                P.dma("sp", dst[h * 128:(h + 1) * 128, tok0:tok0 + n], yn[:, 0:n], syn, reads=[Tyn])
            else:
                P.op("pool", lambda e, y=y, yn=yn, n=n: e.tensor_copy(yn[:, 0:n], y[:, 0:n]), reads=[Ty], writes=[Tyn])
            if kind >= 1:
                nb = n // 128
                ps, Tp = K.ps(0)
                psb = ps[:].bitcast(BF16)
                for j in range(nb):
                    P.op("pe", lambda e, j=j, psb=psb, yn=yn: e.transpose(psb[:, j * 128:(j + 1) * 128], yn[:, j * 128:(j + 1) * 128], K.identb[:]), reads=[Tyn, K.T_const], writes=[Tp])
                tm_, Ttm, stm = tmr.next()
                P.op("act", lambda e, psb=psb, tm_=tm_, nb=nb: e.copy(tm_[:, 0:nb, :], psb[:, 0:nb * 128].rearrange("p (j f) -> p j f", f=128)), reads=[Tp], writes=[Ttm])
                dst = S["ktm_d"] if kind == 1 else S["vtm_d"]
                P.dma("sp", dst[tok0:tok0 + n, h * 128:(h + 1) * 128].rearrange("(j p) f -> p j f", p=128), tm_[:, 0:nb, :], stm, reads=[Ttm])
    K.end()


def phase_dn_scan(K):
    nc, P = K.nc, K.P
    S = K.dscr
    K.scr("of_d", [TOKC, 1024], F32)
    catT = S["catT_d"]
    K.begin()
    H = 8
    tri = K.sb("tri", [128, 4, 128], F32)
    dtb = K.sb("dtb", [128, 16], F32)
    nea = K.sb("nea", [128, 16], F32)
    gon = K.sb("gon", [128, 128], F32)
    T_c = Tile("dnc")
    sc = K.getsem()
    P.dma("sp", tri[:], K.din["tri"].rearrange("m k c -> k m c"), sc, writes=[T_c])
    P.dma("sp", nea[:], K.din["alog_dt"][0:1, :].partition_broadcast(128), sc, writes=[T_c])
    P.dma("sp", dtb[:], K.din["alog_dt"][1:2, :].partition_broadcast(128), sc, writes=[T_c])
    P.dma("sp", gon[:], K.din["onorm"].partition_broadcast(128), sc, writes=[T_c])
    P.op("act", lambda e: e.activation(out=nea[:], in_=nea[:], func=AF.Exp), reads=[T_c], writes=[T_c])
    P.op("dve", lambda e: e.tensor_scalar_mul(nea[:], nea[:], -1.0), reads=[T_c], writes=[T_c])
    LM, UM, SLM, SUM = 0, 1, 2, 3
    St = K.sb("St", [128, H, 128], F32)
    Sb = K.sb("Sb", [128, H, 128], BF16)
    T_S, T_Sb = P.tiles(2, "S")
    T_of = {}
    big = lambda name, dt, n=2: Ring(K, name, [128, H, 128], dt, n)
    r_kT, r_qT, r_k, r_v = big("lkT", BF16), big("lqT", BF16), big("lk", BF16), big("lv", BF16)
    r_Rg, r_Rb = big("Rg", F32, 1), big("Rb", BF16, 1)
    r_diff, r_x1, r_e1, r_e2i, r_e2s = big("diff", F32, 1), big("x1", F32, 1), big("e1", F32, 1), big("e2i", F32, 1), big("e2s", F32, 1)
    r_kbT, r_egb, r_qd = big("kbT", BF16, 1), big("egb", BF16, 1), big("qd", BF16, 3)
    r_N, r_M, r_X, r_Y = big("N", F32, 2), big("M", F32, 2), big("X", F32, 2), big("Y", F32, 2)
    r_Xb = big("Xb", BF16, 1)
    r_N0, r_M0, r_X0, r_Y0 = big("N0", F32, 2), big("M0", F32, 2), big("X0", F32, 2), big("Y0", F32, 2)
    r_qk, r_vb, r_kbg, r_kd = big("qk", BF16, 3), big("vb", BF16, 1), big("kbg", BF16, 1), big("kdc", BF16, 2)
    r_u, r_wT, r_vn, r_o = big("u", F32, 2), big("wT", BF16, 2), big("vn", BF16, 1), big("o", F32, 2)
    r_z, r_sq, r_on, r_obT = Ring(K, "z", [128, 1024], F32, 1), big("osq", F32, 1), big("on", BF16, 1), big("obT", BF16, 2)
    r_st = Ring(K, "ost", [128, 16], F32, 2)

    def v8(ap):
        return ap.rearrange("p (h f) -> p h f", h=H)

    def bc_h(ap2):
        return ap2.unsqueeze(2).to_broadcast([128, H, 128])

    def bc_m(m):
        return tri[:, m, :].unsqueeze(1).to_broadcast([128, H, 128])

    def mm8(lhs_fn, rhs_fn, reads, g, extra=None):
        b0, T0 = K.ps(g)
        b1, T1 = K.ps(g)
        banks = ((b0, T0), (b1, T1))
        for h in range(H):
            b, Tb = banks[h // 4]
            o_ = b[:, (h % 4) * 128:(h % 4 + 1) * 128]
            if extra is None:
                P.op("pe", lambda e, h=h, o_=o_: e.matmul(o_, lhs_fn(h), rhs_fn(h), start=True, stop=True), reads=reads, writes=[Tb])
            else:
                l2, r2, reads2 = extra
                P.op("pe", lambda e, h=h, o_=o_: e.matmul(o_, lhs_fn(h), rhs_fn(h), start=True, stop=False), reads=reads, writes=[Tb])
                P.op("pe", lambda e, h=h, o_=o_: e.matmul(o_, l2(h), r2(h), start=False, stop=True), reads=reads2, writes=[Tb])
        return banks

    def ev(eng, banks, fn, reads, writes):
        for i, (b, Tb) in enumerate(banks):
            bv = b[:].rearrange("p (h f) -> p h f", h=4)
            P.op(eng, lambda e, bv=bv, i=i: fn(e, bv, slice(4 * i, 4 * i + 4)), reads=[Tb] + reads, writes=writes)

    def gates(tokd0, nch):
        G = {}
        graw = K.sb("graw", [128, nch, 32], F32)
        beta = K.sb("beta", [128, nch, 16], F32)
        g = K.sb("gg", [128, nch, 16], F32)
        Tg = Tile("gates")
        sg = K.getsem()
        P.dma("sp", graw[:], S["gates_d"][tokd0:tokd0 + nch * 128, :].rearrange("(c p) g -> p c g", p=128), sg, writes=[Tg])
        P.op("act", lambda e: e.activation(out=beta[:], in_=graw[:, :, 0:16], func=AF.Sigmoid), reads=[Tg], writes=[Tg])
        P.op("dve", lambda e: e.tensor_tensor(g[:], graw[:, :, 16:32], dtb[:].unsqueeze(1).to_broadcast([128, nch, 16]), ALU.add), reads=[Tg, T_c], writes=[Tg])
        P.op("act", lambda e: e.activation(out=g[:], in_=g[:], func=AF.Exp), reads=[Tg], writes=[Tg])
        P.op("act", lambda e: e.activation(out=g[:], in_=g[:], func=AF.Ln, bias=1.0, scale=1.0), reads=[Tg], writes=[Tg])
        P.op("dve", lambda e: e.tensor_tensor(g[:], g[:], nea[:].unsqueeze(1).to_broadcast([128, nch, 16]), ALU.mult), reads=[Tg, T_c], writes=[Tg])
        G["beta"], G["T"] = beta, Tg
        for dr in range(2):
            gc = K.sb(f"gc{dr}", [128, nch, 8], F32)
            gl = K.sb(f"gl{dr}", [128, nch, 8], F32)
            eg = K.sb(f"eg{dr}", [128, nch, 8], F32)
            bg = K.sb(f"bg{dr}", [128, nch, 8], F32)
            kd = K.sb(f"kd{dr}", [128, nch, 8], F32)
            cd = K.sb(f"cd{dr}", [128, nch, 8], F32)
            tr = UM if dr == 0 else LM
            ps, Tp = K.ps(0)
            gsl = g[:, :, dr * 8:(dr + 1) * 8]
            P.op("pe", lambda e, ps=ps, tr=tr, gsl=gsl: e.matmul(ps[:, 0:nch * 8].rearrange("p (c h) -> p c h", h=8), tri[:, tr, :], gsl, start=True, stop=True), reads=[Tg, T_c], writes=[Tp])
            P.op("dve", lambda e, ps=ps, gc=gc: e.tensor_copy(gc[:].rearrange("p c h -> p (c h)"), ps[:, 0:nch * 8]), reads=[Tp], writes=[Tg])
            ps2, Tp2 = K.ps(0)
            P.op("pe", lambda e, ps2=ps2, gsl=gsl: e.matmul(ps2[:, 0:nch * 8].rearrange("p (c h) -> p c h", h=8), K.onesf[:], gsl, start=True, stop=True), reads=[Tg, K.T_const], writes=[Tp2])
            P.op("dve", lambda e, ps2=ps2, gl=gl: e.tensor_copy(gl[:].rearrange("p c h -> p (c h)"), ps2[:, 0:nch * 8]), reads=[Tp2], writes=[Tg])
            P.op("act", lambda e, eg=eg, gc=gc: e.activation(out=eg[:], in_=gc[:], func=AF.Exp), reads=[Tg], writes=[Tg])
            P.op("dve", lambda e, bg=bg, eg=eg, dr=dr: e.tensor_tensor(bg[:], eg[:], beta[:, :, dr * 8:(dr + 1) * 8], ALU.mult), reads=[Tg], writes=[Tg])
            P.op("dve", lambda e, kd=kd, gl=gl, gc=gc: e.tensor_tensor(kd[:], gl[:], gc[:], ALU.subtract), reads=[Tg], writes=[Tg])
            P.op("act", lambda e, kd=kd: e.activation(out=kd[:], in_=kd[:], func=AF.Exp), reads=[Tg], writes=[Tg])
            P.op("act", lambda e, cd=cd, gl=gl: e.activation(out=cd[:], in_=gl[:], func=AF.Exp), reads=[Tg], writes=[Tg])
            G[dr] = dict(gc=gc, eg=eg, bg=bg, kd=kd, cd=cd)
        return G

    def prepA(C):
        G, tokd0, ch, dr, full = C['G'], C['tokd0'], C['ch'], C['dr'], C['full']
        t0 = tokd0 + ch * 128
        Tg = G["T"]
        gd = G[dr]
        beta_c = G["beta"][:, ch, dr * 8:(dr + 1) * 8]
        gc_c = gd["gc"][:, ch, :]
        mL, mU, mSL, mSU = (LM, UM, SLM, SUM) if dr == 0 else (UM, LM, SUM, SLM)
        kT, TkT, s1 = r_kT.next()
        ktm, Tk, s2 = r_k.next()
        vtm, Tv, s3 = r_v.next()
        P.dma("sp", kT[:], S["knT_d"][:, t0:t0 + 128].rearrange("(h p) t -> p h t", p=128), s1, writes=[TkT])
        P.dma("sp", ktm[:].rearrange("p h f -> p (h f)"), S["ktm_d"][t0:t0 + 128, :], s2, writes=[Tk])
        P.dma("sp", vtm[:].rearrange("p h f -> p (h f)"), S["vtm_d"][t0:t0 + 128, :], s3, writes=[Tv])
        if full:
            qT, TqT, s4 = r_qT.next()
            P.dma("sp", qT[:], S["qnT_d"][:, t0:t0 + 128].rearrange("(h p) t -> p h t", p=128), s4, writes=[TqT])
        yield
        Rg, TRg, _ = r_Rg.next()
        Rb, TRb, _ = r_Rb.next()
        idb = K.identf[:].unsqueeze(1).to_broadcast([128, H, 128])
        P.op("dve", lambda e: e.tensor_tensor(Rg[:], bc_h(gc_c), idb, ALU.mult), reads=[Tg, K.T_const], writes=[TRg])
        P.op("pool", lambda e: e.tensor_tensor(Rb[:], bc_h(beta_c), idb, ALU.mult), reads=[Tg, K.T_const], writes=[TRb])
        gcb = mm8(lambda h: K.onesf[:], lambda h: Rg[:, h, :], [TRg, K.T_const], 0)
        btb = mm8(lambda h: K.onesb[:], lambda h: Rb[:, h, :], [TRb, K.T_const], 0)
        diff, Tdiff, _ = r_diff.next()
        ev("dve", gcb, lambda e, bv, hs: e.tensor_tensor(diff[:, hs, :], bc_h(gc_c)[:, hs, :], bv, ALU.subtract), [Tg], [Tdiff])
        kbT, TkbT, _ = r_kbT.next()
        ev("dve", btb, lambda e, bv, hs: e.tensor_tensor(kbT[:, hs, :], kT[:, hs, :], bv, ALU.mult), [TkT], [TkbT])
        if full:
            egb, Tegb, _ = r_egb.next()
            qd, Tqd, _ = r_qd.next()
            ev("act", gcb, lambda e, bv, hs: e.activation(out=egb[:, hs, :], in_=bv, func=AF.Exp), [], [Tegb])
            P.op("pool", lambda e: e.tensor_tensor(qd[:], qT[:], egb[:], ALU.mult), reads=[TqT, Tegb], writes=[Tqd])
        yield
        x1, Tx1, _ = r_x1.next()
        e1, Te1, _ = r_e1.next()
        e2i, Te2i, _ = r_e2i.next()
        e2s, Te2s, _ = r_e2s.next()
        P.op("dve", lambda e: e.tensor_tensor(x1[:], diff[:], bc_m(mL), ALU.mult), reads=[Tdiff, T_c], writes=[Tx1])
        P.op("act", lambda e: e.activation(out=e1[:], in_=x1[:], func=AF.Exp), reads=[Tx1], writes=[Te1])
        P.op("pool", lambda e: e.tensor_tensor(e1[:], e1[:], bc_m(mSL), ALU.mult), reads=[Te1, T_c], writes=[Te1])
        P.op("dve", lambda e: e.tensor_tensor(x1[:], diff[:], bc_m(mU), ALU.mult), reads=[Tdiff, T_c, Te1], writes=[Tx1])
        P.op("act", lambda e: e.activation(out=e2i[:], in_=x1[:], func=AF.Exp, scale=-1.0), reads=[Tx1], writes=[Te2i])
        P.op("pool", lambda e: e.tensor_tensor(e2s[:], e2i[:], bc_m(mSU), ALU.mult), reads=[Te2i, T_c], writes=[Te2s])
        P.op("pool", lambda e: e.tensor_tensor(e2i[:], e2i[:], bc_m(mU), ALU.mult), reads=[Te2i, T_c, Te2s], writes=[Te2i])
        yield
        a1 = mm8(lambda h: kbT[:, h, :], lambda h: kT[:, h, :], [TkbT, TkT], 0)
        a2 = mm8(lambda h: kT[:, h, :], lambda h: kbT[:, h, :], [TkbT, TkT], 1)
        Mj, TM, _ = r_M0.next()
        Nj, TN, _ = r_N0.next()
        ev("dve", a1, lambda e, bv, hs, Mj=Mj: e.tensor_tensor(Mj[:, hs, :], bv, e1[:, hs, :], ALU.mult), [Te1], [TM])
        ev("dve", a2, lambda e, bv, hs, Nj=Nj: e.tensor_tensor(Nj[:, hs, :], bv, e2s[:, hs, :], ALU.mult), [Te2s], [TN])
        if full:
            a3 = mm8(lambda h: kT[:, h, :], lambda h: qT[:, h, :], [TkT, TqT], 0)
            qk, Tqk, _ = r_qk.next()
            ev("dve", a3, lambda e, bv, hs: e.tensor_tensor(qk[:, hs, :], bv, e2i[:, hs, :], ALU.mult), [Te2i], [Tqk])
        X, TX, _ = r_X0.next()
        Y, TY, _ = r_Y0.next()
        idbb = K.identf[:].unsqueeze(1).to_broadcast([128, H, 128])
        P.op("pool", lambda e, X=X, Nj=Nj: e.tensor_tensor(X[:], idbb, Nj[:], ALU.subtract), reads=[TN, K.T_const], writes=[TX])
        P.op("pool", lambda e, Y=Y, Mj=Mj: e.tensor_tensor(Y[:], idbb, Mj[:], ALU.subtract), reads=[TM, K.T_const], writes=[TY])
        C.update(Mj=Mj, TM=TM, Nj=Nj, TN=TN, X=X, TX=TX, Y=Y, TY=TY, vtm=vtm, Tv=Tv, ktm=ktm, Tk=Tk, beta_c=beta_c, gd=gd, Tg=Tg, kT=kT)
        if full:
            C.update(qd=qd, Tqd=Tqd, qk=qk, Tqk=Tqk)
        yield

    def prepB(C):
        ch, full = C['ch'], C['full']
        Mj, TM, Nj, TN, X, TX, Y, TY = C['Mj'], C['TM'], C['Nj'], C['TN'], C['X'], C['TX'], C['Y'], C['TY']
        vtm, Tv, ktm, Tk, beta_c, gd, Tg = C['vtm'], C['Tv'], C['ktm'], C['Tk'], C['beta_c'], C['gd'], C['Tg']
        for j in range(1, 7):
            last = j == 6
            Nn, TNn, _ = r_N.next()
            pn = mm8(lambda h, Mj=Mj: Mj[:, h, :], lambda h, Nj=Nj: Nj[:, h, :], [TM, TN], 0)
            ev("act", pn, lambda e, bv, hs, Nn=Nn: e.copy(Nn[:, hs, :], bv), [], [TNn])
            if not last:
                Mn, TMn, _ = r_M.next()
                pm = mm8(lambda h, Nj=Nj: Nj[:, h, :], lambda h, Mj=Mj: Mj[:, h, :], [TM, TN], 0)
                ev("act", pm, lambda e, bv, hs, Mn=Mn: e.copy(Mn[:, hs, :], bv), [], [TMn])
            Xn, TXn, _ = r_X.next()
            px = mm8(lambda h, Y=Y: Y[:, h, :], lambda h, Nn=Nn: Nn[:, h, :], [TY, TNn], 1)
            ev("dve", px, lambda e, bv, hs, Xn=Xn, X=X: e.tensor_tensor(Xn[:, hs, :], bv, X[:, hs, :], ALU.add), [TX], [TXn])
            if not last:
                Yn, TYn, _ = r_Y.next()
                py = mm8(lambda h, X=X: X[:, h, :], lambda h, Mn=Mn: Mn[:, h, :], [TX, TMn], 1)
                ev("dve", py, lambda e, bv, hs, Yn=Yn, Y=Y: e.tensor_tensor(Yn[:, hs, :], bv, Y[:, hs, :], ALU.add), [TY], [TYn])
                Mj, TM, Y, TY = Mn, TMn, Yn, TYn
            Nj, TN, X, TX = Nn, TNn, Xn, TXn
            yield
        vb, Tvb, _ = r_vb.next()
        kbg, Tkbg, _ = r_kbg.next()
        kdc, Tkdc, _ = r_kd.next()
        P.op("pool", lambda e: e.tensor_tensor(vb[:], vtm[:], bc_h(beta_c), ALU.mult), reads=[Tv, Tg], writes=[Tvb])
        P.op("pool", lambda e: e.tensor_tensor(kbg[:], ktm[:], bc_h(gd["bg"][:, ch, :]), ALU.mult), reads=[Tk, Tg], writes=[Tkbg])
        P.op("pool", lambda e: e.tensor_tensor(kdc[:], ktm[:], bc_h(gd["kd"][:, ch, :]), ALU.mult), reads=[Tk, Tg], writes=[Tkdc])
        Xb, TXb, _ = r_Xb.next()
        P.op("act", lambda e, X=X: e.copy(Xb[:], X[:]), reads=[TX], writes=[TXb])
        pu = mm8(lambda h: Xb[:, h, :], lambda h: vb[:, h, :], [TXb, Tvb], 0)
        u, Tu, _ = r_u.next()
        ev("act", pu, lambda e, bv, hs: e.copy(u[:, hs, :], bv), [], [Tu])
        pw = mm8(lambda h: kbg[:, h, :], lambda h: Xb[:, h, :], [TXb, Tkbg], 0)
        wT, TwT, _ = r_wT.next()
        ev("act", pw, lambda e, bv, hs: e.copy(wT[:, hs, :], bv), [], [TwT])
        C.update(u=u, Tu=Tu, wT=wT, TwT=TwT, kdc=kdc, Tkdc=Tkdc, gd=gd, Tg=Tg)
        yield


    def scan(C):
        ch, full, final, tokc0 = C['ch'], C['full'], C['final'], C['tokc0']
        u, Tu, wT, TwT, kdc, Tkdc, gd, Tg = C['u'], C['Tu'], C['wT'], C['TwT'], C['kdc'], C['Tkdc'], C['gd'], C['Tg']
        if full:
            qd, Tqd, qk, Tqk = C['qd'], C['Tqd'], C['qk'], C['Tqk']
        if C.get('pre') is not None:
            C['pre']()
        pws = mm8(lambda h: wT[:, h, :], lambda h: Sb[:, h, :], [TwT, T_Sb], 1)
        vn, Tvn, _ = r_vn.next()
        ev("dve", pws, lambda e, bv, hs: e.tensor_tensor(vn[:, hs, :], u[:, hs, :], bv, ALU.subtract), [Tu], [Tvn])
        yield
        if full:
            po = mm8(lambda h: qd[:, h, :], lambda h: Sb[:, h, :], [Tqd, T_Sb], 1,
                     extra=(lambda h: qk[:, h, :], lambda h: vn[:, h, :], [Tqk, Tvn]))
            o, To, so = r_o.next()
            if not final:
                ev("act", po, lambda e, bv, hs: e.copy(o[:, hs, :], bv), [], [To])
                P.dma("act", S["of_d"][tokc0 + ch * 128:tokc0 + (ch + 1) * 128, :], o[:].rearrange("p h f -> p (h f)"), so, reads=[To], writes=[T_of.setdefault(tokc0 + ch * 128, Tile("of"))])
            else:
                P.dma("sp", o[:].rearrange("p h f -> p (h f)"), S["of_d"][tokc0 + ch * 128:tokc0 + (ch + 1) * 128, :], so, reads=[T_of[tokc0 + ch * 128]], writes=[To])
                ev("dve", po, lambda e, bv, hs: e.tensor_tensor(o[:, hs, :], o[:, hs, :], bv, ALU.add), [To], [To])
        yield
        pds = mm8(lambda h: kdc[:, h, :], lambda h: vn[:, h, :], [Tkdc, Tvn], 1)
        for h in range(H):
            b, Tb = pds[h // 4]
            P.op("dve", lambda e, h=h, b=b: e.scalar_tensor_tensor(St[:, h, :], St[:, h, :], gd["cd"][:, ch, h:h + 1], b[:, (h % 4) * 128:(h % 4 + 1) * 128], ALU.mult, ALU.add),
                 reads=[Tb, Tg, T_S], writes=[T_S])
        P.op("act", lambda e: e.copy(Sb[:], St[:]), reads=[T_S], writes=[T_Sb])
        yield
        if full and final:
            z, Tz, sz = r_z.next()
            P.dma("sp", z[:], S["z_d"][tokc0 + ch * 128:tokc0 + (ch + 1) * 128, :], sz, writes=[Tz])
            sq, Tsq, _ = r_sq.next()
            st, Tst, _ = r_st.next()
            on, Ton, _ = r_on.next()
            P.op("pool", lambda e: e.tensor_tensor(sq[:], o[:], o[:], ALU.mult), reads=[To], writes=[Tsq])
            P.op("dve", lambda e: e.tensor_reduce(out=st[:, 0:8], in_=sq[:], axis=AX.X, op=ALU.add), reads=[Tsq], writes=[Tst])
            P.op("act", lambda e: e.activation(out=st[:, 8:16], in_=st[:, 0:8], func=AF.Sqrt, bias=EPS, scale=1.0 / 128), reads=[Tst], writes=[Tst])
            P.op("dve", lambda e: e.reciprocal(st[:, 8:16], st[:, 8:16]), reads=[Tst], writes=[Tst])
            P.op("act", lambda e: e.activation(out=z[:], in_=z[:], func=AF.Silu), reads=[Tz], writes=[Tz])
            P.op("dve", lambda e: e.tensor_tensor(sq[:], o[:], bc_h(st[:, 8:16]), ALU.mult), reads=[To, Tst, Tsq], writes=[Tsq])
            P.op("pool", lambda e: e.tensor_tensor(sq[:], sq[:], gon[:].unsqueeze(1).to_broadcast([128, H, 128]), ALU.mult), reads=[Tsq, T_c], writes=[Tsq])
            P.op("dve", lambda e: e.tensor_tensor(on[:], sq[:], v8(z[:]), ALU.mult), reads=[Tsq, Tz], writes=[Ton])
            yield
            ps, Tp = K.ps(0)
            psb = ps[:].bitcast(BF16)
            for h in range(H):
                P.op("pe", lambda e, h=h, psb=psb: e.transpose(psb[:, h * 128:(h + 1) * 128], on[:, h, :], K.identb[:]), reads=[Ton, K.T_const], writes=[Tp])
            obT, TobT, sob = r_obT.next()
            P.op("act", lambda e, psb=psb: e.copy(obT[:].rearrange("p h f -> p (h f)"), psb), reads=[Tp], writes=[TobT])
            P.dma("act", catT[1024:2048, tokc0 + ch * 128:tokc0 + (ch + 1) * 128].rearrange("(h p) t -> p h t", p=128), obT[:], sob, reads=[TobT])

        if C.get('post') is not None:
            C['post']()
        yield


    def set_state(src):
        ss_ = K.getsem()
        if src is None:
            P.op("pool", lambda e: e.memset(St[:], 0.0), reads=[T_Sb], writes=[T_S])
        else:
            P.dma("sp", St[:], src.rearrange("h k v -> k h v"), ss_, reads=[T_Sb], writes=[T_S])
        P.op("act", lambda e: e.copy(Sb[:], St[:]), reads=[T_S], writes=[T_Sb])

    def save_state(dst):
        ss_ = K.getsem()
        P.dma("sp", dst.rearrange("h k v -> k h v"), St[:], ss_, reads=[T_S])

    jobs = []

    def job(G, tokd0, ch, dr, full, final, tokc0, pre=None, post=None):
        jobs.append(dict(G=G, tokd0=tokd0, ch=ch, dr=dr, full=full, final=final, tokc0=tokc0, pre=pre, post=post))

    for s_ in range(4):
        G = gates(s_ * 256, 2)
        job(G, s_ * 256, 0, 0, True, False, s_ * 256, pre=lambda: set_state(None))
        job(G, s_ * 256, 1, 0, True, False, s_ * 256, post=lambda s_=s_: save_state(K.dout["nbf"][s_]))
        job(G, s_ * 256, 1, 1, True, True, s_ * 256, pre=lambda: set_state(None))
        job(G, s_ * 256, 0, 1, True, True, s_ * 256, post=lambda s_=s_: save_state(K.dout["nbb"][s_]))
    G = gates(1024, 32)
    for ch in range(SEXT):
        job(G, 1024, ch, 0, True, False, 1024, pre=(lambda: set_state(K.din["s0"][0])) if ch == 0 else None)
    for ch in range(31, -1, -1):
        job(G, 1024, ch, 1, ch < SEXT, True, 1024, pre=(lambda: set_state(K.din["s0"][1])) if ch == 31 else None)
    nj = len(jobs)
    for t in range(nj + 2):
        gens = []
        if t < nj:
            gens.append(prepA(jobs[t]))
        if 0 <= t - 1 < nj:
            gens.append(prepB(jobs[t - 1]))
        if 0 <= t - 2 < nj:
            gens.append(scan(jobs[t - 2]))
        while gens:
            for g_ in list(gens):
                try:
                    next(g_)
                except StopIteration:
                    gens.remove(g_)
    K.end()


def phase_mlp(K, l):
    nc, P = K.nc, K.P
    S = K.dscr
    ntb = NCAT if l == 0 else NPT + 16
    groups = token_groups(ntb, breaks=(NPT,))
    if l == 0:
        x1_d = K.scr("x1_d", [TOKC, D], F32)
        oT_d, Wo, W1, W2 = S["catT_d"], K.din["ab_w_out"], K.din["w_mlp_in"][0], K.din["w_mlp_out"][0]
        xsrc = lambda g: K.din["xp"][g * 128:(g + 1) * 128, :] if g < NPT else K.din["xs"][(g - NPT) * 128:(g - NPT + 1) * 128, :]
    else:
        oT_d, Wo, W1, W2 = S["o1T_d"], K.din["c_w_out"], K.din["w_mlp_in"][1], K.din["w_mlp_out"][1]
        xsrc = lambda g: S["x1_d"][g * 128:(g + 1) * 128, :]
    K.begin()
    actT = K.sb("actT", [128, KC, 512], BF16)
    T_act = [[Tile() for kc in range(KC)] for i in range(4)]
    xres = K.sb("xres", [128, 4, D], F32)
    T_x = [Tile() for i in range(4)]
    uT = K.sb("uT", [128, 64, 512], BF16)
    T_u = [Tile() for fc in range(64)]
    gate = [K.sb(f"gate{i}", [128, D], F32) for i in range(2)]
    T_gate = Tile("gate")
    wr = Ring(K, "wm", [128, KC, 512], BF16, 3, sw=True)
    tr = Ring(K, "tg", [128, 512], F32, 1)
    rr = Ring(K, "rl", [128, 512], F32, 2)
    nt = NormT(K)
    sx = [K.getsem() for i in range(4)]
    sg, so = K.getsem(), K.getsem()
    if l == 1:
        fng = K.sb("fng", [128, D], F32)
        fss = Ring(K, "fss", [128, 2], F32, 2)
        T_fng = Tile("fng")
        P.dma("sp", fng[:], K.din["final_norm"].partition_broadcast(128), sg, writes=[T_fng])
    cur_c = None
    for (t0, m) in groups:
        n = m * 128
        c = 0 if t0 < NPT else 1
        if c != cur_c:
            P.dma("sp", gate[0][:], S["modrow_d"][l, c:c + 1, 2 * D:3 * D].partition_broadcast(128), sg, writes=[T_gate])
            P.dma("sp", gate[1][:], S["modrow_d"][l, c:c + 1, 5 * D:6 * D].partition_broadcast(128), sg, writes=[T_gate])
            cur_c = c
        P.dma("sp", actT[:, :, 0:n], oT_d[:, t0 * 128:t0 * 128 + n].rearrange("(fc p) t -> p fc t", p=128), so,
              writes=[T_act[i][kc] for i in range(m) for kc in range(KC)])
        for i in range(m):
            P.dma("sp", xres[:, i, :], xsrc(t0 + i), sx[i], writes=[T_x[i]])

        def second(Wsrc, nfq, lhs_fn, lhs_tiles_fn, gi):
            for dg in range(4):
                banks = [K.ps(1 - dg % 2) for i in range(m)]
                for fq in range(nfq):
                    wt, Tw, sw = wr.next()
                    P.dma("pool", wt[:], Wsrc[fq * 2048:(fq + 1) * 2048, dg * 512:(dg + 1) * 512].rearrange("(kc p) n -> p kc n", p=128), sw, writes=[Tw])
                    for i in range(m):
                        b, Tb = banks[i]
                        for kc in range(KC):
                            fc = fq * KC + kc
                            P.op("pe", lambda e, b=b, i=i, fc=fc, kc=kc, wt=wt: e.matmul(b[:, :], lhs_fn(fc, i), wt[:, kc, :], start=(fc == 0), stop=(fc == nfq * KC - 1)),
                                 reads=[Tw] + lhs_tiles_fn(fc, i), writes=[Tb])
                for i in range(m):
                    b, Tb = banks[i]
                    tt, Tt, _ = tr.next()
                    P.op("dve", lambda e, b=b, tt=tt, dg=dg: e.tensor_tensor(tt[:], b[:, :], gate[gi][:, dg * 512:(dg + 1) * 512], ALU.mult), reads=[Tb, T_gate], writes=[Tt])
                    P.op("dve", lambda e, i=i, tt=tt, dg=dg: e.tensor_tensor(xres[:, i, dg * 512:(dg + 1) * 512], xres[:, i, dg * 512:(dg + 1) * 512], tt[:], ALU.add), reads=[Tt, T_x[i]], writes=[T_x[i]])

        second(Wo, 1, lambda fc, i: actT[:, fc, i * 128:(i + 1) * 128], lambda fc, i: [T_act[i][fc]], 0)
        for i in range(m):
            nt.run(xres[:, i, :], T_x[i], K.gsF[l][1][c], K.modF[l][c][:, 3 * KC:4 * KC],
                   lambda kc, i=i: actT[:, kc, i * 128:(i + 1) * 128], lambda kc, i=i: T_act[i][kc])
        for fg in range(16):
            wt, Tw, sw = wr.next()
            P.dma("pool", wt[:], W1[:, fg * 512:(fg + 1) * 512].rearrange("(kc p) n -> p kc n", p=128), sw, writes=[Tw])
            for sub in range(4):
                fc = fg * 4 + sub
                ps, Tp = K.ps(0)
                for kc in range(KC):
                    P.op("pe", lambda e, ps=ps, kc=kc, sub=sub, wt=wt, n=n: e.matmul(ps[:, 0:n], wt[:, kc, sub * 128:(sub + 1) * 128], actT[:, kc, 0:n], start=(kc == 0), stop=(kc == KC - 1)),
                         reads=[Tw] + [T_act[i][kc] for i in range(m)], writes=[Tp])
                r, Tr, _ = rr.next()
                P.op("act", lambda e, ps=ps, r=r, n=n: e.activation(out=r[:, 0:n], in_=ps[:, 0:n], func=AF.Relu), reads=[Tp], writes=[Tr])
                P.op("dve", lambda e, r=r, fc=fc, n=n: e.tensor_tensor(uT[:, fc, 0:n], r[:, 0:n], r[:, 0:n], ALU.mult), reads=[Tr], writes=[T_u[fc]])
        second(W2, 4, lambda fc, i: uT[:, fc, i * 128:(i + 1) * 128], lambda fc, i: [T_u[fc]], 1)
        for i in range(m):
            g = t0 + i
            if l == 0:
                P.dma("sp", x1_d[g * 128:(g + 1) * 128, :], xres[:, i, :], sx[i], reads=[T_x[i]])
            else:
                ss, Tss, _ = fss.next()
                fjk, T_fjk, _ = nt.xn.next()
                P.op("act", lambda e, i=i, ss=ss, fjk=fjk: e.activation(out=fjk[:], in_=xres[:, i, :], func=AF.Square, accum_out=ss[:, 0:1]), reads=[T_x[i]], writes=[T_fjk, Tss])
                P.op("act", lambda e, ss=ss: e.activation(out=ss[:, 1:2], in_=ss[:, 0:1], func=AF.Sqrt, bias=EPS, scale=1.0 / D), reads=[Tss], writes=[Tss])
                P.op("dve", lambda e, ss=ss: e.reciprocal(ss[:, 1:2], ss[:, 1:2]), reads=[Tss], writes=[Tss])
                P.op("dve", lambda e, i=i, ss=ss: e.scalar_tensor_tensor(xres[:, i, :], xres[:, i, :], ss[:, 1:2], fng[:], ALU.mult, ALU.mult), reads=[T_x[i], Tss, T_fng], writes=[T_x[i]])
                dst = K.dout["y_p"][g * 128:(g + 1) * 128, :] if g < NPT else K.dout["y_s"][(g - NPT) * 128:(g - NPT + 1) * 128, :]
                P.dma("sp", dst, xres[:, i, :], sx[i], reads=[T_x[i]])
    K.end()


def phase_l1_inproj(K):
    nc, P = K.nc, K.P
    S = K.dscr
    q1T = K.scr("q1T_d", [D, TOKC], BF16)
    k1T = K.scr("k1T_d", [256, TOKC], BF16)
    v1 = K.scr("v1_d", [TOKC, 256], BF16)
    K.begin()
    ntb = NCAT
    groups = token_groups(ntb, breaks=(NPT,))
    hT = K.sb("h1T", [128, KC, ntb * 128], BF16)
    T_h = [[Tile() for kc in range(KC)] for g in range(ntb)]
    xr = Ring(K, "x1in", [128, D], F32, 2)
    nt = NormT(K)
    for g in range(ntb):
        c = 0 if g < NPT else 1
        xt, T_x, sx = xr.next()
        P.dma("sp", xt[:], S["x1_d"][g * 128:(g + 1) * 128, :], sx, writes=[T_x])
        nt.run(xt[:], T_x, K.gsF[1][0][c], K.modF[1][c][:, 0:KC],
               lambda kc, g=g: hT[:, kc, g * 128:(g + 1) * 128], lambda kc, g=g: T_h[g][kc])
    cosT = K.sb("cosT", [128, SEXT * 128], F32)
    sinT = K.sb("sinT", [128, SEXT * 128], F32)
    perm = K.sb("perm", [128, 128], F32)
    T_rc = Tile("ropec")
    sr = K.getsem()
    P.dma("sp", cosT[:], K.din["rope_cos"], sr, writes=[T_rc])
    P.dma("sp", sinT[:], K.din["rope_sin"], sr, writes=[T_rc])
    P.dma("sp", perm[:], K.din["rope_perm"], sr, writes=[T_rc])
    wr = Ring(K, "w1q", [128, KC, 512], BF16, 2, sw=True)
    q32r = Ring(K, "q32", [128, 512], F32, 2)
    t1r = Ring(K, "rt1", [128, 512], F32, 2)
    st16 = Ring(K, "s16", [128, 512], BF16, 3)
    st32 = Ring(K, "s32", [128, 512], F32, 2)
    W = K.din["c_w_qkv"]
    for t in range(5):
        wt, Tw, sw = wr.next()
        P.dma("pool", wt[:], W[:, t * 512:(t + 1) * 512].rearrange("(kc p) n -> p kc n", p=128), sw, writes=[Tw])
        nsub = 4 if t < 4 else 2
        for sub in range(nsub):
            dst, row0 = (q1T, t * 512 + sub * 128) if t < 4 else (k1T, sub * 128)
            for (t0, m) in groups:
                n = m * 128
                ps, Tp = K.ps(0)
                for kc in range(KC):
                    P.op("pe", lambda e, ps=ps, kc=kc, sub=sub, wt=wt, t0=t0, n=n: e.matmul(ps[:, 0:n], wt[:, kc, sub * 128:(sub + 1) * 128], hT[:, kc, t0 * 128:t0 * 128 + n], start=(kc == 0), stop=(kc == KC - 1)),
                         reads=[Tw] + [T_h[t0 + i][kc] for i in range(m)], writes=[Tp])
                sg, Ts, ss_ = st16.next()
                if t0 < NPT:
                    P.op("act", lambda e, ps=ps, sg=sg, n=n: e.copy(sg[:, 0:n], ps[:, 0:n]), reads=[Tp], writes=[Ts])
                else:
                    s0 = (t0 - NPT) * 128
                    q32, Tq, _ = q32r.next()
                    t1, Tt1, _ = t1r.next()
                    P.op("act", lambda e, ps=ps, q32=q32, n=n: e.copy(q32[:, 0:n], ps[:, 0:n]), reads=[Tp], writes=[Tq])
                    ps2, Tp2 = K.ps(1)
                    P.op("pe", lambda e, ps2=ps2, q32=q32, n=n: e.matmul(ps2[:, 0:n], perm[:], q32[:, 0:n], start=True, stop=True), reads=[Tq, T_rc], writes=[Tp2])
                    P.op("pool", lambda e, q32=q32, t1=t1, n=n, s0=s0: e.tensor_tensor(t1[:, 0:n], q32[:, 0:n], cosT[:, s0:s0 + n], ALU.mult), reads=[Tq, T_rc], writes=[Tt1])
                    P.op("dve", lambda e, ps2=ps2, q32=q32, n=n, s0=s0: e.tensor_tensor(q32[:, 0:n], ps2[:, 0:n], sinT[:, s0:s0 + n], ALU.mult), reads=[Tp2, T_rc, Tt1], writes=[Tq])
                    P.op("dve", lambda e, q32=q32, t1=t1, sg=sg, n=n: e.tensor_tensor(sg[:, 0:n], q32[:, 0:n], t1[:, 0:n], ALU.add), reads=[Tq, Tt1], writes=[Ts])
                P.dma("sp", dst[row0:row0 + 128, t0 * 128:t0 * 128 + n], sg[:, 0:n], ss_, reads=[Ts])
        if t == 4:
            for g in range(ntb):
                ps, Tp = K.ps(0)
                for kc in range(KC):
                    P.op("pe", lambda e, ps=ps, kc=kc, g=g, wt=wt: e.matmul(ps[:, :], hT[:, kc, g * 128:(g + 1) * 128], wt[:, kc, :], start=(kc == 0), stop=(kc == KC - 1)),
                         reads=[Tw, T_h[g][kc]], writes=[Tp])
                sg, Ts, ss_ = st16.next()
                P.op("act", lambda e, ps=ps, sg=sg: e.copy(sg[:, 0:256], ps[:, 256:512]), reads=[Tp], writes=[Ts])
                P.dma("sp", v1[g * 128:(g + 1) * 128, :], sg[:, 0:256], ss_, reads=[Ts])
                if g < NPT:
                    s32, Ts32, ss32 = st32.next()
                    P.op("dve", lambda e, ps=ps, s32=s32: e.tensor_copy(s32[:], ps[:, :]), reads=[Tp], writes=[Ts32])
                    P.dma("sp", K.dout["nck"][g * 128:(g + 1) * 128, :], s32[:, 0:256], ss32, reads=[Ts32])
                    P.dma("sp", K.dout["ncv"][g * 128:(g + 1) * 128, :], s32[:, 256:512], ss32, reads=[Ts32])
    K.end()


def phase_attn_c(K):
    nc, P = K.nc, K.P
    S = K.dscr
    o1T = K.scr("o1T_d", [D, TOKC], BF16)
    scale = 64 ** -0.5
    K.begin()
    pt_ring = Ring(K, "pt", [128, 512], BF16, 4)
    rec_ring = Ring(K, "rec", [128, 512], F32, 2)
    pools = (pt_ring, rec_ring)
    snk = K.sb("snk", [128, 32], F32)
    trib = K.sb("trib", [128, 2, 128], BF16)
    ck_tm = K.sb("cck", [128, 2, 256], BF16)
    cvt = K.sb("ccv", [128, 2, 256], BF16)
    ckT = K.sb("cckT", [64, 4, 256], BF16)
    T_c, T_ck, T_cv, T_ckT = P.tiles(4, "ac")
    s0, sw0 = K.getsem(), K.getsem(True)
    P.dma("sp", snk[:], K.din["c_sink"].partition_broadcast(128), s0, writes=[T_c])
    P.op("act", lambda e: e.activation(out=snk[:], in_=snk[:], func=AF.Exp), reads=[T_c], writes=[T_c])
    P.dma("pool", trib[:], K.din["tri"][0:2].rearrange("m k c -> k m c"), sw0, writes=[T_c])
    P.dma("pool", ck_tm[:], K.din["cache_c_k"].rearrange("(c p) f -> p c f", p=128), sw0, writes=[T_ck])
    P.dma("pool", cvt[:], K.din["cache_c_v"].rearrange("(c p) f -> p c f", p=128), sw0, writes=[T_cv])
    for c in range(2):
        ps, Tp = K.ps(0)
        psb = ps[:].bitcast(BF16)
        for kh in range(4):
            P.op("pe", lambda e, c=c, kh=kh, psb=psb: e.transpose(psb[0:64, kh * 128:(kh + 1) * 128], ck_tm[:, c, kh * 64:(kh + 1) * 64], K.identb[:]), reads=[T_ck, K.T_const], writes=[Tp])
        P.op("dve", lambda e, c=c, psb=psb: e.tensor_copy(ckT[:, :, c * 128:(c + 1) * 128], psb[0:64, 0:512].rearrange("p (h k) -> p h k", h=4)), reads=[Tp], writes=[T_ckT])
    qr = Ring(K, "pq", [64, 8, 256], BF16, 2)
    kr = Ring(K, "pk", [64, 256], BF16, 2)
    vr = Ring(K, "pv", [128, 2, 64], BF16, 2)
    orr = Ring(K, "po", [64, 8, 256], BF16, 2)
    for kh in range(4):
        for s in range(4):
            qt, Tq, sq = qr.next()
            kt, Tk, sk = kr.next()
            vt, Tv, sv = vr.next()
            ot, To, so = orr.next()
            P.dma("sp", qt[:], S["q1T_d"][kh * 512:(kh + 1) * 512, s * 256:(s + 1) * 256].rearrange("(g d) t -> d g t", d=64), sq, writes=[Tq])
            P.dma("sp", kt[:], S["k1T_d"][kh * 64:(kh + 1) * 64, s * 256:(s + 1) * 256], sk, writes=[Tk])
            P.dma("sp", vt[:], S["v1_d"][s * 256:(s + 1) * 256, kh * 64:(kh + 1) * 64].rearrange("(c p) f -> p c f", p=128), sv, writes=[Tv])
            for g in range(8):
                sl = [(kt[:, c * 128:(c + 1) * 128], [Tk], vt[:, c, :], [Tv], None, [], 0) for c in range(2)]
                hq = kh * 8 + g
                attn_core(K, sl, 256, (qt[:, g, :], [Tq]), ot[:, g, :], To, scale, extra_den=(snk[0:64, hq:hq + 1], [T_c]), pools=pools)
            P.dma("sp", o1T[kh * 512:(kh + 1) * 512, s * 256:(s + 1) * 256].rearrange("(g d) t -> d g t", d=64), ot[:], so, reads=[To])
    NT = SEXT * 128
    kh_k = Ring(K, "sk", [64, NT], BF16, 2)
    kh_v = Ring(K, "sv", [128, SEXT, 64], BF16, 2)
    kh_q = Ring(K, "sq", [64, 8, 2048], BF16, 1)
    kh_o = Ring(K, "so", [64, 8, 2048], BF16, 1)
    for kh in range(4):
        kt, Tk, sk = kh_k.next()
        vt, Tv, sv = kh_v.next()
        qt, Tq, sq = kh_q.next()
        ot, To, so = kh_o.next()
        P.dma("sp", kt[:], S["k1T_d"][kh * 64:(kh + 1) * 64, 1024:1024 + NT], sk, writes=[Tk])
        P.dma("sp", vt[:], S["v1_d"][1024:1024 + NT, kh * 64:(kh + 1) * 64].rearrange("(c p) f -> p c f", p=128), sv, writes=[Tv])
        P.dma("sp", qt[:], S["q1T_d"][kh * 512:(kh + 1) * 512, 1024:1024 + 2048].rearrange("(g d) t -> d g t", d=64), sq, writes=[Tq])
        for i in range(16):
            for gh in range(2):
                sl = []
                for j in (i - 1, i, i + 1):
                    if j < 0 or j > 16:
                        continue
                    mask = None
                    if j == i - 1:
                        mask = trib[:, 0, :].unsqueeze(1).to_broadcast([128, 4, 128])
                    elif j == i + 1:
                        mask = trib[:, 1, :].unsqueeze(1).to_broadcast([128, 4, 128])
                    sl.append((kt[:, j * 128:(j + 1) * 128], [Tk], vt[:, j, :], [Tv], mask, [T_c], 0))
                for c in range(2):
                    sl.append((ckT[:, kh, c * 128:(c + 1) * 128], [T_ckT], cvt[:, c, kh * 64:(kh + 1) * 64], [T_cv], None, [], 0))
                hq0 = kh * 8 + gh * 4
                ed = snk[0:64, hq0:hq0 + 4].unsqueeze(2).to_broadcast([64, 4, 128])
                attn_core(K, sl, 512, (qt[:, gh * 4:gh * 4 + 4, i * 128:(i + 1) * 128], [Tq]), ot[:, gh * 4:gh * 4 + 4, i * 128:(i + 1) * 128], To, scale,
                          extra_den=(ed, [T_c]), pools=pools, g4=True)
        P.dma("sp", o1T[kh * 512:(kh + 1) * 512, 1024:1024 + 2048].rearrange("(g d) t -> d g t", d=64), ot[:], so, reads=[To])
    K.end()


def phase_attn_a_and_prep(K):
    K.begin()
    phase_attn_a(K)
    phase_dn_prep(K)
    K.end()


def declare_io(K):
    K.inp("ident", [128, 128])
    K.inp("cond", [2, D])
    K.inp("xp", [NPT * 128, D])
    K.inp("xs", [4096, D])
    K.inp("w_ada", [2, D, 6 * D])
    K.inp("b_ada", [2, 6 * D])
    K.inp("norm_mix", [2, D])
    K.inp("norm_mlp", [2, D])
    K.inp("ab_w_in", [D, 7200])
    K.inp("w_gates", [D, 32])
    K.inp("cache_a_k", [256, 1024])
    K.inp("cache_a_v", [256, 1024])
    K.inp("na_mask", [2, 8, 6, 128, 256])
    K.inp("conv_w", [3, 3072])
    K.inp("alog_dt", [2, 16])
    K.inp("tri", [4, 128, 128])
    K.inp("s0", [2, 8, 128, 128])
    K.inp("onorm", [1, 128])
    K.inp("ab_w_out", [D, D])
    K.inp("w_mlp_in", [2, D, 4 * D])
    K.inp("w_mlp_out", [2, 4 * D, D])
    K.inp("c_w_qkv", [D, 2560])
    K.inp("c_w_out", [D, D])
    K.inp("cache_c_k", [256, 256])
    K.inp("cache_c_v", [256, 256])
    K.inp("c_sink", [1, 32])
    K.inp("final_norm", [1, D])
    K.inp("rope_cos", [128, SEXT * 128])
    K.inp("rope_sin", [128, SEXT * 128])
    K.inp("rope_perm", [128, 128])
    K.outp("nck", [NPT * 128, 256])
    K.outp("ncv", [NPT * 128, 256])
    K.outp("y_p", [NPT * 128, D])
    K.outp("y_s", [2048, D])
    K.outp("nbf", [4, 8, 128, 128])
    K.outp("nbb", [4, 8, 128, 128])
    K.outp("nak", [NPT * 128, 1024])
    K.outp("nav", [NPT * 128, 1024])


def build(stop=99, debug=()):
    nc = bass.Bass("TRN2", target_bir_lowering=False)
    K = Ctx(nc)
    declare_io(K)
    phases = [phase_consts, phase_ada, lambda K: phase_l0_inproj(K, 1), lambda K: phase_l0_inproj(K, 2), phase_attn_a_and_prep, phase_dn_scan, lambda K: phase_mlp(K, 0), phase_l1_inproj, phase_attn_c, lambda K: phase_mlp(K, 1)]
    for i, ph in enumerate(phases):
        if i >= stop:
            break
        ph(K)
    if debug:
        K.begin()
        s = K.getsem()
        for name in debug:
            src = K.dscr[name]
            o = K.outp("dbg_" + name, src.shape, src.dtype)
            nr = src.shape[0]
            step = max(1, min(nr, (1 << 20) // (src.shape[1] * 4)))
            for r0 in range(0, nr, step):
                K.P.dma("sp", o[r0:min(nr, r0 + step)], src[r0:min(nr, r0 + step)], s)
        K.end()
    K.pes.close()
    return nc, K


def na_mask_host(rel_bias, flip):
    out = np.full((2, 8, 6, 128, 256), -30000.0, np.float32)
    qq = np.arange(256)
    kk = np.arange(768)
    for cl in range(2):
        qr = (0 if cl == 0 else 12) + qq // 64
        qc = qq % 64
        kr = (0 if cl == 0 else 8) + kk // 64
        kc = kk % 64
        if flip:
            qr, qc, kr, kc = 63 - qr, 63 - qc, 63 - kr, 63 - kc
        rs = np.clip(qr - 4, 0, 56)
        cs = np.clip(qc - 8, 0, 48)
        vr = (kr[:, None] >= rs[None, :]) & (kr[:, None] < rs[None, :] + 8)
        vc = (kc[:, None] >= cs[None, :]) & (kc[:, None] < cs[None, :] + 16)
        valid = vr & vc
        dr = np.clip(kr[:, None] - qr[None, :] + 7, 0, 14)
        dc = np.clip(kc[:, None] - qc[None, :] + 15, 0, 30)
        for h in range(8):
            b = rel_bias[h][dr, dc]
            m = np.where(valid, b, np.float32(-30000.0)).astype(np.float32)
            out[cl, h] = m.reshape(6, 128, 256)
    return out


def tri_host():
    i = np.arange(128)[:, None]
    j = np.arange(128)[None, :]
    return np.stack([(j <= i), (j >= i), (j < i), (j > i)]).astype(np.float32)


def rope_host(flip):
    nf = 16
    inv = (10000.0 ** (-np.arange(nf, dtype=np.float32) / nf)).astype(np.float32)
    tloc = np.arange(SEXT * 128)
    tok = (4095 - tloc) if flip else tloc
    pos = np.stack([tok // 64, tok % 64], 0).astype(np.float32)
    d = np.arange(128) % 64
    a, b, fidx = d // 32, (d % 32) // 16, d % 16
    ang = pos[a, :] * inv[fidx][:, None]
    cos = np.cos(ang).astype(np.float32)
    sin = np.sin(ang).astype(np.float32)
    perm = np.zeros((128, 128), np.float32)
    for m in range(128):
        bm = (m % 32) // 16
        partner = m + 16 if bm == 0 else m - 16
        perm[partner, m] = -1.0 if bm == 0 else 1.0
    return cos, sin, perm


def host_inputs(inputs, c):
    seq, flip = c // 2, c % 2
    f = lambda a: np.ascontiguousarray(a, dtype=np.float32)
    xp = inputs["x_prompt"][4 * c:4 * c + 4]
    xs = inputs["x_sample"][seq]
    if flip:
        xp = xp[:, ::-1]
        xs = xs[::-1]
    wg = inputs["ab_w_in"][0][:, 7168:7200]
    if flip:
        wg = np.concatenate([wg[:, 8:16], wg[:, 0:8], wg[:, 24:32], wg[:, 16:24]], axis=1)
    m = {
        "ident": np.eye(128, dtype=np.float32),
        "cond": f(np.stack([inputs["c_ctx"], inputs["c"][seq]])),
        "xp": f(xp.reshape(NPT * 128, D)),
        "xs": f(xs),
        "w_ada": inputs["w_ada"], "b_ada": inputs["b_ada"],
        "norm_mix": inputs["norm_mix"], "norm_mlp": inputs["norm_mlp"],
        "ab_w_in": inputs["ab_w_in"][0], "w_gates": f(wg),
        "cache_a_k": f(inputs["cache_a_k"][seq, 0].reshape(256, 1024)),
        "cache_a_v": f(inputs["cache_a_v"][seq, 0].reshape(256, 1024)),
        "na_mask": na_mask_host(inputs["a_rel_bias"][0], flip),
        "ab_w_out": inputs["ab_w_out"][0], "w_mlp_in": inputs["w_mlp_in"], "w_mlp_out": inputs["w_mlp_out"],
        "c_w_qkv": inputs["c_w_qkv"][0], "c_w_out": inputs["c_w_out"][0],
        "cache_c_k": f(inputs["cache_c_k"][seq, 0].reshape(256, 256)), "cache_c_v": f(inputs["cache_c_v"][seq, 0].reshape(256, 256)),
        "c_sink": f(inputs["c_sink"][0].reshape(1, 32)), "final_norm": f(inputs["final_norm"].reshape(1, D)),
        "rope_cos": rope_host(flip)[0], "rope_sin": rope_host(flip)[1], "rope_perm": rope_host(flip)[2],
        "conv_w": f(inputs["b_conv"][0][::-1] if flip else inputs["b_conv"][0]),
        "alog_dt": f(np.stack([(inputs["b_a_log"][0][::-1] if flip else inputs["b_a_log"][0]).reshape(16),
                               (inputs["b_dt_bias"][0][::-1] if flip else inputs["b_dt_bias"][0]).reshape(16)])),
        "tri": tri_host(),
        "s0": f(np.stack([inputs["state_b_bwd"][seq, 0], inputs["state_b_fwd"][seq, 0]]) if flip else
                np.stack([inputs["state_b_fwd"][seq, 0], inputs["state_b_bwd"][seq, 0]])),
        "onorm": f(inputs["b_out_norm"][0].reshape(1, 128)),
    }
    return m


_CACHE = {}


def kernel(**inputs):
    inputs = {k: np.asarray(v) for k, v in inputs.items()}
    if "nc" not in _CACHE:
        _CACHE["nc"] = build()
    nc, K = _CACHE["nc"]
    maps = [host_inputs(inputs, c) for c in range(8)]
    res = run_bass_kernel_spmd(nc, maps, core_ids=list(range(8))).results
    f32 = np.float32
    y_p = np.zeros((32, 256, D), f32)
    y_s = np.zeros((4, 4096, D), f32)
    nak = np.zeros((32, 1, 256, 8, 128), f32)
    nav = np.zeros((32, 1, 256, 8, 128), f32)
    nbf = np.zeros((32, 1, 8, 128, 128), f32)
    nbb = np.zeros((32, 1, 8, 128, 128), f32)
    nck = np.zeros((32, 1, 256, 4, 64), f32)
    ncv = np.zeros((32, 1, 256, 4, 64), f32)
    for c in range(8):
        seq, flip = c // 2, c % 2
        r = {k: np.asarray(v, dtype=f32) for k, v in res[c].items()}
        fl = (lambda a: a[:, ::-1]) if flip else (lambda a: a)
        sl = slice(4 * c, 4 * c + 4)
        y_p[sl] = fl(r["y_p"].reshape(4, 256, D))
        if flip:
            y_s[seq, 2048:4096] = r["y_s"][::-1]
        else:
            y_s[seq, 0:2048] = r["y_s"]
        nak[sl, 0] = fl(r["nak"].reshape(4, 256, 8, 128))
        nav[sl, 0] = fl(r["nav"].reshape(4, 256, 8, 128))
        nck[sl, 0] = fl(r["nck"].reshape(4, 256, 4, 64))
        ncv[sl, 0] = fl(r["ncv"].reshape(4, 256, 4, 64))
        if flip:
            nbf[sl, 0], nbb[sl, 0] = r["nbb"], r["nbf"]
        else:
            nbf[sl, 0], nbb[sl, 0] = r["nbf"], r["nbb"]
    return (y_p, y_s, nak, nav, nbf, nbb, nck, ncv)
```
